# Optimizing a Trainium2 kernel written in Bass

```python
import math
import jax
import jax.numpy as jnp
from jax import lax
import numpy as np


D_MODEL = 2048
BATCH = 2
SEQ = 8192
DEPTH = 2
DEC_BATCH = 8
DEC_SEQ = 32
PAST_LEN = 4096

CHUNK = 64
Q_BLOCK = 128
N_EVEN = (DEPTH + 1) // 2
N_ODD = DEPTH // 2
H_A = 8
DK_A = 128
DV_A = 128
W_A = H_A * DV_A
CONV_W = 4
H_B = 8
DK_B = 128
DV_B = 128
W_B = H_B * DV_B
H_C = 8
DH_C = 128
W_C = H_C * DH_C
GC_D = 16
G_D = 64
P_D = 64
W_D = G_D * GC_D
EV_SIZES = (3 * W_A, W_A, H_A, H_A, W_B, W_B, W_B, W_B, W_B, H_B, H_B)
EV_IN = sum(EV_SIZES)
EV_OUT = W_A + W_B
OD_SIZES = (W_C, W_C, W_C, W_C, W_D, W_D)
OD_IN = sum(OD_SIZES)
OD_OUT = W_C + W_D
DEEPNORM_ALPHA = (2 * DEPTH) ** 0.25
DEEPNORM_BETA = (8 * DEPTH) ** -0.25
LN_EPS = 1e-5
NORM_EPS = 1e-6

kernel_name = 'hybrid_streaming_gdn_mlstm_sb_s5_step'


def _split(y, sizes):
    idx = [int(i) for i in np.cumsum(sizes)[:-1]]
    return jnp.split(y, idx, axis=-1)


def _layer_norm(x, g, b):
    xf = x.astype(jnp.float32)
    mu = jnp.mean(xf, -1, keepdims=True)
    var = jnp.mean(jnp.square(xf - mu), -1, keepdims=True)
    return ((xf - mu) * lax.rsqrt(var + LN_EPS) * g + b).astype(x.dtype)


def _head_rms_norm(x, g):
    return x * lax.rsqrt(jnp.mean(x * x, -1, keepdims=True) + NORM_EPS) * g


def _l2norm(x):
    return x * lax.rsqrt(jnp.sum(x * x, -1, keepdims=True) + NORM_EPS)


def _causal_conv(x_ext, w):
    t = x_ext.shape[1] - (CONV_W - 1)
    return sum(x_ext[:, i:i + t] * w[i] for i in range(CONV_W))


def _to_chunks(x, L):
    b, t, h = x.shape[:3]
    x = x.reshape((b, t // L, L, h) + x.shape[3:])
    return jnp.moveaxis(x, (1, 3), (0, 2))


def _from_chunks(x):
    x = jnp.moveaxis(x, (0, 2), (1, 3))
    return x.reshape((x.shape[0], x.shape[1] * x.shape[2]) + x.shape[3:])


def _gated_delta_chunked(q, k, v, beta, g, S0):
    L = min(CHUNK, q.shape[1])
    qc, kc, vc = _to_chunks(q, L), _to_chunks(k, L), _to_chunks(v, L)
    bc = _to_chunks(beta, L)
    gc = jnp.cumsum(_to_chunks(g, L), axis=-1)
    tri_incl = jnp.tril(jnp.ones((L, L), bool))
    tri_strict = jnp.tril(jnp.ones((L, L), bool), -1)
    decay = jnp.exp(jnp.where(tri_incl, gc[..., :, None] - gc[..., None, :], -jnp.inf))
    kk = jnp.einsum('nbhtd,nbhsd->nbhts', kc, kc)
    m_mat = jnp.where(tri_strict, bc[..., None] * kk * decay, 0.0)
    eye = jnp.eye(L, dtype=q.dtype)
    gam = jnp.exp(gc)
    rhs = jnp.concatenate([vc * bc[..., None], kc * (bc * gam)[..., None]], axis=-1)
    sol = lax.linalg.triangular_solve(eye + m_mat, rhs, left_side=True, lower=True)
    u, w = sol[..., :DV_A], sol[..., DV_A:]
    qk = jnp.einsum('nbhtd,nbhsd->nbhts', qc, kc) * decay
    q_dec = qc * gam[..., None]
    g_last = gc[..., -1]
    k_dec = kc * jnp.exp(g_last[..., None] - gc)[..., None]

    def step(S, xs):
        u_, w_, qk_, qd_, kd_, gl_ = xs
        e = u_ - jnp.einsum('bhtk,bhkv->bhtv', w_, S)
        o = jnp.einsum('bhtk,bhkv->bhtv', qd_, S) + jnp.einsum('bhts,bhsv->bhtv', qk_, e)
        S = jnp.exp(gl_)[..., None, None] * S + jnp.einsum('bhsk,bhsv->bhkv', kd_, e)
        return S, o

    S_fin, o = lax.scan(step, S0, (u, w, qk, q_dec, k_dec, g_last))
    return _from_chunks(o), S_fin


def _mlstm_chunked(q, k, v, ig, logf, C0, n0, m0):
    L = min(CHUNK, q.shape[1])
    qc, kc, vc = _to_chunks(q, L), _to_chunks(k, L), _to_chunks(v, L)
    igc = _to_chunks(ig, L)
    bcum = jnp.cumsum(_to_chunks(logf, L), axis=-1)
    tri = jnp.tril(jnp.ones((L, L), bool))
    d_log = jnp.where(tri, bcum[..., :, None] - bcum[..., None, :] + igc[..., None, :], -jnp.inf)
    m_intra = jnp.max(d_log, axis=-1)
    qk = jnp.einsum('nbhtd,nbhsd->nbhts', qc, kc)

    def step(carry, xs):
        C, n, m = carry
        q_, k_, v_, b_, dl_, mi_, qk_ = xs
        m_t = jnp.maximum(b_ + m[..., None], mi_)
        p = jnp.exp(dl_ - m_t[..., None])
        inter = jnp.exp(b_ + m[..., None] - m_t)
        pqk = p * qk_
        num = inter[..., None] * jnp.einsum('bhtk,bhkv->bhtv', q_, C) + jnp.einsum('bhts,bhsv->bhtv', pqk, v_)
        den = inter * jnp.einsum('bhtk,bhk->bht', q_, n) + jnp.sum(pqk, axis=-1)
        h = num / jnp.maximum(jnp.abs(den), jnp.exp(-m_t))[..., None]
        p_last, i_last = p[..., -1, :], inter[..., -1]
        C = i_last[..., None, None] * C + jnp.einsum('bhs,bhsk,bhsv->bhkv', p_last, k_, v_)
        n = i_last[..., None] * n + jnp.einsum('bhs,bhsk->bhk', p_last, k_)
        return (C, n, m_t[..., -1]), h

    (C, n, m), h = lax.scan(step, (C0, n0, m0), (qc, kc, vc, bcum, d_log, m_intra, qk))
    return _from_chunks(h), C, n, m


def _sb_block(qb, q_pos, k, v, k_pos):
    z = jnp.einsum('bqhd,bkhd->bhqk', qb, k) * (DH_C ** -0.5)
    causal = k_pos[None, :] < q_pos[:, None]
    l1m = jnp.where(causal, jax.nn.log_sigmoid(-z), 0.0)
    between = lax.cumsum(l1m, axis=3, reverse=True) - l1m
    att = jnp.where(causal, jnp.exp(jax.nn.log_sigmoid(z) + between), 0.0)
    return jnp.einsum('bhqk,bkhd->bqhd', att, v)


def _sb_prompt(q, k, v):
    b, t, h, d = q.shape
    nblk = t // Q_BLOCK
    k_pos = jnp.arange(t)
    qb = jnp.moveaxis(q.reshape(b, nblk, Q_BLOCK, h, d), 1, 0)
    pos = jnp.arange(t).reshape(nblk, Q_BLOCK)
    o = lax.map(lambda a: _sb_block(a[0], a[1], k, v, k_pos), (qb, pos))
    return jnp.moveaxis(o, 0, 1).reshape(b, t, h, d)


def _sb_sample(q, k, v, k_cache, v_cache):
    past = k_cache.shape[1]
    t = q.shape[1]
    k_all = jnp.concatenate([k_cache.astype(jnp.float32), k], axis=1)
    v_all = jnp.concatenate([v_cache.astype(jnp.float32), v], axis=1)
    return _sb_block(q, past + jnp.arange(t), k_all, v_all, jnp.arange(past + t))


def _cplx_combine(e1, e2):
    a1r, a1i, b1r, b1i = e1
    a2r, a2i, b2r, b2i = e2
    return (a2r * a1r - a2i * a1i, a2r * a1i + a2i * a1r,
            a2r * b1r - a2i * b1i + b2r, a2r * b1i + a2i * b1r + b2i)


def _s5(u, h_re0, h_im0, lam_re, lam_im, b_re, b_im, c_re, c_im, d, log_dt):
    bsz, t, _ = u.shape
    dt = jnp.exp(log_dt)[:, None]
    mag = jnp.exp(lam_re * dt)
    a_re, a_im = mag * jnp.cos(lam_im * dt), mag * jnp.sin(lam_im * dt)
    den = lam_re * lam_re + lam_im * lam_im
    f_re = ((a_re - 1.0) * lam_re + a_im * lam_im) / den
    f_im = (a_im * lam_re - (a_re - 1.0) * lam_im) / den
    bb_re = f_re[..., None] * b_re - f_im[..., None] * b_im
    bb_im = f_re[..., None] * b_im + f_im[..., None] * b_re
    ug = u.reshape(bsz, t, G_D, GC_D)
    bu_re = jnp.einsum('btgc,gpc->btgp', ug, bb_re)
    bu_im = jnp.einsum('btgc,gpc->btgp', ug, bb_im)
    bu_re = bu_re.at[:, 0].add(a_re * h_re0 - a_im * h_im0)
    bu_im = bu_im.at[:, 0].add(a_re * h_im0 + a_im * h_re0)
    ar = jnp.broadcast_to(a_re, (1, t, G_D, P_D))
    ai = jnp.broadcast_to(a_im, (1, t, G_D, P_D))
    _, _, h_re, h_im = lax.associative_scan(_cplx_combine, (ar, ai, bu_re, bu_im), axis=1)
    y = jnp.einsum('btgp,gcp->btgc', h_re, c_re) - jnp.einsum('btgp,gcp->btgc', h_im, c_im)
    y = y.reshape(bsz, t, W_D) + d * u
    return y, h_re[:, -1], h_im[:, -1]


def _even_layer(x, S0, conv0, C0, n0, m0, w_in, conv_w, a_log, dt_bias, norm_a,
                ig_bias, fg_bias, norm_b, w_out, ln_g, ln_b):
    f32 = jnp.float32
    bsz, t, _ = x.shape
    proj = jnp.einsum('btd,de->bte', x, w_in).astype(f32)
    qkv_a, z_a, b_a, a_a, q_b, k_b, v_b, o_b, z_b, i_b, f_b = _split(proj, EV_SIZES)
    conv_in = jnp.concatenate([conv0.astype(f32), qkv_a], axis=1)
    conv_new = conv_in[:, -(CONV_W - 1):]
    qkv = jax.nn.silu(_causal_conv(conv_in, conv_w.astype(f32)))
    q_a, k_a, v_a = jnp.split(qkv, 3, axis=-1)
    q_a = _l2norm(q_a.reshape(bsz, t, H_A, DK_A)) * (DK_A ** -0.5)
    k_a = _l2norm(k_a.reshape(bsz, t, H_A, DK_A))
    v_a = v_a.reshape(bsz, t, H_A, DV_A)
    beta = jax.nn.sigmoid(b_a)
    g = -jnp.exp(a_log.astype(f32)) * jax.nn.softplus(a_a + dt_bias)
    o_a, S_new = _gated_delta_chunked(q_a, k_a, v_a, beta, g, S0.astype(f32))
    y_a = _head_rms_norm(o_a, norm_a).reshape(bsz, t, W_A) * jax.nn.silu(z_a)
    qh = q_b.reshape(bsz, t, H_B, DK_B)
    kh = k_b.reshape(bsz, t, H_B, DK_B) * (DK_B ** -0.5)
    vh = v_b.reshape(bsz, t, H_B, DV_B)
    h, C_new, n_new, m_new = _mlstm_chunked(qh, kh, vh, i_b + ig_bias, jax.nn.log_sigmoid(f_b + fg_bias),
                                            C0.astype(f32), n0.astype(f32), m0.astype(f32))
    h = jax.nn.sigmoid(o_b).reshape(bsz, t, H_B, DV_B) * h
    y_b = _head_rms_norm(h, norm_b.reshape(H_B, DV_B)).reshape(bsz, t, W_B) * jax.nn.silu(z_b)
    out = jnp.einsum('bte,ed->btd', jnp.concatenate([y_a, y_b], axis=-1).astype(x.dtype), w_out)
    x = _layer_norm(DEEPNORM_ALPHA * x + out, ln_g, ln_b)
    dt_ = x.dtype
    return x, S_new.astype(dt_), conv_new.astype(dt_), C_new.astype(dt_), n_new.astype(dt_), m_new.astype(dt_)


def _odd_layer(x, k_cache, v_cache, h_re0, h_im0, w_in, lam_re, lam_im, b_re, b_im, c_re, c_im,
               d, log_dt, w_glu, b_glu, w_out, ln_g, ln_b):
    f32 = jnp.float32
    bsz, t, _ = x.shape
    proj = jnp.einsum('btd,de->bte', x, w_in).astype(f32)
    q_c, k_c, v_c, z_c, u_d, z_d = _split(proj, OD_SIZES)
    q = q_c.reshape(bsz, t, H_C, DH_C)
    k = k_c.reshape(bsz, t, H_C, DH_C)
    v = v_c.reshape(bsz, t, H_C, DH_C)
    if k_cache is None:
        o_c = _sb_prompt(q, k, v)
    else:
        o_c = _sb_sample(q, k, v, k_cache, v_cache)
    y_c = o_c.reshape(bsz, t, W_C) * jax.nn.silu(z_c)
    y_s, h_re, h_im = _s5(u_d, h_re0.astype(f32), h_im0.astype(f32), lam_re, lam_im, b_re, b_im,
                          c_re, c_im, d, log_dt)
    y_s = jax.nn.gelu(y_s)
    y_s = y_s * jax.nn.sigmoid(jnp.einsum('bte,ef->btf', y_s, w_glu.astype(f32)) + b_glu)
    y_d = y_s * jax.nn.silu(z_d)
    out = jnp.einsum('bte,ed->btd', jnp.concatenate([y_c, y_d], axis=-1).astype(x.dtype), w_out)
    x = _layer_norm(DEEPNORM_ALPHA * x + out, ln_g, ln_b)
    dt_ = x.dtype
    return x, k.astype(dt_), v.astype(dt_), h_re.astype(dt_), h_im.astype(dt_)


def setup_inputs(seed: int = 0) -> dict:
    key = jax.random.key(seed)
    keys = list(jax.random.split(key, 40))
    f32 = jnp.float32

    def nrm(i, shape, scale):
        return jax.random.normal(keys[i], shape, f32) * scale

    def uni(i, shape, lo, hi):
        return jax.random.uniform(keys[i], shape, f32, lo, hi)

    dt_a = jnp.exp(uni(10, (N_EVEN, H_A), math.log(1e-3), math.log(1e-1)))
    n_idx = jnp.arange(P_D, dtype=f32)
    return {
        'x_prompt': nrm(0, (BATCH, SEQ, D_MODEL), 1.0),
        'x_sample': nrm(1, (DEC_BATCH, DEC_SEQ, D_MODEL), 1.0),
        'state_delta_S': nrm(2, (N_EVEN, DEC_BATCH, H_A, DK_A, DV_A), 0.1),
        'state_delta_conv': nrm(3, (N_EVEN, DEC_BATCH, CONV_W - 1, 3 * W_A), 1.0),
        'state_mlstm_C': nrm(4, (N_EVEN, DEC_BATCH, H_B, DK_B, DV_B), 0.1),
        'state_mlstm_n': nrm(5, (N_EVEN, DEC_BATCH, H_B, DK_B), 0.1),
        'state_mlstm_m': nrm(6, (N_EVEN, DEC_BATCH, H_B), 1.0),
        'cache_sb_k': nrm(7, (N_ODD, DEC_BATCH, PAST_LEN, H_C, DH_C), 1.0),
        'cache_sb_v': nrm(8, (N_ODD, DEC_BATCH, PAST_LEN, H_C, DH_C), 1.0),
        'state_s5_re': nrm(9, (N_ODD, DEC_BATCH, G_D, P_D), 0.1),
        'state_s5_im': nrm(11, (N_ODD, DEC_BATCH, G_D, P_D), 0.1),
        'ev_w_in': nrm(12, (N_EVEN, D_MODEL, EV_IN), D_MODEL ** -0.5),
        'ev_conv_w': nrm(13, (N_EVEN, CONV_W, 3 * W_A), CONV_W ** -0.5),
        'ev_a_log': jnp.log(uni(14, (N_EVEN, H_A), 1.0, 16.0)),
        'ev_dt_bias': dt_a + jnp.log(-jnp.expm1(-dt_a)),
        'ev_norm_a': 1.0 + nrm(15, (N_EVEN, DV_A), 0.01),
        'ev_ig_bias': nrm(16, (N_EVEN, H_B), 0.1),
        'ev_fg_bias': 3.0 + nrm(17, (N_EVEN, H_B), 0.5),
        'ev_norm_b': 1.0 + nrm(18, (N_EVEN, W_B), 0.01),
        'ev_w_out': nrm(19, (N_EVEN, EV_OUT, D_MODEL), EV_OUT ** -0.5 * DEEPNORM_BETA),
        'ev_ln_g': 1.0 + nrm(20, (N_EVEN, D_MODEL), 0.01),
        'ev_ln_b': nrm(21, (N_EVEN, D_MODEL), 0.01),
        'od_w_in': nrm(22, (N_ODD, D_MODEL, OD_IN), D_MODEL ** -0.5),
        'od_lam_re': -0.5 + nrm(23, (N_ODD, G_D, P_D), 0.01),
        'od_lam_im': math.pi * n_idx + nrm(24, (N_ODD, G_D, P_D), 0.01),
        'od_b_re': nrm(25, (N_ODD, G_D, P_D, GC_D), (2 * GC_D) ** -0.5),
        'od_b_im': nrm(26, (N_ODD, G_D, P_D, GC_D), (2 * GC_D) ** -0.5),
        'od_c_re': nrm(27, (N_ODD, G_D, GC_D, P_D), P_D ** -0.5),
        'od_c_im': nrm(28, (N_ODD, G_D, GC_D, P_D), P_D ** -0.5),
        'od_d': nrm(29, (N_ODD, W_D), 0.5),
        'od_log_dt': uni(30, (N_ODD, G_D), math.log(1e-3), math.log(1e-1)),
        'od_w_glu': nrm(31, (N_ODD, W_D, W_D), W_D ** -0.5),
        'od_b_glu': nrm(32, (N_ODD, W_D), 0.01),
        'od_w_out': nrm(33, (N_ODD, OD_OUT, D_MODEL), OD_OUT ** -0.5 * DEEPNORM_BETA),
        'od_ln_g': 1.0 + nrm(34, (N_ODD, D_MODEL), 0.01),
        'od_ln_b': nrm(35, (N_ODD, D_MODEL), 0.01),
    }


def reference(x_prompt, x_sample, state_delta_S, state_delta_conv, state_mlstm_C, state_mlstm_n,
              state_mlstm_m, cache_sb_k, cache_sb_v, state_s5_re, state_s5_im,
              ev_w_in, ev_conv_w, ev_a_log, ev_dt_bias, ev_norm_a, ev_ig_bias, ev_fg_bias, ev_norm_b,
              ev_w_out, ev_ln_g, ev_ln_b,
              od_w_in, od_lam_re, od_lam_im, od_b_re, od_b_im, od_c_re, od_c_im, od_d, od_log_dt,
              od_w_glu, od_b_glu, od_w_out, od_ln_g, od_ln_b):
    f32 = jnp.float32
    nb = x_prompt.shape[0]
    xp, xs = x_prompt, x_sample
    dS_p, dS_s, dc_p, dc_s, mC_p, mC_s, mn_p, mn_s, mm_p, mm_s = ([] for _ in range(10))
    sk_p, sk_s, sv_p, sv_s, sr_p, sr_s, si_p, si_s = ([] for _ in range(8))
    for layer in range(DEPTH):
        j = layer // 2
        if layer % 2 == 0:
            w = (ev_w_in[j], ev_conv_w[j], ev_a_log[j], ev_dt_bias[j], ev_norm_a[j], ev_ig_bias[j],
                 ev_fg_bias[j], ev_norm_b[j], ev_w_out[j], ev_ln_g[j], ev_ln_b[j])
            xp, S, cv, C, n, m = _even_layer(
                xp, jnp.zeros((nb, H_A, DK_A, DV_A), f32), jnp.zeros((nb, CONV_W - 1, 3 * W_A), f32),
                jnp.zeros((nb, H_B, DK_B, DV_B), f32), jnp.zeros((nb, H_B, DK_B), f32),
                jnp.zeros((nb, H_B), f32), *w)
            dS_p.append(S)
            dc_p.append(cv)
            mC_p.append(C)
            mn_p.append(n)
            mm_p.append(m)
            xs, S, cv, C, n, m = _even_layer(
                xs, state_delta_S[j], state_delta_conv[j], state_mlstm_C[j], state_mlstm_n[j],
                state_mlstm_m[j], *w)
            dS_s.append(S)
            dc_s.append(cv)
            mC_s.append(C)
            mn_s.append(n)
            mm_s.append(m)
        else:
            w = (od_w_in[j], od_lam_re[j], od_lam_im[j], od_b_re[j], od_b_im[j], od_c_re[j], od_c_im[j],
                 od_d[j], od_log_dt[j], od_w_glu[j], od_b_glu[j], od_w_out[j], od_ln_g[j], od_ln_b[j])
            xp, k, v, hr, hi = _odd_layer(xp, None, None, jnp.zeros((nb, G_D, P_D), f32),
                                          jnp.zeros((nb, G_D, P_D), f32), *w)
            sk_p.append(k)
            sv_p.append(v)
            sr_p.append(hr)
            si_p.append(hi)
            xs, k, v, hr, hi = _odd_layer(xs, cache_sb_k[j], cache_sb_v[j], state_s5_re[j], state_s5_im[j], *w)
            sk_s.append(k)
            sv_s.append(v)
            sr_s.append(hr)
            si_s.append(hi)
    return (xp, xs,
            jnp.stack(dS_p), jnp.stack(dS_s), jnp.stack(dc_p), jnp.stack(dc_s),
            jnp.stack(mC_p), jnp.stack(mC_s), jnp.stack(mn_p), jnp.stack(mn_s),
            jnp.stack(mm_p), jnp.stack(mm_s),
            jnp.stack(sk_p), jnp.stack(sk_s), jnp.stack(sv_p), jnp.stack(sv_s),
            jnp.stack(sr_p), jnp.stack(sr_s), jnp.stack(si_p), jnp.stack(si_s))
```

```python
import math
from contextlib import ExitStack

import numpy as np
import ml_dtypes
import concourse.bass as bass
import concourse.mybir as mybir
from concourse.bass_utils import run_bass_kernel_spmd

F32 = mybir.dt.float32
BF16 = mybir.dt.bfloat16
AF = mybir.ActivationFunctionType
ALU = mybir.AluOpType
AX = mybir.AxisListType
ENGS = ("pe", "act", "dve", "pool", "sp")
NCORES = 8

D = 2048
KC = D // 128
SEQ = 8192
NB = 2
DB = 8
DS = 32
PAST = 4096
NTP = NB * SEQ
NTS = DB * DS
NTOK = NTP + NTS
TOKC = NTOK // NCORES
ALPHA = (2 * 2) ** 0.25
NEG = -30000.0


class Prog:
    def __init__(self, nc, stack):
        self.nc = nc
        self.stack = stack
        self.q = {e: [] for e in ENGS}
        self.cnt = {e: 0 for e in ENGS}
        self.sem = {e: stack.enter_context(nc.semaphore("sem_" + e)) for e in ENGS}
        self.waited = {e: {} for e in ENGS}
        self.lastw = {}
        self.readers = {}
        self.dsem = {}
        self.dcnt = {}
        self.ninstr = 0

    def _deps(self, reads, writes):
        deps = []
        for k in reads:
            t = self.lastw.get(k)
            if t is not None:
                deps.append(t)
        for k in writes:
            t = self.lastw.get(k)
            if t is not None:
                deps.append(t)
            deps.extend(self.readers.get(k, ()))
        return deps

    def _emit_waits(self, eng, deps, skip_self=False):
        need = {}
        for (sid, sem, val) in deps:
            if skip_self and sid == eng:
                continue
            if self.waited[eng].get(sid, 0) >= val:
                continue
            if need.get(sid, (None, 0))[1] < val:
                need[sid] = (sem, val)
        for sid, (sem, val) in need.items():
            self.waited[eng][sid] = val
            self.q[eng].append(lambda e, sem=sem, val=val: e.wait_ge(sem, val))
            self.ninstr += 1

    def _commit(self, tok, reads, writes):
        for k in writes:
            self.lastw[k] = tok
            self.readers[k] = []
        for k in reads:
            self.readers.setdefault(k, []).append(tok)

    @staticmethod
    def _bank(k):
        if isinstance(k, str) and len(k) >= 2 and k[0] == "p" and (k[1].isupper() or k[:2] in ("pj", "pg")):
            return k[:3] if k[:2] == "pj" else k[:2]
        return None

    def _norm(self, reads, writes):
        r2, w2 = [], []
        for k in reads:
            bk = self._bank(k)
            if bk is None:
                r2.append(k)
            elif bk not in w2:
                w2.append(bk)
        for k in writes:
            bk = self._bank(k)
            k2 = k if bk is None else bk
            if k2 not in w2:
                w2.append(k2)
        return r2, w2

    def op(self, eng, fn, reads=(), writes=(), skip_self=False):
        reads, writes = self._norm(reads, writes)
        deps = self._deps(reads, writes)
        self._emit_waits(eng, deps, skip_self=skip_self)
        self.cnt[eng] += 1
        sem = self.sem[eng]
        tok = (eng, sem, self.cnt[eng])
        self.q[eng].append(lambda e, fn=fn, sem=sem: fn(e).then_inc(sem, 1))
        self.ninstr += 1
        self._commit(tok, reads, writes)
        return tok

    def dma(self, eng, out, in_, reads=(), writes=(), slot=None, **kw):
        assert slot is not None
        if slot not in self.dsem:
            self.dsem[slot] = self.stack.enter_context(self.nc.semaphore("dsem_%d" % len(self.dsem)))
            self.dcnt[slot] = 0
        deps = self._deps(reads, writes)
        self._emit_waits(eng, deps)
        self.dcnt[slot] += 16
        sem = self.dsem[slot]
        tok = (("d", slot), sem, self.dcnt[slot])
        self.q[eng].append(
            lambda e, out=out, in_=in_, sem=sem, kw=kw: e.dma_start(out=out, in_=in_, **kw).then_inc(sem, 16))
        self.ninstr += 1
        self._commit(tok, reads, writes)
        return tok

    def finalize(self):
        nc = self.nc
        alltok = list(self.lastw.values())
        for ts in self.readers.values():
            alltok.extend(ts)
        self._emit_waits("sp", alltok)
        engmap = {"pe": "tensor", "act": "scalar", "dve": "vector", "pool": "gpsimd", "sp": "sync"}
        with nc.Block() as block:
            for e in ENGS:
                lst = self.q[e]
                if not lst:
                    continue

                def body(engine, lst=lst):
                    for f in lst:
                        f(engine)
                getattr(block, engmap[e])(body)


class B:
    def __init__(self, nc, st):
        self.nc = nc
        self.st = st
        self.P = Prog(nc, st)
        self.rr = 0

    def sb(self, name, shape, dt=F32):
        return self.st.enter_context(self.nc.sbuf_tensor(name, list(shape), dt))

    def ps(self, name, shape=(128, 512), dt=F32):
        return self.st.enter_context(self.nc.psum_tensor(name, list(shape), dt))

    def din(self, name, shape, dt=F32):
        return self.nc.dram_tensor(name, list(shape), dt, kind="ExternalInput").ap()

    def dout(self, name, shape, dt=F32):
        return self.nc.dram_tensor(name, list(shape), dt, kind="ExternalOutput").ap()

    def mm(self, out, lhsT, rhs, r, w, start=True, stop=True):
        self.P.op("pe", lambda e: e.matmul(out, lhsT=lhsT, rhs=rhs, start=start, stop=stop, skip_group_check=True),
                  reads=r, writes=w, skip_self=True)

    def tr(self, out, in_, ident, r, w):
        self.P.op("pe", lambda e: e.transpose(out=out, in_=in_, identity=ident), reads=r, writes=w, skip_self=True)

    def act(self, out, in_, func, r, w, bias=None, scale=None, accum_out=None):
        kw = {}
        if bias is not None:
            kw["bias"] = bias
        if scale is not None:
            kw["scale"] = scale
        if accum_out is not None:
            kw["accum_out"] = accum_out
        self.P.op("act", lambda e: e.activation(out=out, in_=in_, func=func, **kw), reads=r, writes=w)

    def ts(self, eng, out, in0, s1, op0, r, w, s2=None, op1=None, accum_out=None):
        kw = {}
        if op1 is not None:
            kw["op1"] = op1
        if accum_out is not None:
            kw["accum_out"] = accum_out
        self.P.op(eng, lambda e: e.tensor_scalar(out=out, in0=in0, scalar1=s1, scalar2=s2, op0=op0, **kw),
                  reads=r, writes=w)

    def tt(self, eng, out, in0, in1, op, r, w):
        self.P.op(eng, lambda e: e.tensor_tensor(out=out, in0=in0, in1=in1, op=op), reads=r, writes=w)

    def stt(self, out, in0, scalar, in1, op0, op1, r, w, accum_out=None):
        kw = {}
        if accum_out is not None:
            kw["accum_out"] = accum_out
        self.P.op("dve", lambda e: e.scalar_tensor_tensor(out=out, in0=in0, scalar=scalar, in1=in1, op0=op0, op1=op1, **kw),
                  reads=r, writes=w)

    def cp(self, eng, out, in_, r, w):
        if eng == "act":
            self.P.op("act", lambda e: e.copy(out=out, in_=in_), reads=r, writes=w)
        else:
            self.P.op(eng, lambda e: e.tensor_copy(out=out, in_=in_), reads=r, writes=w)

    def memset(self, eng, ap, val, w):
        self.P.op(eng, lambda e: e.memset(ap, val), writes=w)

    def dma(self, eng, out, in_, r, w, slot, **kw):
        self.P.dma(eng, out, in_, reads=r, writes=w, slot=slot, **kw)


def _consts_np():
    c = np.zeros((128, 8, 128), np.float32)
    i = np.arange(128)
    c[:, 0, :] = np.eye(128)
    c[:, 1, :] = (i[:, None] <= i[None, :])
    c[:, 2, :] = np.where(i[None, :] < i[:, None], NEG, 0.0)
    c[:, 3, :] = np.where(i[None, :] > i[:, None], NEG, 0.0)
    c[:, 4, :] = 1.0
    c[:, 5, :] = (i[:, None] == 127)
    c[:, 6, :] = (i[:, None] == 31)
    c[:, 7, :] = (i[:, None] >= i[None, :])
    return c


class _Stop(Exception):
    pass


def build_phase1(ntiles_prompt=32, do_sample=True, debug=0):
    nc = bass.Bass("TRN2", target_bir_lowering=False)
    with ExitStack() as st:
        b = B(nc, st)
        P = b.P
        xT = b.din("xT", [D, NTOK])
        w1 = b.din("w1", [D, 1152])
        wg = b.din("wg", [128, KC * 4])
        convw = b.din("convw", [128, 12])
        scal = b.din("scal", [128, 8])
        consts = b.din("consts", [128, 8, 128])
        S0 = b.din("S0", [DB, 128, 128])
        conv0 = b.din("conv0", [128, DB, 3, 3])
        C0 = b.din("C0", [DB, 128, 128])
        n0 = b.din("n0", [128, DB])
        m0 = b.din("m0", [128, DB])
        yT = b.dout("yT", [256, NTOK], BF16)
        oS = b.dout("oS", [NB + DB, 128, 128])
        oconv = b.dout("oconv", [128, NB + DB, 3, 3])
        oC = b.dout("oC", [NB + DB, 128, 128])
        on = b.dout("on", [128, NB + DB])
        om = b.dout("om", [1, NB + DB])

        cst = b.sb("cst", [128, 8, 128])
        ident, triU, NEGU, NEGL, ones, SEL127, SEL31 = (cst[:, i, :] for i in range(7))
        w1b = b.sb("w1b", [128, KC, 1152], BF16)
        wgb = b.sb("wgb", [128, KC, 4], BF16)
        wst = b.sb("wst", [128, 2, 1152])
        cw = b.sb("cw", [128, 12])
        sc = b.sb("sc", [128, 8])
        dsc = b.sb("dsc", [128, 8])
        xst = [b.sb("xst%d" % i, [128, 4, 512]) for i in range(2)]
        xb = [b.sb("xb%d" % i, [128, KC, 512], BF16) for i in range(2)]
        cin = [b.sb("cin%d" % f, [128, 520]) for f in range(3)]
        pr = [b.sb("pr%d" % f, [128, 512]) for f in range(9)]
        cv = [b.sb("cv%d" % f, [128, 512]) for f in range(3)]
        tmpA = b.sb("tmpA", [128, 512])
        tmpB = b.sb("tmpB", [128, 512])
        qn = b.sb("qn", [128, 512])
        kn = b.sb("kn", [128, 512])
        oT = b.sb("oT", [128, 512])
        hT = b.sb("hT", [128, 512])
        yb = b.sb("yb", [128, 2, 512], BF16)
        gat = b.sb("gat", [128, 8, 12])
        Gbc = b.sb("Gbc", [128, 128])
        grow = b.sb("grow", [128, 128])
        DEC = b.sb("DEC", [128, 128])
        DECs = b.sb("DECs", [128, 128])
        X = b.sb("X", [128, 128])
        XT = b.sb("XT", [128, 128])
        IXT = b.sb("IXT", [128, 128])
        Pm = b.sb("Pm", [128, 128])
        QKd = b.sb("QKd", [128, 128])
        kg = b.sb("kg", [128, 128])
        kdec = b.sb("kdec", [128, 128])
        vtok = b.sb("vtok", [128, 128])
        nW = b.sb("nW", [128, 128])
        ee = b.sb("ee", [128, 128])
        qdec = b.sb("qdec", [128, 128])
        Sst = b.sb("Sst", [128, 128])
        col = b.sb("col", [128, 16])
        mprev = b.sb("mprev", [128, 1])
        bc2 = b.sb("bc2", [128, 2])
        c2 = b.sb("c2", [128, 2])
        Abc = b.sb("Abc", [128, 128])
        pmat = b.sb("pmat", [128, 128])
        pqk = b.sb("pqk", [128, 128])
        pqkT = b.sb("pqkT", [128, 128])
        v1 = b.sb("v1", [128, 129])
        kp = b.sb("kp", [128, 128])
        CN = b.sb("CN", [128, 129])
        n2s = b.sb("n2s", [128, 129])
        num = b.sb("num", [128, 128])
        htok = b.sb("htok", [128, 128])
        m0s = b.sb("m0s", [128, DB])
        n0s = b.sb("n0s", [128, DB])
        c0s = b.sb("c0s", [128, DB * 9])

        pj = [b.ps("pj%d" % i) for i in range(2)]
        pg = b.ps("pg")
        pA = b.ps("pA")
        pB = b.ps("pB")
        pC = b.ps("pC")
        pD = b.ps("pD")
        pE = b.ps("pE")

        b.dma("sp", cst[:], consts, [], ["cst"], "cst")
        b.dma("sp", cw[:], convw, [], ["cw"], "cw")
        b.dma("sp", sc[:], scal, [], ["sc"], "sc")
        b.dma("sp", m0s[:], m0, [], ["m0s"], "m0s")
        b.dma("sp", n0s[:], n0, [], ["n0s"], "n0s")
        b.dma("sp", c0s[:], conv0.rearrange("p b f r -> p (b f r)"), [], ["c0s"], "c0s")
        w1v = w1.rearrange("(kc p) n -> p kc n", p=128)
        for kc0 in range(0, KC, 2):
            b.dma("sp", wst[:, :, :], w1v[:, kc0:kc0 + 2, :], [], ["wst"], "wst")
            b.cp("dve", w1b[:, kc0, :], wst[:, 0, :], ["wst"], ["w1b"])
            b.cp("pool", w1b[:, kc0 + 1, :], wst[:, 1, :], ["wst"], ["w1b"])
        b.dma("sp", wst[:, 0, 0:KC * 4], wg, [], ["wst"], "wst")
        b.cp("dve", wgb[:].rearrange("p kc n -> p (kc n)"), wst[:, 0, 0:KC * 4], ["wst"], ["wgb"])
        b.act(dsc[:, 0:1], sc[:, 0:1], AF.Exp, ["sc"], ["dsc"])
        b.ts("dve", dsc[:, 0:1], dsc[:, 0:1], -1.0, ALU.mult, ["dsc"], ["dsc"])
        b.ts("dve", dsc[:, 1:2], sc[:, 3:4], -1.0, ALU.mult, ["sc"], ["dsc"])
        b.memset("pool", v1[:, 128:129], 1.0, ["v1"])

        xTv = xT.rearrange("(kc p) n -> p kc n", p=128)

        tiles = []
        for bi in range(NB):
            for t in range(ntiles_prompt // NB):
                tiles.append(dict(kind="p", seq=bi, tok0=bi * SEQ + t * 512, N=512, L=128, nch=4,
                                  first=(t == 0), last=(t == ntiles_prompt // NB - 1)))
        if do_sample:
            tiles.append(dict(kind="s", seq=None, tok0=NTP, N=256, L=32, nch=8, first=True, last=True))

        def load_x(ti, slot):
            T = tiles[ti]
            N = T["N"]
            for q4 in range(4):
                b.dma("sp", xst[q4 % 2][:, :, 0:N], xTv[:, q4 * 4:(q4 + 1) * 4, T["tok0"]:T["tok0"] + N],
                      [], ["xst%d" % (q4 % 2)], "xst%d" % (q4 % 2))
                b.cp("pool", xb[slot][:, q4 * 4:(q4 + 1) * 4, 0:N], xst[q4 % 2][:, :, 0:N],
                     ["xst%d" % (q4 % 2)], ["xb%d" % slot])

        load_x(0, 0)
        try:
          if debug == 0.5:
              raise _Stop
          for ti, T in enumerate(tiles):
              slot = ti % 2
              N, L, nch = T["N"], T["L"], T["nch"]
              xbk = "xb%d" % slot
              if ti + 1 < len(tiles):
                  load_x(ti + 1, 1 - slot)
              nseq = 1 if T["kind"] == "p" else DB
              Ls = N // nseq
              W = Ls + 3

              def cin3(f, a, bnd):
                  return cin[f][:, 0:nseq * W].rearrange("p (s l) -> p s l", l=W)[:, :, a:bnd]

              if T["first"]:
                  if T["kind"] == "p":
                      for f in range(3):
                          b.memset("pool", cin[f][:, 0:3], 0.0, ["cin%d" % f])
                  else:
                      for f in range(3):
                          b.cp("pool", cin3(f, 0, 3),
                               c0s[:].rearrange("p (b f r) -> p b f r", f=3, r=3)[:, :, f, :],
                               ["c0s", "cin%d" % f], ["cin%d" % f])
              for f in range(9):
                  bank = pj[f % 2]
                  bk = "pj%d" % (f % 2)
                  for kc in range(KC):
                      b.mm(bank[:, 0:N], w1b[:, kc, f * 128:(f + 1) * 128], xb[slot][:, kc, 0:N],
                           ["w1b", xbk], [bk], start=(kc == 0), stop=(kc == KC - 1))
                  if f < 3:
                      b.cp("act", cin3(f, 3, W), bank[:, 0:N].rearrange("p (s l) -> p s l", l=Ls), [bk], ["cin%d" % f])
                  elif f == 5:
                      b.act(pr[f][:, 0:N], bank[:, 0:N], AF.Copy, [bk], ["pr%d" % f], scale=128.0 ** -0.5)
                  else:
                      b.cp("dve" if f % 2 else "act", pr[f][:, 0:N], bank[:, 0:N], [bk], ["pr%d" % f])
              for j in range(nch):
                  for kc in range(KC):
                      b.mm(pg[0:L, j * 4:(j + 1) * 4], xb[slot][:, kc, j * L:(j + 1) * L], wgb[:, kc, :],
                           [xbk, "wgb"], ["pg_g"], start=(kc == 0), stop=(kc == KC - 1))
              G = pg[0:L, 0:nch * 4].rearrange("p (j g) -> p j g", g=4)
              gt = lambda i: gat[0:L, 0:nch, i]
              b.act(gt(0), G[:, :, 0], AF.Sigmoid, ["pg_g"], ["gat"])
              b.ts("dve", gt(1), gt(0), -1.0, ALU.mult, ["gat"], ["gat"])
              b.act(gt(9), G[:, :, 1], AF.Exp, ["pg_g", "sc"], ["gat"], bias=sc[0:L, 1:2])
              b.act(gt(9), gt(9), AF.Ln, ["gat"], ["gat"], bias=1.0)
              b.ts("dve", gt(2), gt(9), dsc[0:L, 0:1], ALU.mult, ["gat", "dsc"], ["gat"])
              b.ts("dve", gt(3), G[:, :, 2], sc[0:L, 2:3], ALU.add, ["pg_g", "sc"], ["gat"])
              b.act(gt(10), G[:, :, 3], AF.Exp, ["pg_g", "dsc"], ["gat"], bias=dsc[0:L, 1:2], scale=-1.0)
              b.act(gt(10), gt(10), AF.Ln, ["gat"], ["gat"], bias=1.0)
              b.ts("dve", gt(4), gt(10), -1.0, ALU.mult, ["gat"], ["gat"])
              gtmp = tmpA[0:L, 0:2 * nch]
              b.cp("dve", gtmp[:, 0:nch], gt(2), ["gat"], ["tmpA"])
              b.cp("dve", gtmp[:, nch:2 * nch], gt(4), ["gat"], ["tmpA"])
              b.mm(pg[0:L, 64:64 + 2 * nch], triU[0:L, 0:L], gtmp, ["cst", "tmpA"], ["pg_c"])
              b.cp("dve", gt(5), pg[0:L, 64:64 + nch], ["pg_c"], ["gat"])
              b.cp("dve", gt(7), pg[0:L, 64 + nch:64 + 2 * nch], ["pg_c"], ["gat"])
              b.act(gt(6), gt(5), AF.Exp, ["gat"], ["gat"])
              b.ts("dve", gt(11), gt(5), -1.0, ALU.mult, ["gat"], ["gat"])
              b.tt("dve", gt(8), gt(3), gt(7), ALU.subtract, ["gat"], ["gat"])

              if debug == 1:
                  raise _Stop
              for f in range(3):
                  ck = "cin%d" % f
                  o3 = tmpA[:, 0:N].rearrange("p (s l) -> p s l", l=Ls)
                  b.ts("dve", o3, cin3(f, 0, Ls), cw[:, f * 4:f * 4 + 1], ALU.mult, [ck, "cw"], ["tmpA"])
                  for i in range(1, 4):
                      b.stt(o3, cin3(f, i, i + Ls), cw[:, f * 4 + i:f * 4 + i + 1], o3, ALU.mult, ALU.add,
                            [ck, "cw", "tmpA"], ["tmpA"])
                  b.act(cv[f][:, 0:N], tmpA[:, 0:N], AF.Silu, ["tmpA"], ["cv%d" % f])
                  if T["last"]:
                      if T["kind"] == "p":
                          b.dma("sp", oconv[:, T["seq"], f, :], cin[f][:, Ls:Ls + 3], [ck], ["oconv"], "oconv%d" % f)
                      else:
                          b.dma("sp", oconv[:, NB:NB + DB, f, :], cin3(f, Ls, Ls + 3), [ck], ["oconv"], "oconv%d" % f)
                  else:
                      b.cp("pool", cin[f][:, 0:3], cin[f][:, Ls:Ls + 3], [ck], [ck])
              for f, dst, dk, lnb in ((0, qn, "qn", -0.5 * math.log(128.0)), (1, kn, "kn", 0.0)):
                  b.tt("pool", tmpB[:, 0:N], cv[f][:, 0:N], cv[f][:, 0:N], ALU.mult, ["cv%d" % f], ["tmpB"])
                  b.mm(pj[f][:, 0:N], ones, tmpB[:, 0:N], ["cst", "tmpB"], ["pj%d" % f])
                  b.act(tmpB[:, 0:N], pj[f][:, 0:N], AF.Ln, ["pj%d" % f], ["tmpB"], bias=1e-6)
                  b.act(tmpB[:, 0:N], tmpB[:, 0:N], AF.Exp, ["tmpB"], ["tmpB"], scale=-0.5, bias=lnb)
                  b.tt("dve", dst[:, 0:N], cv[f][:, 0:N], tmpB[:, 0:N], ALU.mult, ["cv%d" % f, "tmpB"], [dk])

              if debug == 2:
                  raise _Stop
              for j in range(nch):
                  c0, c1 = j * L, (j + 1) * L
                  if T["kind"] == "s":
                      seq = NB + j
                      b.dma("sp", Sst[:], S0[j], [], ["S"], "S_in")
                      b.dma("sp", CN[:, 0:128], C0[j], [], ["CN"], "CN_in")
                      b.cp("pool", CN[:, 128:129], n0s[:, j:j + 1], ["n0s"], ["CN"])
                      b.cp("pool", mprev[:], m0s[:, j:j + 1], ["m0s"], ["mprev"])
                      seq_last = True
                  else:
                      seq = T["seq"]
                      if T["first"] and j == 0:
                          b.memset("pool", Sst[:], 0.0, ["S"])
                          b.memset("pool", CN[:], 0.0, ["CN"])
                          b.memset("pool", mprev[:], 0.0, ["mprev"])
                      seq_last = T["last"] and j == nch - 1
                  I_L = ident[0:L, 0:L]
                  gcol = gat[0:L, j, 2:3]
                  b.ts("dve", Gbc[0:L, :], ones[0:L, :], gcol, ALU.mult, ["cst", "gat"], ["Gbc"])
                  b.mm(pA[:, 0:L], Gbc[0:L, :], triU[0:L, 0:L], ["Gbc", "cst"], ["pA0"])
                  b.mm(pA[0:L, 128:128 + L], Gbc[0:L, 0:L], triU[0:L, 0:L], ["Gbc", "cst"], ["pA1"], start=True, stop=False)
                  b.mm(pA[0:L, 128:128 + L], I_L, NEGU[0:L, 0:L], ["cst"], ["pA1"], start=False, stop=True)
                  b.act(grow[:, 0:L], pA[:, 0:L], AF.Exp, ["pA0"], ["grow"])
                  b.act(DEC[0:L, 0:L], pA[0:L, 128:128 + L], AF.Exp, ["pA1", "gat"], ["DEC"], bias=gat[0:L, j, 11:12])
                  b.cp("dve", col[:, 0:1], pA[:, L - 1:L], ["pA0"], ["col0"])
                  b.act(col[:, 1:2], col[:, 0:1], AF.Exp, ["col0"], ["col1"])
                  b.act(col[0:L, 2:3], gat[0:L, j, 5:6], AF.Exp, ["gat", "col0"], ["col2"], scale=-1.0, bias=col[0:L, 0:1])
                  b.tt("pool", DECs[0:L, 0:L], DEC[0:L, 0:L], I_L, ALU.subtract, ["DEC", "cst"], ["DECs"])
                  b.mm(pA[0:L, 256:256 + L], kn[:, c0:c1], kn[:, c0:c1], ["kn"], ["pA2"])
                  b.mm(pA[0:L, 384:384 + L], kn[:, c0:c1], qn[:, c0:c1], ["kn", "qn"], ["pA3"])
                  b.stt(X[0:L, 0:L], pA[0:L, 256:256 + L], gat[0:L, j, 1:2], DECs[0:L, 0:L], ALU.mult, ALU.mult,
                        ["pA2", "gat", "DECs"], ["X"])
                  b.tt("dve", QKd[0:L, 0:L], pA[0:L, 384:384 + L], DEC[0:L, 0:L], ALU.mult, ["pA3", "DEC"], ["QKd"])
                  b.tr(pC[0:L, 0:128], kn[:, c0:c1], ident, ["kn", "cst"], ["pC0"])
                  b.tr(pC[0:L, 128:256], cv[2][:, c0:c1], ident, ["cv2", "cst"], ["pC1"])
                  b.ts("dve", kg[0:L, :], pC[0:L, 0:128], gat[0:L, j, 6:7], ALU.mult, ["pC0", "gat"], ["kg"])
                  b.ts("dve", kdec[0:L, :], pC[0:L, 0:128], col[0:L, 2:3], ALU.mult, ["pC0", "col2"], ["kdec"])
                  b.cp("act", vtok[0:L, :], pC[0:L, 128:256], ["pC1"], ["vtok"])
                  b.tr(pB[0:L, 384:384 + L], X[0:L, 0:L], I_L, ["X", "cst"], ["pB3"])
                  b.cp("act", XT[0:L, 0:L], pB[0:L, 384:384 + L], ["pB3"], ["XT"])
                  b.tt("pool", Pm[0:L, 0:L], X[0:L, 0:L], I_L, ALU.add, ["X", "cst"], ["Pm"])
                  nsq = int(math.log2(L)) - 1
                  for it in range(nsq):
                      lastit = it == nsq - 1
                      if not lastit:
                          b.mm(pB[0:L, 0:L], XT[0:L, 0:L], X[0:L, 0:L], ["XT", "X"], ["pB0"])
                      b.mm(pB[0:L, 128:128 + L], X[0:L, 0:L], XT[0:L, 0:L], ["XT", "X"], ["pB1"])
                      if not lastit:
                          b.cp("act", X[0:L, 0:L], pB[0:L, 0:L], ["pB0"], ["X"])
                          b.cp("dve", XT[0:L, 0:L], pB[0:L, 128:128 + L], ["pB1"], ["XT"])
                      b.tt("dve", IXT[0:L, 0:L], pB[0:L, 128:128 + L], I_L, ALU.add, ["pB1", "cst"], ["IXT"])
                      b.mm(pB[0:L, 256:256 + L], IXT[0:L, 0:L], Pm[0:L, 0:L], ["IXT", "Pm"], ["pB2"])
                      b.cp("act", Pm[0:L, 0:L], pB[0:L, 256:256 + L], ["pB2"], ["Pm"])
                  b.mm(pC[:, 256:256 + L], kg[0:L, :], Pm[0:L, 0:L], ["kg", "Pm"], ["pC2"])
                  b.ts("dve", nW[:, 0:L], pC[:, 256:256 + L], -1.0, ALU.mult, ["pC2"], ["nW"])
                  b.mm(pC[0:L, 384:512], Pm[0:L, 0:L], vtok[0:L, :], ["Pm", "vtok"], ["pC3"], start=True, stop=False)
                  b.mm(pC[0:L, 384:512], nW[:, 0:L], Sst[:], ["nW", "S"], ["pC3"], start=False, stop=True)
                  b.ts("dve", ee[0:L, :], pC[0:L, 384:512], gat[0:L, j, 0:1], ALU.mult, ["pC3", "gat"], ["ee"])
                  b.tt("pool", qdec[:, 0:L], qn[:, c0:c1], grow[:, 0:L], ALU.mult, ["qn", "grow"], ["qdec"])
                  b.mm(pD[:, 0:L], Sst[:], qdec[:, 0:L], ["S", "qdec"], ["pD0"], start=True, stop=False)
                  b.mm(pD[:, 0:L], ee[0:L, :], QKd[0:L, 0:L], ["ee", "QKd"], ["pD0"], start=False, stop=True)
                  b.cp("act", oT[:, c0:c1], pD[:, 0:L], ["pD0"], ["oT"])
                  b.mm(pD[:, 128:256], kdec[0:L, :], ee[0:L, :], ["kdec", "ee"], ["pD1"])
                  b.stt(Sst[:], Sst[:], col[:, 1:2], pD[:, 128:256], ALU.mult, ALU.add, ["S", "col1", "pD1"], ["S"])
                  if seq_last:
                      b.dma("sp", oS[seq], Sst[:], ["S"], ["oS"], "oS")

                  if debug == 3:
                      raise _Stop
                  qb, kb, vb = pr[4], pr[5], pr[6]
                  acol = gat[0:L, j, 8:9]
                  b.ts("dve", Abc[0:L, 0:L], ones[0:L, 0:L], acol, ALU.mult, ["cst", "gat"], ["Abc"])
                  b.mm(pD[0:L, 256:256 + L], Abc[0:L, 0:L], I_L, ["Abc", "cst"], ["pD2"], start=True, stop=False)
                  b.mm(pD[0:L, 256:256 + L], I_L, NEGL[0:L, 0:L], ["cst"], ["pD2"], start=False, stop=True)
                  b.P.op("dve", lambda e, o=col[0:L, 3:4], i=pD[0:L, 256:256 + L]: e.tensor_reduce(out=o, in_=i, axis=AX.X, op=ALU.max),
                         reads=["pD2"], writes=["col3"])
                  b.tt("dve", col[0:L, 4:5], col[0:L, 3:4], mprev[0:L, :], ALU.max, ["col3", "mprev"], ["col4"])
                  b.ts("dve", col[0:L, 5:6], col[0:L, 4:5], -1.0, ALU.mult, ["col4"], ["col5"])
                  b.act(pmat[0:L, 0:L], pD[0:L, 256:256 + L], AF.Exp, ["pD2", "col5"], ["pmat"], bias=col[0:L, 5:6])
                  b.act(col[0:L, 6:7], mprev[0:L, :], AF.Exp, ["mprev", "col5"], ["col6"], bias=col[0:L, 5:6])
                  b.tt("dve", col[0:L, 7:8], gat[0:L, j, 7:8], col[0:L, 4:5], ALU.add, ["gat", "col4"], ["col7"])
                  b.act(col[0:L, 8:9], col[0:L, 7:8], AF.Exp, ["col7"], ["col8"], scale=-1.0)
                  b.mm(pD[0:L, 384:384 + L], qb[:, c0:c1], kb[:, c0:c1], ["pr4", "pr5"], ["pD3"])
                  b.stt(pqk[0:L, 0:L], pD[0:L, 384:384 + L], 1.0, pmat[0:L, 0:L], ALU.mult, ALU.mult,
                        ["pD3", "pmat"], ["pqk", "col9"], accum_out=col[0:L, 9:10])
                  b.tr(pE[0:L, 0:L], pqk[0:L, 0:L], I_L, ["pqk", "cst"], ["pE0"])
                  b.cp("act", pqkT[0:L, 0:L], pE[0:L, 0:L], ["pE0"], ["pqkT"])
                  b.cp("dve", c2[0:L, 0:1], col[0:L, 4:5], ["col4"], ["c2"])
                  b.cp("dve", c2[0:L, 1:2], gat[0:L, j, 7:8], ["gat"], ["c2"])
                  SEL = SEL127 if L == 128 else SEL31
                  b.mm(pg[:, 128:130], SEL[0:L, :], c2[0:L, :], ["cst", "c2"], ["pg_b"])
                  b.cp("dve", bc2[:], pg[:, 128:130], ["pg_b"], ["bc2"])
                  b.ts("dve", col[:, 10:11], bc2[:, 0:1], -1.0, ALU.mult, ["bc2"], ["col10"])
                  b.act(col[0:L, 11:12], acol, AF.Exp, ["gat", "col10"], ["col11"], bias=col[0:L, 10:11])
                  b.act(col[:, 12:13], mprev[:], AF.Exp, ["mprev", "col10"], ["col12"], bias=col[:, 10:11])
                  b.tr(pE[0:L, 128:256], kb[:, c0:c1], ident, ["pr5", "cst"], ["pE1"])
                  b.tr(pE[0:L, 256:384], vb[:, c0:c1], ident, ["pr6", "cst"], ["pE2"])
                  b.ts("dve", kp[0:L, :], pE[0:L, 128:256], col[0:L, 11:12], ALU.mult, ["pE1", "col11"], ["kp"])
                  b.cp("act", v1[0:L, 0:128], pE[0:L, 256:384], ["pE2"], ["v1"])
                  b.mm(pC[0:L, 0:128], pqkT[0:L, 0:L], v1[0:L, 0:128], ["pqkT", "v1"], ["pC0"])
                  b.mm(pE[0:L, 0:129], qb[:, c0:c1], CN[:, 0:129], ["pr4", "CN"], ["pE0", "pE1"])
                  b.act(n2s[0:L, :], pE[0:L, 0:129], AF.Copy, ["pE0", "pE1", "col6"], ["n2s"], scale=col[0:L, 6:7])
                  b.tt("dve", num[0:L, :], n2s[0:L, 0:128], pC[0:L, 0:128], ALU.add, ["n2s", "pC0"], ["num"])
                  b.tt("dve", col[0:L, 13:14], n2s[0:L, 128:129], col[0:L, 9:10], ALU.add, ["n2s", "col9"], ["col13"])
                  b.act(col[0:L, 13:14], col[0:L, 13:14], AF.Abs, ["col13"], ["col13"])
                  b.tt("dve", col[0:L, 13:14], col[0:L, 13:14], col[0:L, 8:9], ALU.max, ["col13", "col8"], ["col13"])
                  b.P.op("dve", lambda e, o=col[0:L, 14:15], i=col[0:L, 13:14]: e.reciprocal(out=o, in_=i),
                         reads=["col13"], writes=["col14"])
                  b.ts("dve", htok[0:L, :], num[0:L, :], col[0:L, 14:15], ALU.mult, ["num", "col14"], ["htok"])
                  b.tr(pE[:, 384:384 + L], htok[0:L, :], I_L, ["htok", "cst"], ["pE3"])
                  b.cp("act", hT[:, c0:c1], pE[:, 384:384 + L], ["pE3"], ["hT"])
                  b.mm(pC[:, 128:257], kp[0:L, :], v1[0:L, :], ["kp", "v1"], ["pC1", "pC2"])
                  b.stt(CN[:], CN[:], col[:, 12:13], pC[:, 128:257], ALU.mult, ALU.add, ["CN", "col12", "pC1", "pC2"], ["CN"])
                  b.tt("dve", mprev[:], bc2[:, 0:1], bc2[:, 1:2], ALU.add, ["bc2"], ["mprev"])
                  if seq_last:
                      b.dma("sp", oC[seq], CN[:, 0:128], ["CN"], ["oC"], "oC")
                      b.dma("sp", on[:, seq:seq + 1], CN[:, 128:129], ["CN"], ["on"], "on", allow_slow_non_contiguous=True)
                      b.dma("sp", om[0:1, seq:seq + 1], mprev[0:1, 0:1], ["mprev"], ["om"], "om")

              if debug == 4:
                  raise _Stop
              for (src, srck, zf, of, ncol, half) in ((oT, "oT", 3, None, 4, 0), (hT, "hT", 8, 7, 5, 1)):
                  if of is not None:
                      b.act(tmpA[:, 0:N], pr[of][:, 0:N], AF.Sigmoid, ["pr%d" % of], ["tmpA"])
                      b.tt("dve", src[:, 0:N], src[:, 0:N], tmpA[:, 0:N], ALU.mult, [srck, "tmpA"], [srck])
                  b.tt("pool", tmpB[:, 0:N], src[:, 0:N], src[:, 0:N], ALU.mult, [srck], ["tmpB"])
                  b.mm(pj[half][:, 0:N], ones, tmpB[:, 0:N], ["cst", "tmpB"], ["pj%d" % half])
                  b.act(tmpB[:, 0:N], pj[half][:, 0:N], AF.Ln, ["pj%d" % half], ["tmpB"], scale=1.0 / 128.0, bias=1e-6)
                  b.act(tmpB[:, 0:N], tmpB[:, 0:N], AF.Exp, ["tmpB"], ["tmpB"], scale=-0.5)
                  b.act(tmpA[:, 0:N], pr[zf][:, 0:N], AF.Silu, ["pr%d" % zf], ["tmpA"])
                  b.stt(tmpB[:, 0:N], tmpB[:, 0:N], sc[:, ncol:ncol + 1], tmpA[:, 0:N], ALU.mult, ALU.mult,
                        ["tmpB", "sc", "tmpA"], ["tmpB"])
                  b.tt("dve", yb[:, half, 0:N], src[:, 0:N], tmpB[:, 0:N], ALU.mult, [srck, "tmpB"], ["yb%d" % half])
                  b.dma("sp", yT[half * 128:(half + 1) * 128, T["tok0"]:T["tok0"] + N], yb[:, half, 0:N],
                        ["yb%d" % half], ["yT"], "yT%d" % half)

        except _Stop:
            pass
        P.finalize()
        print("phase1 instrs", P.ninstr, {e: P.cnt[e] for e in ENGS})
    return nc


NTT = 17


def build_out_phase(glu, emit_T):
    nc = bass.Bass("TRN2", target_bir_lowering=False)
    KCI = 24 if glu else 16
    with ExitStack() as st:
        b = B(nc, st)
        P = b.P
        yin = b.din("yin", [NTT, 128, KCI, 128], BF16)
        wout = b.din("wout", [D, D])
        xres = b.din("xres", [NTT * 128, D])
        lng = b.din("lng", [128, D])
        lnb = b.din("lnb", [128, D])
        consts = b.din("consts", [128, 8, 128])
        if glu:
            wglu = b.din("wglu", [1024, 1024])
            bglu = b.din("bglu", [128, 8])
        xo = b.dout("xo", [NTT * 128, D])
        if emit_T:
            xoT = b.dout("xoT", [NTT, 128, KC, 128], BF16)

        cst = b.sb("cst", [128, 8, 128])
        ident = cst[:, 0, :]
        woutb = b.sb("woutb", [128, KC, D], BF16)
        wst = b.sb("wst", [128, 2, D])
        g_sb = b.sb("g_sb", [128, D])
        b_sb = b.sb("b_sb", [128, D])
        ysb = [b.sb("ysb%d" % i, [128, KCI, 128], BF16) for i in range(2)]
        xr = [b.sb("xr%d" % i, [128, D]) for i in range(2)]
        z = b.sb("z", [128, D])
        stats = b.sb("stats", [128, 4, 6])
        mv = b.sb("mv", [128, 4])
        xT_sb = [b.sb("xT_sb%d" % i, [128, KC, 128], BF16) for i in range(2)]
        if glu:
            wglub = b.sb("wglub", [128, 8, 1024], BF16)
            bg = b.sb("bg", [128, 8])
            sig = b.sb("sig", [128, 8, 128])
            yd = b.sb("yd", [128, 8, 128], BF16)
        po = [b.ps("po%d" % i) for i in range(4)]
        pt = [b.ps("pt%d" % i) for i in range(4)]
        PK = lambda i: "pj%d" % i

        b.dma("sp", cst[:], consts, [], ["cst"], "cst")
        b.dma("sp", g_sb[:], lng, [], ["g_sb"], "g_sb")
        b.dma("sp", b_sb[:], lnb, [], ["b_sb"], "b_sb")
        wv = wout.rearrange("(kc p) n -> p kc n", p=128)
        for kc0 in range(0, KC, 2):
            b.dma("sp", wst[:], wv[:, kc0:kc0 + 2, :], [], ["wst"], "wst")
            b.cp("dve", woutb[:, kc0, :], wst[:, 0, :], ["wst"], ["woutb"])
            b.cp("pool", woutb[:, kc0 + 1, :], wst[:, 1, :], ["wst"], ["woutb"])
        if glu:
            gv = wglu.rearrange("(kc p) n -> p kc n", p=128)
            for kc0 in range(0, 8, 4):
                b.dma("sp", wst[:].rearrange("p a (c n) -> p (a c) n", n=1024), gv[:, kc0:kc0 + 4, :], [], ["wst"], "wst")
                b.cp("pool", wglub[:, kc0:kc0 + 4, :], wst[:].rearrange("p a (c n) -> p (a c) n", n=1024), ["wst"], ["wglub"])
            b.dma("sp", bg[:], bglu, [], ["bg"], "bg")

        def load(ti):
            s = ti % 2
            b.dma("sp", ysb[s][:], yin[ti], [], ["ysb%d" % s], "ysb%d" % s)
            b.dma("sp", xr[s][:], xres[ti * 128:(ti + 1) * 128, :], [], ["xr%d" % s], "xr%d" % s)

        load(0)
        for ti in range(NTT):
            s = ti % 2
            if ti + 1 < NTT:
                load(ti + 1)
            yk = "ysb%d" % s
            if glu:
                for f in range(8):
                    bank = pt[f // 4]
                    for kc in range(8):
                        b.mm(bank[:, (f % 4) * 128:(f % 4 + 1) * 128], wglub[:, kc, f * 128:(f + 1) * 128],
                             ysb[s][:, 8 + kc, :], ["wglub", yk], [PK(4 + f // 4)], start=(kc == 0), stop=(kc == 7))
                for f in range(8):
                    b.act(sig[:, f, :], pt[f // 4][:, (f % 4) * 128:(f % 4 + 1) * 128], AF.Sigmoid,
                          [PK(4 + f // 4), "bg"], ["sig"], bias=bg[:, f:f + 1])
                b.tt("dve", sig[:], sig[:], ysb[s][:, 8:16, :], ALU.mult, ["sig", yk], ["sig"])
                b.tt("dve", yd[:], sig[:], ysb[s][:, 16:24, :], ALU.mult, ["sig", yk], ["yd"])
            for n in range(4):
                for kc in range(KC):
                    if glu and kc >= 8:
                        lhsT, lk = yd[:, kc - 8, :], "yd"
                    else:
                        lhsT, lk = ysb[s][:, kc, :], yk
                    b.mm(po[n][:, :], lhsT, woutb[:, kc, n * 512:(n + 1) * 512], [lk, "woutb"], [PK(n)],
                         start=(kc == 0), stop=(kc == KC - 1))
            for n in range(4):
                b.stt(z[:, n * 512:(n + 1) * 512], xr[s][:, n * 512:(n + 1) * 512], ALPHA, po[n][:, :],
                      ALU.mult, ALU.add, ["xr%d" % s, PK(n)], ["z"])
                b.P.op("dve", lambda e, o=stats[:, n, :], i=z[:, n * 512:(n + 1) * 512]: e.bn_stats(out=o, in_=i),
                       reads=["z"], writes=["stats"])
            b.P.op("dve", lambda e, o=mv[:, 0:2], i=stats[:].rearrange("p a c -> p (a c)"): e.bn_aggr(out=o, in_=i),
                   reads=["stats"], writes=["mv"])
            b.act(mv[:, 2:3], mv[:, 1:2], AF.Ln, ["mv"], ["mv"], bias=1e-5)
            b.act(mv[:, 2:3], mv[:, 2:3], AF.Exp, ["mv"], ["mv"], scale=-0.5)
            b.ts("dve", z[:], z[:], mv[:, 0:1], ALU.subtract, ["z", "mv"], ["z"], s2=mv[:, 2:3], op1=ALU.mult)
            b.tt("pool", z[:], z[:], g_sb[:], ALU.mult, ["z", "g_sb"], ["z"])
            b.tt("dve", z[:], z[:], b_sb[:], ALU.add, ["z", "b_sb"], ["z"])
            b.dma("sp", xo[ti * 128:(ti + 1) * 128, :], z[:], ["z"], ["xo"], "xo")
            if emit_T:
                for kc in range(KC):
                    b.tr(pt[kc // 4][:, (kc % 4) * 128:(kc % 4 + 1) * 128], z[:, kc * 128:(kc + 1) * 128], ident,
                         ["z", "cst"], [PK(4 + kc // 4)])
                for q in range(4):
                    b.cp("act" if q % 2 else "dve", xT_sb[s][:, q * 4:(q + 1) * 4, :].rearrange("p a t -> p (a t)"),
                         pt[q][:, :], [PK(4 + q)], ["xT_sb%d" % s])
                b.dma("sp", xoT[ti], xT_sb[s][:], ["xT_sb%d" % s], ["xoT"], "xoT%d" % s)
        P.finalize()
        print("out-phase instrs", P.ninstr, {e: P.cnt[e] for e in ENGS})
    return nc


def _run(nc, in_maps):
    return run_bass_kernel_spmd(nc, in_maps, core_ids=list(range(NCORES))).results


def _tok_major(inp_p, inp_s):
    return np.concatenate([inp_p.reshape(-1, inp_p.shape[-1]), inp_s.reshape(-1, inp_s.shape[-1])], 0)


def _p1_inputs(inp, xT, c):
    H = 128
    W = inp['ev_w_in'][0]
    offs = np.cumsum([0, 3072, 1024, 8, 8, 1024, 1024, 1024, 1024, 1024, 8, 8])

    def colblk(i, sub=0):
        return W[:, offs[i] + sub * 1024 + c * H: offs[i] + sub * 1024 + (c + 1) * H]
    w1 = np.concatenate([colblk(0, 0), colblk(0, 1), colblk(0, 2), colblk(1),
                         colblk(4), colblk(5), colblk(6), colblk(7), colblk(8)], 1)
    wg = np.stack([W[:, offs[2] + c], W[:, offs[3] + c], W[:, offs[9] + c], W[:, offs[10] + c]], 1)
    wg = wg.reshape(KC, 128, 4).transpose(1, 0, 2).reshape(128, KC * 4)
    cw = inp['ev_conv_w'][0]
    convw = np.stack([cw[:, f * 1024 + c * H: f * 1024 + (c + 1) * H].T for f in range(3)], 1).reshape(H, 12)
    scal = np.zeros((128, 8), np.float32)
    scal[:, 0] = inp['ev_a_log'][0, c]
    scal[:, 1] = inp['ev_dt_bias'][0, c]
    scal[:, 2] = inp['ev_ig_bias'][0, c]
    scal[:, 3] = inp['ev_fg_bias'][0, c]
    scal[:, 4] = inp['ev_norm_a'][0]
    scal[:, 5] = inp['ev_norm_b'][0, c * H:(c + 1) * H]
    cs = inp['state_delta_conv'][0]
    conv0 = np.stack([cs[:, :, f * 1024 + c * H: f * 1024 + (c + 1) * H] for f in range(3)], 0)
    conv0 = np.ascontiguousarray(conv0.transpose(3, 1, 0, 2))
    A = np.ascontiguousarray
    return dict(xT=xT, w1=A(w1), wg=A(wg), convw=A(convw), scal=scal, consts=_consts_np(),
                S0=A(inp['state_delta_S'][0, :, c]), conv0=conv0,
                C0=A(inp['state_mlstm_C'][0, :, c]), n0=A(inp['state_mlstm_n'][0, :, c].T),
                m0=A(np.broadcast_to(inp['state_mlstm_m'][0, :, c][None, :], (128, DB))))


def _tile_tokens(Y, c):
    nk = Y.shape[0]
    out = np.zeros((nk, 128, NTT * 128), Y.dtype)
    out[:, :, :TOKC] = Y[:, :, c * TOKC:(c + 1) * TOKC]
    return np.ascontiguousarray(out.reshape(nk, 128, NTT, 128).transpose(2, 1, 0, 3))


def _pad_rows(x, c):
    out = np.zeros((NTT * 128, x.shape[1]), x.dtype)
    out[:TOKC] = x[c * TOKC:(c + 1) * TOKC]
    return out


def _p2_inputs(inp, xtok, Y, c):
    rep = lambda v: np.ascontiguousarray(np.broadcast_to(v[None, :], (128, v.shape[0])))
    return dict(yin=_tile_tokens(Y, c), wout=np.ascontiguousarray(inp['ev_w_out'][0]), xres=_pad_rows(xtok, c),
                lng=rep(inp['ev_ln_g'][0]), lnb=rep(inp['ev_ln_b'][0]), consts=_consts_np())


TWO_PI = 2.0 * math.pi
CW1 = 6.28125
CW2 = TWO_PI - CW1


def _masks3_np():
    m = np.zeros((128, 6, 512), np.float32)
    s = np.arange(128)[:, None]
    t = np.arange(512)[None, :]
    for r in range(4):
        m[:, r, :] = (t > 128 * r + s)
    m[:, 4, :] = ((t % 32) > s)
    m[:, 5, :] = t + 1.0
    return m


def build_phase3(ntiles_prompt=32, do_sample=True, debug=0):
    nc = bass.Bass("TRN2", target_bir_lowering=False)
    with ExitStack() as st:
        b = B(nc, st)
        P = b.P
        x1T = b.din("x1T", [D, NTOK], BF16)
        w3 = b.din("w3", [D, 768])
        consts = b.din("consts", [128, 8, 128])
        masks = b.din("masks", [128, 5, 512], BF16)
        tau = b.din("tau", [128, 512])
        kcT = b.din("kcT", [32, 128, DB, 128])
        vcc = b.din("vcc", [32, 128, DB, 128])
        s5p = b.din("s5p", [128, 4, 4])
        Bbd = b.din("Bbd", [128, 2, 4, 128])
        Cbd = b.din("Cbd", [128, 2, 4, 128])
        h0 = b.din("h0", [128, 2, 4, DB])
        y3T = b.dout("y3T", [384, NTOK], BF16)
        kout = b.dout("kout", [NTOK, 128])
        vout = b.dout("vout", [NTOK, 128])
        hout = b.dout("hout", [128, 2, 4, NB + DB])

        cst = b.sb("cst", [128, 8, 128])
        ident = cst[:, 0, :]
        msk = b.sb("msk", [128, 5, 512], BF16)
        trib = b.sb("trib", [128, 2, 128], BF16)
        w3b = b.sb("w3b", [128, KC, 768], BF16)
        wst = b.sb("wst", [128, 1, 768])
        xb = [b.sb("xb%d" % i, [128, KC, 512], BF16) for i in range(2)]
        KT = b.sb("KT", [128, SEQ], BF16)
        Vt = b.sb("Vt", [128, 64, 128], BF16)
        qb = b.sb("qb", [128, 512], BF16)
        kT32 = b.sb("kT32", [128, 512])
        vT32 = b.sb("vT32", [128, 512])
        zc = b.sb("zc", [128, 512])
        u32 = b.sb("u32", [128, 512])
        ub = b.sb("ub", [128, 512], BF16)
        szd = b.sb("szd", [128, 512], BF16)
        ktok = b.sb("ktok", [128, 4, 128])
        vtok = b.sb("vtok", [128, 4, 128])
        E = [b.sb("E%d" % i, [128, 512]) for i in range(2)]
        SP = [b.sb("SP%d" % i, [128, 512], BF16) for i in range(2)]
        Wt = b.sb("Wt", [128, 512])
        ATT = [b.sb("ATT%d" % i, [128, 512], BF16) for i in range(2)]
        ycb = b.sb("ycb", [128, 512], BF16)
        ysb = b.sb("ysb", [128, 512], BF16)
        sp_ = b.sb("sp_", [128, 4, 4])
        cr = b.sb("cr", [128, 4, 512])
        sr = b.sb("sr", [128, 4, 512])
        ki = b.sb("ki", [128, 512], mybir.dt.int32)
        s5c = b.sb("s5c", [128, 4, 16])
        Bb = b.sb("Bb", [128, 2, 4, 128], BF16)
        Cst = b.sb("Cst", [128, 2, 4, 128])
        Cb = b.sb("Cb", [128, 2, 4, 128], BF16)
        hst = b.sb("hst", [128, 2, 4, DB])
        h0s = b.sb("h0s", [128, 2, 4, DB])
        hfin = b.sb("hfin", [128, 2, 4, NB + DB])
        t1 = b.sb("t1", [128, 512])
        t2 = b.sb("t2", [128, 512])
        bre = b.sb("bre", [128, 512])
        bim = b.sb("bim", [128, 512])
        gre = b.sb("gre", [128, 512])
        gim = b.sb("gim", [128, 512])
        ang, kk = gre, gim
        hreb = b.sb("hreb", [128, 512], BF16)
        himb = b.sb("himb", [128, 512], BF16)
        ys32 = b.sb("ys32", [128, 512])
        kcs = [b.sb("kcs%d" % i, [128, DB, 128]) for i in range(2)]
        vcs = [b.sb("vcs%d" % i, [128, DB, 128]) for i in range(2)]
        kcb = [b.sb("kcb%d" % i, [128, DB, 128], BF16) for i in range(3)]
        vcb = [b.sb("vcb%d" % i, [128, DB, 128], BF16) for i in range(3)]
        vnew = b.sb("vnew", [128, DB, 128], BF16)

        pj = [b.ps("pj%d" % i) for i in range(2)]
        pZ = [b.ps("pZ%d" % i) for i in range(2)]
        pCm = b.ps("pCm")
        pO = b.ps("pO")
        pT = b.ps("pT")
        pY = b.ps("pY")
        ZK = ["pj2", "pj3"]
        CK, OK, TK, YK = "pj4", "pj5", "pj6", "pj7"

        b.dma("sp", cst[:], consts, [], ["cst"], "cst")
        b.dma("sp", msk[:], masks, [], ["msk"], "msk")
        b.dma("sp", t2[:], tau, [], ["t2"], "tau")
        b.dma("sp", sp_[:], s5p, [], ["sp_"], "sp_")
        b.dma("sp", h0s[:], h0, [], ["h0s"], "h0s")
        b.cp("dve", trib[:, 0, :], cst[:, 7, :], ["cst"], ["trib"])
        b.ts("dve", trib[:, 1, :], cst[:, 7, :], -1.0, ALU.mult, ["cst"], ["trib"], s2=1.0, op1=ALU.add)
        w3v = w3.rearrange("(kc p) n -> p kc n", p=128)
        for kc0 in range(KC):
            b.dma("sp", wst[:, 0, :], w3v[:, kc0, :], [], ["wst"], "wst")
            b.cp("dve" if kc0 % 2 else "pool", w3b[:, kc0, :], wst[:, 0, :], ["wst"], ["w3b"])
        for a in range(2):
            b.dma("sp", wst[:, 0, 0:512].rearrange("p (c n) -> p c n", n=128), Bbd[:, a], [], ["wst"], "wst")
            b.cp("dve", Bb[:, a], wst[:, 0, 0:512].rearrange("p (c n) -> p c n", n=128), ["wst"], ["Bb"])
        b.dma("sp", Cst[:], Cbd, [], ["Cst"], "Cst")
        c_ = lambda i: s5c[:, :, i]
        lre, lim = sp_[:, :, 0], sp_[:, :, 1]
        b.act(c_(13), sp_[:, :, 2], AF.Exp, ["sp_"], ["s5c"])
        dtc = c_(13)
        b.tt("dve", c_(8), lre, dtc, ALU.mult, ["sp_", "s5c"], ["s5c"])
        b.act(c_(0), c_(8), AF.Exp, ["s5c"], ["s5c"])
        b.tt("dve", c_(1), lim, dtc, ALU.mult, ["sp_", "s5c"], ["s5c"])
        b.cp("dve", bim[:], t2[:], ["t2"], ["bim"])
        for m in range(4):
            th = s5c[:, m, 1:2]
            b.ts("dve", ang[:], bim[:], th, ALU.mult, ["bim", "s5c"], ["gre"])
            b.ts("dve", kk[:], ang[:], 1.0 / TWO_PI, ALU.mult, ["gre"], ["gim"])
            b.cp("dve", ki[:], kk[:], ["gim"], ["ki"])
            b.cp("dve", kk[:], ki[:], ["ki"], ["gim"])
            b.stt(ang[:], kk[:], -CW1, ang[:], ALU.mult, ALU.add, ["gim", "gre"], ["gre"])
            b.stt(ang[:], kk[:], -CW2, ang[:], ALU.mult, ALU.add, ["gim", "gre"], ["gre"])
            for (dst, shift) in ((sr, 0.0), (cr, 0.5 * math.pi)):
                src = ang
                if shift != 0.0:
                    b.ts("dve", t1[:], ang[:], shift, ALU.add, ["gre"], ["t1"])
                    src = t1
                    sk = "t1"
                else:
                    b.cp("dve", t1[:], ang[:], ["gre"], ["t1"])
                    src, sk = t1, "t1"
                for _ in range(2):
                    b.ts("dve", t2[:], t1[:], math.pi, ALU.is_gt, ["t1"], ["t2"])
                    b.stt(t1[:], t2[:], -TWO_PI, t1[:], ALU.mult, ALU.add, ["t2", "t1"], ["t1"])
                    b.ts("dve", t2[:], t1[:], -math.pi, ALU.is_lt, ["t1"], ["t2"])
                    b.stt(t1[:], t2[:], TWO_PI, t1[:], ALU.mult, ALU.add, ["t2", "t1"], ["t1"])
                b.act(dst[:, m, :], t1[:], AF.Sin, ["t1"], ["cr" if dst is cr else "sr"])
        b.tt("dve", c_(2), c_(0), cr[:, :, 0], ALU.mult, ["s5c", "cr"], ["s5c"])
        b.tt("dve", c_(3), c_(0), sr[:, :, 0], ALU.mult, ["s5c", "sr"], ["s5c"])
        b.ts("dve", c_(2), c_(2), -1.0, ALU.add, ["s5c"], ["s5c"])
        b.tt("dve", c_(4), lre, lre, ALU.mult, ["sp_"], ["s5c"])
        b.tt("dve", c_(8), lim, lim, ALU.mult, ["sp_"], ["s5c"])
        b.tt("dve", c_(4), c_(4), c_(8), ALU.add, ["s5c"], ["s5c"])
        b.P.op("dve", lambda e, o=c_(12), i=c_(4): e.reciprocal(out=o, in_=i), reads=["s5c"], writes=["s5c"])
        b.tt("dve", c_(8), c_(2), lre, ALU.mult, ["s5c", "sp_"], ["s5c"])
        b.tt("dve", c_(9), c_(3), lim, ALU.mult, ["s5c", "sp_"], ["s5c"])
        b.tt("dve", c_(5), c_(8), c_(9), ALU.add, ["s5c"], ["s5c"])
        b.tt("dve", c_(5), c_(5), c_(12), ALU.mult, ["s5c"], ["s5c"])
        b.tt("dve", c_(8), c_(3), lre, ALU.mult, ["s5c", "sp_"], ["s5c"])
        b.tt("dve", c_(9), c_(2), lim, ALU.mult, ["s5c", "sp_"], ["s5c"])
        b.tt("dve", c_(6), c_(8), c_(9), ALU.subtract, ["s5c"], ["s5c"])
        b.tt("dve", c_(6), c_(6), c_(12), ALU.mult, ["s5c"], ["s5c"])
        b.ts("dve", c_(7), c_(6), -1.0, ALU.mult, ["s5c"], ["s5c"])
        b.tt("dve", c_(8), c_(5), c_(5), ALU.mult, ["s5c"], ["s5c"])
        b.tt("dve", c_(9), c_(6), c_(6), ALU.mult, ["s5c"], ["s5c"])
        b.tt("dve", c_(8), c_(8), c_(9), ALU.add, ["s5c"], ["s5c"])
        b.P.op("dve", lambda e, o=c_(9), i=c_(8): e.reciprocal(out=o, in_=i), reads=["s5c"], writes=["s5c"])
        b.tt("dve", c_(10), c_(5), c_(9), ALU.mult, ["s5c"], ["s5c"])
        b.tt("dve", c_(11), c_(7), c_(9), ALU.mult, ["s5c"], ["s5c"])
        for m in range(4):
            fre, fim, nfim = s5c[:, m, 5:6], s5c[:, m, 6:7], s5c[:, m, 7:8]
            b.ts("dve", t1[:, 0:128], Cst[:, 1, m, :], nfim, ALU.mult, ["Cst", "s5c"], ["t1"])
            b.stt(Cb[:, 0, m, :], Cst[:, 0, m, :], fre, t1[:, 0:128], ALU.mult, ALU.add, ["Cst", "s5c", "t1"], ["Cb"])
            b.ts("dve", t1[:, 0:128], Cst[:, 1, m, :], fre, ALU.mult, ["Cst", "s5c"], ["t1"])
            b.stt(t1[:, 0:128], Cst[:, 0, m, :], fim, t1[:, 0:128], ALU.mult, ALU.add, ["Cst", "s5c", "t1"], ["t1"])
            b.ts("dve", Cb[:, 1, m, :], t1[:, 0:128], -1.0, ALU.mult, ["t1"], ["Cb"])
            fr, fi = s5c[:, m, 10:11], s5c[:, m, 11:12]
            b.ts("dve", t2[:, 0:DB], h0s[:, 1, m, :], fi, ALU.mult, ["h0s", "s5c"], ["t2"])
            b.stt(t2[:, 16:16 + DB], h0s[:, 0, m, :], fr, t2[:, 0:DB], ALU.mult, ALU.subtract, ["h0s", "s5c", "t2"], ["t2"])
            b.ts("dve", t2[:, 0:DB], h0s[:, 1, m, :], fr, ALU.mult, ["h0s", "s5c"], ["t2"])
            b.stt(t2[:, 32:32 + DB], h0s[:, 0, m, :], fi, t2[:, 0:DB], ALU.mult, ALU.add, ["h0s", "s5c", "t2"], ["t2"])
            b.cp("dve", h0s[:, 0, m, :], t2[:, 16:16 + DB], ["t2"], ["h0s"])
            b.cp("dve", h0s[:, 1, m, :], t2[:, 32:32 + DB], ["t2"], ["h0s"])

        x1v = x1T.rearrange("(kc p) n -> p kc n", p=128)
        tiles = []
        for bi in range(NB):
            for t in range(ntiles_prompt // NB):
                tiles.append(dict(kind="p", seq=bi, ti=t, tok0=bi * SEQ + t * 512, N=512,
                                  first=(t == 0), last=(t == ntiles_prompt // NB - 1)))
        if do_sample:
            tiles.append(dict(kind="s", seq=None, ti=0, tok0=NTP, N=256, first=True, last=True))

        def load_x(i):
            T = tiles[i]
            s = i % 2
            for h in range(2):
                b.dma("sp", xb[s][:, h * 8:(h + 1) * 8, 0:T["N"]], x1v[:, h * 8:(h + 1) * 8, T["tok0"]:T["tok0"] + T["N"]],
                      [], ["xb%d" % s], "xb%d_%d" % (s, h))

        def load_cache(j):
            s = j % 3
            s2 = j % 2
            b.dma("sp", kcs[s2][:], kcT[j], [], ["kcs%d" % s2], "kcs%d" % s2)
            b.dma("sp", vcs[s2][:], vcc[j], [], ["vcs%d" % s2], "vcs%d" % s2)
            b.cp("pool", kcb[s][:], kcs[s2][:], ["kcs%d" % s2], ["kcb%d" % s])
            b.cp("pool", vcb[s][:], vcs[s2][:], ["vcs%d" % s2], ["vcb%d" % s])

        load_x(0)
        try:
            for i, T in enumerate(tiles):
                s = i % 2
                N = T["N"]
                samp = T["kind"] == "s"
                xk = "xb%d" % s
                if i + 1 < len(tiles):
                    load_x(i + 1)
                for f in range(6):
                    bank, bk = pj[f % 2], "pj%d" % (f % 2)
                    for kc in range(KC):
                        b.mm(bank[:, 0:N], w3b[:, kc, f * 128:(f + 1) * 128], xb[s][:, kc, 0:N], ["w3b", xk], [bk],
                             start=(kc == 0), stop=(kc == KC - 1))
                    if f == 0:
                        b.act(qb[:, 0:N], bank[:, 0:N], AF.Copy, [bk], ["qb"], scale=128.0 ** -0.5)
                    elif f == 1:
                        b.cp("dve", kT32[:, 0:N], bank[:, 0:N], [bk], ["kT32"])
                        if not samp:
                            b.cp("act", KT[:, T["ti"] * 512:T["ti"] * 512 + N], bank[:, 0:N], [bk], ["KT"])
                        else:
                            b.cp("act", KT[:, 0:N], bank[:, 0:N], [bk], ["KT"])
                    elif f == 2:
                        b.cp("dve", vT32[:, 0:N], bank[:, 0:N], [bk], ["vT32"])
                    elif f == 3:
                        b.act(zc[:, 0:N], bank[:, 0:N], AF.Silu, [bk], ["zc"])
                    elif f == 4:
                        b.cp("dve", u32[:, 0:N], bank[:, 0:N], [bk], ["u32"])
                        b.cp("act", ub[:, 0:N], bank[:, 0:N], [bk], ["ub"])
                    else:
                        b.act(szd[:, 0:N], bank[:, 0:N], AF.Silu, [bk], ["szd"])
                        b.dma("sp", y3T[256:384, T["tok0"]:T["tok0"] + N], szd[:, 0:N], ["szd"], ["y3T"], "szd_o")
                if debug == 1:
                    raise _Stop
                nblk = N // 128
                for (src, sk, dst, dk, outap) in ((kT32, "kT32", ktok, "ktok", kout), (vT32, "vT32", vtok, "vtok", vout)):
                    for jb in range(nblk):
                        b.tr(pT[:, jb * 128:(jb + 1) * 128], src[:, jb * 128:(jb + 1) * 128], ident, [sk, "cst"], [TK])
                    b.cp("dve", dst[:, 0:nblk, :].rearrange("p a d -> p (a d)"), pT[:, 0:nblk * 128], [TK], [dk])
                    b.dma("sp", outap[T["tok0"]:T["tok0"] + N, :].rearrange("(a p) d -> p a d", p=128), dst[:, 0:nblk, :],
                          [dk], ["o_" + dk], "o_" + dk)
                if not samp:
                    b.cp("pool", Vt[:, T["ti"] * 4:T["ti"] * 4 + 4, :], vtok[:], ["vtok"], ["Vt"])
                else:
                    for h in range(2):
                        for q4 in range(4):
                            bi = h * 4 + q4
                            b.tr(pT[0:32, q4 * 128:(q4 + 1) * 128], vT32[:, bi * 32:(bi + 1) * 32], ident,
                                 ["vT32", "cst"], [TK])
                        b.cp("dve", vnew[0:32, h * 4:(h + 1) * 4, :].rearrange("p a d -> p (a d)"), pT[0:32, 0:512],
                             [TK], ["vnew"])
                if debug == 2:
                    raise _Stop
                if not samp:
                    jmax = T["ti"] * 4 + 3
                    steps = list(range(jmax, -1, -1))
                else:
                    steps = list(range(32, -1, -1))
                    load_cache(31)

                def stageA(j):
                    z = j % 2
                    Zb, zk = pZ[z], ZK[z]
                    if not samp:
                        Kp = 128
                        b.mm(Zb[:, 0:N], KT[:, j * 128:(j + 1) * 128], qb[:, 0:N], ["KT", "qb"], [zk])
                    elif j == 32:
                        Kp = 32
                        for bi in range(DB):
                            b.mm(Zb[0:32, bi * 32:(bi + 1) * 32], KT[:, bi * 32:(bi + 1) * 32], qb[:, bi * 32:(bi + 1) * 32],
                                 ["KT", "qb"], [zk])
                    else:
                        Kp = 128
                        if j - 1 >= 0:
                            load_cache(j - 1)
                        for bi in range(DB):
                            b.mm(Zb[:, bi * 32:(bi + 1) * 32], kcb[j % 3][:, bi, :], qb[:, bi * 32:(bi + 1) * 32],
                                 ["kcb%d" % (j % 3), "qb"], [zk])
                    b.act(E[z][0:Kp, 0:N], Zb[0:Kp, 0:N], AF.Exp, [zk], ["E%d" % z])
                    b.act(SP[z][0:Kp, 0:N], E[z][0:Kp, 0:N], AF.Ln, ["E%d" % z], ["SP%d" % z], bias=1.0)
                    mk = None
                    if samp and j == 32:
                        mk = msk[0:32, 4, 0:N]
                    elif (not samp) and j >= T["ti"] * 4:
                        mk = msk[:, j - T["ti"] * 4, 0:N]
                    if mk is not None:
                        b.tt("pool", E[z][0:Kp, 0:N], E[z][0:Kp, 0:N], mk, ALU.mult, ["E%d" % z, "msk"], ["E%d" % z])
                        b.tt("pool", SP[z][0:Kp, 0:N], SP[z][0:Kp, 0:N], mk, ALU.mult, ["SP%d" % z, "msk"], ["SP%d" % z])
                    return Kp

                b.memset("dve", pCm[:, 0:N], 0.0, [CK])
                b.memset("dve", pO[:, 0:N], 0.0, [OK])
                KpA = {}
                KpA[steps[0]] = stageA(steps[0])
                for si, j in enumerate(steps):
                    z = j % 2
                    if si + 1 < len(steps):
                        KpA[steps[si + 1]] = stageA(steps[si + 1])
                    Kp = KpA[j]
                    b.mm(pCm[:, 0:N], trib[0:Kp, 0, :], SP[z][0:Kp, 0:N], ["trib", "SP%d" % z], [CK], start=False, stop=True)
                    b.act(Wt[0:Kp, 0:N], pCm[0:Kp, 0:N], AF.Exp, [CK], ["Wt"], scale=-1.0)
                    b.tt("dve", ATT[z][0:Kp, 0:N], E[z][0:Kp, 0:N], Wt[0:Kp, 0:N], ALU.mult, ["E%d" % z, "Wt"], ["ATT%d" % z])
                    b.mm(pCm[:, 0:N], trib[0:Kp, 1, :], SP[z][0:Kp, 0:N], ["trib", "SP%d" % z], [CK], start=False, stop=True)
                    if not samp:
                        b.mm(pO[:, 0:N], Vt[:, j, :], ATT[z][:, 0:N], ["Vt", "ATT%d" % z], [OK], start=False, stop=True)
                    elif j == 32:
                        for bi in range(DB):
                            b.mm(pO[:, bi * 32:(bi + 1) * 32], vnew[0:32, bi, :], ATT[z][0:32, bi * 32:(bi + 1) * 32],
                                 ["vnew", "ATT%d" % z], [OK], start=False, stop=True)
                    else:
                        for bi in range(DB):
                            b.mm(pO[:, bi * 32:(bi + 1) * 32], vcb[j % 3][:, bi, :], ATT[z][:, bi * 32:(bi + 1) * 32],
                                 ["vcb%d" % (j % 3), "ATT%d" % z], [OK], start=False, stop=True)
                b.tt("dve", ycb[:, 0:N], pO[:, 0:N], zc[:, 0:N], ALU.mult, [OK, "zc"], ["ycb"])
                b.dma("sp", y3T[0:128, T["tok0"]:T["tok0"] + N], ycb[:, 0:N], ["ycb"], ["y3T"], "yc_o")
                if debug == 3:
                    raise _Stop
                nseq = DB if samp else 1
                Ls = N // nseq
                v3 = lambda ap: ap.rearrange("p (s l) -> p s l", l=Ls)

                def tab(tb, m):
                    if samp:
                        return tb[:, m, 0:Ls].unsqueeze(1).broadcast_to([128, nseq, Ls])
                    return v3(tb[:, m, 0:N])

                for m in range(4):
                    b.mm(pj[0][:, 0:N], Bb[:, 0, m, :], ub[:, 0:N], ["Bb", "ub"], ["pj0"])
                    b.mm(pj[1][:, 0:N], Bb[:, 1, m, :], ub[:, 0:N], ["Bb", "ub"], ["pj1"])
                    Bre, Bim = v3(pj[0][:, 0:N]), v3(pj[1][:, 0:N])
                    b.tt("dve", v3(t1[:, 0:N]), Bre, tab(cr, m), ALU.mult, ["pj0", "cr"], ["t1"])
                    b.tt("dve", v3(t2[:, 0:N]), Bim, tab(sr, m), ALU.mult, ["pj1", "sr"], ["t2"])
                    b.tt("pool", bre[:, 0:N], t1[:, 0:N], t2[:, 0:N], ALU.add, ["t1", "t2"], ["bre"])
                    b.tt("dve", v3(t1[:, 0:N]), Bim, tab(cr, m), ALU.mult, ["pj1", "cr"], ["t1"])
                    b.tt("dve", v3(t2[:, 0:N]), Bre, tab(sr, m), ALU.mult, ["pj0", "sr"], ["t2"])
                    b.tt("pool", bim[:, 0:N], t1[:, 0:N], t2[:, 0:N], ALU.subtract, ["t1", "t2"], ["bim"])
                    rbc = s5c[:, m, 0:1].to_broadcast([128, Ls])
                    for sq in range(nseq):
                        col = sq if samp else 0
                        for (src, sk, dst, dk, a) in ((bre, "bre", gre, "gre", 0), (bim, "bim", gim, "gim", 1)):
                            if samp:
                                init = h0s[:, a, m, sq:sq + 1]
                                ik = "h0s"
                            elif T["first"]:
                                init, ik = 0.0, None
                            else:
                                init, ik = hst[:, a, m, 0:1], "hst"
                            b.P.op("dve", lambda e, o=dst[:, sq * Ls:(sq + 1) * Ls], d0=rbc, d1=src[:, sq * Ls:(sq + 1) * Ls], ini=init:
                                   e.tensor_tensor_scan(out=o, data0=d0, data1=d1, initial=ini, op0=ALU.mult, op1=ALU.add),
                                   reads=[sk, "s5c"] + ([ik] if ik else []), writes=[dk])
                    b.tt("dve", v3(t1[:, 0:N]), v3(gre[:, 0:N]), tab(cr, m), ALU.mult, ["gre", "cr"], ["t1"])
                    b.tt("dve", v3(t2[:, 0:N]), v3(gim[:, 0:N]), tab(sr, m), ALU.mult, ["gim", "sr"], ["t2"])
                    b.tt("pool", hreb[:, 0:N], t1[:, 0:N], t2[:, 0:N], ALU.subtract, ["t1", "t2"], ["hreb"])
                    lastv = lambda ap: v3(ap[:, 0:N])[:, :, Ls - 1]
                    if samp:
                        cols = slice(NB, NB + DB)
                    else:
                        cols = slice(T["seq"], T["seq"] + 1)
                    b.tt("dve", hst[:, 0, m, 0:nseq], lastv(t1), lastv(t2), ALU.subtract, ["t1", "t2"], ["hst"])
                    b.tt("dve", v3(t1[:, 0:N]), v3(gre[:, 0:N]), tab(sr, m), ALU.mult, ["gre", "sr"], ["t1"])
                    b.tt("dve", v3(t2[:, 0:N]), v3(gim[:, 0:N]), tab(cr, m), ALU.mult, ["gim", "cr"], ["t2"])
                    b.tt("pool", himb[:, 0:N], t1[:, 0:N], t2[:, 0:N], ALU.add, ["t1", "t2"], ["himb"])
                    b.tt("dve", hst[:, 1, m, 0:nseq], lastv(t1), lastv(t2), ALU.add, ["t1", "t2"], ["hst"])
                    if T["last"]:
                        fre, fim, nfim = s5c[:, m, 5:6], s5c[:, m, 6:7], s5c[:, m, 7:8]
                        b.ts("dve", t1[:, 0:nseq], hst[:, 1, m, 0:nseq], nfim, ALU.mult, ["hst", "s5c"], ["t1"])
                        b.stt(hfin[:, 0, m, cols], hst[:, 0, m, 0:nseq], fre, t1[:, 0:nseq], ALU.mult, ALU.add,
                              ["hst", "s5c", "t1"], ["hfin"])
                        b.ts("dve", t1[:, 0:nseq], hst[:, 1, m, 0:nseq], fre, ALU.mult, ["hst", "s5c"], ["t1"])
                        b.stt(hfin[:, 1, m, cols], hst[:, 0, m, 0:nseq], fim, t1[:, 0:nseq], ALU.mult, ALU.add,
                              ["hst", "s5c", "t1"], ["hfin"])
                    b.mm(pY[:, 0:N], Cb[:, 0, m, :], hreb[:, 0:N], ["Cb", "hreb"], [YK], start=(m == 0), stop=False)
                    b.mm(pY[:, 0:N], Cb[:, 1, m, :], himb[:, 0:N], ["Cb", "himb"], [YK], start=False, stop=(m == 3))
                b.stt(ys32[:, 0:N], u32[:, 0:N], sp_[:, 0, 3:4], pY[:, 0:N], ALU.mult, ALU.add, ["u32", "sp_", YK], ["ys32"])
                b.tt("pool", t1[:, 0:N], ys32[:, 0:N], ys32[:, 0:N], ALU.mult, ["ys32"], ["t1"])
                b.ts("dve", t1[:, 0:N], t1[:, 0:N], 0.044715, ALU.mult, ["t1"], ["t1"], s2=1.0, op1=ALU.add)
                b.tt("pool", t1[:, 0:N], t1[:, 0:N], ys32[:, 0:N], ALU.mult, ["t1", "ys32"], ["t1"])
                b.act(t1[:, 0:N], t1[:, 0:N], AF.Sigmoid, ["t1"], ["t1"], scale=2.0 * math.sqrt(2.0 / math.pi))
                b.tt("dve", ysb[:, 0:N], t1[:, 0:N], ys32[:, 0:N], ALU.mult, ["t1", "ys32"], ["ysb"])
                b.dma("sp", y3T[128:256, T["tok0"]:T["tok0"] + N], ysb[:, 0:N], ["ysb"], ["y3T"], "ys_o")
        except _Stop:
            pass
        b.dma("sp", hout, hfin[:], ["hfin"], ["hout"], "hout")
        P.finalize()
        print("phase3 instrs", P.ninstr, {e: P.cnt[e] for e in ENGS})
    return nc


def _p3_inputs(inp, x1T, c):
    A = np.ascontiguousarray
    W = inp['od_w_in'][0]
    blk = lambda i: W[:, i * 1024 + c * 128: i * 1024 + (c + 1) * 128]
    w3 = np.concatenate([blk(0), blk(1), blk(2), blk(3), blk(4), blk(5)], 1)
    kc = inp['cache_sb_k'][0, :, :, c, :]
    vc = inp['cache_sb_v'][0, :, :, c, :]
    kcT = A(kc.reshape(DB, 32, 128, 128).transpose(1, 3, 0, 2))
    vcc = A(vc.reshape(DB, 32, 128, 128).transpose(1, 2, 0, 3))
    G = slice(8 * c, 8 * c + 8)
    lre = inp['od_lam_re'][0, G]
    lim = inp['od_lam_im'][0, G]
    dt = np.broadcast_to(inp['od_log_dt'][0, G][:, None], (8, 64))
    s5p = np.zeros((128, 4, 4), np.float32)
    qm = lambda a: a.reshape(4, 128).T
    s5p[:, :, 0] = qm(lre)
    s5p[:, :, 1] = qm(lim)
    s5p[:, :, 2] = qm(dt)
    s5p[:, 0, 3] = inp['od_d'][0, c * 128:(c + 1) * 128]
    Bbd = np.zeros((128, 2, 4, 128), np.float32)
    Cbd = np.zeros((128, 2, 4, 128), np.float32)
    for a, (bk, ck) in enumerate((('od_b_re', 'od_c_re'), ('od_b_im', 'od_c_im'))):
        bb = inp[bk][0, G]
        cc = inp[ck][0, G]
        for g in range(8):
            m, g2 = g // 2, g % 2
            Bbd[g * 16:(g + 1) * 16, a, m, g2 * 64:(g2 + 1) * 64] = bb[g].T
            Cbd[g2 * 64:(g2 + 1) * 64, a, m, g * 16:(g + 1) * 16] = cc[g].T
    h0 = np.zeros((128, 2, 4, DB), np.float32)
    for a, k in enumerate(('state_s5_re', 'state_s5_im')):
        hs = inp[k][0, :, G]
        h0[:, a] = hs.reshape(DB, 4, 128).transpose(2, 1, 0)
    return dict(x1T=x1T, w3=A(w3), consts=_consts_np(), masks=np.ascontiguousarray(_masks3_np()[:, 0:5]).astype(ml_dtypes.bfloat16), tau=np.ascontiguousarray(_masks3_np()[:, 5]), kcT=kcT, vcc=vcc, s5p=s5p,
                Bbd=Bbd, Cbd=Cbd, h0=h0)


def _p4_inputs(inp, x1tok, Y3, c):
    rep = lambda v: np.ascontiguousarray(np.broadcast_to(v[None, :], (128, v.shape[0])))
    return dict(yin=_tile_tokens(Y3, c), wout=np.ascontiguousarray(inp['od_w_out'][0]), xres=_pad_rows(x1tok, c),
                lng=rep(inp['od_ln_g'][0]), lnb=rep(inp['od_ln_b'][0]), consts=_consts_np(),
                wglu=np.ascontiguousarray(inp['od_w_glu'][0]),
                bglu=np.ascontiguousarray(inp['od_b_glu'][0].reshape(8, 128).T))


_NC_CACHE = {}


def _get_nc(name, fn):
    if name not in _NC_CACHE:
        _NC_CACHE[name] = fn()
    return _NC_CACHE[name]


def kernel(**inp):
    inp = {k: np.asarray(v) for k, v in inp.items()}
    R = range(NCORES)
    xtok = _tok_major(inp['x_prompt'], inp['x_sample'])
    xT = np.ascontiguousarray(xtok.T)
    r1 = _run(_get_nc("p1", build_phase1), [_p1_inputs(inp, xT, c) for c in R])
    del xT
    Y = np.stack([r1[k]['yT'][0:128] for k in R] + [r1[k]['yT'][128:256] for k in R])
    r2 = _run(_get_nc("p2", lambda: build_out_phase(False, True)), [_p2_inputs(inp, xtok, Y, c) for c in R])
    x1tok = np.concatenate([r2[c]['xo'][:TOKC] for c in R], 0)
    x1T = np.ascontiguousarray(
        np.concatenate([r2[c]['xoT'].transpose(2, 1, 0, 3).reshape(D, -1)[:, :TOKC] for c in R], 1))
    r3 = _run(_get_nc("p3", build_phase3), [_p3_inputs(inp, x1T, c) for c in R])
    Y3 = np.stack([r3[k]['y3T'][0:128] for k in R] + [r3[k]['y3T'][128:256] for k in R]
                  + [r3[k]['y3T'][256:384] for k in R])
    r4 = _run(_get_nc("p4", lambda: build_out_phase(True, False)), [_p4_inputs(inp, x1tok, Y3, c) for c in R])
    ytok = np.concatenate([r4[c]['xo'][:TOKC] for c in R], 0)
    f32 = np.float32
    y_prompt = ytok[:NTP].reshape(NB, SEQ, D).astype(f32)
    y_sample = ytok[NTP:].reshape(DB, DS, D).astype(f32)

    def heads(key, sl):
        return np.stack([r1[c][key][sl] for c in R], 1)[None].astype(f32)
    pS, sS = slice(0, NB), slice(NB, NB + DB)
    dS_p, dS_s = heads('oS', pS), heads('oS', sS)
    mC_p, mC_s = heads('oC', pS), heads('oC', sS)

    def convout(sl):
        n = sl.stop - sl.start
        o = np.zeros((1, n, 3, 3072), f32)
        for c in R:
            oc = r1[c]['oconv'][:, sl]
            for f in range(3):
                o[0, :, :, f * 1024 + c * 128: f * 1024 + (c + 1) * 128] = oc[:, :, f, :].transpose(1, 2, 0)
        return o
    dc_p, dc_s = convout(pS), convout(sS)
    mn = lambda sl: np.stack([r1[c]['on'][:, sl].T for c in R], 1)[None].astype(f32)
    mm = lambda sl: np.stack([r1[c]['om'][0, sl] for c in R], 1)[None].astype(f32)

    def kv(key, lo, hi, shape):
        return np.stack([r3[c][key][lo:hi] for c in R], 1).reshape(shape).astype(f32)
    sk_p = kv('kout', 0, NTP, (NB, SEQ, 8, 128))[None]
    sk_s = kv('kout', NTP, NTOK, (DB, DS, 8, 128))[None]
    sv_p = kv('vout', 0, NTP, (NB, SEQ, 8, 128))[None]
    sv_s = kv('vout', NTP, NTOK, (DB, DS, 8, 128))[None]

    def s5state(a, sl):
        n = sl.stop - sl.start
        o = np.zeros((1, n, 64, 64), f32)
        for c in R:
            ho = r3[c]['hout'][:, a, :, sl]
            o[0, :, 8 * c:8 * c + 8, :] = ho.transpose(2, 1, 0).reshape(n, 8, 64)
        return o
    return (y_prompt, y_sample, dS_p, dS_s, dc_p, dc_s, mC_p, mC_s, mn(pS), mn(sS), mm(pS), mm(sS),
            sk_p, sk_s, sv_p, sv_s, s5state(0, pS), s5state(0, sS), s5state(1, pS), s5state(1, sS))
```

```python
import math
from contextlib import ExitStack

import numpy as np
import ml_dtypes
import concourse.bass as bass
import concourse.mybir as mybir
from concourse.bass_utils import run_bass_kernel_spmd

F32 = mybir.dt.float32
BF16 = mybir.dt.bfloat16
AF = mybir.ActivationFunctionType
ALU = mybir.AluOpType
AX = mybir.AxisListType
ENGS = ("pe", "act", "dve", "pool", "sp")
NCORES = 8

D = 2048
KC = D // 128
SEQ = 8192
NB = 2
DB = 8
DS = 32
PAST = 4096
NTP = NB * SEQ
NTS = DB * DS
NTOK = NTP + NTS
TOKC = NTOK // NCORES
ALPHA = (2 * 2) ** 0.25
NEG = -30000.0


class Prog:
    def __init__(self, nc, stack):
        self.nc = nc
        self.stack = stack
        self.q = {e: [] for e in ENGS}
        self.cnt = {e: 0 for e in ENGS}
        self.sem = {e: stack.enter_context(nc.semaphore("sem_" + e)) for e in ENGS}
        self.waited = {e: {} for e in ENGS}
        self.lastw = {}
        self.readers = {}
        self.dsem = {}
        self.dcnt = {}
        self.ninstr = 0

    def _deps(self, reads, writes):
        deps = []
        for k in reads:
            t = self.lastw.get(k)
            if t is not None:
                deps.append(t)
        for k in writes:
            t = self.lastw.get(k)
            if t is not None:
                deps.append(t)
            deps.extend(self.readers.get(k, ()))
        return deps

    def _emit_waits(self, eng, deps, skip_self=False):
        need = {}
        for (sid, sem, val) in deps:
            if skip_self and sid == eng:
                continue
            if self.waited[eng].get(sid, 0) >= val:
                continue
            if need.get(sid, (None, 0))[1] < val:
                need[sid] = (sem, val)
        for sid, (sem, val) in need.items():
            self.waited[eng][sid] = val
            self.q[eng].append(lambda e, sem=sem, val=val: e.wait_ge(sem, val))
            self.ninstr += 1

    def _commit(self, tok, reads, writes):
        for k in writes:
            self.lastw[k] = tok
            self.readers[k] = []
        for k in reads:
            self.readers.setdefault(k, []).append(tok)

    @staticmethod
    def _bank(k):
        if isinstance(k, str) and len(k) >= 2 and k[0] == "p" and (k[1].isupper() or k[:2] in ("pj", "pg")):
            return k[:3] if k[:2] == "pj" else k[:2]
        return None

    def _norm(self, reads, writes):
        r2, w2 = [], []
        for k in reads:
            bk = self._bank(k)
            if bk is None:
                r2.append(k)
            elif bk not in w2:
                w2.append(bk)
        for k in writes:
            bk = self._bank(k)
            k2 = k if bk is None else bk
            if k2 not in w2:
                w2.append(k2)
        return r2, w2

    def op(self, eng, fn, reads=(), writes=(), skip_self=False):
        reads, writes = self._norm(reads, writes)
        deps = self._deps(reads, writes)
        self._emit_waits(eng, deps, skip_self=skip_self)
        self.cnt[eng] += 1
        sem = self.sem[eng]
        tok = (eng, sem, self.cnt[eng])
        self.q[eng].append(lambda e, fn=fn, sem=sem: fn(e).then_inc(sem, 1))
        self.ninstr += 1
        self._commit(tok, reads, writes)
        return tok

    def dma(self, eng, out, in_, reads=(), writes=(), slot=None, **kw):
        assert slot is not None
        if slot not in self.dsem:
            self.dsem[slot] = self.stack.enter_context(self.nc.semaphore("dsem_%d" % len(self.dsem)))
            self.dcnt[slot] = 0
        deps = self._deps(reads, writes)
        self._emit_waits(eng, deps)
        self.dcnt[slot] += 16
        sem = self.dsem[slot]
        tok = (("d", slot), sem, self.dcnt[slot])
        self.q[eng].append(
            lambda e, out=out, in_=in_, sem=sem, kw=kw: e.dma_start(out=out, in_=in_, **kw).then_inc(sem, 16))
        self.ninstr += 1
        self._commit(tok, reads, writes)
        return tok

    def finalize(self):
        nc = self.nc
        alltok = list(self.lastw.values())
        for ts in self.readers.values():
            alltok.extend(ts)
        self._emit_waits("sp", alltok)
        engmap = {"pe": "tensor", "act": "scalar", "dve": "vector", "pool": "gpsimd", "sp": "sync"}
        with nc.Block() as block:
            for e in ENGS:
                lst = self.q[e]
                if not lst:
                    continue

                def body(engine, lst=lst):
                    for f in lst:
                        f(engine)
                getattr(block, engmap[e])(body)


class B:
    def __init__(self, nc, st):
        self.nc = nc
        self.st = st
        self.P = Prog(nc, st)
        self.rr = 0

    def sb(self, name, shape, dt=F32):
        return self.st.enter_context(self.nc.sbuf_tensor(name, list(shape), dt))

    def ps(self, name, shape=(128, 512), dt=F32):
        return self.st.enter_context(self.nc.psum_tensor(name, list(shape), dt))

    def din(self, name, shape, dt=F32):
        return self.nc.dram_tensor(name, list(shape), dt, kind="ExternalInput").ap()

    def dout(self, name, shape, dt=F32):
        return self.nc.dram_tensor(name, list(shape), dt, kind="ExternalOutput").ap()

    def mm(self, out, lhsT, rhs, r, w, start=True, stop=True):
        self.P.op("pe", lambda e: e.matmul(out, lhsT=lhsT, rhs=rhs, start=start, stop=stop, skip_group_check=True),
                  reads=r, writes=w, skip_self=True)

    def tr(self, out, in_, ident, r, w):
        self.P.op("pe", lambda e: e.transpose(out=out, in_=in_, identity=ident), reads=r, writes=w, skip_self=True)

    def act(self, out, in_, func, r, w, bias=None, scale=None, accum_out=None):
        kw = {}
        if bias is not None:
            kw["bias"] = bias
        if scale is not None:
            kw["scale"] = scale
        if accum_out is not None:
            kw["accum_out"] = accum_out
        self.P.op("act", lambda e: e.activation(out=out, in_=in_, func=func, **kw), reads=r, writes=w)

    def ts(self, eng, out, in0, s1, op0, r, w, s2=None, op1=None, accum_out=None):
        kw = {}
        if op1 is not None:
            kw["op1"] = op1
        if accum_out is not None:
            kw["accum_out"] = accum_out
        self.P.op(eng, lambda e: e.tensor_scalar(out=out, in0=in0, scalar1=s1, scalar2=s2, op0=op0, **kw),
                  reads=r, writes=w)

    def tt(self, eng, out, in0, in1, op, r, w):
        self.P.op(eng, lambda e: e.tensor_tensor(out=out, in0=in0, in1=in1, op=op), reads=r, writes=w)

    def stt(self, out, in0, scalar, in1, op0, op1, r, w, accum_out=None):
        kw = {}
        if accum_out is not None:
            kw["accum_out"] = accum_out
        self.P.op("dve", lambda e: e.scalar_tensor_tensor(out=out, in0=in0, scalar=scalar, in1=in1, op0=op0, op1=op1, **kw),
                  reads=r, writes=w)

    def cp(self, eng, out, in_, r, w):
        if eng == "act":
            self.P.op("act", lambda e: e.copy(out=out, in_=in_), reads=r, writes=w)
        else:
            self.P.op(eng, lambda e: e.tensor_copy(out=out, in_=in_), reads=r, writes=w)

    def memset(self, eng, ap, val, w):
        self.P.op(eng, lambda e: e.memset(ap, val), writes=w)

    def dma(self, eng, out, in_, r, w, slot, **kw):
        self.P.dma(eng, out, in_, reads=r, writes=w, slot=slot, **kw)


def _consts_np():
    c = np.zeros((128, 8, 128), np.float32)
    i = np.arange(128)
    c[:, 0, :] = np.eye(128)
    c[:, 1, :] = (i[:, None] <= i[None, :])
    c[:, 2, :] = np.where(i[None, :] < i[:, None], NEG, 0.0)
    c[:, 3, :] = np.where(i[None, :] > i[:, None], NEG, 0.0)
    c[:, 4, :] = 1.0
    c[:, 5, :] = (i[:, None] == 127)
    c[:, 6, :] = (i[:, None] == 31)
    c[:, 7, :] = (i[:, None] >= i[None, :])
    return c


class _Stop(Exception):
    pass


def _interleave(*gens):
    gens = list(gens)
    while gens:
        for g in list(gens):
            try:
                next(g)
            except StopIteration:
                gens.remove(g)


def build_phase1(ntiles_prompt=32, do_sample=True, debug=0):
    nc = bass.Bass("TRN2", target_bir_lowering=False)
    with ExitStack() as st:
        b = B(nc, st)
        P = b.P
        xT = b.din("xT", [D, NTOK])
        w1 = b.din("w1", [D, 1152])
        wg = b.din("wg", [128, KC * 4])
        convw = b.din("convw", [128, 12])
        scal = b.din("scal", [128, 8])
        consts = b.din("consts", [128, 8, 128])
        S0 = b.din("S0", [DB, 128, 128])
        conv0 = b.din("conv0", [128, DB, 3, 3])
        C0 = b.din("C0", [DB, 128, 128])
        n0 = b.din("n0", [128, DB])
        m0 = b.din("m0", [128, DB])
        yT = b.dout("yT", [256, NTOK], BF16)
        oS = b.dout("oS", [NB + DB, 128, 128])
        oconv = b.dout("oconv", [128, NB + DB, 3, 3])
        oC = b.dout("oC", [NB + DB, 128, 128])
        on = b.dout("on", [128, NB + DB])
        om = b.dout("om", [1, NB + DB])

        cst = b.sb("cst", [128, 8, 128])
        ident, triU, NEGU, NEGL, ones, SEL127, SEL31 = (cst[:, i, :] for i in range(7))
        w1b = b.sb("w1b", [128, KC, 1152], BF16)
        wgb = b.sb("wgb", [128, KC, 4], BF16)
        wst = b.sb("wst", [128, 2, 1152])
        cw = b.sb("cw", [128, 12])
        sc = b.sb("sc", [128, 8])
        dsc = b.sb("dsc", [128, 8])
        xst = [b.sb("xst%d" % i, [128, 4, 512]) for i in range(2)]
        xb = [b.sb("xb%d" % i, [128, KC, 512], BF16) for i in range(2)]
        cin = [b.sb("cin%d" % f, [128, 520]) for f in range(3)]
        pr = [b.sb("pr%d" % f, [128, 512]) for f in range(9)]
        cv = [b.sb("cv%d" % f, [128, 512]) for f in range(3)]
        tmpA = b.sb("tmpA", [128, 512])
        tmpB = b.sb("tmpB", [128, 512])
        qn = b.sb("qn", [128, 512])
        kn = b.sb("kn", [128, 512])
        oT = b.sb("oT", [128, 512])
        hT = b.sb("hT", [128, 512])
        yb = b.sb("yb", [128, 2, 512], BF16)
        gat = b.sb("gat", [128, 8, 12])
        Gbc = b.sb("Gbc", [128, 128])
        grow = b.sb("grow", [128, 128])
        DEC = b.sb("DEC", [128, 128])
        DECs = b.sb("DECs", [128, 128])
        X = b.sb("X", [128, 128])
        XT = b.sb("XT", [128, 128])
        IXT = b.sb("IXT", [128, 128])
        Pm = b.sb("Pm", [128, 128])
        QKd = b.sb("QKd", [128, 128])
        kg = b.sb("kg", [128, 128])
        kdec = b.sb("kdec", [128, 128])
        vtok = b.sb("vtok", [128, 128])
        nW = b.sb("nW", [128, 128])
        ee = b.sb("ee", [128, 128])
        qdec = b.sb("qdec", [128, 128])
        Sst = b.sb("Sst", [128, 128])
        col = b.sb("col", [128, 16])
        mprev = b.sb("mprev", [128, 1])
        bc2 = b.sb("bc2", [128, 2])
        c2 = b.sb("c2", [128, 2])
        Abc = b.sb("Abc", [128, 128])
        pmat = b.sb("pmat", [128, 128])
        pqk = b.sb("pqk", [128, 128])
        pqkT = b.sb("pqkT", [128, 128])
        v1 = b.sb("v1", [128, 129])
        kp = b.sb("kp", [128, 128])
        CN = b.sb("CN", [128, 129])
        n2s = b.sb("n2s", [128, 129])
        num = b.sb("num", [128, 128])
        htok = b.sb("htok", [128, 128])
        m0s = b.sb("m0s", [128, DB])
        n0s = b.sb("n0s", [128, DB])
        c0s = b.sb("c0s", [128, DB * 9])

        pj = [b.ps("pj%d" % i) for i in range(2)]
        pg = b.ps("pg")
        pA = b.ps("pA")
        pB = b.ps("pB")
        pC = b.ps("pC")
        pD = b.ps("pD")
        pE = b.ps("pE")

        b.dma("sp", cst[:], consts, [], ["cst"], "cst")
        b.dma("sp", cw[:], convw, [], ["cw"], "cw")
        b.dma("sp", sc[:], scal, [], ["sc"], "sc")
        b.dma("sp", m0s[:], m0, [], ["m0s"], "m0s")
        b.dma("sp", n0s[:], n0, [], ["n0s"], "n0s")
        b.dma("sp", c0s[:], conv0.rearrange("p b f r -> p (b f r)"), [], ["c0s"], "c0s")
        w1v = w1.rearrange("(kc p) n -> p kc n", p=128)
        for kc0 in range(0, KC, 2):
            b.dma("sp", wst[:, :, :], w1v[:, kc0:kc0 + 2, :], [], ["wst"], "wst")
            b.cp("dve", w1b[:, kc0, :], wst[:, 0, :], ["wst"], ["w1b"])
            b.cp("pool", w1b[:, kc0 + 1, :], wst[:, 1, :], ["wst"], ["w1b"])
        b.dma("sp", wst[:, 0, 0:KC * 4], wg, [], ["wst"], "wst")
        b.cp("dve", wgb[:].rearrange("p kc n -> p (kc n)"), wst[:, 0, 0:KC * 4], ["wst"], ["wgb"])
        b.act(dsc[:, 0:1], sc[:, 0:1], AF.Exp, ["sc"], ["dsc"])
        b.ts("dve", dsc[:, 0:1], dsc[:, 0:1], -1.0, ALU.mult, ["dsc"], ["dsc"])
        b.ts("dve", dsc[:, 1:2], sc[:, 3:4], -1.0, ALU.mult, ["sc"], ["dsc"])
        b.memset("pool", v1[:, 128:129], 1.0, ["v1"])

        xTv = xT.rearrange("(kc p) n -> p kc n", p=128)

        tiles = []
        for bi in range(NB):
            for t in range(ntiles_prompt // NB):
                tiles.append(dict(kind="p", seq=bi, tok0=bi * SEQ + t * 512, N=512, L=128, nch=4,
                                  first=(t == 0), last=(t == ntiles_prompt // NB - 1)))
        if do_sample:
            tiles.append(dict(kind="s", seq=None, tok0=NTP, N=256, L=32, nch=8, first=True, last=True))

        def load_x(ti, slot):
            T = tiles[ti]
            N = T["N"]
            for q4 in range(4):
                b.dma("sp", xst[q4 % 2][:, :, 0:N], xTv[:, q4 * 4:(q4 + 1) * 4, T["tok0"]:T["tok0"] + N],
                      [], ["xst%d" % (q4 % 2)], "xst%d" % (q4 % 2))
                b.cp("pool", xb[slot][:, q4 * 4:(q4 + 1) * 4, 0:N], xst[q4 % 2][:, :, 0:N],
                     ["xst%d" % (q4 % 2)], ["xb%d" % slot])

        load_x(0, 0)
        try:
          if debug == 0.5:
              raise _Stop
          for ti, T in enumerate(tiles):
              slot = ti % 2
              N, L, nch = T["N"], T["L"], T["nch"]
              xbk = "xb%d" % slot
              if ti + 1 < len(tiles):
                  load_x(ti + 1, 1 - slot)
              nseq = 1 if T["kind"] == "p" else DB
              Ls = N // nseq
              W = Ls + 3

              def cin3(f, a, bnd):
                  return cin[f][:, 0:nseq * W].rearrange("p (s l) -> p s l", l=W)[:, :, a:bnd]

              if T["first"]:
                  if T["kind"] == "p":
                      for f in range(3):
                          b.memset("pool", cin[f][:, 0:3], 0.0, ["cin%d" % f])
                  else:
                      for f in range(3):
                          b.cp("pool", cin3(f, 0, 3),
                               c0s[:].rearrange("p (b f r) -> p b f r", f=3, r=3)[:, :, f, :],
                               ["c0s", "cin%d" % f], ["cin%d" % f])
              for f in range(9):
                  bank = pj[f % 2]
                  bk = "pj%d" % (f % 2)
                  for kc in range(KC):
                      b.mm(bank[:, 0:N], w1b[:, kc, f * 128:(f + 1) * 128], xb[slot][:, kc, 0:N],
                           ["w1b", xbk], [bk], start=(kc == 0), stop=(kc == KC - 1))
                  if f < 3:
                      b.cp("act", cin3(f, 3, W), bank[:, 0:N].rearrange("p (s l) -> p s l", l=Ls), [bk], ["cin%d" % f])
                  elif f == 5:
                      b.act(pr[f][:, 0:N], bank[:, 0:N], AF.Copy, [bk], ["pr%d" % f], scale=128.0 ** -0.5)
                  else:
                      b.cp("dve" if f % 2 else "act", pr[f][:, 0:N], bank[:, 0:N], [bk], ["pr%d" % f])
              for j in range(nch):
                  for kc in range(KC):
                      b.mm(pg[0:L, j * 4:(j + 1) * 4], xb[slot][:, kc, j * L:(j + 1) * L], wgb[:, kc, :],
                           [xbk, "wgb"], ["pg_g"], start=(kc == 0), stop=(kc == KC - 1))
              G = pg[0:L, 0:nch * 4].rearrange("p (j g) -> p j g", g=4)
              gt = lambda i: gat[0:L, 0:nch, i]
              b.act(gt(0), G[:, :, 0], AF.Sigmoid, ["pg_g"], ["gat"])
              b.ts("dve", gt(1), gt(0), -1.0, ALU.mult, ["gat"], ["gat"])
              b.act(gt(9), G[:, :, 1], AF.Exp, ["pg_g", "sc"], ["gat"], bias=sc[0:L, 1:2])
              b.act(gt(9), gt(9), AF.Ln, ["gat"], ["gat"], bias=1.0)
              b.ts("dve", gt(2), gt(9), dsc[0:L, 0:1], ALU.mult, ["gat", "dsc"], ["gat"])
              b.ts("dve", gt(3), G[:, :, 2], sc[0:L, 2:3], ALU.add, ["pg_g", "sc"], ["gat"])
              b.act(gt(10), G[:, :, 3], AF.Exp, ["pg_g", "dsc"], ["gat"], bias=dsc[0:L, 1:2], scale=-1.0)
              b.act(gt(10), gt(10), AF.Ln, ["gat"], ["gat"], bias=1.0)
              b.ts("dve", gt(4), gt(10), -1.0, ALU.mult, ["gat"], ["gat"])
              gtmp = tmpA[0:L, 0:2 * nch]
              b.cp("dve", gtmp[:, 0:nch], gt(2), ["gat"], ["tmpA"])
              b.cp("dve", gtmp[:, nch:2 * nch], gt(4), ["gat"], ["tmpA"])
              b.mm(pg[0:L, 32:32 + 2 * nch], triU[0:L, 0:L], gtmp, ["cst", "tmpA"], ["pg_c"])
              b.cp("dve", gt(5), pg[0:L, 32:32 + nch], ["pg_c"], ["gat"])
              b.cp("dve", gt(7), pg[0:L, 32 + nch:32 + 2 * nch], ["pg_c"], ["gat"])
              b.act(gt(6), gt(5), AF.Exp, ["gat"], ["gat"])
              b.ts("dve", gt(11), gt(5), -1.0, ALU.mult, ["gat"], ["gat"])
              b.tt("dve", gt(8), gt(3), gt(7), ALU.subtract, ["gat"], ["gat"])

              if debug == 1:
                  raise _Stop
              for f in range(3):
                  ck = "cin%d" % f
                  o3 = tmpA[:, 0:N].rearrange("p (s l) -> p s l", l=Ls)
                  b.ts("dve", o3, cin3(f, 0, Ls), cw[:, f * 4:f * 4 + 1], ALU.mult, [ck, "cw"], ["tmpA"])
                  for i in range(1, 4):
                      b.stt(o3, cin3(f, i, i + Ls), cw[:, f * 4 + i:f * 4 + i + 1], o3, ALU.mult, ALU.add,
                            [ck, "cw", "tmpA"], ["tmpA"])
                  b.act(cv[f][:, 0:N], tmpA[:, 0:N], AF.Silu, ["tmpA"], ["cv%d" % f])
                  if T["last"]:
                      if T["kind"] == "p":
                          b.dma("sp", oconv[:, T["seq"], f, :], cin[f][:, Ls:Ls + 3], [ck], ["oconv"], "oconv%d" % f)
                      else:
                          b.dma("sp", oconv[:, NB:NB + DB, f, :], cin3(f, Ls, Ls + 3), [ck], ["oconv"], "oconv%d" % f)
                  else:
                      b.cp("pool", cin[f][:, 0:3], cin[f][:, Ls:Ls + 3], [ck], [ck])
              for f, dst, dk, lnb in ((0, qn, "qn", -0.5 * math.log(128.0)), (1, kn, "kn", 0.0)):
                  b.tt("pool", tmpB[:, 0:N], cv[f][:, 0:N], cv[f][:, 0:N], ALU.mult, ["cv%d" % f], ["tmpB"])
                  b.mm(pj[f][:, 0:N], ones, tmpB[:, 0:N], ["cst", "tmpB"], ["pj%d" % f])
                  b.act(tmpB[:, 0:N], pj[f][:, 0:N], AF.Ln, ["pj%d" % f], ["tmpB"], bias=1e-6)
                  b.act(tmpB[:, 0:N], tmpB[:, 0:N], AF.Exp, ["tmpB"], ["tmpB"], scale=-0.5, bias=lnb)
                  b.tt("dve", dst[:, 0:N], cv[f][:, 0:N], tmpB[:, 0:N], ALU.mult, ["cv%d" % f, "tmpB"], [dk])

              if debug == 2:
                  raise _Stop
              for j in range(nch):
                  c0, c1 = j * L, (j + 1) * L
                  if T["kind"] == "s":
                      seq = NB + j
                      b.dma("sp", Sst[:], S0[j], [], ["S"], "S_in")
                      b.dma("sp", CN[:, 0:128], C0[j], [], ["CN"], "CN_in")
                      b.cp("pool", CN[:, 128:129], n0s[:, j:j + 1], ["n0s"], ["CN"])
                      b.cp("pool", mprev[:], m0s[:, j:j + 1], ["m0s"], ["mprev"])
                      seq_last = True
                  else:
                      seq = T["seq"]
                      if T["first"] and j == 0:
                          b.memset("pool", Sst[:], 0.0, ["S"])
                          b.memset("pool", CN[:], 0.0, ["CN"])
                          b.memset("pool", mprev[:], 0.0, ["mprev"])
                      seq_last = T["last"] and j == nch - 1
                  I_L = ident[0:L, 0:L]
                  def g_gdn():
                      gcol = gat[0:L, j, 2:3]
                      b.ts('dve', Gbc[0:L, :], ones[0:L, :], gcol, ALU.mult, ['cst', 'gat'], ['Gbc'])
                      yield
                      b.mm(pA[:, 0:L], Gbc[0:L, :], triU[0:L, 0:L], ['Gbc', 'cst'], ['pA0'])
                      yield
                      b.mm(pA[0:L, 128:128 + L], Gbc[0:L, 0:L], triU[0:L, 0:L], ['Gbc', 'cst'], ['pA1'], start=True, stop=False)
                      yield
                      b.mm(pA[0:L, 128:128 + L], I_L, NEGU[0:L, 0:L], ['cst'], ['pA1'], start=False, stop=True)
                      yield
                      b.act(grow[:, 0:L], pA[:, 0:L], AF.Exp, ['pA0'], ['grow'])
                      yield
                      b.act(DEC[0:L, 0:L], pA[0:L, 128:128 + L], AF.Exp, ['pA1', 'gat'], ['DEC'], bias=gat[0:L, j, 11:12])
                      yield
                      b.cp('dve', col[:, 0:1], pA[:, L - 1:L], ['pA0'], ['col0'])
                      yield
                      b.act(col[:, 1:2], col[:, 0:1], AF.Exp, ['col0'], ['col1'])
                      yield
                      b.act(col[0:L, 2:3], gat[0:L, j, 5:6], AF.Exp, ['gat', 'col0'], ['col2'], scale=-1.0, bias=col[0:L, 0:1])
                      yield
                      b.tt('pool', DECs[0:L, 0:L], DEC[0:L, 0:L], I_L, ALU.subtract, ['DEC', 'cst'], ['DECs'])
                      yield
                      b.mm(pA[0:L, 256:256 + L], kn[:, c0:c1], kn[:, c0:c1], ['kn'], ['pA2'])
                      yield
                      b.mm(pA[0:L, 384:384 + L], kn[:, c0:c1], qn[:, c0:c1], ['kn', 'qn'], ['pA3'])
                      yield
                      b.stt(X[0:L, 0:L], pA[0:L, 256:256 + L], gat[0:L, j, 1:2], DECs[0:L, 0:L], ALU.mult, ALU.mult, ['pA2', 'gat', 'DECs'], ['X'])
                      yield
                      b.tt('dve', QKd[0:L, 0:L], pA[0:L, 384:384 + L], DEC[0:L, 0:L], ALU.mult, ['pA3', 'DEC'], ['QKd'])
                      yield
                      b.tr(pC[0:L, 0:128], kn[:, c0:c1], ident, ['kn', 'cst'], ['pC0'])
                      yield
                      b.tr(pC[0:L, 128:256], cv[2][:, c0:c1], ident, ['cv2', 'cst'], ['pC1'])
                      yield
                      b.ts('dve', kg[0:L, :], pC[0:L, 0:128], gat[0:L, j, 6:7], ALU.mult, ['pC0', 'gat'], ['kg'])
                      yield
                      b.ts('dve', kdec[0:L, :], pC[0:L, 0:128], col[0:L, 2:3], ALU.mult, ['pC0', 'col2'], ['kdec'])
                      yield
                      b.cp('act', vtok[0:L, :], pC[0:L, 128:256], ['pC1'], ['vtok'])
                      yield
                      b.tr(pB[0:L, 384:384 + L], X[0:L, 0:L], I_L, ['X', 'cst'], ['pB3'])
                      yield
                      b.cp('act', XT[0:L, 0:L], pB[0:L, 384:384 + L], ['pB3'], ['XT'])
                      yield
                      b.tt('pool', Pm[0:L, 0:L], X[0:L, 0:L], I_L, ALU.add, ['X', 'cst'], ['Pm'])
                      yield
                      nsq = int(math.log2(L)) - 1
                      for it in range(nsq):
                          lastit = it == nsq - 1
                          if not lastit:
                              b.mm(pB[0:L, 0:L], XT[0:L, 0:L], X[0:L, 0:L], ['XT', 'X'], ['pB0'])
                              yield
                          b.mm(pB[0:L, 128:128 + L], X[0:L, 0:L], XT[0:L, 0:L], ['XT', 'X'], ['pB1'])
                          yield
                          if not lastit:
                              b.cp('act', X[0:L, 0:L], pB[0:L, 0:L], ['pB0'], ['X'])
                              yield
                              b.cp('dve', XT[0:L, 0:L], pB[0:L, 128:128 + L], ['pB1'], ['XT'])
                              yield
                          b.tt('dve', IXT[0:L, 0:L], pB[0:L, 128:128 + L], I_L, ALU.add, ['pB1', 'cst'], ['IXT'])
                          yield
                          b.mm(pB[0:L, 256:256 + L], IXT[0:L, 0:L], Pm[0:L, 0:L], ['IXT', 'Pm'], ['pB2'])
                          yield
                          b.cp('act', Pm[0:L, 0:L], pB[0:L, 256:256 + L], ['pB2'], ['Pm'])
                          yield
                      b.mm(pC[:, 256:256 + L], kg[0:L, :], Pm[0:L, 0:L], ['kg', 'Pm'], ['pC2'])
                      yield
                      b.ts('dve', nW[:, 0:L], pC[:, 256:256 + L], -1.0, ALU.mult, ['pC2'], ['nW'])
                      yield
                      b.mm(pC[0:L, 384:512], Pm[0:L, 0:L], vtok[0:L, :], ['Pm', 'vtok'], ['pC3'], start=True, stop=False)
                      yield
                      b.mm(pC[0:L, 384:512], nW[:, 0:L], Sst[:], ['nW', 'S'], ['pC3'], start=False, stop=True)
                      yield
                      b.ts('dve', ee[0:L, :], pC[0:L, 384:512], gat[0:L, j, 0:1], ALU.mult, ['pC3', 'gat'], ['ee'])
                      yield
                      b.tt('pool', qdec[:, 0:L], qn[:, c0:c1], grow[:, 0:L], ALU.mult, ['qn', 'grow'], ['qdec'])
                      yield
                      b.mm(pD[:, 0:L], Sst[:], qdec[:, 0:L], ['S', 'qdec'], ['pD0'], start=True, stop=False)
                      yield
                      b.mm(pD[:, 0:L], ee[0:L, :], QKd[0:L, 0:L], ['ee', 'QKd'], ['pD0'], start=False, stop=True)
                      yield
                      b.cp('act', oT[:, c0:c1], pD[:, 0:L], ['pD0'], ['oT'])
                      yield
                      b.mm(pD[:, 128:256], kdec[0:L, :], ee[0:L, :], ['kdec', 'ee'], ['pD1'])
                      yield
                      b.stt(Sst[:], Sst[:], col[:, 1:2], pD[:, 128:256], ALU.mult, ALU.add, ['S', 'col1', 'pD1'], ['S'])
                      yield
                      if seq_last:
                          b.dma('sp', oS[seq], Sst[:], ['S'], ['oS'], 'oS')
                          yield
                      yield
                  def g_ml():
                      qb, kb, vb = (pr[4], pr[5], pr[6])
                      acol = gat[0:L, j, 8:9]
                      b.ts('dve', Abc[0:L, 0:L], ones[0:L, 0:L], acol, ALU.mult, ['cst', 'gat'], ['Abc'])
                      yield
                      b.mm(pD[0:L, 256:256 + L], Abc[0:L, 0:L], I_L, ['Abc', 'cst'], ['pD2'], start=True, stop=False)
                      yield
                      b.mm(pD[0:L, 256:256 + L], I_L, NEGL[0:L, 0:L], ['cst'], ['pD2'], start=False, stop=True)
                      yield
                      b.P.op('dve', lambda e, o=col[0:L, 3:4], i=pD[0:L, 256:256 + L]: e.tensor_reduce(out=o, in_=i, axis=AX.X, op=ALU.max), reads=['pD2'], writes=['col3'])
                      yield
                      b.tt('dve', col[0:L, 4:5], col[0:L, 3:4], mprev[0:L, :], ALU.max, ['col3', 'mprev'], ['col4'])
                      yield
                      b.ts('dve', col[0:L, 5:6], col[0:L, 4:5], -1.0, ALU.mult, ['col4'], ['col5'])
                      yield
                      b.act(pmat[0:L, 0:L], pD[0:L, 256:256 + L], AF.Exp, ['pD2', 'col5'], ['pmat'], bias=col[0:L, 5:6])
                      yield
                      b.act(col[0:L, 6:7], mprev[0:L, :], AF.Exp, ['mprev', 'col5'], ['col6'], bias=col[0:L, 5:6])
                      yield
                      b.tt('dve', col[0:L, 7:8], gat[0:L, j, 7:8], col[0:L, 4:5], ALU.add, ['gat', 'col4'], ['col7'])
                      yield
                      b.act(col[0:L, 8:9], col[0:L, 7:8], AF.Exp, ['col7'], ['col8'], scale=-1.0)
                      yield
                      b.mm(pD[0:L, 384:384 + L], qb[:, c0:c1], kb[:, c0:c1], ['pr4', 'pr5'], ['pD3'])
                      yield
                      b.stt(pqk[0:L, 0:L], pD[0:L, 384:384 + L], 1.0, pmat[0:L, 0:L], ALU.mult, ALU.mult, ['pD3', 'pmat'], ['pqk', 'col9'], accum_out=col[0:L, 9:10])
                      yield
                      b.tr(pE[0:L, 0:L], pqk[0:L, 0:L], I_L, ['pqk', 'cst'], ['pE0'])
                      yield
                      b.cp('act', pqkT[0:L, 0:L], pE[0:L, 0:L], ['pE0'], ['pqkT'])
                      yield
                      b.cp('dve', c2[0:L, 0:1], col[0:L, 4:5], ['col4'], ['c2'])
                      yield
                      b.cp('dve', c2[0:L, 1:2], gat[0:L, j, 7:8], ['gat'], ['c2'])
                      yield
                      SEL = SEL127 if L == 128 else SEL31
                      b.mm(pg[:, 48:50], SEL[0:L, :], c2[0:L, :], ['cst', 'c2'], ['pg_b'])
                      yield
                      b.cp('dve', bc2[:], pg[:, 48:50], ['pg_b'], ['bc2'])
                      yield
                      b.ts('dve', col[:, 10:11], bc2[:, 0:1], -1.0, ALU.mult, ['bc2'], ['col10'])
                      yield
                      b.act(col[0:L, 11:12], acol, AF.Exp, ['gat', 'col10'], ['col11'], bias=col[0:L, 10:11])
                      yield
                      b.act(col[:, 12:13], mprev[:], AF.Exp, ['mprev', 'col10'], ['col12'], bias=col[:, 10:11])
                      yield
                      b.tr(pE[0:L, 128:256], kb[:, c0:c1], ident, ['pr5', 'cst'], ['pE1'])
                      yield
                      b.tr(pE[0:L, 256:384], vb[:, c0:c1], ident, ['pr6', 'cst'], ['pE2'])
                      yield
                      b.ts('dve', kp[0:L, :], pE[0:L, 128:256], col[0:L, 11:12], ALU.mult, ['pE1', 'col11'], ['kp'])
                      yield
                      b.cp('act', v1[0:L, 0:128], pE[0:L, 256:384], ['pE2'], ['v1'])
                      yield
                      b.mm(pg[0:L, 64:192], pqkT[0:L, 0:L], v1[0:L, 0:128], ['pqkT', 'v1'], ['pg_n1'])
                      yield
                      b.mm(pg[0:L, 192:321], qb[:, c0:c1], CN[:, 0:129], ['pr4', 'CN'], ['pg_n2'])
                      yield
                      b.act(n2s[0:L, :], pg[0:L, 192:321], AF.Copy, ['pg_n2', 'col6'], ['n2s'], scale=col[0:L, 6:7])
                      yield
                      b.tt('dve', num[0:L, :], n2s[0:L, 0:128], pg[0:L, 64:192], ALU.add, ['n2s', 'pg_n1'], ['num'])
                      yield
                      b.tt('dve', col[0:L, 13:14], n2s[0:L, 128:129], col[0:L, 9:10], ALU.add, ['n2s', 'col9'], ['col13'])
                      yield
                      b.act(col[0:L, 13:14], col[0:L, 13:14], AF.Abs, ['col13'], ['col13'])
                      yield
                      b.tt('dve', col[0:L, 13:14], col[0:L, 13:14], col[0:L, 8:9], ALU.max, ['col13', 'col8'], ['col13'])
                      yield
                      b.P.op('dve', lambda e, o=col[0:L, 14:15], i=col[0:L, 13:14]: e.reciprocal(out=o, in_=i), reads=['col13'], writes=['col14'])
                      yield
                      b.ts('dve', htok[0:L, :], num[0:L, :], col[0:L, 14:15], ALU.mult, ['num', 'col14'], ['htok'])
                      yield
                      b.tr(pE[:, 384:384 + L], htok[0:L, :], I_L, ['htok', 'cst'], ['pE3'])
                      yield
                      b.cp('act', hT[:, c0:c1], pE[:, 384:384 + L], ['pE3'], ['hT'])
                      yield
                      b.mm(pg[:, 321:450], kp[0:L, :], v1[0:L, :], ['kp', 'v1'], ['pg_c2'])
                      yield
                      b.stt(CN[:], CN[:], col[:, 12:13], pg[:, 321:450], ALU.mult, ALU.add, ['CN', 'col12', 'pg_c2'], ['CN'])
                      yield
                      b.tt('dve', mprev[:], bc2[:, 0:1], bc2[:, 1:2], ALU.add, ['bc2'], ['mprev'])
                      yield
                      if seq_last:
                          b.dma('sp', oC[seq], CN[:, 0:128], ['CN'], ['oC'], 'oC')
                          yield
                          b.dma('sp', on[:, seq:seq + 1], CN[:, 128:129], ['CN'], ['on'], 'on', allow_slow_non_contiguous=True)
                          yield
                          b.dma('sp', om[0:1, seq:seq + 1], mprev[0:1, 0:1], ['mprev'], ['om'], 'om')
                          yield
                      yield
                  _interleave(g_gdn(), g_ml())
              if debug == 4:
                  raise _Stop
              for (src, srck, zf, of, ncol, half) in ((oT, "oT", 3, None, 4, 0), (hT, "hT", 8, 7, 5, 1)):
                  if of is not None:
                      b.act(tmpA[:, 0:N], pr[of][:, 0:N], AF.Sigmoid, ["pr%d" % of], ["tmpA"])
                      b.tt("dve", src[:, 0:N], src[:, 0:N], tmpA[:, 0:N], ALU.mult, [srck, "tmpA"], [srck])
                  b.tt("pool", tmpB[:, 0:N], src[:, 0:N], src[:, 0:N], ALU.mult, [srck], ["tmpB"])
                  b.mm(pj[half][:, 0:N], ones, tmpB[:, 0:N], ["cst", "tmpB"], ["pj%d" % half])
                  b.act(tmpB[:, 0:N], pj[half][:, 0:N], AF.Ln, ["pj%d" % half], ["tmpB"], scale=1.0 / 128.0, bias=1e-6)
                  b.act(tmpB[:, 0:N], tmpB[:, 0:N], AF.Exp, ["tmpB"], ["tmpB"], scale=-0.5)
                  b.act(tmpA[:, 0:N], pr[zf][:, 0:N], AF.Silu, ["pr%d" % zf], ["tmpA"])
                  b.stt(tmpB[:, 0:N], tmpB[:, 0:N], sc[:, ncol:ncol + 1], tmpA[:, 0:N], ALU.mult, ALU.mult,
                        ["tmpB", "sc", "tmpA"], ["tmpB"])
                  b.tt("dve", yb[:, half, 0:N], src[:, 0:N], tmpB[:, 0:N], ALU.mult, [srck, "tmpB"], ["yb%d" % half])
                  b.dma("sp", yT[half * 128:(half + 1) * 128, T["tok0"]:T["tok0"] + N], yb[:, half, 0:N],
                        ["yb%d" % half], ["yT"], "yT%d" % half)

        except _Stop:
            pass
        P.finalize()
        print("phase1 instrs", P.ninstr, {e: P.cnt[e] for e in ENGS})
    return nc


NTT = 17


def build_out_phase(glu, emit_T):
    nc = bass.Bass("TRN2", target_bir_lowering=False)
    KCI = 24 if glu else 16
    with ExitStack() as st:
        b = B(nc, st)
        P = b.P
        yin = b.din("yin", [NTT, 128, KCI, 128], BF16)
        wout = b.din("wout", [D, D])
        xres = b.din("xres", [NTT * 128, D])
        lng = b.din("lng", [128, D])
        lnb = b.din("lnb", [128, D])
        consts = b.din("consts", [128, 8, 128])
        if glu:
            wglu = b.din("wglu", [1024, 1024])
            bglu = b.din("bglu", [128, 8])
        xo = b.dout("xo", [NTT * 128, D])
        if emit_T:
            xoT = b.dout("xoT", [NTT, 128, KC, 128], BF16)

        cst = b.sb("cst", [128, 8, 128])
        ident = cst[:, 0, :]
        woutb = b.sb("woutb", [128, KC, D], BF16)
        wst = b.sb("wst", [128, 2, D])
        g_sb = b.sb("g_sb", [128, D])
        b_sb = b.sb("b_sb", [128, D])
        ysb = [b.sb("ysb%d" % i, [128, KCI, 128], BF16) for i in range(2)]
        xr = [b.sb("xr%d" % i, [128, D]) for i in range(2)]
        z = b.sb("z", [128, D])
        stats = b.sb("stats", [128, 4, 6])
        mv = b.sb("mv", [128, 4])
        xT_sb = [b.sb("xT_sb%d" % i, [128, KC, 128], BF16) for i in range(2)]
        if glu:
            wglub = b.sb("wglub", [128, 8, 1024], BF16)
            bg = b.sb("bg", [128, 8])
            sig = b.sb("sig", [128, 8, 128])
            yd = b.sb("yd", [128, 8, 128], BF16)
        po = [b.ps("po%d" % i) for i in range(4)]
        pt = [b.ps("pt%d" % i) for i in range(4)]
        PK = lambda i: "pj%d" % i

        b.dma("sp", cst[:], consts, [], ["cst"], "cst")
        b.dma("sp", g_sb[:], lng, [], ["g_sb"], "g_sb")
        b.dma("sp", b_sb[:], lnb, [], ["b_sb"], "b_sb")
        wv = wout.rearrange("(kc p) n -> p kc n", p=128)
        for kc0 in range(0, KC, 2):
            b.dma("sp", wst[:], wv[:, kc0:kc0 + 2, :], [], ["wst"], "wst")
            b.cp("dve", woutb[:, kc0, :], wst[:, 0, :], ["wst"], ["woutb"])
            b.cp("pool", woutb[:, kc0 + 1, :], wst[:, 1, :], ["wst"], ["woutb"])
        if glu:
            gv = wglu.rearrange("(kc p) n -> p kc n", p=128)
            for kc0 in range(0, 8, 4):
                b.dma("sp", wst[:].rearrange("p a (c n) -> p (a c) n", n=1024), gv[:, kc0:kc0 + 4, :], [], ["wst"], "wst")
                b.cp("pool", wglub[:, kc0:kc0 + 4, :], wst[:].rearrange("p a (c n) -> p (a c) n", n=1024), ["wst"], ["wglub"])
            b.dma("sp", bg[:], bglu, [], ["bg"], "bg")

        def load(ti):
            s = ti % 2
            b.dma("sp", ysb[s][:], yin[ti], [], ["ysb%d" % s], "ysb%d" % s)
            b.dma("sp", xr[s][:], xres[ti * 128:(ti + 1) * 128, :], [], ["xr%d" % s], "xr%d" % s)

        load(0)
        for ti in range(NTT):
            s = ti % 2
            if ti + 1 < NTT:
                load(ti + 1)
            yk = "ysb%d" % s
            if glu:
                for f in range(8):
                    bank = pt[f // 4]
                    for kc in range(8):
                        b.mm(bank[:, (f % 4) * 128:(f % 4 + 1) * 128], wglub[:, kc, f * 128:(f + 1) * 128],
                             ysb[s][:, 8 + kc, :], ["wglub", yk], [PK(4 + f // 4)], start=(kc == 0), stop=(kc == 7))
                for f in range(8):
                    b.act(sig[:, f, :], pt[f // 4][:, (f % 4) * 128:(f % 4 + 1) * 128], AF.Sigmoid,
                          [PK(4 + f // 4), "bg"], ["sig"], bias=bg[:, f:f + 1])
                b.tt("dve", sig[:], sig[:], ysb[s][:, 8:16, :], ALU.mult, ["sig", yk], ["sig"])
                b.tt("dve", yd[:], sig[:], ysb[s][:, 16:24, :], ALU.mult, ["sig", yk], ["yd"])
            for n in range(4):
                for kc in range(KC):
                    if glu and kc >= 8:
                        lhsT, lk = yd[:, kc - 8, :], "yd"
                    else:
                        lhsT, lk = ysb[s][:, kc, :], yk
                    b.mm(po[n][:, :], lhsT, woutb[:, kc, n * 512:(n + 1) * 512], [lk, "woutb"], [PK(n)],
                         start=(kc == 0), stop=(kc == KC - 1))
            for n in range(4):
                b.stt(z[:, n * 512:(n + 1) * 512], xr[s][:, n * 512:(n + 1) * 512], ALPHA, po[n][:, :],
                      ALU.mult, ALU.add, ["xr%d" % s, PK(n)], ["z"])
                b.P.op("dve", lambda e, o=stats[:, n, :], i=z[:, n * 512:(n + 1) * 512]: e.bn_stats(out=o, in_=i),
                       reads=["z"], writes=["stats"])
            b.P.op("dve", lambda e, o=mv[:, 0:2], i=stats[:].rearrange("p a c -> p (a c)"): e.bn_aggr(out=o, in_=i),
                   reads=["stats"], writes=["mv"])
            b.act(mv[:, 2:3], mv[:, 1:2], AF.Ln, ["mv"], ["mv"], bias=1e-5)
            b.act(mv[:, 2:3], mv[:, 2:3], AF.Exp, ["mv"], ["mv"], scale=-0.5)
            b.ts("dve", z[:], z[:], mv[:, 0:1], ALU.subtract, ["z", "mv"], ["z"], s2=mv[:, 2:3], op1=ALU.mult)
            b.tt("pool", z[:], z[:], g_sb[:], ALU.mult, ["z", "g_sb"], ["z"])
            b.tt("dve", z[:], z[:], b_sb[:], ALU.add, ["z", "b_sb"], ["z"])
            b.dma("sp", xo[ti * 128:(ti + 1) * 128, :], z[:], ["z"], ["xo"], "xo")
            if emit_T:
                for kc in range(KC):
                    b.tr(pt[kc // 4][:, (kc % 4) * 128:(kc % 4 + 1) * 128], z[:, kc * 128:(kc + 1) * 128], ident,
                         ["z", "cst"], [PK(4 + kc // 4)])
                for q in range(4):
                    b.cp("act" if q % 2 else "dve", xT_sb[s][:, q * 4:(q + 1) * 4, :].rearrange("p a t -> p (a t)"),
                         pt[q][:, :], [PK(4 + q)], ["xT_sb%d" % s])
                b.dma("sp", xoT[ti], xT_sb[s][:], ["xT_sb%d" % s], ["xoT"], "xoT%d" % s)
        P.finalize()
        print("out-phase instrs", P.ninstr, {e: P.cnt[e] for e in ENGS})
    return nc


def _run(nc, in_maps):
    return run_bass_kernel_spmd(nc, in_maps, core_ids=list(range(NCORES))).results


def _tok_major(inp_p, inp_s):
    return np.concatenate([inp_p.reshape(-1, inp_p.shape[-1]), inp_s.reshape(-1, inp_s.shape[-1])], 0)


def _p1_inputs(inp, xT, c):
    H = 128
    W = inp['ev_w_in'][0]
    offs = np.cumsum([0, 3072, 1024, 8, 8, 1024, 1024, 1024, 1024, 1024, 8, 8])

    def colblk(i, sub=0):
        return W[:, offs[i] + sub * 1024 + c * H: offs[i] + sub * 1024 + (c + 1) * H]
    w1 = np.concatenate([colblk(0, 0), colblk(0, 1), colblk(0, 2), colblk(1),
                         colblk(4), colblk(5), colblk(6), colblk(7), colblk(8)], 1)
    wg = np.stack([W[:, offs[2] + c], W[:, offs[3] + c], W[:, offs[9] + c], W[:, offs[10] + c]], 1)
    wg = wg.reshape(KC, 128, 4).transpose(1, 0, 2).reshape(128, KC * 4)
    cw = inp['ev_conv_w'][0]
    convw = np.stack([cw[:, f * 1024 + c * H: f * 1024 + (c + 1) * H].T for f in range(3)], 1).reshape(H, 12)
    scal = np.zeros((128, 8), np.float32)
    scal[:, 0] = inp['ev_a_log'][0, c]
    scal[:, 1] = inp['ev_dt_bias'][0, c]
    scal[:, 2] = inp['ev_ig_bias'][0, c]
    scal[:, 3] = inp['ev_fg_bias'][0, c]
    scal[:, 4] = inp['ev_norm_a'][0]
    scal[:, 5] = inp['ev_norm_b'][0, c * H:(c + 1) * H]
    cs = inp['state_delta_conv'][0]
    conv0 = np.stack([cs[:, :, f * 1024 + c * H: f * 1024 + (c + 1) * H] for f in range(3)], 0)
    conv0 = np.ascontiguousarray(conv0.transpose(3, 1, 0, 2))
    A = np.ascontiguousarray
    return dict(xT=xT, w1=A(w1), wg=A(wg), convw=A(convw), scal=scal, consts=_consts_np(),
                S0=A(inp['state_delta_S'][0, :, c]), conv0=conv0,
                C0=A(inp['state_mlstm_C'][0, :, c]), n0=A(inp['state_mlstm_n'][0, :, c].T),
                m0=A(np.broadcast_to(inp['state_mlstm_m'][0, :, c][None, :], (128, DB))))


def _tile_tokens(Y, c):
    nk = Y.shape[0]
    out = np.zeros((nk, 128, NTT * 128), Y.dtype)
    out[:, :, :TOKC] = Y[:, :, c * TOKC:(c + 1) * TOKC]
    return np.ascontiguousarray(out.reshape(nk, 128, NTT, 128).transpose(2, 1, 0, 3))


def _pad_rows(x, c):
    out = np.zeros((NTT * 128, x.shape[1]), x.dtype)
    out[:TOKC] = x[c * TOKC:(c + 1) * TOKC]
    return out


def _p2_inputs(inp, xtok, Y, c):
    rep = lambda v: np.ascontiguousarray(np.broadcast_to(v[None, :], (128, v.shape[0])))
    return dict(yin=_tile_tokens(Y, c), wout=np.ascontiguousarray(inp['ev_w_out'][0]), xres=_pad_rows(xtok, c),
                lng=rep(inp['ev_ln_g'][0]), lnb=rep(inp['ev_ln_b'][0]), consts=_consts_np())


TWO_PI = 2.0 * math.pi
CW1 = 6.28125
CW2 = TWO_PI - CW1


def _masks3_np():
    m = np.zeros((128, 6, 512), np.float32)
    s = np.arange(128)[:, None]
    t = np.arange(512)[None, :]
    for r in range(4):
        m[:, r, :] = (t > 128 * r + s)
    m[:, 4, :] = ((t % 32) > s)
    m[:, 5, :] = t + 1.0
    return m


def build_phase3(ntiles_prompt=32, do_sample=True, debug=0):
    nc = bass.Bass("TRN2", target_bir_lowering=False)
    with ExitStack() as st:
        b = B(nc, st)
        P = b.P
        x1T = b.din("x1T", [D, NTOK], BF16)
        w3 = b.din("w3", [D, 768])
        consts = b.din("consts", [128, 8, 128])
        masks = b.din("masks", [128, 5, 512], BF16)
        tau = b.din("tau", [128, 512])
        kcT = b.din("kcT", [32, 128, DB, 128])
        vcc = b.din("vcc", [32, 128, DB, 128])
        s5p = b.din("s5p", [128, 4, 4])
        Bbd = b.din("Bbd", [128, 2, 4, 128])
        Cbd = b.din("Cbd", [128, 2, 4, 128])
        h0 = b.din("h0", [128, 2, 4, DB])
        y3T = b.dout("y3T", [384, NTOK], BF16)
        kout = b.dout("kout", [NTOK, 128])
        vout = b.dout("vout", [NTOK, 128])
        hout = b.dout("hout", [128, 2, 4, NB + DB])

        cst = b.sb("cst", [128, 8, 128])
        ident = cst[:, 0, :]
        msk = b.sb("msk", [128, 5, 512], BF16)
        trib = b.sb("trib", [128, 2, 128], BF16)
        w3b = b.sb("w3b", [128, KC, 768], BF16)
        wst = b.sb("wst", [128, 1, 768])
        xb = [b.sb("xb%d" % i, [128, KC, 512], BF16) for i in range(2)]
        KT = b.sb("KT", [128, SEQ], BF16)
        Vt = b.sb("Vt", [128, 64, 128], BF16)
        qb = b.sb("qb", [128, 512], BF16)
        kT32 = b.sb("kT32", [128, 512])
        vT32 = b.sb("vT32", [128, 512])
        zc = b.sb("zc", [128, 512])
        u32 = b.sb("u32", [128, 512])
        ub = b.sb("ub", [128, 512], BF16)
        szd = b.sb("szd", [128, 512], BF16)
        ktok = b.sb("ktok", [128, 4, 128])
        vtok = b.sb("vtok", [128, 4, 128])
        E = [b.sb("E%d" % i, [128, 512]) for i in range(2)]
        SP = [b.sb("SP%d" % i, [128, 512], BF16) for i in range(2)]
        Wt = b.sb("Wt", [128, 512])
        ATT = [b.sb("ATT%d" % i, [128, 512], BF16) for i in range(2)]
        ycb = b.sb("ycb", [128, 512], BF16)
        ysb = b.sb("ysb", [128, 512], BF16)
        sp_ = b.sb("sp_", [128, 4, 4])
        cr = b.sb("cr", [128, 4, 512])
        sr = b.sb("sr", [128, 4, 512])
        ki = b.sb("ki", [128, 512], mybir.dt.int32)
        s5c = b.sb("s5c", [128, 4, 16])
        Bb = b.sb("Bb", [128, 2, 4, 128], BF16)
        Cst = b.sb("Cst", [128, 2, 4, 128])
        Cb = b.sb("Cb", [128, 2, 4, 128], BF16)
        hst = b.sb("hst", [128, 2, 4, DB])
        h0s = b.sb("h0s", [128, 2, 4, DB])
        hfin = b.sb("hfin", [128, 2, 4, NB + DB])
        t1 = b.sb("t1", [128, 512])
        t2 = b.sb("t2", [128, 512])
        bre = b.sb("bre", [128, 512])
        bim = b.sb("bim", [128, 512])
        gre = b.sb("gre", [128, 512])
        gim = b.sb("gim", [128, 512])
        ang, kk = gre, gim
        hreb = b.sb("hreb", [128, 512], BF16)
        himb = b.sb("himb", [128, 512], BF16)
        ys32 = b.sb("ys32", [128, 512])
        kcs = [b.sb("kcs%d" % i, [128, DB, 128]) for i in range(2)]
        vcs = [b.sb("vcs%d" % i, [128, DB, 128]) for i in range(2)]
        kcb = [b.sb("kcb%d" % i, [128, DB, 128], BF16) for i in range(3)]
        vcb = [b.sb("vcb%d" % i, [128, DB, 128], BF16) for i in range(3)]
        vnew = b.sb("vnew", [128, DB, 128], BF16)

        pj = [b.ps("pj%d" % i) for i in range(2)]
        pZ = [b.ps("pZ%d" % i) for i in range(2)]
        pCm = b.ps("pCm")
        pO = b.ps("pO")
        pT = b.ps("pT")
        pY = b.ps("pY")
        ZK = ["pj2", "pj3"]
        CK, OK, TK, YK = "pj4", "pj5", "pj6", "pj7"

        b.dma("sp", cst[:], consts, [], ["cst"], "cst")
        b.dma("sp", msk[:], masks, [], ["msk"], "msk")
        b.dma("sp", t2[:], tau, [], ["t2"], "tau")
        b.dma("sp", sp_[:], s5p, [], ["sp_"], "sp_")
        b.dma("sp", h0s[:], h0, [], ["h0s"], "h0s")
        b.cp("dve", trib[:, 0, :], cst[:, 7, :], ["cst"], ["trib"])
        b.ts("dve", trib[:, 1, :], cst[:, 7, :], -1.0, ALU.mult, ["cst"], ["trib"], s2=1.0, op1=ALU.add)
        w3v = w3.rearrange("(kc p) n -> p kc n", p=128)
        for kc0 in range(KC):
            b.dma("sp", wst[:, 0, :], w3v[:, kc0, :], [], ["wst"], "wst")
            b.cp("dve" if kc0 % 2 else "pool", w3b[:, kc0, :], wst[:, 0, :], ["wst"], ["w3b"])
        for a in range(2):
            b.dma("sp", wst[:, 0, 0:512].rearrange("p (c n) -> p c n", n=128), Bbd[:, a], [], ["wst"], "wst")
            b.cp("dve", Bb[:, a], wst[:, 0, 0:512].rearrange("p (c n) -> p c n", n=128), ["wst"], ["Bb"])
        b.dma("sp", Cst[:], Cbd, [], ["Cst"], "Cst")
        c_ = lambda i: s5c[:, :, i]
        lre, lim = sp_[:, :, 0], sp_[:, :, 1]
        b.act(c_(13), sp_[:, :, 2], AF.Exp, ["sp_"], ["s5c"])
        dtc = c_(13)
        b.tt("dve", c_(8), lre, dtc, ALU.mult, ["sp_", "s5c"], ["s5c"])
        b.act(c_(0), c_(8), AF.Exp, ["s5c"], ["s5c"])
        b.tt("dve", c_(1), lim, dtc, ALU.mult, ["sp_", "s5c"], ["s5c"])
        b.cp("dve", bim[:], t2[:], ["t2"], ["bim"])
        for m in range(4):
            th = s5c[:, m, 1:2]
            b.ts("dve", ang[:], bim[:], th, ALU.mult, ["bim", "s5c"], ["gre"])
            b.ts("dve", kk[:], ang[:], 1.0 / TWO_PI, ALU.mult, ["gre"], ["gim"])
            b.cp("dve", ki[:], kk[:], ["gim"], ["ki"])
            b.cp("dve", kk[:], ki[:], ["ki"], ["gim"])
            b.stt(ang[:], kk[:], -CW1, ang[:], ALU.mult, ALU.add, ["gim", "gre"], ["gre"])
            b.stt(ang[:], kk[:], -CW2, ang[:], ALU.mult, ALU.add, ["gim", "gre"], ["gre"])
            for (dst, shift) in ((sr, 0.0), (cr, 0.5 * math.pi)):
                src = ang
                if shift != 0.0:
                    b.ts("dve", t1[:], ang[:], shift, ALU.add, ["gre"], ["t1"])
                    src = t1
                    sk = "t1"
                else:
                    b.cp("dve", t1[:], ang[:], ["gre"], ["t1"])
                    src, sk = t1, "t1"
                for _ in range(2):
                    b.ts("dve", t2[:], t1[:], math.pi, ALU.is_gt, ["t1"], ["t2"])
                    b.stt(t1[:], t2[:], -TWO_PI, t1[:], ALU.mult, ALU.add, ["t2", "t1"], ["t1"])
                    b.ts("dve", t2[:], t1[:], -math.pi, ALU.is_lt, ["t1"], ["t2"])
                    b.stt(t1[:], t2[:], TWO_PI, t1[:], ALU.mult, ALU.add, ["t2", "t1"], ["t1"])
                b.act(dst[:, m, :], t1[:], AF.Sin, ["t1"], ["cr" if dst is cr else "sr"])
        b.tt("dve", c_(2), c_(0), cr[:, :, 0], ALU.mult, ["s5c", "cr"], ["s5c"])
        b.tt("dve", c_(3), c_(0), sr[:, :, 0], ALU.mult, ["s5c", "sr"], ["s5c"])
        b.ts("dve", c_(2), c_(2), -1.0, ALU.add, ["s5c"], ["s5c"])
        b.tt("dve", c_(4), lre, lre, ALU.mult, ["sp_"], ["s5c"])
        b.tt("dve", c_(8), lim, lim, ALU.mult, ["sp_"], ["s5c"])
        b.tt("dve", c_(4), c_(4), c_(8), ALU.add, ["s5c"], ["s5c"])
        b.P.op("dve", lambda e, o=c_(12), i=c_(4): e.reciprocal(out=o, in_=i), reads=["s5c"], writes=["s5c"])
        b.tt("dve", c_(8), c_(2), lre, ALU.mult, ["s5c", "sp_"], ["s5c"])
        b.tt("dve", c_(9), c_(3), lim, ALU.mult, ["s5c", "sp_"], ["s5c"])
        b.tt("dve", c_(5), c_(8), c_(9), ALU.add, ["s5c"], ["s5c"])
        b.tt("dve", c_(5), c_(5), c_(12), ALU.mult, ["s5c"], ["s5c"])
        b.tt("dve", c_(8), c_(3), lre, ALU.mult, ["s5c", "sp_"], ["s5c"])
        b.tt("dve", c_(9), c_(2), lim, ALU.mult, ["s5c", "sp_"], ["s5c"])
        b.tt("dve", c_(6), c_(8), c_(9), ALU.subtract, ["s5c"], ["s5c"])
        b.tt("dve", c_(6), c_(6), c_(12), ALU.mult, ["s5c"], ["s5c"])
        b.ts("dve", c_(7), c_(6), -1.0, ALU.mult, ["s5c"], ["s5c"])
        b.tt("dve", c_(8), c_(5), c_(5), ALU.mult, ["s5c"], ["s5c"])
        b.tt("dve", c_(9), c_(6), c_(6), ALU.mult, ["s5c"], ["s5c"])
        b.tt("dve", c_(8), c_(8), c_(9), ALU.add, ["s5c"], ["s5c"])
        b.P.op("dve", lambda e, o=c_(9), i=c_(8): e.reciprocal(out=o, in_=i), reads=["s5c"], writes=["s5c"])
        b.tt("dve", c_(10), c_(5), c_(9), ALU.mult, ["s5c"], ["s5c"])
        b.tt("dve", c_(11), c_(7), c_(9), ALU.mult, ["s5c"], ["s5c"])
        for m in range(4):
            fre, fim, nfim = s5c[:, m, 5:6], s5c[:, m, 6:7], s5c[:, m, 7:8]
            b.ts("dve", t1[:, 0:128], Cst[:, 1, m, :], nfim, ALU.mult, ["Cst", "s5c"], ["t1"])
            b.stt(Cb[:, 0, m, :], Cst[:, 0, m, :], fre, t1[:, 0:128], ALU.mult, ALU.add, ["Cst", "s5c", "t1"], ["Cb"])
            b.ts("dve", t1[:, 0:128], Cst[:, 1, m, :], fre, ALU.mult, ["Cst", "s5c"], ["t1"])
            b.stt(t1[:, 0:128], Cst[:, 0, m, :], fim, t1[:, 0:128], ALU.mult, ALU.add, ["Cst", "s5c", "t1"], ["t1"])
            b.ts("dve", Cb[:, 1, m, :], t1[:, 0:128], -1.0, ALU.mult, ["t1"], ["Cb"])
            fr, fi = s5c[:, m, 10:11], s5c[:, m, 11:12]
            b.ts("dve", t2[:, 0:DB], h0s[:, 1, m, :], fi, ALU.mult, ["h0s", "s5c"], ["t2"])
            b.stt(t2[:, 16:16 + DB], h0s[:, 0, m, :], fr, t2[:, 0:DB], ALU.mult, ALU.subtract, ["h0s", "s5c", "t2"], ["t2"])
            b.ts("dve", t2[:, 0:DB], h0s[:, 1, m, :], fr, ALU.mult, ["h0s", "s5c"], ["t2"])
            b.stt(t2[:, 32:32 + DB], h0s[:, 0, m, :], fi, t2[:, 0:DB], ALU.mult, ALU.add, ["h0s", "s5c", "t2"], ["t2"])
            b.cp("dve", h0s[:, 0, m, :], t2[:, 16:16 + DB], ["t2"], ["h0s"])
            b.cp("dve", h0s[:, 1, m, :], t2[:, 32:32 + DB], ["t2"], ["h0s"])

        x1v = x1T.rearrange("(kc p) n -> p kc n", p=128)
        tiles = []
        for bi in range(NB):
            for t in range(ntiles_prompt // NB):
                tiles.append(dict(kind="p", seq=bi, ti=t, tok0=bi * SEQ + t * 512, N=512,
                                  first=(t == 0), last=(t == ntiles_prompt // NB - 1)))
        if do_sample:
            tiles.append(dict(kind="s", seq=None, ti=0, tok0=NTP, N=256, first=True, last=True))

        def load_x(i):
            T = tiles[i]
            s = i % 2
            for h in range(2):
                b.dma("sp", xb[s][:, h * 8:(h + 1) * 8, 0:T["N"]], x1v[:, h * 8:(h + 1) * 8, T["tok0"]:T["tok0"] + T["N"]],
                      [], ["xb%d" % s], "xb%d_%d" % (s, h))

        def load_cache(j):
            s = j % 3
            s2 = j % 2
            b.dma("sp", kcs[s2][:], kcT[j], [], ["kcs%d" % s2], "kcs%d" % s2)
            b.dma("sp", vcs[s2][:], vcc[j], [], ["vcs%d" % s2], "vcs%d" % s2)
            b.cp("pool", kcb[s][:], kcs[s2][:], ["kcs%d" % s2], ["kcb%d" % s])
            b.cp("pool", vcb[s][:], vcs[s2][:], ["vcs%d" % s2], ["vcb%d" % s])

        load_x(0)
        try:
            for i, T in enumerate(tiles):
                s = i % 2
                N = T["N"]
                samp = T["kind"] == "s"
                xk = "xb%d" % s
                if i + 1 < len(tiles):
                    load_x(i + 1)
                for f in range(6):
                    bank, bk = pj[f % 2], "pj%d" % (f % 2)
                    for kc in range(KC):
                        b.mm(bank[:, 0:N], w3b[:, kc, f * 128:(f + 1) * 128], xb[s][:, kc, 0:N], ["w3b", xk], [bk],
                             start=(kc == 0), stop=(kc == KC - 1))
                    if f == 0:
                        b.act(qb[:, 0:N], bank[:, 0:N], AF.Copy, [bk], ["qb"], scale=128.0 ** -0.5)
                    elif f == 1:
                        b.cp("dve", kT32[:, 0:N], bank[:, 0:N], [bk], ["kT32"])
                        if not samp:
                            b.cp("act", KT[:, T["ti"] * 512:T["ti"] * 512 + N], bank[:, 0:N], [bk], ["KT"])
                        else:
                            b.cp("act", KT[:, 0:N], bank[:, 0:N], [bk], ["KT"])
                    elif f == 2:
                        b.cp("dve", vT32[:, 0:N], bank[:, 0:N], [bk], ["vT32"])
                    elif f == 3:
                        b.act(zc[:, 0:N], bank[:, 0:N], AF.Silu, [bk], ["zc"])
                    elif f == 4:
                        b.cp("dve", u32[:, 0:N], bank[:, 0:N], [bk], ["u32"])
                        b.cp("act", ub[:, 0:N], bank[:, 0:N], [bk], ["ub"])
                    else:
                        b.act(szd[:, 0:N], bank[:, 0:N], AF.Silu, [bk], ["szd"])
                        b.dma("sp", y3T[256:384, T["tok0"]:T["tok0"] + N], szd[:, 0:N], ["szd"], ["y3T"], "szd_o")
                if debug == 1:
                    raise _Stop
                nblk = N // 128
                for (src, sk, dst, dk, outap) in ((kT32, "kT32", ktok, "ktok", kout), (vT32, "vT32", vtok, "vtok", vout)):
                    for jb in range(nblk):
                        b.tr(pT[:, jb * 128:(jb + 1) * 128], src[:, jb * 128:(jb + 1) * 128], ident, [sk, "cst"], [TK])
                    b.cp("dve", dst[:, 0:nblk, :].rearrange("p a d -> p (a d)"), pT[:, 0:nblk * 128], [TK], [dk])
                    b.dma("sp", outap[T["tok0"]:T["tok0"] + N, :].rearrange("(a p) d -> p a d", p=128), dst[:, 0:nblk, :],
                          [dk], ["o_" + dk], "o_" + dk)
                if not samp:
                    b.cp("pool", Vt[:, T["ti"] * 4:T["ti"] * 4 + 4, :], vtok[:], ["vtok"], ["Vt"])
                else:
                    for h in range(2):
                        for q4 in range(4):
                            bi = h * 4 + q4
                            b.tr(pT[0:32, q4 * 128:(q4 + 1) * 128], vT32[:, bi * 32:(bi + 1) * 32], ident,
                                 ["vT32", "cst"], [TK])
                        b.cp("dve", vnew[0:32, h * 4:(h + 1) * 4, :].rearrange("p a d -> p (a d)"), pT[0:32, 0:512],
                             [TK], ["vnew"])
                if debug == 2:
                    raise _Stop
                def g_att():
                    if not samp:
                        jmax = T['ti'] * 4 + 3
                        steps = list(range(jmax, -1, -1))
                    else:
                        steps = list(range(32, -1, -1))
                        load_cache(31)
                        yield

                    def stageA(j):
                        z = j % 2
                        Zb, zk = (pZ[z], ZK[z])
                        if not samp:
                            Kp = 128
                            b.mm(Zb[:, 0:N], KT[:, j * 128:(j + 1) * 128], qb[:, 0:N], ['KT', 'qb'], [zk])
                        elif j == 32:
                            Kp = 32
                            for bi in range(DB):
                                b.mm(Zb[0:32, bi * 32:(bi + 1) * 32], KT[:, bi * 32:(bi + 1) * 32], qb[:, bi * 32:(bi + 1) * 32], ['KT', 'qb'], [zk])
                        else:
                            Kp = 128
                            if j - 1 >= 0:
                                load_cache(j - 1)
                            for bi in range(DB):
                                b.mm(Zb[:, bi * 32:(bi + 1) * 32], kcb[j % 3][:, bi, :], qb[:, bi * 32:(bi + 1) * 32], ['kcb%d' % (j % 3), 'qb'], [zk])
                        b.act(E[z][0:Kp, 0:N], Zb[0:Kp, 0:N], AF.Exp, [zk], ['E%d' % z])
                        b.act(SP[z][0:Kp, 0:N], E[z][0:Kp, 0:N], AF.Ln, ['E%d' % z], ['SP%d' % z], bias=1.0)
                        mk = None
                        if samp and j == 32:
                            mk = msk[0:32, 4, 0:N]
                        elif not samp and j >= T['ti'] * 4:
                            mk = msk[:, j - T['ti'] * 4, 0:N]
                        if mk is not None:
                            b.tt('pool', E[z][0:Kp, 0:N], E[z][0:Kp, 0:N], mk, ALU.mult, ['E%d' % z, 'msk'], ['E%d' % z])
                            b.tt('pool', SP[z][0:Kp, 0:N], SP[z][0:Kp, 0:N], mk, ALU.mult, ['SP%d' % z, 'msk'], ['SP%d' % z])
                        return Kp
                    b.memset('dve', pCm[:, 0:N], 0.0, [CK])
                    yield
                    b.memset('dve', pO[:, 0:N], 0.0, [OK])
                    yield
                    KpA = {}
                    KpA[steps[0]] = stageA(steps[0])
                    for si, j in enumerate(steps):
                        z = j % 2
                        if si + 1 < len(steps):
                            KpA[steps[si + 1]] = stageA(steps[si + 1])
                        Kp = KpA[j]
                        b.mm(pCm[:, 0:N], trib[0:Kp, 0, :], SP[z][0:Kp, 0:N], ['trib', 'SP%d' % z], [CK], start=False, stop=True)
                        yield
                        b.act(Wt[0:Kp, 0:N], pCm[0:Kp, 0:N], AF.Exp, [CK], ['Wt'], scale=-1.0)
                        yield
                        b.tt('dve', ATT[z][0:Kp, 0:N], E[z][0:Kp, 0:N], Wt[0:Kp, 0:N], ALU.mult, ['E%d' % z, 'Wt'], ['ATT%d' % z])
                        yield
                        b.mm(pCm[:, 0:N], trib[0:Kp, 1, :], SP[z][0:Kp, 0:N], ['trib', 'SP%d' % z], [CK], start=False, stop=True)
                        yield
                        if not samp:
                            b.mm(pO[:, 0:N], Vt[:, j, :], ATT[z][:, 0:N], ['Vt', 'ATT%d' % z], [OK], start=False, stop=True)
                            yield
                        elif j == 32:
                            for bi in range(DB):
                                b.mm(pO[:, bi * 32:(bi + 1) * 32], vnew[0:32, bi, :], ATT[z][0:32, bi * 32:(bi + 1) * 32], ['vnew', 'ATT%d' % z], [OK], start=False, stop=True)
                                yield
                        else:
                            for bi in range(DB):
                                b.mm(pO[:, bi * 32:(bi + 1) * 32], vcb[j % 3][:, bi, :], ATT[z][:, bi * 32:(bi + 1) * 32], ['vcb%d' % (j % 3), 'ATT%d' % z], [OK], start=False, stop=True)
                                yield
                    b.tt('dve', ycb[:, 0:N], pO[:, 0:N], zc[:, 0:N], ALU.mult, [OK, 'zc'], ['ycb'])
                    yield
                    b.dma('sp', y3T[0:128, T['tok0']:T['tok0'] + N], ycb[:, 0:N], ['ycb'], ['y3T'], 'yc_o')
                    yield
                    yield
                def g_s5():
                    nseq = DB if samp else 1
                    Ls = N // nseq
                    v3 = lambda ap: ap.rearrange('p (s l) -> p s l', l=Ls)

                    def tab(tb, m):
                        if samp:
                            return tb[:, m, 0:Ls].unsqueeze(1).broadcast_to([128, nseq, Ls])
                        return v3(tb[:, m, 0:N])
                    for m in range(4):
                        b.mm(pj[0][:, 0:N], Bb[:, 0, m, :], ub[:, 0:N], ['Bb', 'ub'], ['pj0'])
                        yield
                        b.mm(pj[1][:, 0:N], Bb[:, 1, m, :], ub[:, 0:N], ['Bb', 'ub'], ['pj1'])
                        yield
                        Bre, Bim = (v3(pj[0][:, 0:N]), v3(pj[1][:, 0:N]))
                        b.tt('dve', v3(t1[:, 0:N]), Bre, tab(cr, m), ALU.mult, ['pj0', 'cr'], ['t1'])
                        yield
                        b.tt('dve', v3(t2[:, 0:N]), Bim, tab(sr, m), ALU.mult, ['pj1', 'sr'], ['t2'])
                        yield
                        b.tt('pool', bre[:, 0:N], t1[:, 0:N], t2[:, 0:N], ALU.add, ['t1', 't2'], ['bre'])
                        yield
                        b.tt('dve', v3(t1[:, 0:N]), Bim, tab(cr, m), ALU.mult, ['pj1', 'cr'], ['t1'])
                        yield
                        b.tt('dve', v3(t2[:, 0:N]), Bre, tab(sr, m), ALU.mult, ['pj0', 'sr'], ['t2'])
                        yield
                        b.tt('pool', bim[:, 0:N], t1[:, 0:N], t2[:, 0:N], ALU.subtract, ['t1', 't2'], ['bim'])
                        yield
                        rbc = s5c[:, m, 0:1].to_broadcast([128, Ls])
                        for sq in range(nseq):
                            col = sq if samp else 0
                            for src, sk, dst, dk, a in ((bre, 'bre', gre, 'gre', 0), (bim, 'bim', gim, 'gim', 1)):
                                if samp:
                                    init = h0s[:, a, m, sq:sq + 1]
                                    ik = 'h0s'
                                elif T['first']:
                                    init, ik = (0.0, None)
                                else:
                                    init, ik = (hst[:, a, m, 0:1], 'hst')
                                b.P.op('dve', lambda e, o=dst[:, sq * Ls:(sq + 1) * Ls], d0=rbc, d1=src[:, sq * Ls:(sq + 1) * Ls], ini=init: e.tensor_tensor_scan(out=o, data0=d0, data1=d1, initial=ini, op0=ALU.mult, op1=ALU.add), reads=[sk, 's5c'] + ([ik] if ik else []), writes=[dk])
                                yield
                        b.tt('dve', v3(t1[:, 0:N]), v3(gre[:, 0:N]), tab(cr, m), ALU.mult, ['gre', 'cr'], ['t1'])
                        yield
                        b.tt('dve', v3(t2[:, 0:N]), v3(gim[:, 0:N]), tab(sr, m), ALU.mult, ['gim', 'sr'], ['t2'])
                        yield
                        b.tt('pool', hreb[:, 0:N], t1[:, 0:N], t2[:, 0:N], ALU.subtract, ['t1', 't2'], ['hreb'])
                        yield
                        lastv = lambda ap: v3(ap[:, 0:N])[:, :, Ls - 1]
                        if samp:
                            cols = slice(NB, NB + DB)
                        else:
                            cols = slice(T['seq'], T['seq'] + 1)
                        b.tt('dve', hst[:, 0, m, 0:nseq], lastv(t1), lastv(t2), ALU.subtract, ['t1', 't2'], ['hst'])
                        yield
                        b.tt('dve', v3(t1[:, 0:N]), v3(gre[:, 0:N]), tab(sr, m), ALU.mult, ['gre', 'sr'], ['t1'])
                        yield
                        b.tt('dve', v3(t2[:, 0:N]), v3(gim[:, 0:N]), tab(cr, m), ALU.mult, ['gim', 'cr'], ['t2'])
                        yield
                        b.tt('pool', himb[:, 0:N], t1[:, 0:N], t2[:, 0:N], ALU.add, ['t1', 't2'], ['himb'])
                        yield
                        b.tt('dve', hst[:, 1, m, 0:nseq], lastv(t1), lastv(t2), ALU.add, ['t1', 't2'], ['hst'])
                        yield
                        if T['last']:
                            fre, fim, nfim = (s5c[:, m, 5:6], s5c[:, m, 6:7], s5c[:, m, 7:8])
                            b.ts('dve', t1[:, 0:nseq], hst[:, 1, m, 0:nseq], nfim, ALU.mult, ['hst', 's5c'], ['t1'])
                            yield
                            b.stt(hfin[:, 0, m, cols], hst[:, 0, m, 0:nseq], fre, t1[:, 0:nseq], ALU.mult, ALU.add, ['hst', 's5c', 't1'], ['hfin'])
                            yield
                            b.ts('dve', t1[:, 0:nseq], hst[:, 1, m, 0:nseq], fre, ALU.mult, ['hst', 's5c'], ['t1'])
                            yield
                            b.stt(hfin[:, 1, m, cols], hst[:, 0, m, 0:nseq], fim, t1[:, 0:nseq], ALU.mult, ALU.add, ['hst', 's5c', 't1'], ['hfin'])
                            yield
                        b.mm(pY[:, 0:N], Cb[:, 0, m, :], hreb[:, 0:N], ['Cb', 'hreb'], [YK], start=m == 0, stop=False)
                        yield
                        b.mm(pY[:, 0:N], Cb[:, 1, m, :], himb[:, 0:N], ['Cb', 'himb'], [YK], start=False, stop=m == 3)
                        yield
                    b.stt(ys32[:, 0:N], u32[:, 0:N], sp_[:, 0, 3:4], pY[:, 0:N], ALU.mult, ALU.add, ['u32', 'sp_', YK], ['ys32'])
                    yield
                    b.tt('pool', t1[:, 0:N], ys32[:, 0:N], ys32[:, 0:N], ALU.mult, ['ys32'], ['t1'])
                    yield
                    b.ts('dve', t1[:, 0:N], t1[:, 0:N], 0.044715, ALU.mult, ['t1'], ['t1'], s2=1.0, op1=ALU.add)
                    yield
                    b.tt('pool', t1[:, 0:N], t1[:, 0:N], ys32[:, 0:N], ALU.mult, ['t1', 'ys32'], ['t1'])
                    yield
                    b.act(t1[:, 0:N], t1[:, 0:N], AF.Sigmoid, ['t1'], ['t1'], scale=2.0 * math.sqrt(2.0 / math.pi))
                    yield
                    b.tt('dve', ysb[:, 0:N], t1[:, 0:N], ys32[:, 0:N], ALU.mult, ['t1', 'ys32'], ['ysb'])
                    yield
                    b.dma('sp', y3T[128:256, T['tok0']:T['tok0'] + N], ysb[:, 0:N], ['ysb'], ['y3T'], 'ys_o')
                    yield
                    yield
                _interleave(g_att(), g_s5())
        except _Stop:
            pass
        b.dma("sp", hout, hfin[:], ["hfin"], ["hout"], "hout")
        P.finalize()
        print("phase3 instrs", P.ninstr, {e: P.cnt[e] for e in ENGS})
    return nc


def _p3_inputs(inp, x1T, c):
    A = np.ascontiguousarray
    W = inp['od_w_in'][0]
    blk = lambda i: W[:, i * 1024 + c * 128: i * 1024 + (c + 1) * 128]
    w3 = np.concatenate([blk(0), blk(1), blk(2), blk(3), blk(4), blk(5)], 1)
    kc = inp['cache_sb_k'][0, :, :, c, :]
    vc = inp['cache_sb_v'][0, :, :, c, :]
    kcT = A(kc.reshape(DB, 32, 128, 128).transpose(1, 3, 0, 2))
    vcc = A(vc.reshape(DB, 32, 128, 128).transpose(1, 2, 0, 3))
    G = slice(8 * c, 8 * c + 8)
    lre = inp['od_lam_re'][0, G]
    lim = inp['od_lam_im'][0, G]
    dt = np.broadcast_to(inp['od_log_dt'][0, G][:, None], (8, 64))
    s5p = np.zeros((128, 4, 4), np.float32)
    qm = lambda a: a.reshape(4, 128).T
    s5p[:, :, 0] = qm(lre)
    s5p[:, :, 1] = qm(lim)
    s5p[:, :, 2] = qm(dt)
    s5p[:, 0, 3] = inp['od_d'][0, c * 128:(c + 1) * 128]
    Bbd = np.zeros((128, 2, 4, 128), np.float32)
    Cbd = np.zeros((128, 2, 4, 128), np.float32)
    for a, (bk, ck) in enumerate((('od_b_re', 'od_c_re'), ('od_b_im', 'od_c_im'))):
        bb = inp[bk][0, G]
        cc = inp[ck][0, G]
        for g in range(8):
            m, g2 = g // 2, g % 2
            Bbd[g * 16:(g + 1) * 16, a, m, g2 * 64:(g2 + 1) * 64] = bb[g].T
            Cbd[g2 * 64:(g2 + 1) * 64, a, m, g * 16:(g + 1) * 16] = cc[g].T
    h0 = np.zeros((128, 2, 4, DB), np.float32)
    for a, k in enumerate(('state_s5_re', 'state_s5_im')):
        hs = inp[k][0, :, G]
        h0[:, a] = hs.reshape(DB, 4, 128).transpose(2, 1, 0)
    return dict(x1T=x1T, w3=A(w3), consts=_consts_np(), masks=np.ascontiguousarray(_masks3_np()[:, 0:5]).astype(ml_dtypes.bfloat16), tau=np.ascontiguousarray(_masks3_np()[:, 5]), kcT=kcT, vcc=vcc, s5p=s5p,
                Bbd=Bbd, Cbd=Cbd, h0=h0)


def _p4_inputs(inp, x1tok, Y3, c):
    rep = lambda v: np.ascontiguousarray(np.broadcast_to(v[None, :], (128, v.shape[0])))
    return dict(yin=_tile_tokens(Y3, c), wout=np.ascontiguousarray(inp['od_w_out'][0]), xres=_pad_rows(x1tok, c),
                lng=rep(inp['od_ln_g'][0]), lnb=rep(inp['od_ln_b'][0]), consts=_consts_np(),
                wglu=np.ascontiguousarray(inp['od_w_glu'][0]),
                bglu=np.ascontiguousarray(inp['od_b_glu'][0].reshape(8, 128).T))


_NC_CACHE = {}


def _get_nc(name, fn):
    if name not in _NC_CACHE:
        _NC_CACHE[name] = fn()
    return _NC_CACHE[name]


def kernel(**inp):
    inp = {k: np.asarray(v) for k, v in inp.items()}
    R = range(NCORES)
    xtok = _tok_major(inp['x_prompt'], inp['x_sample'])
    xT = np.ascontiguousarray(xtok.T)
    r1 = _run(_get_nc("p1", build_phase1), [_p1_inputs(inp, xT, c) for c in R])
    del xT
    Y = np.stack([r1[k]['yT'][0:128] for k in R] + [r1[k]['yT'][128:256] for k in R])
    r2 = _run(_get_nc("p2", lambda: build_out_phase(False, True)), [_p2_inputs(inp, xtok, Y, c) for c in R])
    x1tok = np.concatenate([r2[c]['xo'][:TOKC] for c in R], 0)
    x1T = np.ascontiguousarray(
        np.concatenate([r2[c]['xoT'].transpose(2, 1, 0, 3).reshape(D, -1)[:, :TOKC] for c in R], 1))
    r3 = _run(_get_nc("p3", build_phase3), [_p3_inputs(inp, x1T, c) for c in R])
    Y3 = np.stack([r3[k]['y3T'][0:128] for k in R] + [r3[k]['y3T'][128:256] for k in R]
                  + [r3[k]['y3T'][256:384] for k in R])
    r4 = _run(_get_nc("p4", lambda: build_out_phase(True, False)), [_p4_inputs(inp, x1tok, Y3, c) for c in R])
    ytok = np.concatenate([r4[c]['xo'][:TOKC] for c in R], 0)
    f32 = np.float32
    y_prompt = ytok[:NTP].reshape(NB, SEQ, D).astype(f32)
    y_sample = ytok[NTP:].reshape(DB, DS, D).astype(f32)

    def heads(key, sl):
        return np.stack([r1[c][key][sl] for c in R], 1)[None].astype(f32)
    pS, sS = slice(0, NB), slice(NB, NB + DB)
    dS_p, dS_s = heads('oS', pS), heads('oS', sS)
    mC_p, mC_s = heads('oC', pS), heads('oC', sS)

    def convout(sl):
        n = sl.stop - sl.start
        o = np.zeros((1, n, 3, 3072), f32)
        for c in R:
            oc = r1[c]['oconv'][:, sl]
            for f in range(3):
                o[0, :, :, f * 1024 + c * 128: f * 1024 + (c + 1) * 128] = oc[:, :, f, :].transpose(1, 2, 0)
        return o
    dc_p, dc_s = convout(pS), convout(sS)
    mn = lambda sl: np.stack([r1[c]['on'][:, sl].T for c in R], 1)[None].astype(f32)
    mm = lambda sl: np.stack([r1[c]['om'][0, sl] for c in R], 1)[None].astype(f32)

    def kv(key, lo, hi, shape):
        return np.stack([r3[c][key][lo:hi] for c in R], 1).reshape(shape).astype(f32)
    sk_p = kv('kout', 0, NTP, (NB, SEQ, 8, 128))[None]
    sk_s = kv('kout', NTP, NTOK, (DB, DS, 8, 128))[None]
    sv_p = kv('vout', 0, NTP, (NB, SEQ, 8, 128))[None]
    sv_s = kv('vout', NTP, NTOK, (DB, DS, 8, 128))[None]

    def s5state(a, sl):
        n = sl.stop - sl.start
        o = np.zeros((1, n, 64, 64), f32)
        for c in R:
            ho = r3[c]['hout'][:, a, :, sl]
            o[0, :, 8 * c:8 * c + 8, :] = ho.transpose(2, 1, 0).reshape(n, 8, 64)
        return o
    return (y_prompt, y_sample, dS_p, dS_s, dc_p, dc_s, mC_p, mC_s, mn(pS), mn(sS), mm(pS), mm(sS),
            sk_p, sk_s, sv_p, sv_s, s5state(0, pS), s5state(0, sS), s5state(1, pS), s5state(1, sS))
```

```python
import math
from contextlib import ExitStack

import numpy as np
import ml_dtypes
import concourse.bass as bass
import concourse.mybir as mybir
from concourse.bass_utils import run_bass_kernel_spmd

F32 = mybir.dt.float32
BF16 = mybir.dt.bfloat16
AF = mybir.ActivationFunctionType
ALU = mybir.AluOpType
AX = mybir.AxisListType
ENGS = ("pe", "act", "dve", "pool", "sp")
NCORES = 8

D = 2048
KC = D // 128
SEQ = 8192
NB = 2
DB = 8
DS = 32
PAST = 4096
NTP = NB * SEQ
NTS = DB * DS
NTOK = NTP + NTS
TOKC = NTOK // NCORES
ALPHA = (2 * 2) ** 0.25
NEG = -30000.0


class Prog:
    def __init__(self, nc, stack):
        self.nc = nc
        self.stack = stack
        self.q = {e: [] for e in ENGS}
        self.cnt = {e: 0 for e in ENGS}
        self.sem = {e: stack.enter_context(nc.semaphore("sem_" + e)) for e in ENGS}
        self.waited = {e: {} for e in ENGS}
        self.lastw = {}
        self.readers = {}
        self.dsem = {}
        self.dcnt = {}
        self.ninstr = 0

    def _deps(self, reads, writes):
        deps = []
        for k in reads:
            t = self.lastw.get(k)
            if t is not None:
                deps.append(t)
        for k in writes:
            t = self.lastw.get(k)
            if t is not None:
                deps.append(t)
            deps.extend(self.readers.get(k, ()))
        return deps

    def _emit_waits(self, eng, deps, skip_self=False):
        need = {}
        for (sid, sem, val) in deps:
            if skip_self and sid == eng:
                continue
            if self.waited[eng].get(sid, 0) >= val:
                continue
            if need.get(sid, (None, 0))[1] < val:
                need[sid] = (sem, val)
        for sid, (sem, val) in need.items():
            self.waited[eng][sid] = val
            self.q[eng].append(lambda e, sem=sem, val=val: e.wait_ge(sem, val))
            self.ninstr += 1

    def _commit(self, tok, reads, writes):
        for k in writes:
            self.lastw[k] = tok
            self.readers[k] = []
        for k in reads:
            self.readers.setdefault(k, []).append(tok)

    @staticmethod
    def _bank(k):
        if isinstance(k, str) and len(k) >= 2 and k[0] == "p" and (k[1].isupper() or k[:2] in ("pj", "pg")):
            return k[:3] if k[:2] == "pj" else k[:2]
        return None

    def _norm(self, reads, writes):
        r2, w2 = [], []
        for k in reads:
            bk = self._bank(k)
            if bk is None:
                r2.append(k)
            elif bk not in w2:
                w2.append(bk)
        for k in writes:
            bk = self._bank(k)
            k2 = k if bk is None else bk
            if k2 not in w2:
                w2.append(k2)
        return r2, w2

    def op(self, eng, fn, reads=(), writes=(), skip_self=False):
        reads, writes = self._norm(reads, writes)
        deps = self._deps(reads, writes)
        self._emit_waits(eng, deps, skip_self=skip_self)
        self.cnt[eng] += 1
        sem = self.sem[eng]
        tok = (eng, sem, self.cnt[eng])
        self.q[eng].append(lambda e, fn=fn, sem=sem: fn(e).then_inc(sem, 1))
        self.ninstr += 1
        self._commit(tok, reads, writes)
        return tok

    def dma(self, eng, out, in_, reads=(), writes=(), slot=None, **kw):
        assert slot is not None
        if slot not in self.dsem:
            self.dsem[slot] = self.stack.enter_context(self.nc.semaphore("dsem_%d" % len(self.dsem)))
            self.dcnt[slot] = 0
        deps = self._deps(reads, writes)
        self._emit_waits(eng, deps)
        self.dcnt[slot] += 16
        sem = self.dsem[slot]
        tok = (("d", slot), sem, self.dcnt[slot])
        self.q[eng].append(
            lambda e, out=out, in_=in_, sem=sem, kw=kw: e.dma_start(out=out, in_=in_, **kw).then_inc(sem, 16))
        self.ninstr += 1
        self._commit(tok, reads, writes)
        return tok

    def finalize(self):
        nc = self.nc
        alltok = list(self.lastw.values())
        for ts in self.readers.values():
            alltok.extend(ts)
        self._emit_waits("sp", alltok)
        engmap = {"pe": "tensor", "act": "scalar", "dve": "vector", "pool": "gpsimd", "sp": "sync"}
        with nc.Block() as block:
            for e in ENGS:
                lst = self.q[e]
                if not lst:
                    continue

                def body(engine, lst=lst):
                    for f in lst:
                        f(engine)
                getattr(block, engmap[e])(body)


class B:
    def __init__(self, nc, st):
        self.nc = nc
        self.st = st
        self.P = Prog(nc, st)
        self.rr = 0

    def sb(self, name, shape, dt=F32):
        return self.st.enter_context(self.nc.sbuf_tensor(name, list(shape), dt))

    def ps(self, name, shape=(128, 512), dt=F32):
        return self.st.enter_context(self.nc.psum_tensor(name, list(shape), dt))

    def din(self, name, shape, dt=F32):
        return self.nc.dram_tensor(name, list(shape), dt, kind="ExternalInput").ap()

    def dout(self, name, shape, dt=F32):
        return self.nc.dram_tensor(name, list(shape), dt, kind="ExternalOutput").ap()

    def mm(self, out, lhsT, rhs, r, w, start=True, stop=True):
        self.P.op("pe", lambda e: e.matmul(out, lhsT=lhsT, rhs=rhs, start=start, stop=stop, skip_group_check=True),
                  reads=r, writes=w, skip_self=True)

    def tr(self, out, in_, ident, r, w):
        self.P.op("pe", lambda e: e.transpose(out=out, in_=in_, identity=ident), reads=r, writes=w, skip_self=True)

    def act(self, out, in_, func, r, w, bias=None, scale=None, accum_out=None):
        kw = {}
        if bias is not None:
            kw["bias"] = bias
        if scale is not None:
            kw["scale"] = scale
        if accum_out is not None:
            kw["accum_out"] = accum_out
        self.P.op("act", lambda e: e.activation(out=out, in_=in_, func=func, **kw), reads=r, writes=w)

    def ts(self, eng, out, in0, s1, op0, r, w, s2=None, op1=None, accum_out=None):
        kw = {}
        if op1 is not None:
            kw["op1"] = op1
        if accum_out is not None:
            kw["accum_out"] = accum_out
        self.P.op(eng, lambda e: e.tensor_scalar(out=out, in0=in0, scalar1=s1, scalar2=s2, op0=op0, **kw),
                  reads=r, writes=w)

    def tt(self, eng, out, in0, in1, op, r, w):
        self.P.op(eng, lambda e: e.tensor_tensor(out=out, in0=in0, in1=in1, op=op), reads=r, writes=w)

    def stt(self, out, in0, scalar, in1, op0, op1, r, w, accum_out=None):
        kw = {}
        if accum_out is not None:
            kw["accum_out"] = accum_out
        self.P.op("dve", lambda e: e.scalar_tensor_tensor(out=out, in0=in0, scalar=scalar, in1=in1, op0=op0, op1=op1, **kw),
                  reads=r, writes=w)

    def cp(self, eng, out, in_, r, w):
        if eng == "act":
            self.P.op("act", lambda e: e.copy(out=out, in_=in_), reads=r, writes=w)
        else:
            self.P.op(eng, lambda e: e.tensor_copy(out=out, in_=in_), reads=r, writes=w)

    def memset(self, eng, ap, val, w):
        self.P.op(eng, lambda e: e.memset(ap, val), writes=w)

    def dma(self, eng, out, in_, r, w, slot, **kw):
        self.P.dma(eng, out, in_, reads=r, writes=w, slot=slot, **kw)


def _consts_np():
    c = np.zeros((128, 8, 128), np.float32)
    i = np.arange(128)
    c[:, 0, :] = np.eye(128)
    c[:, 1, :] = (i[:, None] <= i[None, :])
    c[:, 2, :] = np.where(i[None, :] < i[:, None], NEG, 0.0)
    c[:, 3, :] = np.where(i[None, :] > i[:, None], NEG, 0.0)
    c[:, 4, :] = 1.0
    c[:, 5, :] = (i[:, None] == 127)
    c[:, 6, :] = (i[:, None] == 31)
    c[:, 7, :] = (i[:, None] >= i[None, :])
    return c


class _Stop(Exception):
    pass


def _interleave(*gens):
    gens = list(gens)
    while gens:
        for g in list(gens):
            try:
                next(g)
            except StopIteration:
                gens.remove(g)


def build_phase1(ntiles_prompt=32, do_sample=True, debug=0):
    nc = bass.Bass("TRN2", target_bir_lowering=False)
    with ExitStack() as st:
        b = B(nc, st)
        P = b.P
        xT = b.din("xT", [D, NTOK])
        w1 = b.din("w1", [D, 1152])
        wg = b.din("wg", [128, KC * 4])
        convw = b.din("convw", [128, 12])
        scal = b.din("scal", [128, 8])
        consts = b.din("consts", [128, 8, 128])
        S0 = b.din("S0", [DB, 128, 128])
        conv0 = b.din("conv0", [128, DB, 3, 3])
        C0 = b.din("C0", [DB, 128, 128])
        n0 = b.din("n0", [128, DB])
        m0 = b.din("m0", [128, DB])
        yT = b.dout("yT", [256, NTOK], BF16)
        oS = b.dout("oS", [NB + DB, 128, 128])
        oconv = b.dout("oconv", [128, NB + DB, 3, 3])
        oC = b.dout("oC", [NB + DB, 128, 128])
        on = b.dout("on", [128, NB + DB])
        om = b.dout("om", [1, NB + DB])

        cst = b.sb("cst", [128, 8, 128])
        ident, triU, NEGU, NEGL, ones, SEL127, SEL31 = (cst[:, i, :] for i in range(7))
        w1b = b.sb("w1b", [128, KC, 1152], BF16)
        wgb = b.sb("wgb", [128, KC, 4], BF16)
        wst = b.sb("wst", [128, 2, 1152])
        cw = b.sb("cw", [128, 12])
        sc = b.sb("sc", [128, 8])
        dsc = b.sb("dsc", [128, 8])
        xst = [b.sb("xst%d" % i, [128, 4, 512]) for i in range(2)]
        xb = [b.sb("xb%d" % i, [128, KC, 512], BF16) for i in range(2)]
        cin = [b.sb("cin%d" % f, [128, 520]) for f in range(3)]
        pr = [b.sb("pr%d" % f, [128, 512]) for f in range(9)]
        cv = [b.sb("cv%d" % f, [128, 512]) for f in range(3)]
        tmpA = b.sb("tmpA", [128, 512])
        tmpB = b.sb("tmpB", [128, 512])
        qn = b.sb("qn", [128, 512])
        kn = b.sb("kn", [128, 512])
        oT = b.sb("oT", [128, 512])
        hT = b.sb("hT", [128, 512])
        yb = b.sb("yb", [128, 2, 512], BF16)
        gat = b.sb("gat", [128, 8, 12])
        GB = []
        for k in range(4):
            GB.append({n: b.sb("%s_%d" % (n, k), [128, 128]) for n in
                       ("Gbc", "grow", "DEC", "DECs", "X", "XT", "IXT", "Pm", "QKd", "kg", "kdec", "vtok", "nW", "qdec")})
            GB[k]["colg"] = b.sb("colg_%d" % k, [128, 4])
        ee = b.sb("ee", [128, 128])
        Sst = b.sb("Sst", [128, 128])
        col = b.sb("col", [128, 16])
        mprev = b.sb("mprev", [128, 1])
        bc2 = b.sb("bc2", [128, 2])
        c2 = b.sb("c2", [128, 2])
        Abc = b.sb("Abc", [128, 128])
        pmat = b.sb("pmat", [128, 128])
        pqk = b.sb("pqk", [128, 128])
        pqkT = b.sb("pqkT", [128, 128])
        v1 = b.sb("v1", [128, 129])
        kp = b.sb("kp", [128, 128])
        CN = b.sb("CN", [128, 129])
        n2s = b.sb("n2s", [128, 129])
        num = b.sb("num", [128, 128])
        htok = b.sb("htok", [128, 128])
        m0s = b.sb("m0s", [128, DB])
        n0s = b.sb("n0s", [128, DB])
        c0s = b.sb("c0s", [128, DB * 9])

        pj = [b.ps("pj%d" % i) for i in range(2)]
        pg = b.ps("pg")
        pA = b.ps("pA")
        pB = b.ps("pB")
        pC = b.ps("pC")
        pD = b.ps("pD")
        pE = b.ps("pE")

        b.dma("sp", cst[:], consts, [], ["cst"], "cst")
        b.dma("sp", cw[:], convw, [], ["cw"], "cw")
        b.dma("sp", sc[:], scal, [], ["sc"], "sc")
        b.dma("sp", m0s[:], m0, [], ["m0s"], "m0s")
        b.dma("sp", n0s[:], n0, [], ["n0s"], "n0s")
        b.dma("sp", c0s[:], conv0.rearrange("p b f r -> p (b f r)"), [], ["c0s"], "c0s")
        w1v = w1.rearrange("(kc p) n -> p kc n", p=128)
        for kc0 in range(0, KC, 2):
            b.dma("sp", wst[:, :, :], w1v[:, kc0:kc0 + 2, :], [], ["wst"], "wst")
            b.cp("dve", w1b[:, kc0, :], wst[:, 0, :], ["wst"], ["w1b"])
            b.cp("pool", w1b[:, kc0 + 1, :], wst[:, 1, :], ["wst"], ["w1b"])
        b.dma("sp", wst[:, 0, 0:KC * 4], wg, [], ["wst"], "wst")
        b.cp("dve", wgb[:].rearrange("p kc n -> p (kc n)"), wst[:, 0, 0:KC * 4], ["wst"], ["wgb"])
        b.act(dsc[:, 0:1], sc[:, 0:1], AF.Exp, ["sc"], ["dsc"])
        b.ts("dve", dsc[:, 0:1], dsc[:, 0:1], -1.0, ALU.mult, ["dsc"], ["dsc"])
        b.ts("dve", dsc[:, 1:2], sc[:, 3:4], -1.0, ALU.mult, ["sc"], ["dsc"])
        b.memset("pool", v1[:, 128:129], 1.0, ["v1"])

        xTv = xT.rearrange("(kc p) n -> p kc n", p=128)

        tiles = []
        for bi in range(NB):
            for t in range(ntiles_prompt // NB):
                tiles.append(dict(kind="p", seq=bi, tok0=bi * SEQ + t * 512, N=512, L=128, nch=4,
                                  first=(t == 0), last=(t == ntiles_prompt // NB - 1)))
        if do_sample:
            tiles.append(dict(kind="s", seq=None, tok0=NTP, N=256, L=32, nch=8, first=True, last=True))

        def load_x(ti, slot):
            T = tiles[ti]
            N = T["N"]
            for q4 in range(4):
                b.dma("sp", xst[q4 % 2][:, :, 0:N], xTv[:, q4 * 4:(q4 + 1) * 4, T["tok0"]:T["tok0"] + N],
                      [], ["xst%d" % (q4 % 2)], "xst%d" % (q4 % 2))
                b.cp("pool", xb[slot][:, q4 * 4:(q4 + 1) * 4, 0:N], xst[q4 % 2][:, :, 0:N],
                     ["xst%d" % (q4 % 2)], ["xb%d" % slot])

        load_x(0, 0)
        try:
          if debug == 0.5:
              raise _Stop
          for ti, T in enumerate(tiles):
              slot = ti % 2
              N, L, nch = T["N"], T["L"], T["nch"]
              xbk = "xb%d" % slot
              if ti + 1 < len(tiles):
                  load_x(ti + 1, 1 - slot)
              nseq = 1 if T["kind"] == "p" else DB
              Ls = N // nseq
              W = Ls + 3

              def cin3(f, a, bnd):
                  return cin[f][:, 0:nseq * W].rearrange("p (s l) -> p s l", l=W)[:, :, a:bnd]

              if T["first"]:
                  if T["kind"] == "p":
                      for f in range(3):
                          b.memset("pool", cin[f][:, 0:3], 0.0, ["cin%d" % f])
                  else:
                      for f in range(3):
                          b.cp("pool", cin3(f, 0, 3),
                               c0s[:].rearrange("p (b f r) -> p b f r", f=3, r=3)[:, :, f, :],
                               ["c0s", "cin%d" % f], ["cin%d" % f])
              for f in range(9):
                  bank = pj[f % 2]
                  bk = "pj%d" % (f % 2)
                  for kc in range(KC):
                      b.mm(bank[:, 0:N], w1b[:, kc, f * 128:(f + 1) * 128], xb[slot][:, kc, 0:N],
                           ["w1b", xbk], [bk], start=(kc == 0), stop=(kc == KC - 1))
                  if f < 3:
                      b.cp("act", cin3(f, 3, W), bank[:, 0:N].rearrange("p (s l) -> p s l", l=Ls), [bk], ["cin%d" % f])
                  elif f == 5:
                      b.act(pr[f][:, 0:N], bank[:, 0:N], AF.Copy, [bk], ["pr%d" % f], scale=128.0 ** -0.5)
                  else:
                      b.cp("dve" if f % 2 else "act", pr[f][:, 0:N], bank[:, 0:N], [bk], ["pr%d" % f])
              for j in range(nch):
                  for kc in range(KC):
                      b.mm(pg[0:L, j * 4:(j + 1) * 4], xb[slot][:, kc, j * L:(j + 1) * L], wgb[:, kc, :],
                           [xbk, "wgb"], ["pg_g"], start=(kc == 0), stop=(kc == KC - 1))
              G = pg[0:L, 0:nch * 4].rearrange("p (j g) -> p j g", g=4)
              gt = lambda i: gat[0:L, 0:nch, i]
              b.act(gt(0), G[:, :, 0], AF.Sigmoid, ["pg_g"], ["gat"])
              b.ts("dve", gt(1), gt(0), -1.0, ALU.mult, ["gat"], ["gat"])
              b.act(gt(9), G[:, :, 1], AF.Exp, ["pg_g", "sc"], ["gat"], bias=sc[0:L, 1:2])
              b.act(gt(9), gt(9), AF.Ln, ["gat"], ["gat"], bias=1.0)
              b.ts("dve", gt(2), gt(9), dsc[0:L, 0:1], ALU.mult, ["gat", "dsc"], ["gat"])
              b.ts("dve", gt(3), G[:, :, 2], sc[0:L, 2:3], ALU.add, ["pg_g", "sc"], ["gat"])
              b.act(gt(10), G[:, :, 3], AF.Exp, ["pg_g", "dsc"], ["gat"], bias=dsc[0:L, 1:2], scale=-1.0)
              b.act(gt(10), gt(10), AF.Ln, ["gat"], ["gat"], bias=1.0)
              b.ts("dve", gt(4), gt(10), -1.0, ALU.mult, ["gat"], ["gat"])
              gtmp = tmpA[0:L, 0:2 * nch]
              b.cp("dve", gtmp[:, 0:nch], gt(2), ["gat"], ["tmpA"])
              b.cp("dve", gtmp[:, nch:2 * nch], gt(4), ["gat"], ["tmpA"])
              b.mm(pg[0:L, 32:32 + 2 * nch], triU[0:L, 0:L], gtmp, ["cst", "tmpA"], ["pg_c"])
              b.cp("dve", gt(5), pg[0:L, 32:32 + nch], ["pg_c"], ["gat"])
              b.cp("dve", gt(7), pg[0:L, 32 + nch:32 + 2 * nch], ["pg_c"], ["gat"])
              b.act(gt(6), gt(5), AF.Exp, ["gat"], ["gat"])
              b.ts("dve", gt(11), gt(5), -1.0, ALU.mult, ["gat"], ["gat"])
              b.tt("dve", gt(8), gt(3), gt(7), ALU.subtract, ["gat"], ["gat"])

              if debug == 1:
                  raise _Stop
              for f in range(3):
                  ck = "cin%d" % f
                  o3 = tmpA[:, 0:N].rearrange("p (s l) -> p s l", l=Ls)
                  b.ts("dve", o3, cin3(f, 0, Ls), cw[:, f * 4:f * 4 + 1], ALU.mult, [ck, "cw"], ["tmpA"])
                  for i in range(1, 4):
                      b.stt(o3, cin3(f, i, i + Ls), cw[:, f * 4 + i:f * 4 + i + 1], o3, ALU.mult, ALU.add,
                            [ck, "cw", "tmpA"], ["tmpA"])
                  b.act(cv[f][:, 0:N], tmpA[:, 0:N], AF.Silu, ["tmpA"], ["cv%d" % f])
                  if T["last"]:
                      if T["kind"] == "p":
                          b.dma("sp", oconv[:, T["seq"], f, :], cin[f][:, Ls:Ls + 3], [ck], ["oconv"], "oconv%d" % f)
                      else:
                          b.dma("sp", oconv[:, NB:NB + DB, f, :], cin3(f, Ls, Ls + 3), [ck], ["oconv"], "oconv%d" % f)
                  else:
                      b.cp("pool", cin[f][:, 0:3], cin[f][:, Ls:Ls + 3], [ck], [ck])
              for f, dst, dk, lnb in ((0, qn, "qn", -0.5 * math.log(128.0)), (1, kn, "kn", 0.0)):
                  b.tt("pool", tmpB[:, 0:N], cv[f][:, 0:N], cv[f][:, 0:N], ALU.mult, ["cv%d" % f], ["tmpB"])
                  b.mm(pj[f][:, 0:N], ones, tmpB[:, 0:N], ["cst", "tmpB"], ["pj%d" % f])
                  b.act(tmpB[:, 0:N], pj[f][:, 0:N], AF.Ln, ["pj%d" % f], ["tmpB"], bias=1e-6)
                  b.act(tmpB[:, 0:N], tmpB[:, 0:N], AF.Exp, ["tmpB"], ["tmpB"], scale=-0.5, bias=lnb)
                  b.tt("dve", dst[:, 0:N], cv[f][:, 0:N], tmpB[:, 0:N], ALU.mult, ["cv%d" % f, "tmpB"], [dk])

              if debug == 2:
                  raise _Stop
              I_L = ident[0:L, 0:L]
              NK = 4
              pre_done = [False] * nch
              seq_done = [False] * nch
              PBK = [(pj[0], "pj0"), (pj[1], "pj1"), (pA, "pA0"), (pB, "pB0")]

              def pre_chain(k):
                  G_ = GB[k]
                  bk, bkey = PBK[k]
                  kx = lambda n: "%s_%d" % (n, k)
                  Gbc, grow, DEC, DECs, X, XT, IXT, Pm = (G_[n] for n in ("Gbc", "grow", "DEC", "DECs", "X", "XT", "IXT", "Pm"))
                  QKd, kg, kdec, vtok, nW, qdec, colg = (G_[n] for n in ("QKd", "kg", "kdec", "vtok", "nW", "qdec", "colg"))
                  R0, R1, R2, R3 = (bk[:, i * 128:(i + 1) * 128] for i in range(4))
                  for j in range(k, nch, NK):
                      if j >= NK:
                          while not seq_done[j - NK]:
                              yield
                      c0, c1 = j * L, (j + 1) * L
                      gcol = gat[0:L, j, 2:3]
                      b.ts('dve', Gbc[0:L, :], ones[0:L, :], gcol, ALU.mult, ['cst', 'gat'], [kx('Gbc')])
                      yield
                      b.mm(R0[:, 0:L], Gbc[0:L, :], triU[0:L, 0:L], [kx('Gbc'), 'cst'], [bkey])
                      yield
                      b.mm(R1[0:L, 0:L], Gbc[0:L, 0:L], triU[0:L, 0:L], [kx('Gbc'), 'cst'], [bkey], start=True, stop=False)
                      yield
                      b.mm(R1[0:L, 0:L], I_L, NEGU[0:L, 0:L], ['cst'], [bkey], start=False, stop=True)
                      yield
                      b.mm(R2[0:L, 0:L], kn[:, c0:c1], kn[:, c0:c1], ['kn'], [bkey])
                      yield
                      b.mm(R3[0:L, 0:L], kn[:, c0:c1], qn[:, c0:c1], ['kn', 'qn'], [bkey])
                      yield
                      b.act(grow[:, 0:L], R0[:, 0:L], AF.Exp, [bkey], [kx('grow')])
                      yield
                      b.act(DEC[0:L, 0:L], R1[0:L, 0:L], AF.Exp, [bkey, 'gat'], [kx('DEC')], bias=gat[0:L, j, 11:12])
                      yield
                      b.cp('dve', colg[:, 0:1], R0[:, L - 1:L], [bkey], [kx('col0')])
                      yield
                      b.act(colg[:, 1:2], colg[:, 0:1], AF.Exp, [kx('col0')], [kx('col1')])
                      yield
                      b.act(colg[0:L, 2:3], gat[0:L, j, 5:6], AF.Exp, ['gat', kx('col0')], [kx('col2')], scale=-1.0, bias=colg[0:L, 0:1])
                      yield
                      b.tt('pool', DECs[0:L, 0:L], DEC[0:L, 0:L], I_L, ALU.subtract, [kx('DEC'), 'cst'], [kx('DECs')])
                      yield
                      b.stt(X[0:L, 0:L], R2[0:L, 0:L], gat[0:L, j, 1:2], DECs[0:L, 0:L], ALU.mult, ALU.mult, [bkey, 'gat', kx('DECs')], [kx('X')])
                      yield
                      b.tt('dve', QKd[0:L, 0:L], R3[0:L, 0:L], DEC[0:L, 0:L], ALU.mult, [bkey, kx('DEC')], [kx('QKd')])
                      yield
                      b.tt('pool', qdec[:, 0:L], qn[:, c0:c1], grow[:, 0:L], ALU.mult, ['qn', kx('grow')], [kx('qdec')])
                      yield
                      b.tr(R0[0:L, 0:128], kn[:, c0:c1], ident, ['kn', 'cst'], [bkey])
                      yield
                      b.tr(R1[0:L, 0:128], cv[2][:, c0:c1], ident, ['cv2', 'cst'], [bkey])
                      yield
                      b.tr(R2[0:L, 0:L], X[0:L, 0:L], I_L, [kx('X'), 'cst'], [bkey])
                      yield
                      b.ts('dve', kg[0:L, :], R0[0:L, 0:128], gat[0:L, j, 6:7], ALU.mult, [bkey, 'gat'], [kx('kg')])
                      yield
                      b.ts('dve', kdec[0:L, :], R0[0:L, 0:128], colg[0:L, 2:3], ALU.mult, [bkey, kx('col2')], [kx('kdec')])
                      yield
                      b.cp('act', vtok[0:L, :], R1[0:L, 0:128], [bkey], [kx('vtok')])
                      yield
                      b.cp('act', XT[0:L, 0:L], R2[0:L, 0:L], [bkey], [kx('XT')])
                      yield
                      b.tt('pool', Pm[0:L, 0:L], X[0:L, 0:L], I_L, ALU.add, [kx('X'), 'cst'], [kx('Pm')])
                      yield
                      nsq = int(math.log2(L)) - 1
                      for it in range(nsq):
                          lastit = it == nsq - 1
                          if not lastit:
                              b.mm(R0[0:L, 0:L], XT[0:L, 0:L], X[0:L, 0:L], [kx('XT'), kx('X')], [bkey])
                              yield
                          b.mm(R1[0:L, 0:L], X[0:L, 0:L], XT[0:L, 0:L], [kx('XT'), kx('X')], [bkey])
                          yield
                          if not lastit:
                              b.cp('act', X[0:L, 0:L], R0[0:L, 0:L], [bkey], [kx('X')])
                              yield
                              b.cp('dve', XT[0:L, 0:L], R1[0:L, 0:L], [bkey], [kx('XT')])
                              yield
                          b.tt('dve', IXT[0:L, 0:L], R1[0:L, 0:L], I_L, ALU.add, [bkey, 'cst'], [kx('IXT')])
                          yield
                          b.mm(R3[0:L, 0:L], IXT[0:L, 0:L], Pm[0:L, 0:L], [kx('IXT'), kx('Pm')], [bkey])
                          yield
                          b.cp('act', Pm[0:L, 0:L], R3[0:L, 0:L], [bkey], [kx('Pm')])
                          yield
                      b.mm(R2[:, 0:L], kg[0:L, :], Pm[0:L, 0:L], [kx('kg'), kx('Pm')], [bkey])
                      yield
                      b.ts('dve', nW[:, 0:L], R2[:, 0:L], -1.0, ALU.mult, [bkey], [kx('nW')])
                      yield
                      pre_done[j] = True

              def seq_chain():
                  for j in range(nch):
                      while not pre_done[j]:
                          yield
                      k = j % NK
                      G_ = GB[k]
                      kx = lambda n: "%s_%d" % (n, k)
                      Pm, QKd, kdec, vtok, nW, qdec, colg = (G_[n] for n in ("Pm", "QKd", "kdec", "vtok", "nW", "qdec", "colg"))
                      c0, c1 = j * L, (j + 1) * L
                      if T["kind"] == "s":
                          seq, seq_last = NB + j, True
                          b.dma("sp", Sst[:], S0[j], [], ["S"], "S_in")
                      else:
                          seq = T["seq"]
                          seq_last = T["last"] and j == nch - 1
                          if T["first"] and j == 0:
                              b.memset("pool", Sst[:], 0.0, ["S"])
                      yield
                      b.mm(pC[0:L, 0:128], Pm[0:L, 0:L], vtok[0:L, :], [kx('Pm'), kx('vtok')], ['pC0'], start=True, stop=False)
                      yield
                      b.mm(pC[0:L, 0:128], nW[:, 0:L], Sst[:], [kx('nW'), 'S'], ['pC0'], start=False, stop=True)
                      yield
                      b.ts('dve', ee[0:L, :], pC[0:L, 0:128], gat[0:L, j, 0:1], ALU.mult, ['pC0', 'gat'], ['ee'])
                      yield
                      b.mm(pC[:, 128:128 + L], Sst[:], qdec[:, 0:L], ['S', kx('qdec')], ['pC1'], start=True, stop=False)
                      yield
                      b.mm(pC[:, 128:128 + L], ee[0:L, :], QKd[0:L, 0:L], ['ee', kx('QKd')], ['pC1'], start=False, stop=True)
                      yield
                      b.mm(pC[:, 256:384], kdec[0:L, :], ee[0:L, :], [kx('kdec'), 'ee'], ['pC2'])
                      yield
                      b.cp('act', oT[:, c0:c1], pC[:, 128:128 + L], ['pC1'], ['oT'])
                      yield
                      b.stt(Sst[:], Sst[:], colg[:, 1:2], pC[:, 256:384], ALU.mult, ALU.add, ['S', kx('col1'), 'pC2'], ['S'])
                      yield
                      if seq_last:
                          b.dma('sp', oS[seq], Sst[:], ['S'], ['oS'], 'oS')
                          yield
                      seq_done[j] = True

              def ml_chain():
                  for j in range(nch):
                      c0, c1 = j * L, (j + 1) * L
                      if T["kind"] == "s":
                          seq, seq_last = NB + j, True
                          b.dma("sp", CN[:, 0:128], C0[j], [], ["CN"], "CN_in")
                          b.cp("pool", CN[:, 128:129], n0s[:, j:j + 1], ["n0s"], ["CN"])
                          b.cp("pool", mprev[:], m0s[:, j:j + 1], ["m0s"], ["mprev"])
                      else:
                          seq = T["seq"]
                          seq_last = T["last"] and j == nch - 1
                          if T["first"] and j == 0:
                              b.memset("pool", CN[:], 0.0, ["CN"])
                              b.memset("pool", mprev[:], 0.0, ["mprev"])
                      yield
                      qb, kb, vb = (pr[4], pr[5], pr[6])
                      acol = gat[0:L, j, 8:9]
                      b.ts('dve', Abc[0:L, 0:L], ones[0:L, 0:L], acol, ALU.mult, ['cst', 'gat'], ['Abc'])
                      yield
                      b.mm(pD[0:L, 256:256 + L], Abc[0:L, 0:L], I_L, ['Abc', 'cst'], ['pD2'], start=True, stop=False)
                      yield
                      b.mm(pD[0:L, 256:256 + L], I_L, NEGL[0:L, 0:L], ['cst'], ['pD2'], start=False, stop=True)
                      yield
                      b.P.op('dve', lambda e, o=col[0:L, 3:4], i=pD[0:L, 256:256 + L]: e.tensor_reduce(out=o, in_=i, axis=AX.X, op=ALU.max), reads=['pD2'], writes=['col3'])
                      yield
                      b.tt('dve', col[0:L, 4:5], col[0:L, 3:4], mprev[0:L, :], ALU.max, ['col3', 'mprev'], ['col4'])
                      yield
                      b.ts('dve', col[0:L, 5:6], col[0:L, 4:5], -1.0, ALU.mult, ['col4'], ['col5'])
                      yield
                      b.act(pmat[0:L, 0:L], pD[0:L, 256:256 + L], AF.Exp, ['pD2', 'col5'], ['pmat'], bias=col[0:L, 5:6])
                      yield
                      b.act(col[0:L, 6:7], mprev[0:L, :], AF.Exp, ['mprev', 'col5'], ['col6'], bias=col[0:L, 5:6])
                      yield
                      b.tt('dve', col[0:L, 7:8], gat[0:L, j, 7:8], col[0:L, 4:5], ALU.add, ['gat', 'col4'], ['col7'])
                      yield
                      b.act(col[0:L, 8:9], col[0:L, 7:8], AF.Exp, ['col7'], ['col8'], scale=-1.0)
                      yield
                      b.mm(pD[0:L, 384:384 + L], qb[:, c0:c1], kb[:, c0:c1], ['pr4', 'pr5'], ['pD3'])
                      yield
                      b.stt(pqk[0:L, 0:L], pD[0:L, 384:384 + L], 1.0, pmat[0:L, 0:L], ALU.mult, ALU.mult, ['pD3', 'pmat'], ['pqk', 'col9'], accum_out=col[0:L, 9:10])
                      yield
                      b.tr(pE[0:L, 0:L], pqk[0:L, 0:L], I_L, ['pqk', 'cst'], ['pE0'])
                      yield
                      b.cp('act', pqkT[0:L, 0:L], pE[0:L, 0:L], ['pE0'], ['pqkT'])
                      yield
                      b.cp('dve', c2[0:L, 0:1], col[0:L, 4:5], ['col4'], ['c2'])
                      yield
                      b.cp('dve', c2[0:L, 1:2], gat[0:L, j, 7:8], ['gat'], ['c2'])
                      yield
                      SEL = SEL127 if L == 128 else SEL31
                      b.mm(pg[:, 48:50], SEL[0:L, :], c2[0:L, :], ['cst', 'c2'], ['pg_b'])
                      yield
                      b.cp('dve', bc2[:], pg[:, 48:50], ['pg_b'], ['bc2'])
                      yield
                      b.ts('dve', col[:, 10:11], bc2[:, 0:1], -1.0, ALU.mult, ['bc2'], ['col10'])
                      yield
                      b.act(col[0:L, 11:12], acol, AF.Exp, ['gat', 'col10'], ['col11'], bias=col[0:L, 10:11])
                      yield
                      b.act(col[:, 12:13], mprev[:], AF.Exp, ['mprev', 'col10'], ['col12'], bias=col[:, 10:11])
                      yield
                      b.tr(pE[0:L, 128:256], kb[:, c0:c1], ident, ['pr5', 'cst'], ['pE1'])
                      yield
                      b.tr(pE[0:L, 256:384], vb[:, c0:c1], ident, ['pr6', 'cst'], ['pE2'])
                      yield
                      b.ts('dve', kp[0:L, :], pE[0:L, 128:256], col[0:L, 11:12], ALU.mult, ['pE1', 'col11'], ['kp'])
                      yield
                      b.cp('act', v1[0:L, 0:128], pE[0:L, 256:384], ['pE2'], ['v1'])
                      yield
                      b.mm(pg[0:L, 64:192], pqkT[0:L, 0:L], v1[0:L, 0:128], ['pqkT', 'v1'], ['pg_n1'])
                      yield
                      b.mm(pg[0:L, 192:321], qb[:, c0:c1], CN[:, 0:129], ['pr4', 'CN'], ['pg_n2'])
                      yield
                      b.act(n2s[0:L, :], pg[0:L, 192:321], AF.Copy, ['pg_n2', 'col6'], ['n2s'], scale=col[0:L, 6:7])
                      yield
                      b.tt('dve', num[0:L, :], n2s[0:L, 0:128], pg[0:L, 64:192], ALU.add, ['n2s', 'pg_n1'], ['num'])
                      yield
                      b.tt('dve', col[0:L, 13:14], n2s[0:L, 128:129], col[0:L, 9:10], ALU.add, ['n2s', 'col9'], ['col13'])
                      yield
                      b.act(col[0:L, 13:14], col[0:L, 13:14], AF.Abs, ['col13'], ['col13'])
                      yield
                      b.tt('dve', col[0:L, 13:14], col[0:L, 13:14], col[0:L, 8:9], ALU.max, ['col13', 'col8'], ['col13'])
                      yield
                      b.P.op('dve', lambda e, o=col[0:L, 14:15], i=col[0:L, 13:14]: e.reciprocal(out=o, in_=i), reads=['col13'], writes=['col14'])
                      yield
                      b.ts('dve', htok[0:L, :], num[0:L, :], col[0:L, 14:15], ALU.mult, ['num', 'col14'], ['htok'])
                      yield
                      b.tr(pE[:, 384:384 + L], htok[0:L, :], I_L, ['htok', 'cst'], ['pE3'])
                      yield
                      b.cp('act', hT[:, c0:c1], pE[:, 384:384 + L], ['pE3'], ['hT'])
                      yield
                      b.mm(pg[:, 321:450], kp[0:L, :], v1[0:L, :], ['kp', 'v1'], ['pg_c2'])
                      yield
                      b.stt(CN[:], CN[:], col[:, 12:13], pg[:, 321:450], ALU.mult, ALU.add, ['CN', 'col12', 'pg_c2'], ['CN'])
                      yield
                      b.tt('dve', mprev[:], bc2[:, 0:1], bc2[:, 1:2], ALU.add, ['bc2'], ['mprev'])
                      yield
                      if seq_last:
                          b.dma('sp', oC[seq], CN[:, 0:128], ['CN'], ['oC'], 'oC')
                          yield
                          b.dma('sp', on[:, seq:seq + 1], CN[:, 128:129], ['CN'], ['on'], 'on', allow_slow_non_contiguous=True)
                          yield
                          b.dma('sp', om[0:1, seq:seq + 1], mprev[0:1, 0:1], ['mprev'], ['om'], 'om')
                          yield
                      yield

              _interleave(*([pre_chain(k) for k in range(NK)] + [seq_chain(), ml_chain()]))
              if debug == 4:
                  raise _Stop
              for (src, srck, zf, of, ncol, half) in ((oT, "oT", 3, None, 4, 0), (hT, "hT", 8, 7, 5, 1)):
                  if of is not None:
                      b.act(tmpA[:, 0:N], pr[of][:, 0:N], AF.Sigmoid, ["pr%d" % of], ["tmpA"])
                      b.tt("dve", src[:, 0:N], src[:, 0:N], tmpA[:, 0:N], ALU.mult, [srck, "tmpA"], [srck])
                  b.tt("pool", tmpB[:, 0:N], src[:, 0:N], src[:, 0:N], ALU.mult, [srck], ["tmpB"])
                  b.mm(pj[half][:, 0:N], ones, tmpB[:, 0:N], ["cst", "tmpB"], ["pj%d" % half])
                  b.act(tmpB[:, 0:N], pj[half][:, 0:N], AF.Ln, ["pj%d" % half], ["tmpB"], scale=1.0 / 128.0, bias=1e-6)
                  b.act(tmpB[:, 0:N], tmpB[:, 0:N], AF.Exp, ["tmpB"], ["tmpB"], scale=-0.5)
                  b.act(tmpA[:, 0:N], pr[zf][:, 0:N], AF.Silu, ["pr%d" % zf], ["tmpA"])
                  b.stt(tmpB[:, 0:N], tmpB[:, 0:N], sc[:, ncol:ncol + 1], tmpA[:, 0:N], ALU.mult, ALU.mult,
                        ["tmpB", "sc", "tmpA"], ["tmpB"])
                  b.tt("dve", yb[:, half, 0:N], src[:, 0:N], tmpB[:, 0:N], ALU.mult, [srck, "tmpB"], ["yb%d" % half])
                  b.dma("sp", yT[half * 128:(half + 1) * 128, T["tok0"]:T["tok0"] + N], yb[:, half, 0:N],
                        ["yb%d" % half], ["yT"], "yT%d" % half)

        except _Stop:
            pass
        P.finalize()
        print("phase1 instrs", P.ninstr, {e: P.cnt[e] for e in ENGS})
    return nc


NTT = 17


def build_out_phase(glu, emit_T):
    nc = bass.Bass("TRN2", target_bir_lowering=False)
    KCI = 24 if glu else 16
    with ExitStack() as st:
        b = B(nc, st)
        P = b.P
        yin = b.din("yin", [NTT, 128, KCI, 128], BF16)
        wout = b.din("wout", [D, D])
        xres = b.din("xres", [NTT * 128, D])
        lng = b.din("lng", [128, D])
        lnb = b.din("lnb", [128, D])
        consts = b.din("consts", [128, 8, 128])
        if glu:
            wglu = b.din("wglu", [1024, 1024])
            bglu = b.din("bglu", [128, 8])
        xo = b.dout("xo", [NTT * 128, D])
        if emit_T:
            xoT = b.dout("xoT", [NTT, 128, KC, 128], BF16)

        cst = b.sb("cst", [128, 8, 128])
        ident = cst[:, 0, :]
        woutb = b.sb("woutb", [128, KC, D], BF16)
        wst = b.sb("wst", [128, 2, D])
        g_sb = b.sb("g_sb", [128, D])
        b_sb = b.sb("b_sb", [128, D])
        ysb = [b.sb("ysb%d" % i, [128, KCI, 128], BF16) for i in range(2)]
        xr = [b.sb("xr%d" % i, [128, D]) for i in range(2)]
        z = b.sb("z", [128, D])
        stats = b.sb("stats", [128, 4, 6])
        mv = b.sb("mv", [128, 4])
        xT_sb = [b.sb("xT_sb%d" % i, [128, KC, 128], BF16) for i in range(2)]
        if glu:
            wglub = b.sb("wglub", [128, 8, 1024], BF16)
            bg = b.sb("bg", [128, 8])
            sig = b.sb("sig", [128, 8, 128])
            yd = b.sb("yd", [128, 8, 128], BF16)
        po = [b.ps("po%d" % i) for i in range(4)]
        pt = [b.ps("pt%d" % i) for i in range(4)]
        PK = lambda i: "pj%d" % i

        b.dma("sp", cst[:], consts, [], ["cst"], "cst")
        b.dma("sp", g_sb[:], lng, [], ["g_sb"], "g_sb")
        b.dma("sp", b_sb[:], lnb, [], ["b_sb"], "b_sb")
        wv = wout.rearrange("(kc p) n -> p kc n", p=128)
        for kc0 in range(0, KC, 2):
            b.dma("sp", wst[:], wv[:, kc0:kc0 + 2, :], [], ["wst"], "wst")
            b.cp("dve", woutb[:, kc0, :], wst[:, 0, :], ["wst"], ["woutb"])
            b.cp("pool", woutb[:, kc0 + 1, :], wst[:, 1, :], ["wst"], ["woutb"])
        if glu:
            gv = wglu.rearrange("(kc p) n -> p kc n", p=128)
            for kc0 in range(0, 8, 4):
                b.dma("sp", wst[:].rearrange("p a (c n) -> p (a c) n", n=1024), gv[:, kc0:kc0 + 4, :], [], ["wst"], "wst")
                b.cp("pool", wglub[:, kc0:kc0 + 4, :], wst[:].rearrange("p a (c n) -> p (a c) n", n=1024), ["wst"], ["wglub"])
            b.dma("sp", bg[:], bglu, [], ["bg"], "bg")

        def load(ti):
            s = ti % 2
            b.dma("sp", ysb[s][:], yin[ti], [], ["ysb%d" % s], "ysb%d" % s)
            b.dma("sp", xr[s][:], xres[ti * 128:(ti + 1) * 128, :], [], ["xr%d" % s], "xr%d" % s)

        load(0)
        for ti in range(NTT):
            s = ti % 2
            if ti + 1 < NTT:
                load(ti + 1)
            yk = "ysb%d" % s
            if glu:
                for f in range(8):
                    bank = pt[f // 4]
                    for kc in range(8):
                        b.mm(bank[:, (f % 4) * 128:(f % 4 + 1) * 128], wglub[:, kc, f * 128:(f + 1) * 128],
                             ysb[s][:, 8 + kc, :], ["wglub", yk], [PK(4 + f // 4)], start=(kc == 0), stop=(kc == 7))
                for f in range(8):
                    b.act(sig[:, f, :], pt[f // 4][:, (f % 4) * 128:(f % 4 + 1) * 128], AF.Sigmoid,
                          [PK(4 + f // 4), "bg"], ["sig"], bias=bg[:, f:f + 1])
                b.tt("dve", sig[:], sig[:], ysb[s][:, 8:16, :], ALU.mult, ["sig", yk], ["sig"])
                b.tt("dve", yd[:], sig[:], ysb[s][:, 16:24, :], ALU.mult, ["sig", yk], ["yd"])
            for n in range(4):
                for kc in range(KC):
                    if glu and kc >= 8:
                        lhsT, lk = yd[:, kc - 8, :], "yd"
                    else:
                        lhsT, lk = ysb[s][:, kc, :], yk
                    b.mm(po[n][:, :], lhsT, woutb[:, kc, n * 512:(n + 1) * 512], [lk, "woutb"], [PK(n)],
                         start=(kc == 0), stop=(kc == KC - 1))
            for n in range(4):
                b.stt(z[:, n * 512:(n + 1) * 512], xr[s][:, n * 512:(n + 1) * 512], ALPHA, po[n][:, :],
                      ALU.mult, ALU.add, ["xr%d" % s, PK(n)], ["z"])
                b.P.op("dve", lambda e, o=stats[:, n, :], i=z[:, n * 512:(n + 1) * 512]: e.bn_stats(out=o, in_=i),
                       reads=["z"], writes=["stats"])
            b.P.op("dve", lambda e, o=mv[:, 0:2], i=stats[:].rearrange("p a c -> p (a c)"): e.bn_aggr(out=o, in_=i),
                   reads=["stats"], writes=["mv"])
            b.act(mv[:, 2:3], mv[:, 1:2], AF.Ln, ["mv"], ["mv"], bias=1e-5)
            b.act(mv[:, 2:3], mv[:, 2:3], AF.Exp, ["mv"], ["mv"], scale=-0.5)
            b.ts("dve", z[:], z[:], mv[:, 0:1], ALU.subtract, ["z", "mv"], ["z"], s2=mv[:, 2:3], op1=ALU.mult)
            b.tt("pool", z[:], z[:], g_sb[:], ALU.mult, ["z", "g_sb"], ["z"])
            b.tt("dve", z[:], z[:], b_sb[:], ALU.add, ["z", "b_sb"], ["z"])
            b.dma("sp", xo[ti * 128:(ti + 1) * 128, :], z[:], ["z"], ["xo"], "xo")
            if emit_T:
                for kc in range(KC):
                    b.tr(pt[kc // 4][:, (kc % 4) * 128:(kc % 4 + 1) * 128], z[:, kc * 128:(kc + 1) * 128], ident,
                         ["z", "cst"], [PK(4 + kc // 4)])
                for q in range(4):
                    b.cp("act" if q % 2 else "dve", xT_sb[s][:, q * 4:(q + 1) * 4, :].rearrange("p a t -> p (a t)"),
                         pt[q][:, :], [PK(4 + q)], ["xT_sb%d" % s])
                b.dma("sp", xoT[ti], xT_sb[s][:], ["xT_sb%d" % s], ["xoT"], "xoT%d" % s)
        P.finalize()
        print("out-phase instrs", P.ninstr, {e: P.cnt[e] for e in ENGS})
    return nc


def _run(nc, in_maps):
    return run_bass_kernel_spmd(nc, in_maps, core_ids=list(range(NCORES))).results


def _tok_major(inp_p, inp_s):
    return np.concatenate([inp_p.reshape(-1, inp_p.shape[-1]), inp_s.reshape(-1, inp_s.shape[-1])], 0)


def _p1_inputs(inp, xT, c):
    H = 128
    W = inp['ev_w_in'][0]
    offs = np.cumsum([0, 3072, 1024, 8, 8, 1024, 1024, 1024, 1024, 1024, 8, 8])

    def colblk(i, sub=0):
        return W[:, offs[i] + sub * 1024 + c * H: offs[i] + sub * 1024 + (c + 1) * H]
    w1 = np.concatenate([colblk(0, 0), colblk(0, 1), colblk(0, 2), colblk(1),
                         colblk(4), colblk(5), colblk(6), colblk(7), colblk(8)], 1)
    wg = np.stack([W[:, offs[2] + c], W[:, offs[3] + c], W[:, offs[9] + c], W[:, offs[10] + c]], 1)
    wg = wg.reshape(KC, 128, 4).transpose(1, 0, 2).reshape(128, KC * 4)
    cw = inp['ev_conv_w'][0]
    convw = np.stack([cw[:, f * 1024 + c * H: f * 1024 + (c + 1) * H].T for f in range(3)], 1).reshape(H, 12)
    scal = np.zeros((128, 8), np.float32)
    scal[:, 0] = inp['ev_a_log'][0, c]
    scal[:, 1] = inp['ev_dt_bias'][0, c]
    scal[:, 2] = inp['ev_ig_bias'][0, c]
    scal[:, 3] = inp['ev_fg_bias'][0, c]
    scal[:, 4] = inp['ev_norm_a'][0]
    scal[:, 5] = inp['ev_norm_b'][0, c * H:(c + 1) * H]
    cs = inp['state_delta_conv'][0]
    conv0 = np.stack([cs[:, :, f * 1024 + c * H: f * 1024 + (c + 1) * H] for f in range(3)], 0)
    conv0 = np.ascontiguousarray(conv0.transpose(3, 1, 0, 2))
    A = np.ascontiguousarray
    return dict(xT=xT, w1=A(w1), wg=A(wg), convw=A(convw), scal=scal, consts=_consts_np(),
                S0=A(inp['state_delta_S'][0, :, c]), conv0=conv0,
                C0=A(inp['state_mlstm_C'][0, :, c]), n0=A(inp['state_mlstm_n'][0, :, c].T),
                m0=A(np.broadcast_to(inp['state_mlstm_m'][0, :, c][None, :], (128, DB))))


def _tile_tokens(Y, c):
    nk = Y.shape[0]
    out = np.zeros((nk, 128, NTT * 128), Y.dtype)
    out[:, :, :TOKC] = Y[:, :, c * TOKC:(c + 1) * TOKC]
    return np.ascontiguousarray(out.reshape(nk, 128, NTT, 128).transpose(2, 1, 0, 3))


def _pad_rows(x, c):
    out = np.zeros((NTT * 128, x.shape[1]), x.dtype)
    out[:TOKC] = x[c * TOKC:(c + 1) * TOKC]
    return out


def _p2_inputs(inp, xtok, Y, c):
    rep = lambda v: np.ascontiguousarray(np.broadcast_to(v[None, :], (128, v.shape[0])))
    return dict(yin=_tile_tokens(Y, c), wout=np.ascontiguousarray(inp['ev_w_out'][0]), xres=_pad_rows(xtok, c),
                lng=rep(inp['ev_ln_g'][0]), lnb=rep(inp['ev_ln_b'][0]), consts=_consts_np())


TWO_PI = 2.0 * math.pi
CW1 = 6.28125
CW2 = TWO_PI - CW1


def _masks3_np():
    m = np.zeros((128, 6, 512), np.float32)
    s = np.arange(128)[:, None]
    t = np.arange(512)[None, :]
    for r in range(4):
        m[:, r, :] = (t > 128 * r + s)
    m[:, 4, :] = ((t % 32) > s)
    m[:, 5, :] = t + 1.0
    return m


def build_phase3(ntiles_prompt=32, do_sample=True, debug=0):
    nc = bass.Bass("TRN2", target_bir_lowering=False)
    with ExitStack() as st:
        b = B(nc, st)
        P = b.P
        x1T = b.din("x1T", [D, NTOK], BF16)
        w3 = b.din("w3", [D, 768])
        consts = b.din("consts", [128, 8, 128])
        masks = b.din("masks", [128, 5, 512], BF16)
        tau = b.din("tau", [128, 512])
        kcT = b.din("kcT", [32, 128, DB, 128])
        vcc = b.din("vcc", [32, 128, DB, 128])
        s5p = b.din("s5p", [128, 4, 4])
        Bbd = b.din("Bbd", [128, 2, 4, 128])
        Cbd = b.din("Cbd", [128, 2, 4, 128])
        h0 = b.din("h0", [128, 2, 4, DB])
        y3T = b.dout("y3T", [384, NTOK], BF16)
        kout = b.dout("kout", [NTOK, 128])
        vout = b.dout("vout", [NTOK, 128])
        hout = b.dout("hout", [128, 2, 4, NB + DB])

        cst = b.sb("cst", [128, 8, 128])
        ident = cst[:, 0, :]
        msk = b.sb("msk", [128, 5, 512], BF16)
        trib = b.sb("trib", [128, 2, 128], BF16)
        w3b = b.sb("w3b", [128, KC, 768], BF16)
        wst = b.sb("wst", [128, 1, 768])
        xb = [b.sb("xb%d" % i, [128, KC, 512], BF16) for i in range(2)]
        KT = b.sb("KT", [128, SEQ], BF16)
        Vt = b.sb("Vt", [128, 64, 128], BF16)
        qb = b.sb("qb", [128, 512], BF16)
        kT32 = b.sb("kT32", [128, 512])
        vT32 = b.sb("vT32", [128, 512])
        zc = b.sb("zc", [128, 512])
        u32 = b.sb("u32", [128, 512])
        ub = b.sb("ub", [128, 512], BF16)
        szd = b.sb("szd", [128, 512], BF16)
        ktok = b.sb("ktok", [128, 4, 128])
        vtok = b.sb("vtok", [128, 4, 128])
        E = [b.sb("E%d" % i, [128, 512]) for i in range(3)]
        SP = [b.sb("SP%d" % i, [128, 512], BF16) for i in range(3)]
        Wt = b.sb("Wt", [128, 512])
        ATT = [b.sb("ATT%d" % i, [128, 512], BF16) for i in range(2)]
        ycb = b.sb("ycb", [128, 512], BF16)
        ysb = b.sb("ysb", [128, 512], BF16)
        sp_ = b.sb("sp_", [128, 4, 4])
        cr = b.sb("cr", [128, 4, 512])
        sr = b.sb("sr", [128, 4, 512])
        ki = b.sb("ki", [128, 512], mybir.dt.int32)
        s5c = b.sb("s5c", [128, 4, 16])
        Bb = b.sb("Bb", [128, 2, 4, 128], BF16)
        Cst = b.sb("Cst", [128, 2, 4, 128])
        Cb = b.sb("Cb", [128, 2, 4, 128], BF16)
        hst = b.sb("hst", [128, 2, 4, DB])
        h0s = b.sb("h0s", [128, 2, 4, DB])
        hfin = b.sb("hfin", [128, 2, 4, NB + DB])
        t1 = b.sb("t1", [128, 512])
        t2 = b.sb("t2", [128, 512])
        bre = b.sb("bre", [128, 512])
        bim = b.sb("bim", [128, 512])
        gre = b.sb("gre", [128, 512])
        gim = b.sb("gim", [128, 512])
        ang, kk = gre, gim
        hreb = b.sb("hreb", [128, 512], BF16)
        himb = b.sb("himb", [128, 512], BF16)
        ys32 = b.sb("ys32", [128, 512])
        kcs = [b.sb("kcs%d" % i, [128, DB, 128]) for i in range(2)]
        vcs = [b.sb("vcs%d" % i, [128, DB, 128]) for i in range(2)]
        kcb = [b.sb("kcb%d" % i, [128, DB, 128], BF16) for i in range(3)]
        vcb = [b.sb("vcb%d" % i, [128, DB, 128], BF16) for i in range(3)]
        vnew = b.sb("vnew", [128, DB, 128], BF16)

        pj = [b.ps("pj%d" % i) for i in range(2)]
        pZ = [b.ps("pZ%d" % i) for i in range(2)]
        pCm = b.ps("pCm")
        pO = b.ps("pO")
        pT = b.ps("pT")
        pY = b.ps("pY")
        ZK = ["pj2", "pj3"]
        CK, OK, TK, YK = "pj4", "pj5", "pj6", "pj7"

        b.dma("sp", cst[:], consts, [], ["cst"], "cst")
        b.dma("sp", msk[:], masks, [], ["msk"], "msk")
        b.dma("sp", t2[:], tau, [], ["t2"], "tau")
        b.dma("sp", sp_[:], s5p, [], ["sp_"], "sp_")
        b.dma("sp", h0s[:], h0, [], ["h0s"], "h0s")
        b.cp("dve", trib[:, 0, :], cst[:, 7, :], ["cst"], ["trib"])
        b.ts("dve", trib[:, 1, :], cst[:, 7, :], -1.0, ALU.mult, ["cst"], ["trib"], s2=1.0, op1=ALU.add)
        w3v = w3.rearrange("(kc p) n -> p kc n", p=128)
        for kc0 in range(KC):
            b.dma("sp", wst[:, 0, :], w3v[:, kc0, :], [], ["wst"], "wst")
            b.cp("dve" if kc0 % 2 else "pool", w3b[:, kc0, :], wst[:, 0, :], ["wst"], ["w3b"])
        for a in range(2):
            b.dma("sp", wst[:, 0, 0:512].rearrange("p (c n) -> p c n", n=128), Bbd[:, a], [], ["wst"], "wst")
            b.cp("dve", Bb[:, a], wst[:, 0, 0:512].rearrange("p (c n) -> p c n", n=128), ["wst"], ["Bb"])
        b.dma("sp", Cst[:], Cbd, [], ["Cst"], "Cst")
        c_ = lambda i: s5c[:, :, i]
        lre, lim = sp_[:, :, 0], sp_[:, :, 1]
        b.act(c_(13), sp_[:, :, 2], AF.Exp, ["sp_"], ["s5c"])
        dtc = c_(13)
        b.tt("dve", c_(8), lre, dtc, ALU.mult, ["sp_", "s5c"], ["s5c"])
        b.act(c_(0), c_(8), AF.Exp, ["s5c"], ["s5c"])
        b.tt("dve", c_(1), lim, dtc, ALU.mult, ["sp_", "s5c"], ["s5c"])
        b.cp("dve", bim[:], t2[:], ["t2"], ["bim"])
        for m in range(4):
            th = s5c[:, m, 1:2]
            b.ts("dve", ang[:], bim[:], th, ALU.mult, ["bim", "s5c"], ["gre"])
            b.ts("dve", kk[:], ang[:], 1.0 / TWO_PI, ALU.mult, ["gre"], ["gim"])
            b.cp("dve", ki[:], kk[:], ["gim"], ["ki"])
            b.cp("dve", kk[:], ki[:], ["ki"], ["gim"])
            b.stt(ang[:], kk[:], -CW1, ang[:], ALU.mult, ALU.add, ["gim", "gre"], ["gre"])
            b.stt(ang[:], kk[:], -CW2, ang[:], ALU.mult, ALU.add, ["gim", "gre"], ["gre"])
            for (dst, shift) in ((sr, 0.0), (cr, 0.5 * math.pi)):
                src = ang
                if shift != 0.0:
                    b.ts("dve", t1[:], ang[:], shift, ALU.add, ["gre"], ["t1"])
                    src = t1
                    sk = "t1"
                else:
                    b.cp("dve", t1[:], ang[:], ["gre"], ["t1"])
                    src, sk = t1, "t1"
                for _ in range(2):
                    b.ts("dve", t2[:], t1[:], math.pi, ALU.is_gt, ["t1"], ["t2"])
                    b.stt(t1[:], t2[:], -TWO_PI, t1[:], ALU.mult, ALU.add, ["t2", "t1"], ["t1"])
                    b.ts("dve", t2[:], t1[:], -math.pi, ALU.is_lt, ["t1"], ["t2"])
                    b.stt(t1[:], t2[:], TWO_PI, t1[:], ALU.mult, ALU.add, ["t2", "t1"], ["t1"])
                b.act(dst[:, m, :], t1[:], AF.Sin, ["t1"], ["cr" if dst is cr else "sr"])
        b.tt("dve", c_(2), c_(0), cr[:, :, 0], ALU.mult, ["s5c", "cr"], ["s5c"])
        b.tt("dve", c_(3), c_(0), sr[:, :, 0], ALU.mult, ["s5c", "sr"], ["s5c"])
        b.ts("dve", c_(2), c_(2), -1.0, ALU.add, ["s5c"], ["s5c"])
        b.tt("dve", c_(4), lre, lre, ALU.mult, ["sp_"], ["s5c"])
        b.tt("dve", c_(8), lim, lim, ALU.mult, ["sp_"], ["s5c"])
        b.tt("dve", c_(4), c_(4), c_(8), ALU.add, ["s5c"], ["s5c"])
        b.P.op("dve", lambda e, o=c_(12), i=c_(4): e.reciprocal(out=o, in_=i), reads=["s5c"], writes=["s5c"])
        b.tt("dve", c_(8), c_(2), lre, ALU.mult, ["s5c", "sp_"], ["s5c"])
        b.tt("dve", c_(9), c_(3), lim, ALU.mult, ["s5c", "sp_"], ["s5c"])
        b.tt("dve", c_(5), c_(8), c_(9), ALU.add, ["s5c"], ["s5c"])
        b.tt("dve", c_(5), c_(5), c_(12), ALU.mult, ["s5c"], ["s5c"])
        b.tt("dve", c_(8), c_(3), lre, ALU.mult, ["s5c", "sp_"], ["s5c"])
        b.tt("dve", c_(9), c_(2), lim, ALU.mult, ["s5c", "sp_"], ["s5c"])
        b.tt("dve", c_(6), c_(8), c_(9), ALU.subtract, ["s5c"], ["s5c"])
        b.tt("dve", c_(6), c_(6), c_(12), ALU.mult, ["s5c"], ["s5c"])
        b.ts("dve", c_(7), c_(6), -1.0, ALU.mult, ["s5c"], ["s5c"])
        b.tt("dve", c_(8), c_(5), c_(5), ALU.mult, ["s5c"], ["s5c"])
        b.tt("dve", c_(9), c_(6), c_(6), ALU.mult, ["s5c"], ["s5c"])
        b.tt("dve", c_(8), c_(8), c_(9), ALU.add, ["s5c"], ["s5c"])
        b.P.op("dve", lambda e, o=c_(9), i=c_(8): e.reciprocal(out=o, in_=i), reads=["s5c"], writes=["s5c"])
        b.tt("dve", c_(10), c_(5), c_(9), ALU.mult, ["s5c"], ["s5c"])
        b.tt("dve", c_(11), c_(7), c_(9), ALU.mult, ["s5c"], ["s5c"])
        for m in range(4):
            fre, fim, nfim = s5c[:, m, 5:6], s5c[:, m, 6:7], s5c[:, m, 7:8]
            b.ts("dve", t1[:, 0:128], Cst[:, 1, m, :], nfim, ALU.mult, ["Cst", "s5c"], ["t1"])
            b.stt(Cb[:, 0, m, :], Cst[:, 0, m, :], fre, t1[:, 0:128], ALU.mult, ALU.add, ["Cst", "s5c", "t1"], ["Cb"])
            b.ts("dve", t1[:, 0:128], Cst[:, 1, m, :], fre, ALU.mult, ["Cst", "s5c"], ["t1"])
            b.stt(t1[:, 0:128], Cst[:, 0, m, :], fim, t1[:, 0:128], ALU.mult, ALU.add, ["Cst", "s5c", "t1"], ["t1"])
            b.ts("dve", Cb[:, 1, m, :], t1[:, 0:128], -1.0, ALU.mult, ["t1"], ["Cb"])
            fr, fi = s5c[:, m, 10:11], s5c[:, m, 11:12]
            b.ts("dve", t2[:, 0:DB], h0s[:, 1, m, :], fi, ALU.mult, ["h0s", "s5c"], ["t2"])
            b.stt(t2[:, 16:16 + DB], h0s[:, 0, m, :], fr, t2[:, 0:DB], ALU.mult, ALU.subtract, ["h0s", "s5c", "t2"], ["t2"])
            b.ts("dve", t2[:, 0:DB], h0s[:, 1, m, :], fr, ALU.mult, ["h0s", "s5c"], ["t2"])
            b.stt(t2[:, 32:32 + DB], h0s[:, 0, m, :], fi, t2[:, 0:DB], ALU.mult, ALU.add, ["h0s", "s5c", "t2"], ["t2"])
            b.cp("dve", h0s[:, 0, m, :], t2[:, 16:16 + DB], ["t2"], ["h0s"])
            b.cp("dve", h0s[:, 1, m, :], t2[:, 32:32 + DB], ["t2"], ["h0s"])

        x1v = x1T.rearrange("(kc p) n -> p kc n", p=128)
        tiles = []
        for bi in range(NB):
            for t in range(ntiles_prompt // NB):
                tiles.append(dict(kind="p", seq=bi, ti=t, tok0=bi * SEQ + t * 512, N=512,
                                  first=(t == 0), last=(t == ntiles_prompt // NB - 1)))
        if do_sample:
            tiles.append(dict(kind="s", seq=None, ti=0, tok0=NTP, N=256, first=True, last=True))

        def load_x(i):
            T = tiles[i]
            s = i % 2
            for h in range(2):
                b.dma("sp", xb[s][:, h * 8:(h + 1) * 8, 0:T["N"]], x1v[:, h * 8:(h + 1) * 8, T["tok0"]:T["tok0"] + T["N"]],
                      [], ["xb%d" % s], "xb%d_%d" % (s, h))

        def load_cache(j):
            s = j % 3
            s2 = j % 2
            b.dma("sp", kcs[s2][:], kcT[j], [], ["kcs%d" % s2], "kcs%d" % s2)
            b.dma("sp", vcs[s2][:], vcc[j], [], ["vcs%d" % s2], "vcs%d" % s2)
            b.cp("pool", kcb[s][:], kcs[s2][:], ["kcs%d" % s2], ["kcb%d" % s])
            b.cp("pool", vcb[s][:], vcs[s2][:], ["vcs%d" % s2], ["vcb%d" % s])

        load_x(0)
        try:
            for i, T in enumerate(tiles):
                s = i % 2
                N = T["N"]
                samp = T["kind"] == "s"
                xk = "xb%d" % s
                if i + 1 < len(tiles):
                    load_x(i + 1)
                for f in range(6):
                    bank, bk = pj[f % 2], "pj%d" % (f % 2)
                    for kc in range(KC):
                        b.mm(bank[:, 0:N], w3b[:, kc, f * 128:(f + 1) * 128], xb[s][:, kc, 0:N], ["w3b", xk], [bk],
                             start=(kc == 0), stop=(kc == KC - 1))
                    if f == 0:
                        b.act(qb[:, 0:N], bank[:, 0:N], AF.Copy, [bk], ["qb"], scale=128.0 ** -0.5)
                    elif f == 1:
                        b.cp("dve", kT32[:, 0:N], bank[:, 0:N], [bk], ["kT32"])
                        if not samp:
                            b.cp("act", KT[:, T["ti"] * 512:T["ti"] * 512 + N], bank[:, 0:N], [bk], ["KT"])
                        else:
                            b.cp("act", KT[:, 0:N], bank[:, 0:N], [bk], ["KT"])
                    elif f == 2:
                        b.cp("dve", vT32[:, 0:N], bank[:, 0:N], [bk], ["vT32"])
                    elif f == 3:
                        b.act(zc[:, 0:N], bank[:, 0:N], AF.Silu, [bk], ["zc"])
                    elif f == 4:
                        b.cp("dve", u32[:, 0:N], bank[:, 0:N], [bk], ["u32"])
                        b.cp("act", ub[:, 0:N], bank[:, 0:N], [bk], ["ub"])
                    else:
                        b.act(szd[:, 0:N], bank[:, 0:N], AF.Silu, [bk], ["szd"])
                        b.dma("sp", y3T[256:384, T["tok0"]:T["tok0"] + N], szd[:, 0:N], ["szd"], ["y3T"], "szd_o")
                if debug == 1:
                    raise _Stop
                nblk = N // 128
                for (src, sk, dst, dk, outap) in ((kT32, "kT32", ktok, "ktok", kout), (vT32, "vT32", vtok, "vtok", vout)):
                    for jb in range(nblk):
                        b.tr(pT[:, jb * 128:(jb + 1) * 128], src[:, jb * 128:(jb + 1) * 128], ident, [sk, "cst"], [TK])
                    b.cp("dve", dst[:, 0:nblk, :].rearrange("p a d -> p (a d)"), pT[:, 0:nblk * 128], [TK], [dk])
                    b.dma("sp", outap[T["tok0"]:T["tok0"] + N, :].rearrange("(a p) d -> p a d", p=128), dst[:, 0:nblk, :],
                          [dk], ["o_" + dk], "o_" + dk)
                if not samp:
                    b.cp("pool", Vt[:, T["ti"] * 4:T["ti"] * 4 + 4, :], vtok[:], ["vtok"], ["Vt"])
                else:
                    for h in range(2):
                        for q4 in range(4):
                            bi = h * 4 + q4
                            b.tr(pT[0:32, q4 * 128:(q4 + 1) * 128], vT32[:, bi * 32:(bi + 1) * 32], ident,
                                 ["vT32", "cst"], [TK])
                        b.cp("dve", vnew[0:32, h * 4:(h + 1) * 4, :].rearrange("p a d -> p (a d)"), pT[0:32, 0:512],
                             [TK], ["vnew"])
                if debug == 2:
                    raise _Stop
                def g_att():
                    if not samp:
                        jmax = T['ti'] * 4 + 3
                        steps = list(range(jmax, -1, -1))
                    else:
                        steps = list(range(32, -1, -1))
                        load_cache(31)
                        yield

                    def stageA(si):
                        j = steps[si]
                        z = si % 2
                        e3 = si % 3
                        Zb, zk = (pZ[z], ZK[z])
                        if not samp:
                            Kp = 128
                            b.mm(Zb[:, 0:N], KT[:, j * 128:(j + 1) * 128], qb[:, 0:N], ['KT', 'qb'], [zk])
                        elif j == 32:
                            Kp = 32
                            for bi in range(DB):
                                b.mm(Zb[0:32, bi * 32:(bi + 1) * 32], KT[:, bi * 32:(bi + 1) * 32], qb[:, bi * 32:(bi + 1) * 32], ['KT', 'qb'], [zk])
                        else:
                            Kp = 128
                            if j - 1 >= 0:
                                load_cache(j - 1)
                            for bi in range(DB):
                                b.mm(Zb[:, bi * 32:(bi + 1) * 32], kcb[j % 3][:, bi, :], qb[:, bi * 32:(bi + 1) * 32], ['kcb%d' % (j % 3), 'qb'], [zk])
                        b.act(E[e3][0:Kp, 0:N], Zb[0:Kp, 0:N], AF.Exp, [zk], ['E%d' % e3])
                        b.act(SP[e3][0:Kp, 0:N], E[e3][0:Kp, 0:N], AF.Ln, ['E%d' % e3], ['SP%d' % e3], bias=1.0)
                        mk = None
                        if samp and j == 32:
                            mk = msk[0:32, 4, 0:N]
                        elif not samp and j >= T['ti'] * 4:
                            mk = msk[:, j - T['ti'] * 4, 0:N]
                        if mk is not None:
                            b.tt('pool', E[e3][0:Kp, 0:N], E[e3][0:Kp, 0:N], mk, ALU.mult, ['E%d' % e3, 'msk'], ['E%d' % e3])
                            b.tt('pool', SP[e3][0:Kp, 0:N], SP[e3][0:Kp, 0:N], mk, ALU.mult, ['SP%d' % e3, 'msk'], ['SP%d' % e3])
                        return Kp

                    def emitAV(si, Kp):
                        j = steps[si]
                        z = si % 2
                        if not samp:
                            b.mm(pO[:, 0:N], Vt[:, j, :], ATT[z][:, 0:N], ['Vt', 'ATT%d' % z], [OK], start=False, stop=True)
                        elif j == 32:
                            for bi in range(DB):
                                b.mm(pO[:, bi * 32:(bi + 1) * 32], vnew[0:32, bi, :], ATT[z][0:32, bi * 32:(bi + 1) * 32], ['vnew', 'ATT%d' % z], [OK], start=False, stop=True)
                        else:
                            for bi in range(DB):
                                b.mm(pO[:, bi * 32:(bi + 1) * 32], vcb[j % 3][:, bi, :], ATT[z][:, bi * 32:(bi + 1) * 32], ['vcb%d' % (j % 3), 'ATT%d' % z], [OK], start=False, stop=True)
                    b.memset('dve', pCm[:, 0:N], 0.0, [CK])
                    yield
                    b.memset('dve', pO[:, 0:N], 0.0, [OK])
                    yield
                    LA = 1 if samp else 2
                    nst = len(steps)
                    KpA = {}
                    for si0 in range(min(LA, nst)):
                        KpA[si0] = stageA(si0)
                        yield
                    for si, j in enumerate(steps):
                        z = si % 2
                        e3 = si % 3
                        Kp = KpA[si]
                        b.mm(pCm[:, 0:N], trib[0:Kp, 0, :], SP[e3][0:Kp, 0:N], ['trib', 'SP%d' % e3], [CK], start=False, stop=True)
                        yield
                        b.act(Wt[0:Kp, 0:N], pCm[0:Kp, 0:N], AF.Exp, [CK], ['Wt'], scale=-1.0)
                        yield
                        b.tt('dve', ATT[z][0:Kp, 0:N], E[e3][0:Kp, 0:N], Wt[0:Kp, 0:N], ALU.mult, ['E%d' % e3, 'Wt'], ['ATT%d' % z])
                        yield
                        if si >= 1:
                            emitAV(si - 1, KpA[si - 1])
                            yield
                        if si + LA < nst:
                            KpA[si + LA] = stageA(si + LA)
                            yield
                        b.mm(pCm[:, 0:N], trib[0:Kp, 1, :], SP[e3][0:Kp, 0:N], ['trib', 'SP%d' % e3], [CK], start=False, stop=True)
                        yield
                    emitAV(nst - 1, KpA[nst - 1])
                    yield
                    b.tt('dve', ycb[:, 0:N], pO[:, 0:N], zc[:, 0:N], ALU.mult, [OK, 'zc'], ['ycb'])
                    yield
                    b.dma('sp', y3T[0:128, T['tok0']:T['tok0'] + N], ycb[:, 0:N], ['ycb'], ['y3T'], 'yc_o')
                    yield
                    yield
                def g_s5():
                    nseq = DB if samp else 1
                    Ls = N // nseq
                    v3 = lambda ap: ap.rearrange('p (s l) -> p s l', l=Ls)

                    def tab(tb, m):
                        if samp:
                            return tb[:, m, 0:Ls].unsqueeze(1).broadcast_to([128, nseq, Ls])
                        return v3(tb[:, m, 0:N])
                    for m in range(4):
                        b.mm(pj[0][:, 0:N], Bb[:, 0, m, :], ub[:, 0:N], ['Bb', 'ub'], ['pj0'])
                        yield
                        b.mm(pj[1][:, 0:N], Bb[:, 1, m, :], ub[:, 0:N], ['Bb', 'ub'], ['pj1'])
                        yield
                        Bre, Bim = (v3(pj[0][:, 0:N]), v3(pj[1][:, 0:N]))
                        b.tt('dve', v3(t1[:, 0:N]), Bre, tab(cr, m), ALU.mult, ['pj0', 'cr'], ['t1'])
                        yield
                        b.tt('dve', v3(t2[:, 0:N]), Bim, tab(sr, m), ALU.mult, ['pj1', 'sr'], ['t2'])
                        yield
                        b.tt('pool', bre[:, 0:N], t1[:, 0:N], t2[:, 0:N], ALU.add, ['t1', 't2'], ['bre'])
                        yield
                        b.tt('dve', v3(t1[:, 0:N]), Bim, tab(cr, m), ALU.mult, ['pj1', 'cr'], ['t1'])
                        yield
                        b.tt('dve', v3(t2[:, 0:N]), Bre, tab(sr, m), ALU.mult, ['pj0', 'sr'], ['t2'])
                        yield
                        b.tt('pool', bim[:, 0:N], t1[:, 0:N], t2[:, 0:N], ALU.subtract, ['t1', 't2'], ['bim'])
                        yield
                        rbc = s5c[:, m, 0:1].to_broadcast([128, Ls])
                        for sq in range(nseq):
                            col = sq if samp else 0
                            for src, sk, dst, dk, a in ((bre, 'bre', gre, 'gre', 0), (bim, 'bim', gim, 'gim', 1)):
                                if samp:
                                    init = h0s[:, a, m, sq:sq + 1]
                                    ik = 'h0s'
                                elif T['first']:
                                    init, ik = (0.0, None)
                                else:
                                    init, ik = (hst[:, a, m, 0:1], 'hst')
                                b.P.op('dve', lambda e, o=dst[:, sq * Ls:(sq + 1) * Ls], d0=rbc, d1=src[:, sq * Ls:(sq + 1) * Ls], ini=init: e.tensor_tensor_scan(out=o, data0=d0, data1=d1, initial=ini, op0=ALU.mult, op1=ALU.add), reads=[sk, 's5c'] + ([ik] if ik else []), writes=[dk])
                                yield
                        b.tt('dve', v3(t1[:, 0:N]), v3(gre[:, 0:N]), tab(cr, m), ALU.mult, ['gre', 'cr'], ['t1'])
                        yield
                        b.tt('dve', v3(t2[:, 0:N]), v3(gim[:, 0:N]), tab(sr, m), ALU.mult, ['gim', 'sr'], ['t2'])
                        yield
                        b.tt('pool', hreb[:, 0:N], t1[:, 0:N], t2[:, 0:N], ALU.subtract, ['t1', 't2'], ['hreb'])
                        yield
                        lastv = lambda ap: v3(ap[:, 0:N])[:, :, Ls - 1]
                        if samp:
                            cols = slice(NB, NB + DB)
                        else:
                            cols = slice(T['seq'], T['seq'] + 1)
                        b.tt('dve', hst[:, 0, m, 0:nseq], lastv(t1), lastv(t2), ALU.subtract, ['t1', 't2'], ['hst'])
                        yield
                        b.tt('dve', v3(t1[:, 0:N]), v3(gre[:, 0:N]), tab(sr, m), ALU.mult, ['gre', 'sr'], ['t1'])
                        yield
                        b.tt('dve', v3(t2[:, 0:N]), v3(gim[:, 0:N]), tab(cr, m), ALU.mult, ['gim', 'cr'], ['t2'])
                        yield
                        b.tt('pool', himb[:, 0:N], t1[:, 0:N], t2[:, 0:N], ALU.add, ['t1', 't2'], ['himb'])
                        yield
                        b.tt('dve', hst[:, 1, m, 0:nseq], lastv(t1), lastv(t2), ALU.add, ['t1', 't2'], ['hst'])
                        yield
                        if T['last']:
                            fre, fim, nfim = (s5c[:, m, 5:6], s5c[:, m, 6:7], s5c[:, m, 7:8])
                            b.ts('dve', t1[:, 0:nseq], hst[:, 1, m, 0:nseq], nfim, ALU.mult, ['hst', 's5c'], ['t1'])
                            yield
                            b.stt(hfin[:, 0, m, cols], hst[:, 0, m, 0:nseq], fre, t1[:, 0:nseq], ALU.mult, ALU.add, ['hst', 's5c', 't1'], ['hfin'])
                            yield
                            b.ts('dve', t1[:, 0:nseq], hst[:, 1, m, 0:nseq], fre, ALU.mult, ['hst', 's5c'], ['t1'])
                            yield
                            b.stt(hfin[:, 1, m, cols], hst[:, 0, m, 0:nseq], fim, t1[:, 0:nseq], ALU.mult, ALU.add, ['hst', 's5c', 't1'], ['hfin'])
                            yield
                        b.mm(pY[:, 0:N], Cb[:, 0, m, :], hreb[:, 0:N], ['Cb', 'hreb'], [YK], start=m == 0, stop=False)
                        yield
                        b.mm(pY[:, 0:N], Cb[:, 1, m, :], himb[:, 0:N], ['Cb', 'himb'], [YK], start=False, stop=m == 3)
                        yield
                    b.stt(ys32[:, 0:N], u32[:, 0:N], sp_[:, 0, 3:4], pY[:, 0:N], ALU.mult, ALU.add, ['u32', 'sp_', YK], ['ys32'])
                    yield
                    b.tt('pool', t1[:, 0:N], ys32[:, 0:N], ys32[:, 0:N], ALU.mult, ['ys32'], ['t1'])
                    yield
                    b.ts('dve', t1[:, 0:N], t1[:, 0:N], 0.044715, ALU.mult, ['t1'], ['t1'], s2=1.0, op1=ALU.add)
                    yield
                    b.tt('pool', t1[:, 0:N], t1[:, 0:N], ys32[:, 0:N], ALU.mult, ['t1', 'ys32'], ['t1'])
                    yield
                    b.act(t1[:, 0:N], t1[:, 0:N], AF.Sigmoid, ['t1'], ['t1'], scale=2.0 * math.sqrt(2.0 / math.pi))
                    yield
                    b.tt('dve', ysb[:, 0:N], t1[:, 0:N], ys32[:, 0:N], ALU.mult, ['t1', 'ys32'], ['ysb'])
                    yield
                    b.dma('sp', y3T[128:256, T['tok0']:T['tok0'] + N], ysb[:, 0:N], ['ysb'], ['y3T'], 'ys_o')
                    yield
                    yield
                _interleave(g_att(), g_s5())
        except _Stop:
            pass
        b.dma("sp", hout, hfin[:], ["hfin"], ["hout"], "hout")
        P.finalize()
        print("phase3 instrs", P.ninstr, {e: P.cnt[e] for e in ENGS})
    return nc


def _p3_inputs(inp, x1T, c):
    A = np.ascontiguousarray
    W = inp['od_w_in'][0]
    blk = lambda i: W[:, i * 1024 + c * 128: i * 1024 + (c + 1) * 128]
    w3 = np.concatenate([blk(0), blk(1), blk(2), blk(3), blk(4), blk(5)], 1)
    kc = inp['cache_sb_k'][0, :, :, c, :]
    vc = inp['cache_sb_v'][0, :, :, c, :]
    kcT = A(kc.reshape(DB, 32, 128, 128).transpose(1, 3, 0, 2))
    vcc = A(vc.reshape(DB, 32, 128, 128).transpose(1, 2, 0, 3))
    G = slice(8 * c, 8 * c + 8)
    lre = inp['od_lam_re'][0, G]
    lim = inp['od_lam_im'][0, G]
    dt = np.broadcast_to(inp['od_log_dt'][0, G][:, None], (8, 64))
    s5p = np.zeros((128, 4, 4), np.float32)
    qm = lambda a: a.reshape(4, 128).T
    s5p[:, :, 0] = qm(lre)
    s5p[:, :, 1] = qm(lim)
    s5p[:, :, 2] = qm(dt)
    s5p[:, 0, 3] = inp['od_d'][0, c * 128:(c + 1) * 128]
    Bbd = np.zeros((128, 2, 4, 128), np.float32)
    Cbd = np.zeros((128, 2, 4, 128), np.float32)
    for a, (bk, ck) in enumerate((('od_b_re', 'od_c_re'), ('od_b_im', 'od_c_im'))):
        bb = inp[bk][0, G]
        cc = inp[ck][0, G]
        for g in range(8):
            m, g2 = g // 2, g % 2
            Bbd[g * 16:(g + 1) * 16, a, m, g2 * 64:(g2 + 1) * 64] = bb[g].T
            Cbd[g2 * 64:(g2 + 1) * 64, a, m, g * 16:(g + 1) * 16] = cc[g].T
    h0 = np.zeros((128, 2, 4, DB), np.float32)
    for a, k in enumerate(('state_s5_re', 'state_s5_im')):
        hs = inp[k][0, :, G]
        h0[:, a] = hs.reshape(DB, 4, 128).transpose(2, 1, 0)
    return dict(x1T=x1T, w3=A(w3), consts=_consts_np(), masks=np.ascontiguousarray(_masks3_np()[:, 0:5]).astype(ml_dtypes.bfloat16), tau=np.ascontiguousarray(_masks3_np()[:, 5]), kcT=kcT, vcc=vcc, s5p=s5p,
                Bbd=Bbd, Cbd=Cbd, h0=h0)


def _p4_inputs(inp, x1tok, Y3, c):
    rep = lambda v: np.ascontiguousarray(np.broadcast_to(v[None, :], (128, v.shape[0])))
    return dict(yin=_tile_tokens(Y3, c), wout=np.ascontiguousarray(inp['od_w_out'][0]), xres=_pad_rows(x1tok, c),
                lng=rep(inp['od_ln_g'][0]), lnb=rep(inp['od_ln_b'][0]), consts=_consts_np(),
                wglu=np.ascontiguousarray(inp['od_w_glu'][0]),
                bglu=np.ascontiguousarray(inp['od_b_glu'][0].reshape(8, 128).T))


_NC_CACHE = {}


def _get_nc(name, fn):
    if name not in _NC_CACHE:
        _NC_CACHE[name] = fn()
    return _NC_CACHE[name]


def kernel(**inp):
    inp = {k: np.asarray(v) for k, v in inp.items()}
    R = range(NCORES)
    xtok = _tok_major(inp['x_prompt'], inp['x_sample'])
    xT = np.ascontiguousarray(xtok.T)
    r1 = _run(_get_nc("p1", build_phase1), [_p1_inputs(inp, xT, c) for c in R])
    del xT
    Y = np.stack([r1[k]['yT'][0:128] for k in R] + [r1[k]['yT'][128:256] for k in R])
    r2 = _run(_get_nc("p2", lambda: build_out_phase(False, True)), [_p2_inputs(inp, xtok, Y, c) for c in R])
    x1tok = np.concatenate([r2[c]['xo'][:TOKC] for c in R], 0)
    x1T = np.ascontiguousarray(
        np.concatenate([r2[c]['xoT'].transpose(2, 1, 0, 3).reshape(D, -1)[:, :TOKC] for c in R], 1))
    r3 = _run(_get_nc("p3", build_phase3), [_p3_inputs(inp, x1T, c) for c in R])
    Y3 = np.stack([r3[k]['y3T'][0:128] for k in R] + [r3[k]['y3T'][128:256] for k in R]
                  + [r3[k]['y3T'][256:384] for k in R])
    r4 = _run(_get_nc("p4", lambda: build_out_phase(True, False)), [_p4_inputs(inp, x1tok, Y3, c) for c in R])
    ytok = np.concatenate([r4[c]['xo'][:TOKC] for c in R], 0)
    f32 = np.float32
    y_prompt = ytok[:NTP].reshape(NB, SEQ, D).astype(f32)
    y_sample = ytok[NTP:].reshape(DB, DS, D).astype(f32)

    def heads(key, sl):
        return np.stack([r1[c][key][sl] for c in R], 1)[None].astype(f32)
    pS, sS = slice(0, NB), slice(NB, NB + DB)
    dS_p, dS_s = heads('oS', pS), heads('oS', sS)
    mC_p, mC_s = heads('oC', pS), heads('oC', sS)

    def convout(sl):
        n = sl.stop - sl.start
        o = np.zeros((1, n, 3, 3072), f32)
        for c in R:
            oc = r1[c]['oconv'][:, sl]
            for f in range(3):
                o[0, :, :, f * 1024 + c * 128: f * 1024 + (c + 1) * 128] = oc[:, :, f, :].transpose(1, 2, 0)
        return o
    dc_p, dc_s = convout(pS), convout(sS)
    mn = lambda sl: np.stack([r1[c]['on'][:, sl].T for c in R], 1)[None].astype(f32)
    mm = lambda sl: np.stack([r1[c]['om'][0, sl] for c in R], 1)[None].astype(f32)

    def kv(key, lo, hi, shape):
        return np.stack([r3[c][key][lo:hi] for c in R], 1).reshape(shape).astype(f32)
    sk_p = kv('kout', 0, NTP, (NB, SEQ, 8, 128))[None]
    sk_s = kv('kout', NTP, NTOK, (DB, DS, 8, 128))[None]
    sv_p = kv('vout', 0, NTP, (NB, SEQ, 8, 128))[None]
    sv_s = kv('vout', NTP, NTOK, (DB, DS, 8, 128))[None]

    def s5state(a, sl):
        n = sl.stop - sl.start
        o = np.zeros((1, n, 64, 64), f32)
        for c in R:
            ho = r3[c]['hout'][:, a, :, sl]
            o[0, :, 8 * c:8 * c + 8, :] = ho.transpose(2, 1, 0).reshape(n, 8, 64)
        return o
    return (y_prompt, y_sample, dS_p, dS_s, dc_p, dc_s, mC_p, mC_s, mn(pS), mn(sS), mm(pS), mm(sS),
            sk_p, sk_s, sv_p, sv_s, s5state(0, pS), s5state(0, sS), s5state(1, pS), s5state(1, sS))
```

```python
import math
from contextlib import ExitStack

import numpy as np
import ml_dtypes
import concourse.bass as bass
import concourse.mybir as mybir
from concourse.bass_utils import run_bass_kernel_spmd

F32 = mybir.dt.float32
BF16 = mybir.dt.bfloat16
AF = mybir.ActivationFunctionType
ALU = mybir.AluOpType
AX = mybir.AxisListType
ENGS = ("pe", "act", "dve", "pool", "sp")
NCORES = 8

D = 2048
KC = D // 128
SEQ = 8192
NB = 2
DB = 8
DS = 32
PAST = 4096
NTP = NB * SEQ
NTS = DB * DS
NTOK = NTP + NTS
TOKC = NTOK // NCORES
ALPHA = (2 * 2) ** 0.25
NEG = -30000.0
SAME_ENG_GAP = 10 ** 9
F32R = mybir.dt.float32r


class Prog:
    def __init__(self, nc, stack):
        self.nc = nc
        self.stack = stack
        self.q = {e: [] for e in ENGS}
        self.cnt = {e: 0 for e in ENGS}
        self.sem = {e: stack.enter_context(nc.semaphore("sem_" + e)) for e in ENGS}
        self.waited = {e: {} for e in ENGS}
        self.lastw = {}
        self.readers = {}
        self.dsem = {}
        self.dcnt = {}
        self.ninstr = 0

    def _deps(self, reads, writes):
        deps = []
        for k in reads:
            t = self.lastw.get(k)
            if t is not None:
                deps.append(t)
        for k in writes:
            t = self.lastw.get(k)
            if t is not None:
                deps.append(t)
            deps.extend(self.readers.get(k, ()))
        return deps

    def _emit_waits(self, eng, deps, skip_self=False):
        need = {}
        for (sid, sem, val) in deps:
            if skip_self and sid == eng:
                continue
            if sid == eng and self.cnt[eng] - val >= SAME_ENG_GAP:
                continue
            if self.waited[eng].get(sid, 0) >= val:
                continue
            if need.get(sid, (None, 0))[1] < val:
                need[sid] = (sem, val)
        for sid, (sem, val) in need.items():
            self.waited[eng][sid] = val
            self.q[eng].append(lambda e, sem=sem, val=val: e.wait_ge(sem, val))
            self.ninstr += 1

    def _commit(self, tok, reads, writes):
        for k in writes:
            self.lastw[k] = tok
            self.readers[k] = []
        for k in reads:
            self.readers.setdefault(k, []).append(tok)

    @staticmethod
    def _bank(k):
        if isinstance(k, str) and len(k) >= 2 and k[0] == "p" and (k[1].isupper() or k[:2] in ("pj", "pg")):
            return k[:3] if k[:2] == "pj" else k[:2]
        return None

    def _norm(self, reads, writes):
        r2, w2 = [], []
        for k in reads:
            bk = self._bank(k)
            if bk is None:
                r2.append(k)
            elif bk not in w2:
                w2.append(bk)
        for k in writes:
            bk = self._bank(k)
            k2 = k if bk is None else bk
            if k2 not in w2:
                w2.append(k2)
        return r2, w2

    def op(self, eng, fn, reads=(), writes=(), skip_self=False):
        reads, writes = self._norm(reads, writes)
        deps = self._deps(reads, writes)
        self._emit_waits(eng, deps, skip_self=skip_self)
        self.cnt[eng] += 1
        sem = self.sem[eng]
        tok = (eng, sem, self.cnt[eng])
        self.q[eng].append(lambda e, fn=fn, sem=sem: fn(e).then_inc(sem, 1))
        self.ninstr += 1
        self._commit(tok, reads, writes)
        return tok

    def dma(self, eng, out, in_, reads=(), writes=(), slot=None, **kw):
        assert slot is not None
        if slot not in self.dsem:
            self.dsem[slot] = self.stack.enter_context(self.nc.semaphore("dsem_%d" % len(self.dsem)))
            self.dcnt[slot] = 0
        deps = self._deps(reads, writes)
        self._emit_waits(eng, deps)
        self.dcnt[slot] += 16
        sem = self.dsem[slot]
        tok = (("d", slot), sem, self.dcnt[slot])
        self.q[eng].append(
            lambda e, out=out, in_=in_, sem=sem, kw=kw: e.dma_start(out=out, in_=in_, **kw).then_inc(sem, 16))
        self.ninstr += 1
        self._commit(tok, reads, writes)
        return tok

    def finalize(self):
        nc = self.nc
        alltok = list(self.lastw.values())
        for ts in self.readers.values():
            alltok.extend(ts)
        self._emit_waits("sp", alltok)
        engmap = {"pe": "tensor", "act": "scalar", "dve": "vector", "pool": "gpsimd", "sp": "sync"}
        with nc.Block() as block:
            for e in ENGS:
                lst = self.q[e]
                if not lst:
                    continue

                def body(engine, lst=lst):
                    for f in lst:
                        f(engine)
                getattr(block, engmap[e])(body)


class B:
    def __init__(self, nc, st):
        self.nc = nc
        self.st = st
        self.P = Prog(nc, st)
        self.rr = 0

    def sb(self, name, shape, dt=F32):
        return self.st.enter_context(self.nc.sbuf_tensor(name, list(shape), dt))

    def ps(self, name, shape=(128, 512), dt=F32):
        return self.st.enter_context(self.nc.psum_tensor(name, list(shape), dt))

    def din(self, name, shape, dt=F32):
        return self.nc.dram_tensor(name, list(shape), dt, kind="ExternalInput").ap()

    def dout(self, name, shape, dt=F32):
        return self.nc.dram_tensor(name, list(shape), dt, kind="ExternalOutput").ap()

    def mm(self, out, lhsT, rhs, r, w, start=True, stop=True):
        self.P.op("pe", lambda e: e.matmul(out, lhsT=lhsT, rhs=rhs, start=start, stop=stop, skip_group_check=True),
                  reads=r, writes=w, skip_self=True)

    def mmr(self, out, lhsT, rhs, r, w, start=True, stop=True):
        self.mm(out, lhsT.bitcast(F32R), rhs.bitcast(F32R), r, w, start=start, stop=stop)

    def tr(self, out, in_, ident, r, w):
        self.P.op("pe", lambda e: e.transpose(out=out, in_=in_, identity=ident), reads=r, writes=w, skip_self=True)

    def act(self, out, in_, func, r, w, bias=None, scale=None, accum_out=None):
        kw = {}
        if bias is not None:
            kw["bias"] = bias
        if scale is not None:
            kw["scale"] = scale
        if accum_out is not None:
            kw["accum_out"] = accum_out
        self.P.op("act", lambda e: e.activation(out=out, in_=in_, func=func, **kw), reads=r, writes=w)

    def ts(self, eng, out, in0, s1, op0, r, w, s2=None, op1=None, accum_out=None):
        kw = {}
        if op1 is not None:
            kw["op1"] = op1
        if accum_out is not None:
            kw["accum_out"] = accum_out
        self.P.op(eng, lambda e: e.tensor_scalar(out=out, in0=in0, scalar1=s1, scalar2=s2, op0=op0, **kw),
                  reads=r, writes=w)

    def tt(self, eng, out, in0, in1, op, r, w):
        self.P.op(eng, lambda e: e.tensor_tensor(out=out, in0=in0, in1=in1, op=op), reads=r, writes=w)

    def stt(self, out, in0, scalar, in1, op0, op1, r, w, accum_out=None):
        kw = {}
        if accum_out is not None:
            kw["accum_out"] = accum_out
        self.P.op("dve", lambda e: e.scalar_tensor_tensor(out=out, in0=in0, scalar=scalar, in1=in1, op0=op0, op1=op1, **kw),
                  reads=r, writes=w)

    def cp(self, eng, out, in_, r, w):
        if eng == "act":
            self.P.op("act", lambda e: e.copy(out=out, in_=in_), reads=r, writes=w)
        else:
            self.P.op(eng, lambda e: e.tensor_copy(out=out, in_=in_), reads=r, writes=w)

    def memset(self, eng, ap, val, w):
        self.P.op(eng, lambda e: e.memset(ap, val), writes=w)

    def dma(self, eng, out, in_, r, w, slot, **kw):
        self.P.dma(eng, out, in_, reads=r, writes=w, slot=slot, **kw)


def _consts_np():
    c = np.zeros((128, 8, 128), np.float32)
    i = np.arange(128)
    c[:, 0, :] = np.eye(128)
    c[:, 1, :] = (i[:, None] <= i[None, :])
    c[:, 2, :] = np.where(i[None, :] < i[:, None], NEG, 0.0)
    c[:, 3, :] = np.where(i[None, :] > i[:, None], NEG, 0.0)
    c[:, 4, :] = 1.0
    c[:, 5, :] = (i[:, None] == 127)
    c[:, 6, :] = (i[:, None] == 31)
    c[:, 7, :] = (i[:, None] >= i[None, :])
    return c


class _Stop(Exception):
    pass


def _interleave(*gens):
    gens = list(gens)
    while gens:
        for g in list(gens):
            try:
                next(g)
            except StopIteration:
                gens.remove(g)


def build_phase1(ntiles_prompt=32, do_sample=True, debug=0):
    nc = bass.Bass("TRN2", target_bir_lowering=False)
    with ExitStack() as st:
        b = B(nc, st)
        P = b.P
        xT = b.din("xT", [D, NTOK])
        w1 = b.din("w1", [D, 1152])
        wg = b.din("wg", [128, KC * 4])
        convw = b.din("convw", [128, 12])
        scal = b.din("scal", [128, 8])
        consts = b.din("consts", [128, 8, 128])
        S0 = b.din("S0", [DB, 128, 128])
        conv0 = b.din("conv0", [128, DB, 3, 3])
        C0 = b.din("C0", [DB, 128, 128])
        n0 = b.din("n0", [128, DB])
        m0 = b.din("m0", [128, DB])
        yT = b.dout("yT", [256, NTOK], BF16)
        oS = b.dout("oS", [NB + DB, 128, 128])
        oconv = b.dout("oconv", [128, NB + DB, 3, 3])
        oC = b.dout("oC", [NB + DB, 128, 128])
        on = b.dout("on", [128, NB + DB])
        om = b.dout("om", [1, NB + DB])

        cst = b.sb("cst", [128, 8, 128])
        ident, triU, NEGU, NEGL, ones, SEL127, SEL31 = (cst[:, i, :] for i in range(7))
        w1b = b.sb("w1b", [128, KC, 1152], BF16)
        wgb = b.sb("wgb", [128, KC, 4], BF16)
        wst = b.sb("wst", [128, 2, 1152])
        cw = b.sb("cw", [128, 12])
        sc = b.sb("sc", [128, 8])
        dsc = b.sb("dsc", [128, 8])
        xst = [b.sb("xst%d" % i, [128, 4, 512]) for i in range(2)]
        xb = [b.sb("xb%d" % i, [128, KC, 512], BF16) for i in range(2)]
        cin = [b.sb("cin%d" % f, [128, 520]) for f in range(3)]
        PR = [[(b.sb("pr%d_%d" % (f, q), [128, 512]) if f >= 3 else None) for f in range(9)] for q in range(2)]
        CV = [[b.sb("cv%d_%d" % (f, q), [128, 512]) for f in range(3)] for q in range(2)]
        tmpA = b.sb("tmpA", [128, 512])
        tmpB = b.sb("tmpB", [128, 512])
        QN = [b.sb("qn_%d" % q, [128, 512]) for q in range(2)]
        KN = [b.sb("kn_%d" % q, [128, 512]) for q in range(2)]
        tmpC = b.sb("tmpC", [128, 512])
        tmpD = b.sb("tmpD", [128, 512])
        oT = b.sb("oT", [128, 512])
        hT = b.sb("hT", [128, 512])
        yb = b.sb("yb", [128, 2, 512], BF16)
        GAT = [b.sb("gat_%d" % q, [128, 8, 12]) for q in range(2)]
        GB = []
        for k in range(2):
            GB.append({n: b.sb("%s_%d" % (n, k), [128, 128]) for n in
                       ("Gbc", "grow", "DEC", "DECs", "X", "XT", "IXT", "Pm", "QKd", "kg", "kdec", "vtok", "nW", "qdec")})
            GB[k]["colg"] = b.sb("colg_%d" % k, [128, 4])
        ee = b.sb("ee", [128, 128])
        Sst = b.sb("Sst", [128, 128])
        col = b.sb("col", [128, 16])
        mprev = b.sb("mprev", [128, 1])
        bc2 = b.sb("bc2", [128, 2])
        c2 = b.sb("c2", [128, 2])
        Abc = b.sb("Abc", [128, 128])
        pmat = b.sb("pmat", [128, 128])
        pqk = b.sb("pqk", [128, 128])
        pqkT = b.sb("pqkT", [128, 128])
        v1 = b.sb("v1", [128, 129])
        kp = b.sb("kp", [128, 128])
        CN = b.sb("CN", [128, 129])
        n2s = b.sb("n2s", [128, 129])
        num = b.sb("num", [128, 128])
        htok = b.sb("htok", [128, 128])
        m0s = b.sb("m0s", [128, DB])
        n0s = b.sb("n0s", [128, DB])
        c0s = b.sb("c0s", [128, DB * 9])

        pj = [b.ps("pj%d" % i) for i in range(2)]
        pg = b.ps("pg")
        pA = b.ps("pA")
        pB = b.ps("pB")
        pC = b.ps("pC")
        pD = b.ps("pD")
        pE = b.ps("pE")

        b.dma("sp", cst[:], consts, [], ["cst"], "cst")
        b.dma("sp", cw[:], convw, [], ["cw"], "cw")
        b.dma("sp", sc[:], scal, [], ["sc"], "sc")
        b.dma("sp", m0s[:], m0, [], ["m0s"], "m0s")
        b.dma("sp", n0s[:], n0, [], ["n0s"], "n0s")
        b.dma("sp", c0s[:], conv0.rearrange("p b f r -> p (b f r)"), [], ["c0s"], "c0s")
        w1v = w1.rearrange("(kc p) n -> p kc n", p=128)
        for kc0 in range(0, KC, 2):
            b.dma("sp", wst[:, :, :], w1v[:, kc0:kc0 + 2, :], [], ["wst"], "wst")
            b.cp("dve", w1b[:, kc0, :], wst[:, 0, :], ["wst"], ["w1b"])
            b.cp("pool", w1b[:, kc0 + 1, :], wst[:, 1, :], ["wst"], ["w1b"])
        b.dma("sp", wst[:, 0, 0:KC * 4], wg, [], ["wst"], "wst")
        b.cp("dve", wgb[:].rearrange("p kc n -> p (kc n)"), wst[:, 0, 0:KC * 4], ["wst"], ["wgb"])
        b.act(dsc[:, 0:1], sc[:, 0:1], AF.Exp, ["sc"], ["dsc"])
        b.ts("dve", dsc[:, 0:1], dsc[:, 0:1], -1.0, ALU.mult, ["dsc"], ["dsc"])
        b.ts("dve", dsc[:, 1:2], sc[:, 3:4], -1.0, ALU.mult, ["sc"], ["dsc"])
        b.memset("pool", v1[:, 128:129], 1.0, ["v1"])

        xTv = xT.rearrange("(kc p) n -> p kc n", p=128)

        tiles = []
        for bi in range(NB):
            for t in range(ntiles_prompt // NB):
                tiles.append(dict(kind="p", seq=bi, tok0=bi * SEQ + t * 512, N=512, L=128, nch=4,
                                  first=(t == 0), last=(t == ntiles_prompt // NB - 1)))
        if do_sample:
            tiles.append(dict(kind="s", seq=None, tok0=NTP, N=256, L=32, nch=8, first=True, last=True))

        def load_x(ti, slot):
            T = tiles[ti]
            N = T["N"]
            for q4 in range(4):
                b.dma("sp", xst[q4 % 2][:, :, 0:N], xTv[:, q4 * 4:(q4 + 1) * 4, T["tok0"]:T["tok0"] + N],
                      [], ["xst%d" % (q4 % 2)], "xst%d" % (q4 % 2))
                b.cp("pool", xb[slot][:, q4 * 4:(q4 + 1) * 4, 0:N], xst[q4 % 2][:, :, 0:N],
                     ["xst%d" % (q4 % 2)], ["xb%d" % slot])

        load_x(0, 0)
        if len(tiles) > 1:
            load_x(1, 1)
        def g_P(ti):
            T = tiles[ti]
            slot = ti % 2
            pset = ti % 2
            N, L, nch = (T['N'], T['L'], T['nch'])
            xbk = 'xb%d' % slot
            KY = lambda n: '%s_%d' % (n, pset)
            pr, cv, qn, kn, gat = (PR[pset], CV[pset], QN[pset], KN[pset], GAT[pset])
            nseq = 1 if T['kind'] == 'p' else DB
            Ls = N // nseq
            W = Ls + 3

            def cin3(f, a, bnd):
                return cin[f][:, 0:nseq * W].rearrange('p (s l) -> p s l', l=W)[:, :, a:bnd]
            if T['first']:
                if T['kind'] == 'p':
                    for f in range(3):
                        b.memset('pool', cin[f][:, 0:3], 0.0, ['cin%d' % f])
                        yield
                else:
                    for f in range(3):
                        b.cp('pool', cin3(f, 0, 3), c0s[:].rearrange('p (b f r) -> p b f r', f=3, r=3)[:, :, f, :], ['c0s', 'cin%d' % f], ['cin%d' % f])
                        yield
            for f in range(9):
                bank = pj[f % 2]
                bk = 'pj%d' % (f % 2)
                for kc in range(KC):
                    b.mm(bank[:, 0:N], w1b[:, kc, f * 128:(f + 1) * 128], xb[slot][:, kc, 0:N], ['w1b', xbk], [bk], start=kc == 0, stop=kc == KC - 1)
                yield
                if f < 3:
                    b.cp('act', cin3(f, 3, W), bank[:, 0:N].rearrange('p (s l) -> p s l', l=Ls), [bk], ['cin%d' % f])
                    yield
                elif f == 5:
                    b.act(pr[f][:, 0:N], bank[:, 0:N], AF.Copy, [bk], [KY('pr%d' % f)], scale=128.0 ** (-0.5))
                    yield
                else:
                    b.cp('dve' if f % 2 else 'act', pr[f][:, 0:N], bank[:, 0:N], [bk], [KY('pr%d' % f)])
                    yield
            for j in range(nch):
                for kc in range(KC):
                    b.mm(pg[0:L, j * 4:(j + 1) * 4], xb[slot][:, kc, j * L:(j + 1) * L], wgb[:, kc, :], [xbk, 'wgb'], ['pg_g'], start=kc == 0, stop=kc == KC - 1)
                yield
            G = pg[0:L, 0:nch * 4].rearrange('p (j g) -> p j g', g=4)
            gt = lambda i: gat[0:L, 0:nch, i]
            b.act(gt(0), G[:, :, 0], AF.Sigmoid, ['pg_g'], [KY('gat')])
            yield
            b.ts('dve', gt(1), gt(0), -1.0, ALU.mult, [KY('gat')], [KY('gat')])
            yield
            b.act(gt(9), G[:, :, 1], AF.Exp, ['pg_g', 'sc'], [KY('gat')], bias=sc[0:L, 1:2])
            yield
            b.act(gt(9), gt(9), AF.Ln, [KY('gat')], [KY('gat')], bias=1.0)
            yield
            b.ts('dve', gt(2), gt(9), dsc[0:L, 0:1], ALU.mult, [KY('gat'), 'dsc'], [KY('gat')])
            yield
            b.ts('dve', gt(3), G[:, :, 2], sc[0:L, 2:3], ALU.add, ['pg_g', 'sc'], [KY('gat')])
            yield
            b.act(gt(10), G[:, :, 3], AF.Exp, ['pg_g', 'dsc'], [KY('gat')], bias=dsc[0:L, 1:2], scale=-1.0)
            yield
            b.act(gt(10), gt(10), AF.Ln, [KY('gat')], [KY('gat')], bias=1.0)
            yield
            b.ts('dve', gt(4), gt(10), -1.0, ALU.mult, [KY('gat')], [KY('gat')])
            yield
            gtmp = tmpA[0:L, 0:2 * nch]
            b.cp('dve', gtmp[:, 0:nch], gt(2), [KY('gat')], ['tmpA'])
            yield
            b.cp('dve', gtmp[:, nch:2 * nch], gt(4), [KY('gat')], ['tmpA'])
            yield
            b.mm(pg[0:L, 32:32 + 2 * nch], triU[0:L, 0:L], gtmp, ['cst', 'tmpA'], ['pg_c'])
            yield
            b.cp('dve', gt(5), pg[0:L, 32:32 + nch], ['pg_c'], [KY('gat')])
            yield
            b.cp('dve', gt(7), pg[0:L, 32 + nch:32 + 2 * nch], ['pg_c'], [KY('gat')])
            yield
            b.act(gt(6), gt(5), AF.Exp, [KY('gat')], [KY('gat')])
            yield
            b.ts('dve', gt(11), gt(5), -1.0, ALU.mult, [KY('gat')], [KY('gat')])
            yield
            b.tt('dve', gt(8), gt(3), gt(7), ALU.subtract, [KY('gat')], [KY('gat')])
            yield
            for f in range(3):
                ck = 'cin%d' % f
                o3 = tmpA[:, 0:N].rearrange('p (s l) -> p s l', l=Ls)
                b.ts('dve', o3, cin3(f, 0, Ls), cw[:, f * 4:f * 4 + 1], ALU.mult, [ck, 'cw'], ['tmpA'])
                yield
                for i in range(1, 4):
                    b.stt(o3, cin3(f, i, i + Ls), cw[:, f * 4 + i:f * 4 + i + 1], o3, ALU.mult, ALU.add, [ck, 'cw', 'tmpA'], ['tmpA'])
                    yield
                b.act(cv[f][:, 0:N], tmpA[:, 0:N], AF.Silu, ['tmpA'], [KY('cv%d' % f)])
                yield
                if T['last']:
                    if T['kind'] == 'p':
                        b.dma('sp', oconv[:, T['seq'], f, :], cin[f][:, Ls:Ls + 3], [ck], ['oconv'], 'oconv%d' % f)
                        yield
                    else:
                        b.dma('sp', oconv[:, NB:NB + DB, f, :], cin3(f, Ls, Ls + 3), [ck], ['oconv'], 'oconv%d' % f)
                        yield
                else:
                    b.cp('pool', cin[f][:, 0:3], cin[f][:, Ls:Ls + 3], [ck], [ck])
                    yield
            for f, dst, dk, lnb in ((0, qn, KY('qn'), -0.5 * math.log(128.0)), (1, kn, KY('kn'), 0.0)):
                b.tt('pool', tmpB[:, 0:N], cv[f][:, 0:N], cv[f][:, 0:N], ALU.mult, [KY('cv%d' % f)], ['tmpB'])
                yield
                b.mm(pj[f][:, 0:N], ones, tmpB[:, 0:N], ['cst', 'tmpB'], ['pj%d' % f])
                yield
                b.act(tmpB[:, 0:N], pj[f][:, 0:N], AF.Ln, ['pj%d' % f], ['tmpB'], bias=1e-06)
                yield
                b.act(tmpB[:, 0:N], tmpB[:, 0:N], AF.Exp, ['tmpB'], ['tmpB'], scale=-0.5, bias=lnb)
                yield
                b.tt('dve', dst[:, 0:N], cv[f][:, 0:N], tmpB[:, 0:N], ALU.mult, [KY('cv%d' % f), 'tmpB'], [dk])
                yield
            yield

        def chunk_gens(ti):
            T = tiles[ti]
            slot = ti % 2
            pset = ti % 2
            N, L, nch = (T['N'], T['L'], T['nch'])
            xbk = 'xb%d' % slot
            KY = lambda n: '%s_%d' % (n, pset)
            pr, cv, qn, kn, gat = (PR[pset], CV[pset], QN[pset], KN[pset], GAT[pset])
            I_L = ident[0:L, 0:L]
            NK = 2
            pre_done = [False] * nch
            seq_done = [False] * nch
            PBK = [(pA, 'pA0'), (pB, 'pB0')]

            def pre_chain(k):
                G_ = GB[k]
                bk, bkey = PBK[k]
                kx = lambda n: '%s_%d' % (n, k)
                Gbc, grow, DEC, DECs, X, XT, IXT, Pm = (G_[n] for n in ('Gbc', 'grow', 'DEC', 'DECs', 'X', 'XT', 'IXT', 'Pm'))
                QKd, kg, kdec, vtok, nW, qdec, colg = (G_[n] for n in ('QKd', 'kg', 'kdec', 'vtok', 'nW', 'qdec', 'colg'))
                R0, R1, R2, R3 = (bk[:, i * 128:(i + 1) * 128] for i in range(4))
                for j in range(k, nch, NK):
                    if j >= NK:
                        while not seq_done[j - NK]:
                            yield
                    c0, c1 = (j * L, (j + 1) * L)
                    gcol = gat[0:L, j, 2:3]
                    b.ts('dve', Gbc[0:L, :], ones[0:L, :], gcol, ALU.mult, ['cst', KY('gat')], [kx('Gbc')])
                    yield
                    b.mm(R0[:, 0:L], Gbc[0:L, :], triU[0:L, 0:L], [kx('Gbc'), 'cst'], [bkey])
                    yield
                    b.mm(R1[0:L, 0:L], Gbc[0:L, 0:L], triU[0:L, 0:L], [kx('Gbc'), 'cst'], [bkey], start=True, stop=False)
                    yield
                    b.mm(R1[0:L, 0:L], I_L, NEGU[0:L, 0:L], ['cst'], [bkey], start=False, stop=True)
                    yield
                    b.mm(R2[0:L, 0:L], kn[:, c0:c1], kn[:, c0:c1], [KY('kn')], [bkey])
                    yield
                    b.mm(R3[0:L, 0:L], kn[:, c0:c1], qn[:, c0:c1], [KY('kn'), KY('qn')], [bkey])
                    yield
                    b.act(grow[:, 0:L], R0[:, 0:L], AF.Exp, [bkey], [kx('grow')])
                    yield
                    b.act(DEC[0:L, 0:L], R1[0:L, 0:L], AF.Exp, [bkey, KY('gat')], [kx('DEC')], bias=gat[0:L, j, 11:12])
                    yield
                    b.cp('dve', colg[:, 0:1], R0[:, L - 1:L], [bkey], [kx('col0')])
                    yield
                    b.act(colg[:, 1:2], colg[:, 0:1], AF.Exp, [kx('col0')], [kx('col1')])
                    yield
                    b.act(colg[0:L, 2:3], gat[0:L, j, 5:6], AF.Exp, [KY('gat'), kx('col0')], [kx('col2')], scale=-1.0, bias=colg[0:L, 0:1])
                    yield
                    b.tt('pool', DECs[0:L, 0:L], DEC[0:L, 0:L], I_L, ALU.subtract, [kx('DEC'), 'cst'], [kx('DECs')])
                    yield
                    b.stt(X[0:L, 0:L], R2[0:L, 0:L], gat[0:L, j, 1:2], DECs[0:L, 0:L], ALU.mult, ALU.mult, [bkey, KY('gat'), kx('DECs')], [kx('X')])
                    yield
                    b.tt('dve', QKd[0:L, 0:L], R3[0:L, 0:L], DEC[0:L, 0:L], ALU.mult, [bkey, kx('DEC')], [kx('QKd')])
                    yield
                    b.tt('pool', qdec[:, 0:L], qn[:, c0:c1], grow[:, 0:L], ALU.mult, [KY('qn'), kx('grow')], [kx('qdec')])
                    yield
                    b.tr(R0[0:L, 0:128], kn[:, c0:c1], ident, [KY('kn'), 'cst'], [bkey])
                    yield
                    b.tr(R1[0:L, 0:128], cv[2][:, c0:c1], ident, [KY('cv2'), 'cst'], [bkey])
                    yield
                    b.tr(R2[0:L, 0:L], X[0:L, 0:L], I_L, [kx('X'), 'cst'], [bkey])
                    yield
                    b.ts('dve', kg[0:L, :], R0[0:L, 0:128], gat[0:L, j, 6:7], ALU.mult, [bkey, KY('gat')], [kx('kg')])
                    yield
                    b.ts('dve', kdec[0:L, :], R0[0:L, 0:128], colg[0:L, 2:3], ALU.mult, [bkey, kx('col2')], [kx('kdec')])
                    yield
                    b.cp('act', vtok[0:L, :], R1[0:L, 0:128], [bkey], [kx('vtok')])
                    yield
                    b.cp('act', XT[0:L, 0:L], R2[0:L, 0:L], [bkey], [kx('XT')])
                    yield
                    b.tt('pool', Pm[0:L, 0:L], X[0:L, 0:L], I_L, ALU.add, [kx('X'), 'cst'], [kx('Pm')])
                    yield
                    nsq = int(math.log2(L)) - 1
                    for it in range(nsq):
                        lastit = it == nsq - 1
                        if not lastit:
                            b.mm(R0[0:L, 0:L], XT[0:L, 0:L], X[0:L, 0:L], [kx('XT'), kx('X')], [bkey])
                            yield
                        b.mm(R1[0:L, 0:L], X[0:L, 0:L], XT[0:L, 0:L], [kx('XT'), kx('X')], [bkey])
                        yield
                        if not lastit:
                            b.cp('act', X[0:L, 0:L], R0[0:L, 0:L], [bkey], [kx('X')])
                            yield
                            b.cp('dve', XT[0:L, 0:L], R1[0:L, 0:L], [bkey], [kx('XT')])
                            yield
                        b.tt('dve', IXT[0:L, 0:L], R1[0:L, 0:L], I_L, ALU.add, [bkey, 'cst'], [kx('IXT')])
                        yield
                        b.mm(R3[0:L, 0:L], IXT[0:L, 0:L], Pm[0:L, 0:L], [kx('IXT'), kx('Pm')], [bkey])
                        yield
                        b.cp('act', Pm[0:L, 0:L], R3[0:L, 0:L], [bkey], [kx('Pm')])
                        yield
                    b.mm(R2[:, 0:L], kg[0:L, :], Pm[0:L, 0:L], [kx('kg'), kx('Pm')], [bkey])
                    yield
                    b.ts('dve', nW[:, 0:L], R2[:, 0:L], -1.0, ALU.mult, [bkey], [kx('nW')])
                    yield
                    pre_done[j] = True

            def seq_chain():
                for j in range(nch):
                    while not pre_done[j]:
                        yield
                    k = j % NK
                    G_ = GB[k]
                    kx = lambda n: '%s_%d' % (n, k)
                    Pm, QKd, kdec, vtok, nW, qdec, colg = (G_[n] for n in ('Pm', 'QKd', 'kdec', 'vtok', 'nW', 'qdec', 'colg'))
                    c0, c1 = (j * L, (j + 1) * L)
                    if T['kind'] == 's':
                        seq, seq_last = (NB + j, True)
                        b.dma('sp', Sst[:], S0[j], [], ['S'], 'S_in')
                    else:
                        seq = T['seq']
                        seq_last = T['last'] and j == nch - 1
                        if T['first'] and j == 0:
                            b.memset('pool', Sst[:], 0.0, ['S'])
                    yield
                    b.mm(pC[0:L, 0:128], Pm[0:L, 0:L], vtok[0:L, :], [kx('Pm'), kx('vtok')], ['pC0'], start=True, stop=False)
                    yield
                    b.mm(pC[0:L, 0:128], nW[:, 0:L], Sst[:], [kx('nW'), 'S'], ['pC0'], start=False, stop=True)
                    yield
                    b.ts('dve', ee[0:L, :], pC[0:L, 0:128], gat[0:L, j, 0:1], ALU.mult, ['pC0', KY('gat')], ['ee'])
                    yield
                    b.mm(pC[:, 128:128 + L], Sst[:], qdec[:, 0:L], ['S', kx('qdec')], ['pC1'], start=True, stop=False)
                    yield
                    b.mm(pC[:, 128:128 + L], ee[0:L, :], QKd[0:L, 0:L], ['ee', kx('QKd')], ['pC1'], start=False, stop=True)
                    yield
                    b.mm(pC[:, 256:384], kdec[0:L, :], ee[0:L, :], [kx('kdec'), 'ee'], ['pC2'])
                    yield
                    b.cp('act', oT[:, c0:c1], pC[:, 128:128 + L], ['pC1'], ['oT'])
                    yield
                    b.stt(Sst[:], Sst[:], colg[:, 1:2], pC[:, 256:384], ALU.mult, ALU.add, ['S', kx('col1'), 'pC2'], ['S'])
                    yield
                    if seq_last:
                        b.dma('sp', oS[seq], Sst[:], ['S'], ['oS'], 'oS')
                        yield
                    seq_done[j] = True

            def ml_chain():
                for j in range(nch):
                    c0, c1 = (j * L, (j + 1) * L)
                    if T['kind'] == 's':
                        seq, seq_last = (NB + j, True)
                        b.dma('sp', CN[:, 0:128], C0[j], [], ['CN'], 'CN_in')
                        b.cp('pool', CN[:, 128:129], n0s[:, j:j + 1], ['n0s'], ['CN'])
                        b.cp('pool', mprev[:], m0s[:, j:j + 1], ['m0s'], ['mprev'])
                    else:
                        seq = T['seq']
                        seq_last = T['last'] and j == nch - 1
                        if T['first'] and j == 0:
                            b.memset('pool', CN[:], 0.0, ['CN'])
                            b.memset('pool', mprev[:], 0.0, ['mprev'])
                    yield
                    qb, kb, vb = (pr[4], pr[5], pr[6])
                    acol = gat[0:L, j, 8:9]
                    b.ts('dve', Abc[0:L, 0:L], ones[0:L, 0:L], acol, ALU.mult, ['cst', KY('gat')], ['Abc'])
                    yield
                    b.mm(pD[0:L, 256:256 + L], Abc[0:L, 0:L], I_L, ['Abc', 'cst'], ['pD2'], start=True, stop=False)
                    yield
                    b.mm(pD[0:L, 256:256 + L], I_L, NEGL[0:L, 0:L], ['cst'], ['pD2'], start=False, stop=True)
                    yield
                    b.P.op('dve', lambda e, o=col[0:L, 3:4], i=pD[0:L, 256:256 + L]: e.tensor_reduce(out=o, in_=i, axis=AX.X, op=ALU.max), reads=['pD2'], writes=['col3'])
                    yield
                    b.tt('dve', col[0:L, 4:5], col[0:L, 3:4], mprev[0:L, :], ALU.max, ['col3', 'mprev'], ['col4'])
                    yield
                    b.ts('dve', col[0:L, 5:6], col[0:L, 4:5], -1.0, ALU.mult, ['col4'], ['col5'])
                    yield
                    b.act(pmat[0:L, 0:L], pD[0:L, 256:256 + L], AF.Exp, ['pD2', 'col5'], ['pmat'], bias=col[0:L, 5:6])
                    yield
                    b.act(col[0:L, 6:7], mprev[0:L, :], AF.Exp, ['mprev', 'col5'], ['col6'], bias=col[0:L, 5:6])
                    yield
                    b.tt('dve', col[0:L, 7:8], gat[0:L, j, 7:8], col[0:L, 4:5], ALU.add, [KY('gat'), 'col4'], ['col7'])
                    yield
                    b.act(col[0:L, 8:9], col[0:L, 7:8], AF.Exp, ['col7'], ['col8'], scale=-1.0)
                    yield
                    b.mm(pD[0:L, 384:384 + L], qb[:, c0:c1], kb[:, c0:c1], [KY('pr4'), KY('pr5')], ['pD3'])
                    yield
                    b.stt(pqk[0:L, 0:L], pD[0:L, 384:384 + L], 1.0, pmat[0:L, 0:L], ALU.mult, ALU.mult, ['pD3', 'pmat'], ['pqk', 'col9'], accum_out=col[0:L, 9:10])
                    yield
                    b.tr(pE[0:L, 0:L], pqk[0:L, 0:L], I_L, ['pqk', 'cst'], ['pE0'])
                    yield
                    b.cp('act', pqkT[0:L, 0:L], pE[0:L, 0:L], ['pE0'], ['pqkT'])
                    yield
                    b.cp('dve', c2[0:L, 0:1], col[0:L, 4:5], ['col4'], ['c2'])
                    yield
                    b.cp('dve', c2[0:L, 1:2], gat[0:L, j, 7:8], [KY('gat')], ['c2'])
                    yield
                    SEL = SEL127 if L == 128 else SEL31
                    b.mm(pg[:, 48:50], SEL[0:L, :], c2[0:L, :], ['cst', 'c2'], ['pg_b'])
                    yield
                    b.cp('dve', bc2[:], pg[:, 48:50], ['pg_b'], ['bc2'])
                    yield
                    b.ts('dve', col[:, 10:11], bc2[:, 0:1], -1.0, ALU.mult, ['bc2'], ['col10'])
                    yield
                    b.act(col[0:L, 11:12], acol, AF.Exp, [KY('gat'), 'col10'], ['col11'], bias=col[0:L, 10:11])
                    yield
                    b.act(col[:, 12:13], mprev[:], AF.Exp, ['mprev', 'col10'], ['col12'], bias=col[:, 10:11])
                    yield
                    b.tr(pE[0:L, 128:256], kb[:, c0:c1], ident, [KY('pr5'), 'cst'], ['pE1'])
                    yield
                    b.tr(pE[0:L, 256:384], vb[:, c0:c1], ident, [KY('pr6'), 'cst'], ['pE2'])
                    yield
                    b.ts('dve', kp[0:L, :], pE[0:L, 128:256], col[0:L, 11:12], ALU.mult, ['pE1', 'col11'], ['kp'])
                    yield
                    b.cp('act', v1[0:L, 0:128], pE[0:L, 256:384], ['pE2'], ['v1'])
                    yield
                    b.mm(pg[0:L, 64:192], pqkT[0:L, 0:L], v1[0:L, 0:128], ['pqkT', 'v1'], ['pg_n1'])
                    yield
                    b.mm(pg[0:L, 192:321], qb[:, c0:c1], CN[:, 0:129], [KY('pr4'), 'CN'], ['pg_n2'])
                    yield
                    b.act(n2s[0:L, :], pg[0:L, 192:321], AF.Copy, ['pg_n2', 'col6'], ['n2s'], scale=col[0:L, 6:7])
                    yield
                    b.tt('dve', num[0:L, :], n2s[0:L, 0:128], pg[0:L, 64:192], ALU.add, ['n2s', 'pg_n1'], ['num'])
                    yield
                    b.tt('dve', col[0:L, 13:14], n2s[0:L, 128:129], col[0:L, 9:10], ALU.add, ['n2s', 'col9'], ['col13'])
                    yield
                    b.act(col[0:L, 13:14], col[0:L, 13:14], AF.Abs, ['col13'], ['col13'])
                    yield
                    b.tt('dve', col[0:L, 13:14], col[0:L, 13:14], col[0:L, 8:9], ALU.max, ['col13', 'col8'], ['col13'])
                    yield
                    b.P.op('dve', lambda e, o=col[0:L, 14:15], i=col[0:L, 13:14]: e.reciprocal(out=o, in_=i), reads=['col13'], writes=['col14'])
                    yield
                    b.ts('dve', htok[0:L, :], num[0:L, :], col[0:L, 14:15], ALU.mult, ['num', 'col14'], ['htok'])
                    yield
                    b.tr(pE[:, 384:384 + L], htok[0:L, :], I_L, ['htok', 'cst'], ['pE3'])
                    yield
                    b.cp('act', hT[:, c0:c1], pE[:, 384:384 + L], ['pE3'], ['hT'])
                    yield
                    b.mm(pg[:, 321:450], kp[0:L, :], v1[0:L, :], ['kp', 'v1'], ['pg_c2'])
                    yield
                    b.stt(CN[:], CN[:], col[:, 12:13], pg[:, 321:450], ALU.mult, ALU.add, ['CN', 'col12', 'pg_c2'], ['CN'])
                    yield
                    b.tt('dve', mprev[:], bc2[:, 0:1], bc2[:, 1:2], ALU.add, ['bc2'], ['mprev'])
                    yield
                    if seq_last:
                        b.dma('sp', oC[seq], CN[:, 0:128], ['CN'], ['oC'], 'oC')
                        yield
                        b.dma('sp', on[:, seq:seq + 1], CN[:, 128:129], ['CN'], ['on'], 'on', allow_slow_non_contiguous=True)
                        yield
                        b.dma('sp', om[0:1, seq:seq + 1], mprev[0:1, 0:1], ['mprev'], ['om'], 'om')
                        yield
                    yield
            return [pre_chain(k) for k in range(NK)] + [seq_chain(), ml_chain()]

        def post(ti):
            T = tiles[ti]
            slot = ti % 2
            pset = ti % 2
            N, L, nch = (T['N'], T['L'], T['nch'])
            xbk = 'xb%d' % slot
            KY = lambda n: '%s_%d' % (n, pset)
            pr, cv, qn, kn, gat = (PR[pset], CV[pset], QN[pset], KN[pset], GAT[pset])
            nseq = 1 if T['kind'] == 'p' else DB
            for src, srck, zf, of, ncol, half in ((oT, 'oT', 3, None, 4, 0), (hT, 'hT', 8, 7, 5, 1)):
                if of is not None:
                    b.act(tmpC[:, 0:N], pr[of][:, 0:N], AF.Sigmoid, [KY('pr%d' % of)], ['tmpC'])
                    b.tt('dve', src[:, 0:N], src[:, 0:N], tmpC[:, 0:N], ALU.mult, [srck, 'tmpC'], [srck])
                b.tt('pool', tmpD[:, 0:N], src[:, 0:N], src[:, 0:N], ALU.mult, [srck], ['tmpD'])
                b.mm(pj[half][:, 0:N], ones, tmpD[:, 0:N], ['cst', 'tmpD'], ['pj%d' % half])
                b.act(tmpD[:, 0:N], pj[half][:, 0:N], AF.Ln, ['pj%d' % half], ['tmpD'], scale=1.0 / 128.0, bias=1e-06)
                b.act(tmpD[:, 0:N], tmpD[:, 0:N], AF.Exp, ['tmpD'], ['tmpD'], scale=-0.5)
                b.act(tmpC[:, 0:N], pr[zf][:, 0:N], AF.Silu, [KY('pr%d' % zf)], ['tmpC'])
                b.stt(tmpD[:, 0:N], tmpD[:, 0:N], sc[:, ncol:ncol + 1], tmpC[:, 0:N], ALU.mult, ALU.mult, ['tmpD', 'sc', 'tmpC'], ['tmpD'])
                b.tt('dve', yb[:, half, 0:N], src[:, 0:N], tmpD[:, 0:N], ALU.mult, [srck, 'tmpD'], ['yb%d' % half])
                b.dma('sp', yT[half * 128:(half + 1) * 128, T['tok0']:T['tok0'] + N], yb[:, half, 0:N], ['yb%d' % half], ['yT'], 'yT%d' % half)

        for _ in g_P(0):
            pass
        for ti in range(len(tiles)):
            if ti + 2 < len(tiles):
                load_x(ti + 2, ti % 2)
            gens = chunk_gens(ti)
            if ti + 1 < len(tiles):
                gens.append(g_P(ti + 1))
            _interleave(*gens)
            post(ti)
        P.finalize()
        print("phase1 instrs", P.ninstr, {e: P.cnt[e] for e in ENGS})
    return nc


NTT = 17


def build_out_phase(glu, emit_T):
    nc = bass.Bass("TRN2", target_bir_lowering=False)
    KCI = 24 if glu else 16
    with ExitStack() as st:
        b = B(nc, st)
        P = b.P
        yin = b.din("yin", [NTT, 128, KCI, 128], BF16)
        wout = b.din("wout", [D, D])
        xres = b.din("xres", [NTT * 128, D])
        lng = b.din("lng", [128, D])
        lnb = b.din("lnb", [128, D])
        consts = b.din("consts", [128, 8, 128])
        if glu:
            wglu = b.din("wglu", [1024, 1024])
            bglu = b.din("bglu", [128, 8])
        xo = b.dout("xo", [NTT * 128, D])
        if emit_T:
            xoT = b.dout("xoT", [NTT, 128, KC, 128], BF16)

        cst = b.sb("cst", [128, 8, 128])
        ident = cst[:, 0, :]
        woutb = b.sb("woutb", [128, KC, D], BF16)
        wst = b.sb("wst", [128, 2, D])
        g_sb = b.sb("g_sb", [128, D])
        b_sb = b.sb("b_sb", [128, D])
        ysb = [b.sb("ysb%d" % i, [128, KCI, 128], BF16) for i in range(2)]
        xr = [b.sb("xr%d" % i, [128, D]) for i in range(2)]
        z = b.sb("z", [128, D])
        stats = b.sb("stats", [128, 4, 6])
        mv = b.sb("mv", [128, 4])
        xT_sb = [b.sb("xT_sb%d" % i, [128, KC, 128], BF16) for i in range(2)]
        if glu:
            wglub = b.sb("wglub", [128, 8, 1024], BF16)
            bg = b.sb("bg", [128, 8])
            sig = b.sb("sig", [128, 8, 128])
            yd = b.sb("yd", [128, 8, 128], BF16)
        po = [b.ps("po%d" % i) for i in range(4)]
        pt = [b.ps("pt%d" % i) for i in range(4)]
        PK = lambda i: "pj%d" % i

        b.dma("sp", cst[:], consts, [], ["cst"], "cst")
        b.dma("sp", g_sb[:], lng, [], ["g_sb"], "g_sb")
        b.dma("sp", b_sb[:], lnb, [], ["b_sb"], "b_sb")
        wv = wout.rearrange("(kc p) n -> p kc n", p=128)
        for kc0 in range(0, KC, 2):
            b.dma("sp", wst[:], wv[:, kc0:kc0 + 2, :], [], ["wst"], "wst")
            b.cp("dve", woutb[:, kc0, :], wst[:, 0, :], ["wst"], ["woutb"])
            b.cp("pool", woutb[:, kc0 + 1, :], wst[:, 1, :], ["wst"], ["woutb"])
        if glu:
            gv = wglu.rearrange("(kc p) n -> p kc n", p=128)
            for kc0 in range(0, 8, 4):
                b.dma("sp", wst[:].rearrange("p a (c n) -> p (a c) n", n=1024), gv[:, kc0:kc0 + 4, :], [], ["wst"], "wst")
                b.cp("pool", wglub[:, kc0:kc0 + 4, :], wst[:].rearrange("p a (c n) -> p (a c) n", n=1024), ["wst"], ["wglub"])
            b.dma("sp", bg[:], bglu, [], ["bg"], "bg")

        def load(ti):
            s = ti % 2
            b.dma("sp", ysb[s][:], yin[ti], [], ["ysb%d" % s], "ysb%d" % s)
            b.dma("sp", xr[s][:], xres[ti * 128:(ti + 1) * 128, :], [], ["xr%d" % s], "xr%d" % s)

        load(0)
        for ti in range(NTT):
            s = ti % 2
            if ti + 1 < NTT:
                load(ti + 1)
            yk = "ysb%d" % s
            if glu:
                for f in range(8):
                    bank = pt[f // 4]
                    for kc in range(8):
                        b.mm(bank[:, (f % 4) * 128:(f % 4 + 1) * 128], wglub[:, kc, f * 128:(f + 1) * 128],
                             ysb[s][:, 8 + kc, :], ["wglub", yk], [PK(4 + f // 4)], start=(kc == 0), stop=(kc == 7))
                for f in range(8):
                    b.act(sig[:, f, :], pt[f // 4][:, (f % 4) * 128:(f % 4 + 1) * 128], AF.Sigmoid,
                          [PK(4 + f // 4), "bg"], ["sig"], bias=bg[:, f:f + 1])
                b.tt("dve", sig[:], sig[:], ysb[s][:, 8:16, :], ALU.mult, ["sig", yk], ["sig"])
                b.tt("dve", yd[:], sig[:], ysb[s][:, 16:24, :], ALU.mult, ["sig", yk], ["yd"])
            for n in range(4):
                for kc in range(KC):
                    if glu and kc >= 8:
                        lhsT, lk = yd[:, kc - 8, :], "yd"
                    else:
                        lhsT, lk = ysb[s][:, kc, :], yk
                    b.mm(po[n][:, :], lhsT, woutb[:, kc, n * 512:(n + 1) * 512], [lk, "woutb"], [PK(n)],
                         start=(kc == 0), stop=(kc == KC - 1))
            for n in range(4):
                b.stt(z[:, n * 512:(n + 1) * 512], xr[s][:, n * 512:(n + 1) * 512], ALPHA, po[n][:, :],
                      ALU.mult, ALU.add, ["xr%d" % s, PK(n)], ["z"])
                b.P.op("dve", lambda e, o=stats[:, n, :], i=z[:, n * 512:(n + 1) * 512]: e.bn_stats(out=o, in_=i),
                       reads=["z"], writes=["stats"])
            b.P.op("dve", lambda e, o=mv[:, 0:2], i=stats[:].rearrange("p a c -> p (a c)"): e.bn_aggr(out=o, in_=i),
                   reads=["stats"], writes=["mv"])
            b.act(mv[:, 2:3], mv[:, 1:2], AF.Ln, ["mv"], ["mv"], bias=1e-5)
            b.act(mv[:, 2:3], mv[:, 2:3], AF.Exp, ["mv"], ["mv"], scale=-0.5)
            b.ts("dve", z[:], z[:], mv[:, 0:1], ALU.subtract, ["z", "mv"], ["z"], s2=mv[:, 2:3], op1=ALU.mult)
            b.tt("pool", z[:], z[:], g_sb[:], ALU.mult, ["z", "g_sb"], ["z"])
            b.tt("dve", z[:], z[:], b_sb[:], ALU.add, ["z", "b_sb"], ["z"])
            b.dma("sp", xo[ti * 128:(ti + 1) * 128, :], z[:], ["z"], ["xo"], "xo")
            if emit_T:
                for kc in range(KC):
                    b.tr(pt[kc // 4][:, (kc % 4) * 128:(kc % 4 + 1) * 128], z[:, kc * 128:(kc + 1) * 128], ident,
                         ["z", "cst"], [PK(4 + kc // 4)])
                for q in range(4):
                    b.cp("act" if q % 2 else "dve", xT_sb[s][:, q * 4:(q + 1) * 4, :].rearrange("p a t -> p (a t)"),
                         pt[q][:, :], [PK(4 + q)], ["xT_sb%d" % s])
                b.dma("sp", xoT[ti], xT_sb[s][:], ["xT_sb%d" % s], ["xoT"], "xoT%d" % s)
        P.finalize()
        print("out-phase instrs", P.ninstr, {e: P.cnt[e] for e in ENGS})
    return nc


def _run(nc, in_maps):
    return run_bass_kernel_spmd(nc, in_maps, core_ids=list(range(NCORES))).results


def _tok_major(inp_p, inp_s):
    return np.concatenate([inp_p.reshape(-1, inp_p.shape[-1]), inp_s.reshape(-1, inp_s.shape[-1])], 0)


def _p1_inputs(inp, xT, c):
    H = 128
    W = inp['ev_w_in'][0]
    offs = np.cumsum([0, 3072, 1024, 8, 8, 1024, 1024, 1024, 1024, 1024, 8, 8])

    def colblk(i, sub=0):
        return W[:, offs[i] + sub * 1024 + c * H: offs[i] + sub * 1024 + (c + 1) * H]
    w1 = np.concatenate([colblk(0, 0), colblk(0, 1), colblk(0, 2), colblk(1),
                         colblk(4), colblk(5), colblk(6), colblk(7), colblk(8)], 1)
    wg = np.stack([W[:, offs[2] + c], W[:, offs[3] + c], W[:, offs[9] + c], W[:, offs[10] + c]], 1)
    wg = wg.reshape(KC, 128, 4).transpose(1, 0, 2).reshape(128, KC * 4)
    cw = inp['ev_conv_w'][0]
    convw = np.stack([cw[:, f * 1024 + c * H: f * 1024 + (c + 1) * H].T for f in range(3)], 1).reshape(H, 12)
    scal = np.zeros((128, 8), np.float32)
    scal[:, 0] = inp['ev_a_log'][0, c]
    scal[:, 1] = inp['ev_dt_bias'][0, c]
    scal[:, 2] = inp['ev_ig_bias'][0, c]
    scal[:, 3] = inp['ev_fg_bias'][0, c]
    scal[:, 4] = inp['ev_norm_a'][0]
    scal[:, 5] = inp['ev_norm_b'][0, c * H:(c + 1) * H]
    cs = inp['state_delta_conv'][0]
    conv0 = np.stack([cs[:, :, f * 1024 + c * H: f * 1024 + (c + 1) * H] for f in range(3)], 0)
    conv0 = np.ascontiguousarray(conv0.transpose(3, 1, 0, 2))
    A = np.ascontiguousarray
    return dict(xT=xT, w1=A(w1), wg=A(wg), convw=A(convw), scal=scal, consts=_consts_np(),
                S0=A(inp['state_delta_S'][0, :, c]), conv0=conv0,
                C0=A(inp['state_mlstm_C'][0, :, c]), n0=A(inp['state_mlstm_n'][0, :, c].T),
                m0=A(np.broadcast_to(inp['state_mlstm_m'][0, :, c][None, :], (128, DB))))


def _tile_tokens(Y, c):
    nk = Y.shape[0]
    out = np.zeros((nk, 128, NTT * 128), Y.dtype)
    out[:, :, :TOKC] = Y[:, :, c * TOKC:(c + 1) * TOKC]
    return np.ascontiguousarray(out.reshape(nk, 128, NTT, 128).transpose(2, 1, 0, 3))


def _pad_rows(x, c):
    out = np.zeros((NTT * 128, x.shape[1]), x.dtype)
    out[:TOKC] = x[c * TOKC:(c + 1) * TOKC]
    return out


def _p2_inputs(inp, xtok, Y, c):
    rep = lambda v: np.ascontiguousarray(np.broadcast_to(v[None, :], (128, v.shape[0])))
    return dict(yin=_tile_tokens(Y, c), wout=np.ascontiguousarray(inp['ev_w_out'][0]), xres=_pad_rows(xtok, c),
                lng=rep(inp['ev_ln_g'][0]), lnb=rep(inp['ev_ln_b'][0]), consts=_consts_np())


TWO_PI = 2.0 * math.pi
CW1 = 6.28125
CW2 = TWO_PI - CW1


def _masks3_np():
    m = np.zeros((128, 6, 512), np.float32)
    s = np.arange(128)[:, None]
    t = np.arange(512)[None, :]
    for r in range(4):
        m[:, r, :] = (t > 128 * r + s)
    m[:, 4, :] = ((t % 32) > s)
    m[:, 5, :] = t + 1.0
    return m


def build_phase3(ntiles_prompt=32, do_sample=True, debug=0):
    nc = bass.Bass("TRN2", target_bir_lowering=False)
    with ExitStack() as st:
        b = B(nc, st)
        P = b.P
        x1T = b.din("x1T", [D, NTOK], BF16)
        w3 = b.din("w3", [D, 768])
        consts = b.din("consts", [128, 8, 128])
        masks = b.din("masks", [128, 5, 512], BF16)
        tau = b.din("tau", [128, 512])
        kcT = b.din("kcT", [32, 128, DB, 128])
        vcc = b.din("vcc", [32, 128, DB, 128])
        s5p = b.din("s5p", [128, 4, 4])
        Bbd = b.din("Bbd", [128, 2, 4, 128])
        Cbd = b.din("Cbd", [128, 2, 4, 128])
        h0 = b.din("h0", [128, 2, 4, DB])
        y3T = b.dout("y3T", [384, NTOK], BF16)
        kout = b.dout("kout", [NTOK, 128])
        vout = b.dout("vout", [NTOK, 128])
        hout = b.dout("hout", [128, 2, 4, NB + DB])

        cst = b.sb("cst", [128, 8, 128])
        ident = cst[:, 0, :]
        msk = b.sb("msk", [128, 5, 512], BF16)
        trib = b.sb("trib", [128, 2, 128], BF16)
        w3b = b.sb("w3b", [128, KC, 768], BF16)
        wst = b.sb("wst", [128, 1, 768])
        xb = [b.sb("xb%d" % i, [128, KC, 512], BF16) for i in range(2)]
        KT = b.sb("KT", [128, SEQ], BF16)
        Vt = b.sb("Vt", [128, 64, 128], BF16)
        qb = b.sb("qb", [128, 512], BF16)
        kT32 = b.sb("kT32", [128, 512])
        vT32 = b.sb("vT32", [128, 512])
        zc = b.sb("zc", [128, 512])
        u32 = b.sb("u32", [128, 512])
        ub = b.sb("ub", [128, 512], BF16)
        szd = b.sb("szd", [128, 512], BF16)
        ktok = b.sb("ktok", [128, 4, 128])
        vtok = b.sb("vtok", [128, 4, 128])
        E = [b.sb("E%d" % i, [128, 512]) for i in range(3)]
        SP = [b.sb("SP%d" % i, [128, 512], BF16) for i in range(3)]
        Wt = b.sb("Wt", [128, 512])
        ATT = [b.sb("ATT%d" % i, [128, 512], BF16) for i in range(2)]
        ycb = b.sb("ycb", [128, 512], BF16)
        ysb = b.sb("ysb", [128, 512], BF16)
        sp_ = b.sb("sp_", [128, 4, 4])
        cr = b.sb("cr", [128, 4, 512])
        sr = b.sb("sr", [128, 4, 512])
        ki = b.sb("ki", [128, 512], mybir.dt.int32)
        s5c = b.sb("s5c", [128, 4, 16])
        Bb = b.sb("Bb", [128, 2, 4, 128], BF16)
        Cst = b.sb("Cst", [128, 2, 4, 128])
        Cb = b.sb("Cb", [128, 2, 4, 128], BF16)
        hst = b.sb("hst", [128, 2, 4, DB])
        h0s = b.sb("h0s", [128, 2, 4, DB])
        hfin = b.sb("hfin", [128, 2, 4, NB + DB])
        t1 = b.sb("t1", [128, 512])
        t2 = b.sb("t2", [128, 512])
        bre = b.sb("bre", [128, 512])
        bim = b.sb("bim", [128, 512])
        gre = b.sb("gre", [128, 512])
        gim = b.sb("gim", [128, 512])
        ang, kk = gre, gim
        hreb = b.sb("hreb", [128, 512], BF16)
        himb = b.sb("himb", [128, 512], BF16)
        ys32 = b.sb("ys32", [128, 512])
        kcs = [b.sb("kcs%d" % i, [128, DB, 128]) for i in range(2)]
        vcs = [b.sb("vcs%d" % i, [128, DB, 128]) for i in range(2)]
        kcb = [b.sb("kcb%d" % i, [128, DB, 128], BF16) for i in range(3)]
        vcb = [b.sb("vcb%d" % i, [128, DB, 128], BF16) for i in range(3)]
        vnew = b.sb("vnew", [128, DB, 128], BF16)

        pj = [b.ps("pj%d" % i) for i in range(2)]
        pZ = [b.ps("pZ%d" % i) for i in range(2)]
        pCm = b.ps("pCm")
        pO = b.ps("pO")
        pT = b.ps("pT")
        pY = b.ps("pY")
        ZK = ["pj2", "pj3"]
        CK, OK, TK, YK = "pj4", "pj5", "pj6", "pj7"

        b.dma("sp", cst[:], consts, [], ["cst"], "cst")
        b.dma("sp", msk[:], masks, [], ["msk"], "msk")
        b.dma("sp", t2[:], tau, [], ["t2"], "tau")
        b.dma("sp", sp_[:], s5p, [], ["sp_"], "sp_")
        b.dma("sp", h0s[:], h0, [], ["h0s"], "h0s")
        b.cp("dve", trib[:, 0, :], cst[:, 7, :], ["cst"], ["trib"])
        b.ts("dve", trib[:, 1, :], cst[:, 7, :], -1.0, ALU.mult, ["cst"], ["trib"], s2=1.0, op1=ALU.add)
        w3v = w3.rearrange("(kc p) n -> p kc n", p=128)
        for kc0 in range(KC):
            b.dma("sp", wst[:, 0, :], w3v[:, kc0, :], [], ["wst"], "wst")
            b.cp("dve" if kc0 % 2 else "pool", w3b[:, kc0, :], wst[:, 0, :], ["wst"], ["w3b"])
        for a in range(2):
            b.dma("sp", wst[:, 0, 0:512].rearrange("p (c n) -> p c n", n=128), Bbd[:, a], [], ["wst"], "wst")
            b.cp("dve", Bb[:, a], wst[:, 0, 0:512].rearrange("p (c n) -> p c n", n=128), ["wst"], ["Bb"])
        b.dma("sp", Cst[:], Cbd, [], ["Cst"], "Cst")
        c_ = lambda i: s5c[:, :, i]
        lre, lim = sp_[:, :, 0], sp_[:, :, 1]
        b.act(c_(13), sp_[:, :, 2], AF.Exp, ["sp_"], ["s5c"])
        dtc = c_(13)
        b.tt("dve", c_(8), lre, dtc, ALU.mult, ["sp_", "s5c"], ["s5c"])
        b.act(c_(0), c_(8), AF.Exp, ["s5c"], ["s5c"])
        b.tt("dve", c_(1), lim, dtc, ALU.mult, ["sp_", "s5c"], ["s5c"])
        b.cp("dve", bim[:], t2[:], ["t2"], ["bim"])
        for m in range(4):
            th = s5c[:, m, 1:2]
            b.ts("dve", ang[:], bim[:], th, ALU.mult, ["bim", "s5c"], ["gre"])
            b.ts("dve", kk[:], ang[:], 1.0 / TWO_PI, ALU.mult, ["gre"], ["gim"])
            b.cp("dve", ki[:], kk[:], ["gim"], ["ki"])
            b.cp("dve", kk[:], ki[:], ["ki"], ["gim"])
            b.stt(ang[:], kk[:], -CW1, ang[:], ALU.mult, ALU.add, ["gim", "gre"], ["gre"])
            b.stt(ang[:], kk[:], -CW2, ang[:], ALU.mult, ALU.add, ["gim", "gre"], ["gre"])
            for (dst, shift) in ((sr, 0.0), (cr, 0.5 * math.pi)):
                src = ang
                if shift != 0.0:
                    b.ts("dve", t1[:], ang[:], shift, ALU.add, ["gre"], ["t1"])
                    src = t1
                    sk = "t1"
                else:
                    b.cp("dve", t1[:], ang[:], ["gre"], ["t1"])
                    src, sk = t1, "t1"
                for _ in range(2):
                    b.ts("dve", t2[:], t1[:], math.pi, ALU.is_gt, ["t1"], ["t2"])
                    b.stt(t1[:], t2[:], -TWO_PI, t1[:], ALU.mult, ALU.add, ["t2", "t1"], ["t1"])
                    b.ts("dve", t2[:], t1[:], -math.pi, ALU.is_lt, ["t1"], ["t2"])
                    b.stt(t1[:], t2[:], TWO_PI, t1[:], ALU.mult, ALU.add, ["t2", "t1"], ["t1"])
                b.act(dst[:, m, :], t1[:], AF.Sin, ["t1"], ["cr" if dst is cr else "sr"])
        b.tt("dve", c_(2), c_(0), cr[:, :, 0], ALU.mult, ["s5c", "cr"], ["s5c"])
        b.tt("dve", c_(3), c_(0), sr[:, :, 0], ALU.mult, ["s5c", "sr"], ["s5c"])
        b.ts("dve", c_(2), c_(2), -1.0, ALU.add, ["s5c"], ["s5c"])
        b.tt("dve", c_(4), lre, lre, ALU.mult, ["sp_"], ["s5c"])
        b.tt("dve", c_(8), lim, lim, ALU.mult, ["sp_"], ["s5c"])
        b.tt("dve", c_(4), c_(4), c_(8), ALU.add, ["s5c"], ["s5c"])
        b.P.op("dve", lambda e, o=c_(12), i=c_(4): e.reciprocal(out=o, in_=i), reads=["s5c"], writes=["s5c"])
        b.tt("dve", c_(8), c_(2), lre, ALU.mult, ["s5c", "sp_"], ["s5c"])
        b.tt("dve", c_(9), c_(3), lim, ALU.mult, ["s5c", "sp_"], ["s5c"])
        b.tt("dve", c_(5), c_(8), c_(9), ALU.add, ["s5c"], ["s5c"])
        b.tt("dve", c_(5), c_(5), c_(12), ALU.mult, ["s5c"], ["s5c"])
        b.tt("dve", c_(8), c_(3), lre, ALU.mult, ["s5c", "sp_"], ["s5c"])
        b.tt("dve", c_(9), c_(2), lim, ALU.mult, ["s5c", "sp_"], ["s5c"])
        b.tt("dve", c_(6), c_(8), c_(9), ALU.subtract, ["s5c"], ["s5c"])
        b.tt("dve", c_(6), c_(6), c_(12), ALU.mult, ["s5c"], ["s5c"])
        b.ts("dve", c_(7), c_(6), -1.0, ALU.mult, ["s5c"], ["s5c"])
        b.tt("dve", c_(8), c_(5), c_(5), ALU.mult, ["s5c"], ["s5c"])
        b.tt("dve", c_(9), c_(6), c_(6), ALU.mult, ["s5c"], ["s5c"])
        b.tt("dve", c_(8), c_(8), c_(9), ALU.add, ["s5c"], ["s5c"])
        b.P.op("dve", lambda e, o=c_(9), i=c_(8): e.reciprocal(out=o, in_=i), reads=["s5c"], writes=["s5c"])
        b.tt("dve", c_(10), c_(5), c_(9), ALU.mult, ["s5c"], ["s5c"])
        b.tt("dve", c_(11), c_(7), c_(9), ALU.mult, ["s5c"], ["s5c"])
        for m in range(4):
            fre, fim, nfim = s5c[:, m, 5:6], s5c[:, m, 6:7], s5c[:, m, 7:8]
            b.ts("dve", t1[:, 0:128], Cst[:, 1, m, :], nfim, ALU.mult, ["Cst", "s5c"], ["t1"])
            b.stt(Cb[:, 0, m, :], Cst[:, 0, m, :], fre, t1[:, 0:128], ALU.mult, ALU.add, ["Cst", "s5c", "t1"], ["Cb"])
            b.ts("dve", t1[:, 0:128], Cst[:, 1, m, :], fre, ALU.mult, ["Cst", "s5c"], ["t1"])
            b.stt(t1[:, 0:128], Cst[:, 0, m, :], fim, t1[:, 0:128], ALU.mult, ALU.add, ["Cst", "s5c", "t1"], ["t1"])
            b.ts("dve", Cb[:, 1, m, :], t1[:, 0:128], -1.0, ALU.mult, ["t1"], ["Cb"])
            fr, fi = s5c[:, m, 10:11], s5c[:, m, 11:12]
            b.ts("dve", t2[:, 0:DB], h0s[:, 1, m, :], fi, ALU.mult, ["h0s", "s5c"], ["t2"])
            b.stt(t2[:, 16:16 + DB], h0s[:, 0, m, :], fr, t2[:, 0:DB], ALU.mult, ALU.subtract, ["h0s", "s5c", "t2"], ["t2"])
            b.ts("dve", t2[:, 0:DB], h0s[:, 1, m, :], fr, ALU.mult, ["h0s", "s5c"], ["t2"])
            b.stt(t2[:, 32:32 + DB], h0s[:, 0, m, :], fi, t2[:, 0:DB], ALU.mult, ALU.add, ["h0s", "s5c", "t2"], ["t2"])
            b.cp("dve", h0s[:, 0, m, :], t2[:, 16:16 + DB], ["t2"], ["h0s"])
            b.cp("dve", h0s[:, 1, m, :], t2[:, 32:32 + DB], ["t2"], ["h0s"])

        x1v = x1T.rearrange("(kc p) n -> p kc n", p=128)
        tiles = []
        for bi in range(NB):
            for t in range(ntiles_prompt // NB):
                tiles.append(dict(kind="p", seq=bi, ti=t, tok0=bi * SEQ + t * 512, N=512,
                                  first=(t == 0), last=(t == ntiles_prompt // NB - 1)))
        if do_sample:
            tiles.append(dict(kind="s", seq=None, ti=0, tok0=NTP, N=256, first=True, last=True))

        def load_x(i):
            T = tiles[i]
            s = i % 2
            for h in range(2):
                b.dma("sp", xb[s][:, h * 8:(h + 1) * 8, 0:T["N"]], x1v[:, h * 8:(h + 1) * 8, T["tok0"]:T["tok0"] + T["N"]],
                      [], ["xb%d" % s], "xb%d_%d" % (s, h))

        def load_cache(j):
            s = j % 3
            s2 = j % 2
            b.dma("sp", kcs[s2][:], kcT[j], [], ["kcs%d" % s2], "kcs%d" % s2)
            b.dma("sp", vcs[s2][:], vcc[j], [], ["vcs%d" % s2], "vcs%d" % s2)
            b.cp("pool", kcb[s][:], kcs[s2][:], ["kcs%d" % s2], ["kcb%d" % s])
            b.cp("pool", vcb[s][:], vcs[s2][:], ["vcs%d" % s2], ["vcb%d" % s])

        load_x(0)
        try:
            for i, T in enumerate(tiles):
                s = i % 2
                N = T["N"]
                samp = T["kind"] == "s"
                xk = "xb%d" % s
                if i + 1 < len(tiles):
                    load_x(i + 1)
                for f in range(6):
                    bank, bk = pj[f % 2], "pj%d" % (f % 2)
                    for kc in range(KC):
                        b.mm(bank[:, 0:N], w3b[:, kc, f * 128:(f + 1) * 128], xb[s][:, kc, 0:N], ["w3b", xk], [bk],
                             start=(kc == 0), stop=(kc == KC - 1))
                    if f == 0:
                        b.act(qb[:, 0:N], bank[:, 0:N], AF.Copy, [bk], ["qb"], scale=128.0 ** -0.5)
                    elif f == 1:
                        b.cp("dve", kT32[:, 0:N], bank[:, 0:N], [bk], ["kT32"])
                        if not samp:
                            b.cp("act", KT[:, T["ti"] * 512:T["ti"] * 512 + N], bank[:, 0:N], [bk], ["KT"])
                        else:
                            b.cp("act", KT[:, 0:N], bank[:, 0:N], [bk], ["KT"])
                    elif f == 2:
                        b.cp("dve", vT32[:, 0:N], bank[:, 0:N], [bk], ["vT32"])
                    elif f == 3:
                        b.act(zc[:, 0:N], bank[:, 0:N], AF.Silu, [bk], ["zc"])
                    elif f == 4:
                        b.cp("dve", u32[:, 0:N], bank[:, 0:N], [bk], ["u32"])
                        b.cp("act", ub[:, 0:N], bank[:, 0:N], [bk], ["ub"])
                    else:
                        b.act(szd[:, 0:N], bank[:, 0:N], AF.Silu, [bk], ["szd"])
                        b.dma("sp", y3T[256:384, T["tok0"]:T["tok0"] + N], szd[:, 0:N], ["szd"], ["y3T"], "szd_o")
                if debug == 1:
                    raise _Stop
                nblk = N // 128
                for (src, sk, dst, dk, outap) in ((kT32, "kT32", ktok, "ktok", kout), (vT32, "vT32", vtok, "vtok", vout)):
                    for jb in range(nblk):
                        b.tr(pT[:, jb * 128:(jb + 1) * 128], src[:, jb * 128:(jb + 1) * 128], ident, [sk, "cst"], [TK])
                    b.cp("dve", dst[:, 0:nblk, :].rearrange("p a d -> p (a d)"), pT[:, 0:nblk * 128], [TK], [dk])
                    b.dma("sp", outap[T["tok0"]:T["tok0"] + N, :].rearrange("(a p) d -> p a d", p=128), dst[:, 0:nblk, :],
                          [dk], ["o_" + dk], "o_" + dk)
                if not samp:
                    b.cp("pool", Vt[:, T["ti"] * 4:T["ti"] * 4 + 4, :], vtok[:], ["vtok"], ["Vt"])
                else:
                    for h in range(2):
                        for q4 in range(4):
                            bi = h * 4 + q4
                            b.tr(pT[0:32, q4 * 128:(q4 + 1) * 128], vT32[:, bi * 32:(bi + 1) * 32], ident,
                                 ["vT32", "cst"], [TK])
                        b.cp("dve", vnew[0:32, h * 4:(h + 1) * 4, :].rearrange("p a d -> p (a d)"), pT[0:32, 0:512],
                             [TK], ["vnew"])
                if debug == 2:
                    raise _Stop
                def g_att():
                    if not samp:
                        jmax = T['ti'] * 4 + 3
                        steps = list(range(jmax, -1, -1))
                    else:
                        steps = list(range(32, -1, -1))
                        load_cache(31)
                        yield

                    def stageA(si):
                        j = steps[si]
                        z = si % 2
                        e3 = si % 3
                        Zb, zk = (pZ[z], ZK[z])
                        if not samp:
                            Kp = 128
                            b.mm(Zb[:, 0:N], KT[:, j * 128:(j + 1) * 128], qb[:, 0:N], ['KT', 'qb'], [zk])
                        elif j == 32:
                            Kp = 32
                            for bi in range(DB):
                                b.mm(Zb[0:32, bi * 32:(bi + 1) * 32], KT[:, bi * 32:(bi + 1) * 32], qb[:, bi * 32:(bi + 1) * 32], ['KT', 'qb'], [zk])
                        else:
                            Kp = 128
                            if j - 1 >= 0:
                                load_cache(j - 1)
                            for bi in range(DB):
                                b.mm(Zb[:, bi * 32:(bi + 1) * 32], kcb[j % 3][:, bi, :], qb[:, bi * 32:(bi + 1) * 32], ['kcb%d' % (j % 3), 'qb'], [zk])
                        b.act(E[e3][0:Kp, 0:N], Zb[0:Kp, 0:N], AF.Exp, [zk], ['E%d' % e3])
                        b.act(SP[e3][0:Kp, 0:N], E[e3][0:Kp, 0:N], AF.Ln, ['E%d' % e3], ['SP%d' % e3], bias=1.0)
                        mk = None
                        if samp and j == 32:
                            mk = msk[0:32, 4, 0:N]
                        elif not samp and j >= T['ti'] * 4:
                            mk = msk[:, j - T['ti'] * 4, 0:N]
                        if mk is not None:
                            b.tt('pool', E[e3][0:Kp, 0:N], E[e3][0:Kp, 0:N], mk, ALU.mult, ['E%d' % e3, 'msk'], ['E%d' % e3])
                            b.tt('pool', SP[e3][0:Kp, 0:N], SP[e3][0:Kp, 0:N], mk, ALU.mult, ['SP%d' % e3, 'msk'], ['SP%d' % e3])
                        return Kp

                    def emitAV(si, Kp):
                        j = steps[si]
                        z = si % 2
                        if not samp:
                            b.mm(pO[:, 0:N], Vt[:, j, :], ATT[z][:, 0:N], ['Vt', 'ATT%d' % z], [OK], start=False, stop=True)
                        elif j == 32:
                            for bi in range(DB):
                                b.mm(pO[:, bi * 32:(bi + 1) * 32], vnew[0:32, bi, :], ATT[z][0:32, bi * 32:(bi + 1) * 32], ['vnew', 'ATT%d' % z], [OK], start=False, stop=True)
                        else:
                            for bi in range(DB):
                                b.mm(pO[:, bi * 32:(bi + 1) * 32], vcb[j % 3][:, bi, :], ATT[z][:, bi * 32:(bi + 1) * 32], ['vcb%d' % (j % 3), 'ATT%d' % z], [OK], start=False, stop=True)
                    b.memset('dve', pCm[:, 0:N], 0.0, [CK])
                    yield
                    b.memset('dve', pO[:, 0:N], 0.0, [OK])
                    yield
                    LA = 1 if samp else 2
                    nst = len(steps)
                    KpA = {}
                    for si0 in range(min(LA, nst)):
                        KpA[si0] = stageA(si0)
                        yield
                    for si, j in enumerate(steps):
                        z = si % 2
                        e3 = si % 3
                        Kp = KpA[si]
                        b.mm(pCm[:, 0:N], trib[0:Kp, 0, :], SP[e3][0:Kp, 0:N], ['trib', 'SP%d' % e3], [CK], start=False, stop=True)
                        yield
                        b.act(Wt[0:Kp, 0:N], pCm[0:Kp, 0:N], AF.Exp, [CK], ['Wt'], scale=-1.0)
                        yield
                        b.tt('dve', ATT[z][0:Kp, 0:N], E[e3][0:Kp, 0:N], Wt[0:Kp, 0:N], ALU.mult, ['E%d' % e3, 'Wt'], ['ATT%d' % z])
                        yield
                        if si >= 1:
                            emitAV(si - 1, KpA[si - 1])
                            yield
                        if si + LA < nst:
                            KpA[si + LA] = stageA(si + LA)
                            yield
                        b.mm(pCm[:, 0:N], trib[0:Kp, 1, :], SP[e3][0:Kp, 0:N], ['trib', 'SP%d' % e3], [CK], start=False, stop=True)
                        yield
                    emitAV(nst - 1, KpA[nst - 1])
                    yield
                    b.tt('dve', ycb[:, 0:N], pO[:, 0:N], zc[:, 0:N], ALU.mult, [OK, 'zc'], ['ycb'])
                    yield
                    b.dma('sp', y3T[0:128, T['tok0']:T['tok0'] + N], ycb[:, 0:N], ['ycb'], ['y3T'], 'yc_o')
                    yield
                    yield
                def g_s5():
                    nseq = DB if samp else 1
                    Ls = N // nseq
                    v3 = lambda ap: ap.rearrange('p (s l) -> p s l', l=Ls)

                    def tab(tb, m):
                        if samp:
                            return tb[:, m, 0:Ls].unsqueeze(1).broadcast_to([128, nseq, Ls])
                        return v3(tb[:, m, 0:N])
                    for m in range(4):
                        b.mm(pj[0][:, 0:N], Bb[:, 0, m, :], ub[:, 0:N], ['Bb', 'ub'], ['pj0'])
                        yield
                        b.mm(pj[1][:, 0:N], Bb[:, 1, m, :], ub[:, 0:N], ['Bb', 'ub'], ['pj1'])
                        yield
                        Bre, Bim = (v3(pj[0][:, 0:N]), v3(pj[1][:, 0:N]))
                        b.tt('dve', v3(t1[:, 0:N]), Bre, tab(cr, m), ALU.mult, ['pj0', 'cr'], ['t1'])
                        yield
                        b.tt('dve', v3(t2[:, 0:N]), Bim, tab(sr, m), ALU.mult, ['pj1', 'sr'], ['t2'])
                        yield
                        b.tt('pool', bre[:, 0:N], t1[:, 0:N], t2[:, 0:N], ALU.add, ['t1', 't2'], ['bre'])
                        yield
                        b.tt('dve', v3(t1[:, 0:N]), Bim, tab(cr, m), ALU.mult, ['pj1', 'cr'], ['t1'])
                        yield
                        b.tt('dve', v3(t2[:, 0:N]), Bre, tab(sr, m), ALU.mult, ['pj0', 'sr'], ['t2'])
                        yield
                        b.tt('pool', bim[:, 0:N], t1[:, 0:N], t2[:, 0:N], ALU.subtract, ['t1', 't2'], ['bim'])
                        yield
                        rbc = s5c[:, m, 0:1].to_broadcast([128, Ls])
                        for sq in range(nseq):
                            col = sq if samp else 0
                            for src, sk, dst, dk, a in ((bre, 'bre', gre, 'gre', 0), (bim, 'bim', gim, 'gim', 1)):
                                if samp:
                                    init = h0s[:, a, m, sq:sq + 1]
                                    ik = 'h0s'
                                elif T['first']:
                                    init, ik = (0.0, None)
                                else:
                                    init, ik = (hst[:, a, m, 0:1], 'hst')
                                b.P.op('dve', lambda e, o=dst[:, sq * Ls:(sq + 1) * Ls], d0=rbc, d1=src[:, sq * Ls:(sq + 1) * Ls], ini=init: e.tensor_tensor_scan(out=o, data0=d0, data1=d1, initial=ini, op0=ALU.mult, op1=ALU.add), reads=[sk, 's5c'] + ([ik] if ik else []), writes=[dk])
                                yield
                        b.tt('dve', v3(t1[:, 0:N]), v3(gre[:, 0:N]), tab(cr, m), ALU.mult, ['gre', 'cr'], ['t1'])
                        yield
                        b.tt('dve', v3(t2[:, 0:N]), v3(gim[:, 0:N]), tab(sr, m), ALU.mult, ['gim', 'sr'], ['t2'])
                        yield
                        b.tt('pool', hreb[:, 0:N], t1[:, 0:N], t2[:, 0:N], ALU.subtract, ['t1', 't2'], ['hreb'])
                        yield
                        lastv = lambda ap: v3(ap[:, 0:N])[:, :, Ls - 1]
                        if samp:
                            cols = slice(NB, NB + DB)
                        else:
                            cols = slice(T['seq'], T['seq'] + 1)
                        b.tt('dve', hst[:, 0, m, 0:nseq], lastv(t1), lastv(t2), ALU.subtract, ['t1', 't2'], ['hst'])
                        yield
                        b.tt('dve', v3(t1[:, 0:N]), v3(gre[:, 0:N]), tab(sr, m), ALU.mult, ['gre', 'sr'], ['t1'])
                        yield
                        b.tt('dve', v3(t2[:, 0:N]), v3(gim[:, 0:N]), tab(cr, m), ALU.mult, ['gim', 'cr'], ['t2'])
                        yield
                        b.tt('pool', himb[:, 0:N], t1[:, 0:N], t2[:, 0:N], ALU.add, ['t1', 't2'], ['himb'])
                        yield
                        b.tt('dve', hst[:, 1, m, 0:nseq], lastv(t1), lastv(t2), ALU.add, ['t1', 't2'], ['hst'])
                        yield
                        if T['last']:
                            fre, fim, nfim = (s5c[:, m, 5:6], s5c[:, m, 6:7], s5c[:, m, 7:8])
                            b.ts('dve', t1[:, 0:nseq], hst[:, 1, m, 0:nseq], nfim, ALU.mult, ['hst', 's5c'], ['t1'])
                            yield
                            b.stt(hfin[:, 0, m, cols], hst[:, 0, m, 0:nseq], fre, t1[:, 0:nseq], ALU.mult, ALU.add, ['hst', 's5c', 't1'], ['hfin'])
                            yield
                            b.ts('dve', t1[:, 0:nseq], hst[:, 1, m, 0:nseq], fre, ALU.mult, ['hst', 's5c'], ['t1'])
                            yield
                            b.stt(hfin[:, 1, m, cols], hst[:, 0, m, 0:nseq], fim, t1[:, 0:nseq], ALU.mult, ALU.add, ['hst', 's5c', 't1'], ['hfin'])
                            yield
                        b.mm(pY[:, 0:N], Cb[:, 0, m, :], hreb[:, 0:N], ['Cb', 'hreb'], [YK], start=m == 0, stop=False)
                        yield
                        b.mm(pY[:, 0:N], Cb[:, 1, m, :], himb[:, 0:N], ['Cb', 'himb'], [YK], start=False, stop=m == 3)
                        yield
                    b.stt(ys32[:, 0:N], u32[:, 0:N], sp_[:, 0, 3:4], pY[:, 0:N], ALU.mult, ALU.add, ['u32', 'sp_', YK], ['ys32'])
                    yield
                    b.tt('pool', t1[:, 0:N], ys32[:, 0:N], ys32[:, 0:N], ALU.mult, ['ys32'], ['t1'])
                    yield
                    b.ts('dve', t1[:, 0:N], t1[:, 0:N], 0.044715, ALU.mult, ['t1'], ['t1'], s2=1.0, op1=ALU.add)
                    yield
                    b.tt('pool', t1[:, 0:N], t1[:, 0:N], ys32[:, 0:N], ALU.mult, ['t1', 'ys32'], ['t1'])
                    yield
                    b.act(t1[:, 0:N], t1[:, 0:N], AF.Sigmoid, ['t1'], ['t1'], scale=2.0 * math.sqrt(2.0 / math.pi))
                    yield
                    b.tt('dve', ysb[:, 0:N], t1[:, 0:N], ys32[:, 0:N], ALU.mult, ['t1', 'ys32'], ['ysb'])
                    yield
                    b.dma('sp', y3T[128:256, T['tok0']:T['tok0'] + N], ysb[:, 0:N], ['ysb'], ['y3T'], 'ys_o')
                    yield
                    yield
                _interleave(g_att(), g_s5())
        except _Stop:
            pass
        b.dma("sp", hout, hfin[:], ["hfin"], ["hout"], "hout")
        P.finalize()
        print("phase3 instrs", P.ninstr, {e: P.cnt[e] for e in ENGS})
    return nc


def _p3_inputs(inp, x1T, c):
    A = np.ascontiguousarray
    W = inp['od_w_in'][0]
    blk = lambda i: W[:, i * 1024 + c * 128: i * 1024 + (c + 1) * 128]
    w3 = np.concatenate([blk(0), blk(1), blk(2), blk(3), blk(4), blk(5)], 1)
    kc = inp['cache_sb_k'][0, :, :, c, :]
    vc = inp['cache_sb_v'][0, :, :, c, :]
    kcT = A(kc.reshape(DB, 32, 128, 128).transpose(1, 3, 0, 2))
    vcc = A(vc.reshape(DB, 32, 128, 128).transpose(1, 2, 0, 3))
    G = slice(8 * c, 8 * c + 8)
    lre = inp['od_lam_re'][0, G]
    lim = inp['od_lam_im'][0, G]
    dt = np.broadcast_to(inp['od_log_dt'][0, G][:, None], (8, 64))
    s5p = np.zeros((128, 4, 4), np.float32)
    qm = lambda a: a.reshape(4, 128).T
    s5p[:, :, 0] = qm(lre)
    s5p[:, :, 1] = qm(lim)
    s5p[:, :, 2] = qm(dt)
    s5p[:, 0, 3] = inp['od_d'][0, c * 128:(c + 1) * 128]
    Bbd = np.zeros((128, 2, 4, 128), np.float32)
    Cbd = np.zeros((128, 2, 4, 128), np.float32)
    for a, (bk, ck) in enumerate((('od_b_re', 'od_c_re'), ('od_b_im', 'od_c_im'))):
        bb = inp[bk][0, G]
        cc = inp[ck][0, G]
        for g in range(8):
            m, g2 = g // 2, g % 2
            Bbd[g * 16:(g + 1) * 16, a, m, g2 * 64:(g2 + 1) * 64] = bb[g].T
            Cbd[g2 * 64:(g2 + 1) * 64, a, m, g * 16:(g + 1) * 16] = cc[g].T
    h0 = np.zeros((128, 2, 4, DB), np.float32)
    for a, k in enumerate(('state_s5_re', 'state_s5_im')):
        hs = inp[k][0, :, G]
        h0[:, a] = hs.reshape(DB, 4, 128).transpose(2, 1, 0)
    return dict(x1T=x1T, w3=A(w3), consts=_consts_np(), masks=np.ascontiguousarray(_masks3_np()[:, 0:5]).astype(ml_dtypes.bfloat16), tau=np.ascontiguousarray(_masks3_np()[:, 5]), kcT=kcT, vcc=vcc, s5p=s5p,
                Bbd=Bbd, Cbd=Cbd, h0=h0)


def _p4_inputs(inp, x1tok, Y3, c):
    rep = lambda v: np.ascontiguousarray(np.broadcast_to(v[None, :], (128, v.shape[0])))
    return dict(yin=_tile_tokens(Y3, c), wout=np.ascontiguousarray(inp['od_w_out'][0]), xres=_pad_rows(x1tok, c),
                lng=rep(inp['od_ln_g'][0]), lnb=rep(inp['od_ln_b'][0]), consts=_consts_np(),
                wglu=np.ascontiguousarray(inp['od_w_glu'][0]),
                bglu=np.ascontiguousarray(inp['od_b_glu'][0].reshape(8, 128).T))


_NC_CACHE = {}


def _get_nc(name, fn):
    if name not in _NC_CACHE:
        _NC_CACHE[name] = fn()
    return _NC_CACHE[name]


def kernel(**inp):
    inp = {k: np.asarray(v) for k, v in inp.items()}
    R = range(NCORES)
    xtok = _tok_major(inp['x_prompt'], inp['x_sample'])
    xT = np.ascontiguousarray(xtok.T)
    r1 = _run(_get_nc("p1", build_phase1), [_p1_inputs(inp, xT, c) for c in R])
    del xT
    Y = np.stack([r1[k]['yT'][0:128] for k in R] + [r1[k]['yT'][128:256] for k in R])
    r2 = _run(_get_nc("p2", lambda: build_out_phase(False, True)), [_p2_inputs(inp, xtok, Y, c) for c in R])
    x1tok = np.concatenate([r2[c]['xo'][:TOKC] for c in R], 0)
    x1T = np.ascontiguousarray(
        np.concatenate([r2[c]['xoT'].transpose(2, 1, 0, 3).reshape(D, -1)[:, :TOKC] for c in R], 1))
    r3 = _run(_get_nc("p3", build_phase3), [_p3_inputs(inp, x1T, c) for c in R])
    Y3 = np.stack([r3[k]['y3T'][0:128] for k in R] + [r3[k]['y3T'][128:256] for k in R]
                  + [r3[k]['y3T'][256:384] for k in R])
    r4 = _run(_get_nc("p4", lambda: build_out_phase(True, False)), [_p4_inputs(inp, x1tok, Y3, c) for c in R])
    ytok = np.concatenate([r4[c]['xo'][:TOKC] for c in R], 0)
    f32 = np.float32
    y_prompt = ytok[:NTP].reshape(NB, SEQ, D).astype(f32)
    y_sample = ytok[NTP:].reshape(DB, DS, D).astype(f32)

    def heads(key, sl):
        return np.stack([r1[c][key][sl] for c in R], 1)[None].astype(f32)
    pS, sS = slice(0, NB), slice(NB, NB + DB)
    dS_p, dS_s = heads('oS', pS), heads('oS', sS)
    mC_p, mC_s = heads('oC', pS), heads('oC', sS)

    def convout(sl):
        n = sl.stop - sl.start
        o = np.zeros((1, n, 3, 3072), f32)
        for c in R:
            oc = r1[c]['oconv'][:, sl]
            for f in range(3):
                o[0, :, :, f * 1024 + c * 128: f * 1024 + (c + 1) * 128] = oc[:, :, f, :].transpose(1, 2, 0)
        return o
    dc_p, dc_s = convout(pS), convout(sS)
    mn = lambda sl: np.stack([r1[c]['on'][:, sl].T for c in R], 1)[None].astype(f32)
    mm = lambda sl: np.stack([r1[c]['om'][0, sl] for c in R], 1)[None].astype(f32)

    def kv(key, lo, hi, shape):
        return np.stack([r3[c][key][lo:hi] for c in R], 1).reshape(shape).astype(f32)
    sk_p = kv('kout', 0, NTP, (NB, SEQ, 8, 128))[None]
    sk_s = kv('kout', NTP, NTOK, (DB, DS, 8, 128))[None]
    sv_p = kv('vout', 0, NTP, (NB, SEQ, 8, 128))[None]
    sv_s = kv('vout', NTP, NTOK, (DB, DS, 8, 128))[None]

    def s5state(a, sl):
        n = sl.stop - sl.start
        o = np.zeros((1, n, 64, 64), f32)
        for c in R:
            ho = r3[c]['hout'][:, a, :, sl]
            o[0, :, 8 * c:8 * c + 8, :] = ho.transpose(2, 1, 0).reshape(n, 8, 64)
        return o
    return (y_prompt, y_sample, dS_p, dS_s, dc_p, dc_s, mC_p, mC_s, mn(pS), mn(sS), mm(pS), mm(sS),
            sk_p, sk_s, sv_p, sv_s, s5state(0, pS), s5state(0, sS), s5state(1, pS), s5state(1, sS))
```

```python
import math
from contextlib import ExitStack

import numpy as np
import ml_dtypes
import concourse.bass as bass
import concourse.mybir as mybir
from concourse.bass_utils import run_bass_kernel_spmd

F32 = mybir.dt.float32
BF16 = mybir.dt.bfloat16
AF = mybir.ActivationFunctionType
ALU = mybir.AluOpType
AX = mybir.AxisListType
ENGS = ("pe", "act", "dve", "pool", "sp")
NCORES = 8

D = 2048
KC = D // 128
SEQ = 8192
NB = 2
DB = 8
DS = 32
PAST = 4096
NTP = NB * SEQ
NTS = DB * DS
NTOK = NTP + NTS
TOKC = NTOK // NCORES
ALPHA = (2 * 2) ** 0.25
NEG = -30000.0
SAME_ENG_GAP = 10 ** 9
F32R = mybir.dt.float32r


class Prog:
    def __init__(self, nc, stack):
        self.nc = nc
        self.stack = stack
        self.q = {e: [] for e in ENGS}
        self.cnt = {e: 0 for e in ENGS}
        self.sem = {e: stack.enter_context(nc.semaphore("sem_" + e)) for e in ENGS}
        self.waited = {e: {} for e in ENGS}
        self.lastw = {}
        self.readers = {}
        self.dsem = {}
        self.dcnt = {}
        self.ninstr = 0

    def _deps(self, reads, writes):
        deps = []
        for k in reads:
            t = self.lastw.get(k)
            if t is not None:
                deps.append(t)
        for k in writes:
            t = self.lastw.get(k)
            if t is not None:
                deps.append(t)
            deps.extend(self.readers.get(k, ()))
        return deps

    def _emit_waits(self, eng, deps, skip_self=False):
        need = {}
        for (sid, sem, val) in deps:
            if skip_self and sid == eng:
                continue
            if sid == eng and self.cnt[eng] - val >= SAME_ENG_GAP:
                continue
            if self.waited[eng].get(sid, 0) >= val:
                continue
            if need.get(sid, (None, 0))[1] < val:
                need[sid] = (sem, val)
        for sid, (sem, val) in need.items():
            self.waited[eng][sid] = val
            self.q[eng].append(lambda e, sem=sem, val=val: e.wait_ge(sem, val))
            self.ninstr += 1

    def _commit(self, tok, reads, writes):
        for k in writes:
            self.lastw[k] = tok
            self.readers[k] = []
        for k in reads:
            self.readers.setdefault(k, []).append(tok)

    @staticmethod
    def _bank(k):
        if isinstance(k, str) and len(k) >= 2 and k[0] == "p" and (k[1].isupper() or k[:2] in ("pj", "pg")):
            return k[:3] if k[:2] == "pj" else k[:2]
        return None

    def _norm(self, reads, writes):
        r2, w2 = [], []
        for k in reads:
            bk = self._bank(k)
            if bk is None:
                r2.append(k)
            elif bk not in w2:
                w2.append(bk)
        for k in writes:
            bk = self._bank(k)
            k2 = k if bk is None else bk
            if k2 not in w2:
                w2.append(k2)
        return r2, w2

    def op(self, eng, fn, reads=(), writes=(), skip_self=False):
        reads, writes = self._norm(reads, writes)
        deps = self._deps(reads, writes)
        self._emit_waits(eng, deps, skip_self=skip_self)
        self.cnt[eng] += 1
        sem = self.sem[eng]
        tok = (eng, sem, self.cnt[eng])
        self.q[eng].append(lambda e, fn=fn, sem=sem: fn(e).then_inc(sem, 1))
        self.ninstr += 1
        self._commit(tok, reads, writes)
        return tok

    def dma(self, eng, out, in_, reads=(), writes=(), slot=None, **kw):
        assert slot is not None
        if slot not in self.dsem:
            self.dsem[slot] = self.stack.enter_context(self.nc.semaphore("dsem_%d" % len(self.dsem)))
            self.dcnt[slot] = 0
        deps = self._deps(reads, writes)
        self._emit_waits(eng, deps)
        self.dcnt[slot] += 16
        sem = self.dsem[slot]
        tok = (("d", slot), sem, self.dcnt[slot])
        self.q[eng].append(
            lambda e, out=out, in_=in_, sem=sem, kw=kw: e.dma_start(out=out, in_=in_, **kw).then_inc(sem, 16))
        self.ninstr += 1
        self._commit(tok, reads, writes)
        return tok

    def finalize(self):
        nc = self.nc
        alltok = list(self.lastw.values())
        for ts in self.readers.values():
            alltok.extend(ts)
        self._emit_waits("sp", alltok)
        engmap = {"pe": "tensor", "act": "scalar", "dve": "vector", "pool": "gpsimd", "sp": "sync"}
        with nc.Block() as block:
            for e in ENGS:
                lst = self.q[e]
                if not lst:
                    continue

                def body(engine, lst=lst):
                    for f in lst:
                        f(engine)
                getattr(block, engmap[e])(body)


class B:
    def __init__(self, nc, st):
        self.nc = nc
        self.st = st
        self.P = Prog(nc, st)
        self.rr = 0

    def sb(self, name, shape, dt=F32):
        return self.st.enter_context(self.nc.sbuf_tensor(name, list(shape), dt))

    def ps(self, name, shape=(128, 512), dt=F32):
        return self.st.enter_context(self.nc.psum_tensor(name, list(shape), dt))

    def din(self, name, shape, dt=F32):
        return self.nc.dram_tensor(name, list(shape), dt, kind="ExternalInput").ap()

    def dout(self, name, shape, dt=F32):
        return self.nc.dram_tensor(name, list(shape), dt, kind="ExternalOutput").ap()

    def mm(self, out, lhsT, rhs, r, w, start=True, stop=True):
        self.P.op("pe", lambda e: e.matmul(out, lhsT=lhsT, rhs=rhs, start=start, stop=stop, skip_group_check=True),
                  reads=r, writes=w, skip_self=True)

    def mmr(self, out, lhsT, rhs, r, w, start=True, stop=True):
        self.mm(out, lhsT.bitcast(F32R), rhs.bitcast(F32R), r, w, start=start, stop=stop)

    def tr(self, out, in_, ident, r, w):
        self.P.op("pe", lambda e: e.transpose(out=out, in_=in_, identity=ident), reads=r, writes=w, skip_self=True)

    def act(self, out, in_, func, r, w, bias=None, scale=None, accum_out=None):
        kw = {}
        if bias is not None:
            kw["bias"] = bias
        if scale is not None:
            kw["scale"] = scale
        if accum_out is not None:
            kw["accum_out"] = accum_out
        self.P.op("act", lambda e: e.activation(out=out, in_=in_, func=func, **kw), reads=r, writes=w)

    def ts(self, eng, out, in0, s1, op0, r, w, s2=None, op1=None, accum_out=None):
        kw = {}
        if op1 is not None:
            kw["op1"] = op1
        if accum_out is not None:
            kw["accum_out"] = accum_out
        self.P.op(eng, lambda e: e.tensor_scalar(out=out, in0=in0, scalar1=s1, scalar2=s2, op0=op0, **kw),
                  reads=r, writes=w)

    def tt(self, eng, out, in0, in1, op, r, w):
        self.P.op(eng, lambda e: e.tensor_tensor(out=out, in0=in0, in1=in1, op=op), reads=r, writes=w)

    def stt(self, out, in0, scalar, in1, op0, op1, r, w, accum_out=None):
        kw = {}
        if accum_out is not None:
            kw["accum_out"] = accum_out
        self.P.op("dve", lambda e: e.scalar_tensor_tensor(out=out, in0=in0, scalar=scalar, in1=in1, op0=op0, op1=op1, **kw),
                  reads=r, writes=w)

    def cp(self, eng, out, in_, r, w):
        if eng == "act":
            self.P.op("act", lambda e: e.copy(out=out, in_=in_), reads=r, writes=w)
        else:
            self.P.op(eng, lambda e: e.tensor_copy(out=out, in_=in_), reads=r, writes=w)

    def memset(self, eng, ap, val, w):
        self.P.op(eng, lambda e: e.memset(ap, val), writes=w)

    def dma(self, eng, out, in_, r, w, slot, **kw):
        self.P.dma(eng, out, in_, reads=r, writes=w, slot=slot, **kw)


def _consts_np():
    c = np.zeros((128, 8, 128), np.float32)
    i = np.arange(128)
    c[:, 0, :] = np.eye(128)
    c[:, 1, :] = (i[:, None] <= i[None, :])
    c[:, 2, :] = np.where(i[None, :] < i[:, None], NEG, 0.0)
    c[:, 3, :] = np.where(i[None, :] > i[:, None], NEG, 0.0)
    c[:, 4, :] = 1.0
    c[:, 5, :] = (i[:, None] == 127)
    c[:, 6, :] = (i[:, None] == 31)
    c[:, 7, :] = (i[:, None] >= i[None, :])
    return c


class _Stop(Exception):
    pass


def _interleave(*gens):
    gens = list(gens)
    while gens:
        for g in list(gens):
            try:
                next(g)
            except StopIteration:
                gens.remove(g)


def build_phase1(ntiles_prompt=32, do_sample=True, debug=0):
    nc = bass.Bass("TRN2", target_bir_lowering=False)
    with ExitStack() as st:
        b = B(nc, st)
        P = b.P
        xT = b.din("xT", [D, NTOK])
        w1 = b.din("w1", [D, 1152])
        wg = b.din("wg", [128, KC * 4])
        convw = b.din("convw", [128, 12])
        scal = b.din("scal", [128, 8])
        consts = b.din("consts", [128, 8, 128])
        S0 = b.din("S0", [DB, 128, 128])
        conv0 = b.din("conv0", [128, DB, 3, 3])
        C0 = b.din("C0", [DB, 128, 128])
        n0 = b.din("n0", [128, DB])
        m0 = b.din("m0", [128, DB])
        yT = b.dout("yT", [256, NTOK], BF16)
        oS = b.dout("oS", [NB + DB, 128, 128])
        oconv = b.dout("oconv", [128, NB + DB, 3, 3])
        oC = b.dout("oC", [NB + DB, 128, 128])
        on = b.dout("on", [128, NB + DB])
        om = b.dout("om", [1, NB + DB])

        cst = b.sb("cst", [128, 8, 128])
        ident, triU, NEGU, NEGL, ones, SEL127, SEL31 = (cst[:, i, :] for i in range(7))
        w1b = b.sb("w1b", [128, KC, 1152], BF16)
        wgb = b.sb("wgb", [128, KC, 4], BF16)
        wst = b.sb("wst", [128, 2, 1152])
        cw = b.sb("cw", [128, 12])
        sc = b.sb("sc", [128, 8])
        dsc = b.sb("dsc", [128, 8])
        xst = [b.sb("xst%d" % i, [128, 4, 512]) for i in range(2)]
        xb = [b.sb("xb%d" % i, [128, KC, 512], BF16) for i in range(2)]
        cin = [b.sb("cin%d" % f, [128, 520]) for f in range(3)]
        PR = [[(b.sb("pr%d_%d" % (f, q), [128, 512]) if f >= 3 else None) for f in range(9)] for q in range(2)]
        CV = [[b.sb("cv%d_%d" % (f, q), [128, 512]) for f in range(3)] for q in range(2)]
        tmpA = b.sb("tmpA", [128, 512])
        tmpB = b.sb("tmpB", [128, 512])
        QN = [b.sb("qn_%d" % q, [128, 512]) for q in range(2)]
        KN = [b.sb("kn_%d" % q, [128, 512]) for q in range(2)]
        tmpC = b.sb("tmpC", [128, 512])
        tmpD = b.sb("tmpD", [128, 512])
        oT = b.sb("oT", [128, 512])
        hT = b.sb("hT", [128, 512])
        yb = b.sb("yb", [128, 2, 512], BF16)
        GAT = [b.sb("gat_%d" % q, [128, 8, 12]) for q in range(2)]
        GB = []
        for k in range(2):
            GB.append({n: b.sb("%s_%d" % (n, k), [128, 128]) for n in
                       ("Gbc", "grow", "DEC", "DECs", "X", "XT", "IXT", "Pm", "QKd", "kg", "kdec", "vtok", "nW", "qdec")})
            GB[k]["colg"] = b.sb("colg_%d" % k, [128, 4])
        ee = b.sb("ee", [128, 128])
        Sst = b.sb("Sst", [128, 128])
        col = b.sb("col", [128, 16])
        mprev = b.sb("mprev", [128, 1])
        bc2 = b.sb("bc2", [128, 2])
        c2 = b.sb("c2", [128, 2])
        Abc = b.sb("Abc", [128, 128])
        pmat = b.sb("pmat", [128, 128])
        pqk = b.sb("pqk", [128, 128])
        pqkT = b.sb("pqkT", [128, 128])
        v1 = b.sb("v1", [128, 129])
        kp = b.sb("kp", [128, 128])
        CN = b.sb("CN", [128, 129])
        n2s = b.sb("n2s", [128, 129])
        num = b.sb("num", [128, 128])
        htok = b.sb("htok", [128, 128])
        m0s = b.sb("m0s", [128, DB])
        n0s = b.sb("n0s", [128, DB])
        c0s = b.sb("c0s", [128, DB * 9])

        pj = [b.ps("pj%d" % i) for i in range(2)]
        pg = b.ps("pg")
        pA = b.ps("pA")
        pB = b.ps("pB")
        pC = b.ps("pC")
        pD = b.ps("pD")
        pE = b.ps("pE")

        b.dma("sp", cst[:], consts, [], ["cst"], "cst")
        b.dma("sp", cw[:], convw, [], ["cw"], "cw")
        b.dma("sp", sc[:], scal, [], ["sc"], "sc")
        b.dma("sp", m0s[:], m0, [], ["m0s"], "m0s")
        b.dma("sp", n0s[:], n0, [], ["n0s"], "n0s")
        b.dma("sp", c0s[:], conv0.rearrange("p b f r -> p (b f r)"), [], ["c0s"], "c0s")
        w1v = w1.rearrange("(kc p) n -> p kc n", p=128)
        for kc0 in range(0, KC, 2):
            b.dma("sp", wst[:, :, :], w1v[:, kc0:kc0 + 2, :], [], ["wst"], "wst")
            b.cp("dve", w1b[:, kc0, :], wst[:, 0, :], ["wst"], ["w1b"])
            b.cp("pool", w1b[:, kc0 + 1, :], wst[:, 1, :], ["wst"], ["w1b"])
        b.dma("sp", wst[:, 0, 0:KC * 4], wg, [], ["wst"], "wst")
        b.cp("dve", wgb[:].rearrange("p kc n -> p (kc n)"), wst[:, 0, 0:KC * 4], ["wst"], ["wgb"])
        b.act(dsc[:, 0:1], sc[:, 0:1], AF.Exp, ["sc"], ["dsc"])
        b.ts("dve", dsc[:, 0:1], dsc[:, 0:1], -1.0, ALU.mult, ["dsc"], ["dsc"])
        b.ts("dve", dsc[:, 1:2], sc[:, 3:4], -1.0, ALU.mult, ["sc"], ["dsc"])
        b.memset("pool", v1[:, 128:129], 1.0, ["v1"])

        xTv = xT.rearrange("(kc p) n -> p kc n", p=128)

        tiles = []
        for bi in range(NB):
            for t in range(ntiles_prompt // NB):
                tiles.append(dict(kind="p", seq=bi, tok0=bi * SEQ + t * 512, N=512, L=128, nch=4,
                                  first=(t == 0), last=(t == ntiles_prompt // NB - 1)))
        if do_sample:
            tiles.append(dict(kind="s", seq=None, tok0=NTP, N=256, L=32, nch=8, first=True, last=True))

        def load_x(ti, slot):
            T = tiles[ti]
            N = T["N"]
            for q4 in range(4):
                b.dma("sp", xst[q4 % 2][:, :, 0:N], xTv[:, q4 * 4:(q4 + 1) * 4, T["tok0"]:T["tok0"] + N],
                      [], ["xst%d" % (q4 % 2)], "xst%d" % (q4 % 2))
                b.cp("pool", xb[slot][:, q4 * 4:(q4 + 1) * 4, 0:N], xst[q4 % 2][:, :, 0:N],
                     ["xst%d" % (q4 % 2)], ["xb%d" % slot])

        load_x(0, 0)
        if len(tiles) > 1:
            load_x(1, 1)
        def g_P(ti):
            T = tiles[ti]
            slot = ti % 2
            pset = ti % 2
            N, L, nch = (T['N'], T['L'], T['nch'])
            xbk = 'xb%d' % slot
            KY = lambda n: '%s_%d' % (n, pset)
            pr, cv, qn, kn, gat = (PR[pset], CV[pset], QN[pset], KN[pset], GAT[pset])
            nseq = 1 if T['kind'] == 'p' else DB
            Ls = N // nseq
            W = Ls + 3

            def cin3(f, a, bnd):
                return cin[f][:, 0:nseq * W].rearrange('p (s l) -> p s l', l=W)[:, :, a:bnd]
            if T['first']:
                if T['kind'] == 'p':
                    for f in range(3):
                        b.memset('pool', cin[f][:, 0:3], 0.0, ['cin%d' % f])
                        yield
                else:
                    for f in range(3):
                        b.cp('pool', cin3(f, 0, 3), c0s[:].rearrange('p (b f r) -> p b f r', f=3, r=3)[:, :, f, :], ['c0s', 'cin%d' % f], ['cin%d' % f])
                        yield
            for f in range(9):
                bank = pj[f % 2]
                bk = 'pj%d' % (f % 2)
                for kc in range(KC):
                    b.mm(bank[:, 0:N], w1b[:, kc, f * 128:(f + 1) * 128], xb[slot][:, kc, 0:N], ['w1b', xbk], [bk], start=kc == 0, stop=kc == KC - 1)
                yield
                if f < 3:
                    b.cp('act', cin3(f, 3, W), bank[:, 0:N].rearrange('p (s l) -> p s l', l=Ls), [bk], ['cin%d' % f])
                    yield
                elif f == 5:
                    b.act(pr[f][:, 0:N], bank[:, 0:N], AF.Copy, [bk], [KY('pr%d' % f)], scale=128.0 ** (-0.5))
                    yield
                else:
                    b.cp('dve' if f % 2 else 'act', pr[f][:, 0:N], bank[:, 0:N], [bk], [KY('pr%d' % f)])
                    yield
            for j in range(nch):
                for kc in range(KC):
                    b.mm(pg[0:L, j * 4:(j + 1) * 4], xb[slot][:, kc, j * L:(j + 1) * L], wgb[:, kc, :], [xbk, 'wgb'], ['pg_g'], start=kc == 0, stop=kc == KC - 1)
                yield
            G = pg[0:L, 0:nch * 4].rearrange('p (j g) -> p j g', g=4)
            gt = lambda i: gat[0:L, 0:nch, i]
            b.act(gt(0), G[:, :, 0], AF.Sigmoid, ['pg_g'], [KY('gat')])
            yield
            b.ts('dve', gt(1), gt(0), -1.0, ALU.mult, [KY('gat')], [KY('gat')])
            yield
            b.act(gt(9), G[:, :, 1], AF.Exp, ['pg_g', 'sc'], [KY('gat')], bias=sc[0:L, 1:2])
            yield
            b.act(gt(9), gt(9), AF.Ln, [KY('gat')], [KY('gat')], bias=1.0)
            yield
            b.ts('dve', gt(2), gt(9), dsc[0:L, 0:1], ALU.mult, [KY('gat'), 'dsc'], [KY('gat')])
            yield
            b.ts('dve', gt(3), G[:, :, 2], sc[0:L, 2:3], ALU.add, ['pg_g', 'sc'], [KY('gat')])
            yield
            b.act(gt(10), G[:, :, 3], AF.Exp, ['pg_g', 'dsc'], [KY('gat')], bias=dsc[0:L, 1:2], scale=-1.0)
            yield
            b.act(gt(10), gt(10), AF.Ln, [KY('gat')], [KY('gat')], bias=1.0)
            yield
            b.ts('dve', gt(4), gt(10), -1.0, ALU.mult, [KY('gat')], [KY('gat')])
            yield
            gtmp = tmpA[0:L, 0:2 * nch]
            b.cp('dve', gtmp[:, 0:nch], gt(2), [KY('gat')], ['tmpA'])
            yield
            b.cp('dve', gtmp[:, nch:2 * nch], gt(4), [KY('gat')], ['tmpA'])
            yield
            b.mm(pg[0:L, 32:32 + 2 * nch], triU[0:L, 0:L], gtmp, ['cst', 'tmpA'], ['pg_c'])
            yield
            b.cp('dve', gt(5), pg[0:L, 32:32 + nch], ['pg_c'], [KY('gat')])
            yield
            b.cp('dve', gt(7), pg[0:L, 32 + nch:32 + 2 * nch], ['pg_c'], [KY('gat')])
            yield
            b.act(gt(6), gt(5), AF.Exp, [KY('gat')], [KY('gat')])
            yield
            b.ts('dve', gt(11), gt(5), -1.0, ALU.mult, [KY('gat')], [KY('gat')])
            yield
            b.tt('dve', gt(8), gt(3), gt(7), ALU.subtract, [KY('gat')], [KY('gat')])
            yield
            for f in range(3):
                ck = 'cin%d' % f
                o3 = tmpA[:, 0:N].rearrange('p (s l) -> p s l', l=Ls)
                b.ts('dve', o3, cin3(f, 0, Ls), cw[:, f * 4:f * 4 + 1], ALU.mult, [ck, 'cw'], ['tmpA'])
                yield
                for i in range(1, 4):
                    b.stt(o3, cin3(f, i, i + Ls), cw[:, f * 4 + i:f * 4 + i + 1], o3, ALU.mult, ALU.add, [ck, 'cw', 'tmpA'], ['tmpA'])
                    yield
                b.act(cv[f][:, 0:N], tmpA[:, 0:N], AF.Silu, ['tmpA'], [KY('cv%d' % f)])
                yield
                if T['last']:
                    if T['kind'] == 'p':
                        b.dma('sp', oconv[:, T['seq'], f, :], cin[f][:, Ls:Ls + 3], [ck], ['oconv'], 'oconv%d' % f)
                        yield
                    else:
                        b.dma('sp', oconv[:, NB:NB + DB, f, :], cin3(f, Ls, Ls + 3), [ck], ['oconv'], 'oconv%d' % f)
                        yield
                else:
                    b.cp('pool', cin[f][:, 0:3], cin[f][:, Ls:Ls + 3], [ck], [ck])
                    yield
            for f, dst, dk, lnb in ((0, qn, KY('qn'), -0.5 * math.log(128.0)), (1, kn, KY('kn'), 0.0)):
                b.tt('pool', tmpB[:, 0:N], cv[f][:, 0:N], cv[f][:, 0:N], ALU.mult, [KY('cv%d' % f)], ['tmpB'])
                yield
                b.mm(pj[f][:, 0:N], ones, tmpB[:, 0:N], ['cst', 'tmpB'], ['pj%d' % f])
                yield
                b.act(tmpB[:, 0:N], pj[f][:, 0:N], AF.Ln, ['pj%d' % f], ['tmpB'], bias=1e-06)
                yield
                b.act(tmpB[:, 0:N], tmpB[:, 0:N], AF.Exp, ['tmpB'], ['tmpB'], scale=-0.5, bias=lnb)
                yield
                b.tt('dve', dst[:, 0:N], cv[f][:, 0:N], tmpB[:, 0:N], ALU.mult, [KY('cv%d' % f), 'tmpB'], [dk])
                yield
            yield

        def chunk_gens(ti):
            T = tiles[ti]
            slot = ti % 2
            pset = ti % 2
            N, L, nch = (T['N'], T['L'], T['nch'])
            xbk = 'xb%d' % slot
            KY = lambda n: '%s_%d' % (n, pset)
            pr, cv, qn, kn, gat = (PR[pset], CV[pset], QN[pset], KN[pset], GAT[pset])
            I_L = ident[0:L, 0:L]
            NK = 2
            pre_done = [False] * nch
            seq_done = [False] * nch
            PBK = [(pA, 'pA0'), (pB, 'pB0')]

            def pre_chain(k):
                G_ = GB[k]
                bk, bkey = PBK[k]
                kx = lambda n: '%s_%d' % (n, k)
                Gbc, grow, DEC, DECs, X, XT, IXT, Pm = (G_[n] for n in ('Gbc', 'grow', 'DEC', 'DECs', 'X', 'XT', 'IXT', 'Pm'))
                QKd, kg, kdec, vtok, nW, qdec, colg = (G_[n] for n in ('QKd', 'kg', 'kdec', 'vtok', 'nW', 'qdec', 'colg'))
                R0, R1, R2, R3 = (bk[:, i * 128:(i + 1) * 128] for i in range(4))
                for j in range(k, nch, NK):
                    if j >= NK:
                        while not seq_done[j - NK]:
                            yield
                    c0, c1 = (j * L, (j + 1) * L)
                    gcol = gat[0:L, j, 2:3]
                    b.ts('dve', Gbc[0:L, :], ones[0:L, :], gcol, ALU.mult, ['cst', KY('gat')], [kx('Gbc')])
                    yield
                    b.mm(R0[:, 0:L], Gbc[0:L, :], triU[0:L, 0:L], [kx('Gbc'), 'cst'], [bkey])
                    yield
                    b.mm(R1[0:L, 0:L], Gbc[0:L, 0:L], triU[0:L, 0:L], [kx('Gbc'), 'cst'], [bkey], start=True, stop=False)
                    yield
                    b.mm(R1[0:L, 0:L], I_L, NEGU[0:L, 0:L], ['cst'], [bkey], start=False, stop=True)
                    yield
                    b.mm(R2[0:L, 0:L], kn[:, c0:c1], kn[:, c0:c1], [KY('kn')], [bkey])
                    yield
                    b.mm(R3[0:L, 0:L], kn[:, c0:c1], qn[:, c0:c1], [KY('kn'), KY('qn')], [bkey])
                    yield
                    b.act(grow[:, 0:L], R0[:, 0:L], AF.Exp, [bkey], [kx('grow')])
                    yield
                    b.act(DEC[0:L, 0:L], R1[0:L, 0:L], AF.Exp, [bkey, KY('gat')], [kx('DEC')], bias=gat[0:L, j, 11:12])
                    yield
                    b.cp('dve', colg[:, 0:1], R0[:, L - 1:L], [bkey], [kx('col0')])
                    yield
                    b.act(colg[:, 1:2], colg[:, 0:1], AF.Exp, [kx('col0')], [kx('col1')])
                    yield
                    b.act(colg[0:L, 2:3], gat[0:L, j, 5:6], AF.Exp, [KY('gat'), kx('col0')], [kx('col2')], scale=-1.0, bias=colg[0:L, 0:1])
                    yield
                    b.tt('pool', DECs[0:L, 0:L], DEC[0:L, 0:L], I_L, ALU.subtract, [kx('DEC'), 'cst'], [kx('DECs')])
                    yield
                    b.stt(X[0:L, 0:L], R2[0:L, 0:L], gat[0:L, j, 1:2], DECs[0:L, 0:L], ALU.mult, ALU.mult, [bkey, KY('gat'), kx('DECs')], [kx('X')])
                    yield
                    b.tt('dve', QKd[0:L, 0:L], R3[0:L, 0:L], DEC[0:L, 0:L], ALU.mult, [bkey, kx('DEC')], [kx('QKd')])
                    yield
                    b.tt('pool', qdec[:, 0:L], qn[:, c0:c1], grow[:, 0:L], ALU.mult, [KY('qn'), kx('grow')], [kx('qdec')])
                    yield
                    b.tr(R0[0:L, 0:128], kn[:, c0:c1], ident, [KY('kn'), 'cst'], [bkey])
                    yield
                    b.tr(R1[0:L, 0:128], cv[2][:, c0:c1], ident, [KY('cv2'), 'cst'], [bkey])
                    yield
                    b.tr(R2[0:L, 0:L], X[0:L, 0:L], I_L, [kx('X'), 'cst'], [bkey])
                    yield
                    b.ts('dve', kg[0:L, :], R0[0:L, 0:128], gat[0:L, j, 6:7], ALU.mult, [bkey, KY('gat')], [kx('kg')])
                    yield
                    b.ts('dve', kdec[0:L, :], R0[0:L, 0:128], colg[0:L, 2:3], ALU.mult, [bkey, kx('col2')], [kx('kdec')])
                    yield
                    b.cp('act', vtok[0:L, :], R1[0:L, 0:128], [bkey], [kx('vtok')])
                    yield
                    b.cp('act', XT[0:L, 0:L], R2[0:L, 0:L], [bkey], [kx('XT')])
                    yield
                    b.tt('pool', Pm[0:L, 0:L], X[0:L, 0:L], I_L, ALU.add, [kx('X'), 'cst'], [kx('Pm')])
                    yield
                    nsq = int(math.log2(L)) - 1
                    for it in range(nsq):
                        lastit = it == nsq - 1
                        if not lastit:
                            b.mm(R0[0:L, 0:L], XT[0:L, 0:L], X[0:L, 0:L], [kx('XT'), kx('X')], [bkey])
                            yield
                        b.mm(R1[0:L, 0:L], X[0:L, 0:L], XT[0:L, 0:L], [kx('XT'), kx('X')], [bkey])
                        yield
                        if not lastit:
                            b.cp('act', X[0:L, 0:L], R0[0:L, 0:L], [bkey], [kx('X')])
                            yield
                            b.cp('dve', XT[0:L, 0:L], R1[0:L, 0:L], [bkey], [kx('XT')])
                            yield
                        b.tt('dve', IXT[0:L, 0:L], R1[0:L, 0:L], I_L, ALU.add, [bkey, 'cst'], [kx('IXT')])
                        yield
                        b.mm(R3[0:L, 0:L], IXT[0:L, 0:L], Pm[0:L, 0:L], [kx('IXT'), kx('Pm')], [bkey])
                        yield
                        b.cp('act', Pm[0:L, 0:L], R3[0:L, 0:L], [bkey], [kx('Pm')])
                        yield
                    b.mm(R2[:, 0:L], kg[0:L, :], Pm[0:L, 0:L], [kx('kg'), kx('Pm')], [bkey])
                    yield
                    b.ts('dve', nW[:, 0:L], R2[:, 0:L], -1.0, ALU.mult, [bkey], [kx('nW')])
                    yield
                    pre_done[j] = True

            def seq_chain():
                for j in range(nch):
                    while not pre_done[j]:
                        yield
                    k = j % NK
                    G_ = GB[k]
                    kx = lambda n: '%s_%d' % (n, k)
                    Pm, QKd, kdec, vtok, nW, qdec, colg = (G_[n] for n in ('Pm', 'QKd', 'kdec', 'vtok', 'nW', 'qdec', 'colg'))
                    c0, c1 = (j * L, (j + 1) * L)
                    if T['kind'] == 's':
                        seq, seq_last = (NB + j, True)
                        b.dma('sp', Sst[:], S0[j], [], ['S'], 'S_in')
                    else:
                        seq = T['seq']
                        seq_last = T['last'] and j == nch - 1
                        if T['first'] and j == 0:
                            b.memset('pool', Sst[:], 0.0, ['S'])
                    yield
                    b.mm(pC[0:L, 0:128], Pm[0:L, 0:L], vtok[0:L, :], [kx('Pm'), kx('vtok')], ['pC0'], start=True, stop=False)
                    yield
                    b.mm(pC[0:L, 0:128], nW[:, 0:L], Sst[:], [kx('nW'), 'S'], ['pC0'], start=False, stop=True)
                    yield
                    b.ts('dve', ee[0:L, :], pC[0:L, 0:128], gat[0:L, j, 0:1], ALU.mult, ['pC0', KY('gat')], ['ee'])
                    yield
                    b.mm(pC[:, 128:128 + L], Sst[:], qdec[:, 0:L], ['S', kx('qdec')], ['pC1'], start=True, stop=False)
                    yield
                    b.mm(pC[:, 128:128 + L], ee[0:L, :], QKd[0:L, 0:L], ['ee', kx('QKd')], ['pC1'], start=False, stop=True)
                    yield
                    b.mm(pC[:, 256:384], kdec[0:L, :], ee[0:L, :], [kx('kdec'), 'ee'], ['pC2'])
                    yield
                    b.cp('act', oT[:, c0:c1], pC[:, 128:128 + L], ['pC1'], ['oT'])
                    yield
                    b.stt(Sst[:], Sst[:], colg[:, 1:2], pC[:, 256:384], ALU.mult, ALU.add, ['S', kx('col1'), 'pC2'], ['S'])
                    yield
                    if seq_last:
                        b.dma('sp', oS[seq], Sst[:], ['S'], ['oS'], 'oS')
                        yield
                    seq_done[j] = True

            def ml_chain():
                for j in range(nch):
                    c0, c1 = (j * L, (j + 1) * L)
                    if T['kind'] == 's':
                        seq, seq_last = (NB + j, True)
                        b.dma('sp', CN[:, 0:128], C0[j], [], ['CN'], 'CN_in')
                        b.cp('pool', CN[:, 128:129], n0s[:, j:j + 1], ['n0s'], ['CN'])
                        b.cp('pool', mprev[:], m0s[:, j:j + 1], ['m0s'], ['mprev'])
                    else:
                        seq = T['seq']
                        seq_last = T['last'] and j == nch - 1
                        if T['first'] and j == 0:
                            b.memset('pool', CN[:], 0.0, ['CN'])
                            b.memset('pool', mprev[:], 0.0, ['mprev'])
                    yield
                    qb, kb, vb = (pr[4], pr[5], pr[6])
                    acol = gat[0:L, j, 8:9]
                    b.ts('dve', Abc[0:L, 0:L], ones[0:L, 0:L], acol, ALU.mult, ['cst', KY('gat')], ['Abc'])
                    yield
                    b.mm(pD[0:L, 256:256 + L], Abc[0:L, 0:L], I_L, ['Abc', 'cst'], ['pD2'], start=True, stop=False)
                    yield
                    b.mm(pD[0:L, 256:256 + L], I_L, NEGL[0:L, 0:L], ['cst'], ['pD2'], start=False, stop=True)
                    yield
                    b.P.op('dve', lambda e, o=col[0:L, 3:4], i=pD[0:L, 256:256 + L]: e.tensor_reduce(out=o, in_=i, axis=AX.X, op=ALU.max), reads=['pD2'], writes=['col3'])
                    yield
                    b.tt('dve', col[0:L, 4:5], col[0:L, 3:4], mprev[0:L, :], ALU.max, ['col3', 'mprev'], ['col4'])
                    yield
                    b.ts('dve', col[0:L, 5:6], col[0:L, 4:5], -1.0, ALU.mult, ['col4'], ['col5'])
                    yield
                    b.act(pmat[0:L, 0:L], pD[0:L, 256:256 + L], AF.Exp, ['pD2', 'col5'], ['pmat'], bias=col[0:L, 5:6])
                    yield
                    b.act(col[0:L, 6:7], mprev[0:L, :], AF.Exp, ['mprev', 'col5'], ['col6'], bias=col[0:L, 5:6])
                    yield
                    b.tt('dve', col[0:L, 7:8], gat[0:L, j, 7:8], col[0:L, 4:5], ALU.add, [KY('gat'), 'col4'], ['col7'])
                    yield
                    b.act(col[0:L, 8:9], col[0:L, 7:8], AF.Exp, ['col7'], ['col8'], scale=-1.0)
                    yield
                    b.mm(pD[0:L, 384:384 + L], qb[:, c0:c1], kb[:, c0:c1], [KY('pr4'), KY('pr5')], ['pD3'])
                    yield
                    b.stt(pqk[0:L, 0:L], pD[0:L, 384:384 + L], 1.0, pmat[0:L, 0:L], ALU.mult, ALU.mult, ['pD3', 'pmat'], ['pqk', 'col9'], accum_out=col[0:L, 9:10])
                    yield
                    b.tr(pE[0:L, 0:L], pqk[0:L, 0:L], I_L, ['pqk', 'cst'], ['pE0'])
                    yield
                    b.cp('act', pqkT[0:L, 0:L], pE[0:L, 0:L], ['pE0'], ['pqkT'])
                    yield
                    b.cp('dve', c2[0:L, 0:1], col[0:L, 4:5], ['col4'], ['c2'])
                    yield
                    b.cp('dve', c2[0:L, 1:2], gat[0:L, j, 7:8], [KY('gat')], ['c2'])
                    yield
                    SEL = SEL127 if L == 128 else SEL31
                    b.mm(pg[:, 48:50], SEL[0:L, :], c2[0:L, :], ['cst', 'c2'], ['pg_b'])
                    yield
                    b.cp('dve', bc2[:], pg[:, 48:50], ['pg_b'], ['bc2'])
                    yield
                    b.ts('dve', col[:, 10:11], bc2[:, 0:1], -1.0, ALU.mult, ['bc2'], ['col10'])
                    yield
                    b.act(col[0:L, 11:12], acol, AF.Exp, [KY('gat'), 'col10'], ['col11'], bias=col[0:L, 10:11])
                    yield
                    b.act(col[:, 12:13], mprev[:], AF.Exp, ['mprev', 'col10'], ['col12'], bias=col[:, 10:11])
                    yield
                    b.tr(pE[0:L, 128:256], kb[:, c0:c1], ident, [KY('pr5'), 'cst'], ['pE1'])
                    yield
                    b.tr(pE[0:L, 256:384], vb[:, c0:c1], ident, [KY('pr6'), 'cst'], ['pE2'])
                    yield
                    b.ts('dve', kp[0:L, :], pE[0:L, 128:256], col[0:L, 11:12], ALU.mult, ['pE1', 'col11'], ['kp'])
                    yield
                    b.cp('act', v1[0:L, 0:128], pE[0:L, 256:384], ['pE2'], ['v1'])
                    yield
                    b.mm(pg[0:L, 64:192], pqkT[0:L, 0:L], v1[0:L, 0:128], ['pqkT', 'v1'], ['pg_n1'])
                    yield
                    b.mm(pg[0:L, 192:321], qb[:, c0:c1], CN[:, 0:129], [KY('pr4'), 'CN'], ['pg_n2'])
                    yield
                    b.act(n2s[0:L, :], pg[0:L, 192:321], AF.Copy, ['pg_n2', 'col6'], ['n2s'], scale=col[0:L, 6:7])
                    yield
                    b.tt('dve', num[0:L, :], n2s[0:L, 0:128], pg[0:L, 64:192], ALU.add, ['n2s', 'pg_n1'], ['num'])
                    yield
                    b.tt('dve', col[0:L, 13:14], n2s[0:L, 128:129], col[0:L, 9:10], ALU.add, ['n2s', 'col9'], ['col13'])
                    yield
                    b.act(col[0:L, 13:14], col[0:L, 13:14], AF.Abs, ['col13'], ['col13'])
                    yield
                    b.tt('dve', col[0:L, 13:14], col[0:L, 13:14], col[0:L, 8:9], ALU.max, ['col13', 'col8'], ['col13'])
                    yield
                    b.P.op('dve', lambda e, o=col[0:L, 14:15], i=col[0:L, 13:14]: e.reciprocal(out=o, in_=i), reads=['col13'], writes=['col14'])
                    yield
                    b.ts('dve', htok[0:L, :], num[0:L, :], col[0:L, 14:15], ALU.mult, ['num', 'col14'], ['htok'])
                    yield
                    b.tr(pE[:, 384:384 + L], htok[0:L, :], I_L, ['htok', 'cst'], ['pE3'])
                    yield
                    b.cp('act', hT[:, c0:c1], pE[:, 384:384 + L], ['pE3'], ['hT'])
                    yield
                    b.mm(pg[:, 321:450], kp[0:L, :], v1[0:L, :], ['kp', 'v1'], ['pg_c2'])
                    yield
                    b.stt(CN[:], CN[:], col[:, 12:13], pg[:, 321:450], ALU.mult, ALU.add, ['CN', 'col12', 'pg_c2'], ['CN'])
                    yield
                    b.tt('dve', mprev[:], bc2[:, 0:1], bc2[:, 1:2], ALU.add, ['bc2'], ['mprev'])
                    yield
                    if seq_last:
                        b.dma('sp', oC[seq], CN[:, 0:128], ['CN'], ['oC'], 'oC')
                        yield
                        b.dma('sp', on[:, seq:seq + 1], CN[:, 128:129], ['CN'], ['on'], 'on', allow_slow_non_contiguous=True)
                        yield
                        b.dma('sp', om[0:1, seq:seq + 1], mprev[0:1, 0:1], ['mprev'], ['om'], 'om')
                        yield
                    yield
            return [pre_chain(k) for k in range(NK)] + [seq_chain(), ml_chain()]

        def post(ti):
            T = tiles[ti]
            slot = ti % 2
            pset = ti % 2
            N, L, nch = (T['N'], T['L'], T['nch'])
            xbk = 'xb%d' % slot
            KY = lambda n: '%s_%d' % (n, pset)
            pr, cv, qn, kn, gat = (PR[pset], CV[pset], QN[pset], KN[pset], GAT[pset])
            nseq = 1 if T['kind'] == 'p' else DB
            for src, srck, zf, of, ncol, half in ((oT, 'oT', 3, None, 4, 0), (hT, 'hT', 8, 7, 5, 1)):
                if of is not None:
                    b.act(tmpC[:, 0:N], pr[of][:, 0:N], AF.Sigmoid, [KY('pr%d' % of)], ['tmpC'])
                    b.tt('dve', src[:, 0:N], src[:, 0:N], tmpC[:, 0:N], ALU.mult, [srck, 'tmpC'], [srck])
                b.tt('pool', tmpD[:, 0:N], src[:, 0:N], src[:, 0:N], ALU.mult, [srck], ['tmpD'])
                b.mm(pj[half][:, 0:N], ones, tmpD[:, 0:N], ['cst', 'tmpD'], ['pj%d' % half])
                b.act(tmpD[:, 0:N], pj[half][:, 0:N], AF.Ln, ['pj%d' % half], ['tmpD'], scale=1.0 / 128.0, bias=1e-06)
                b.act(tmpD[:, 0:N], tmpD[:, 0:N], AF.Exp, ['tmpD'], ['tmpD'], scale=-0.5)
                b.act(tmpC[:, 0:N], pr[zf][:, 0:N], AF.Silu, [KY('pr%d' % zf)], ['tmpC'])
                b.stt(tmpD[:, 0:N], tmpD[:, 0:N], sc[:, ncol:ncol + 1], tmpC[:, 0:N], ALU.mult, ALU.mult, ['tmpD', 'sc', 'tmpC'], ['tmpD'])
                b.tt('dve', yb[:, half, 0:N], src[:, 0:N], tmpD[:, 0:N], ALU.mult, [srck, 'tmpD'], ['yb%d' % half])
                b.dma('sp', yT[half * 128:(half + 1) * 128, T['tok0']:T['tok0'] + N], yb[:, half, 0:N], ['yb%d' % half], ['yT'], 'yT%d' % half)

        for _ in g_P(0):
            pass
        for ti in range(len(tiles)):
            if ti + 2 < len(tiles):
                load_x(ti + 2, ti % 2)
            gens = chunk_gens(ti)
            if ti + 1 < len(tiles):
                gens.append(g_P(ti + 1))
            _interleave(*gens)
            post(ti)
        P.finalize()
        print("phase1 instrs", P.ninstr, {e: P.cnt[e] for e in ENGS})
    return nc


NTT = 17


def build_out_phase(glu, emit_T):
    nc = bass.Bass("TRN2", target_bir_lowering=False)
    KCI = 24 if glu else 16
    with ExitStack() as st:
        b = B(nc, st)
        P = b.P
        yin = b.din("yin", [NTT, 128, KCI, 128], BF16)
        wout = b.din("wout", [D, D])
        xres = b.din("xres", [NTT * 128, D])
        lng = b.din("lng", [128, D])
        lnb = b.din("lnb", [128, D])
        consts = b.din("consts", [128, 8, 128])
        if glu:
            wglu = b.din("wglu", [1024, 1024])
            bglu = b.din("bglu", [128, 8])
        xo = b.dout("xo", [NTT * 128, D])
        if emit_T:
            xoT = b.dout("xoT", [NTT, 128, KC, 128], BF16)

        cst = b.sb("cst", [128, 8, 128])
        ident = cst[:, 0, :]
        woutb = b.sb("woutb", [128, KC, D], BF16)
        wst = b.sb("wst", [128, 2, D])
        g_sb = b.sb("g_sb", [128, D])
        b_sb = b.sb("b_sb", [128, D])
        ysb = [b.sb("ysb%d" % i, [128, KCI, 128], BF16) for i in range(2)]
        xr = [b.sb("xr%d" % i, [128, D]) for i in range(2)]
        Z2 = [b.sb("z%d" % i, [128, D]) for i in range(2)]
        stats = b.sb("stats", [128, 4, 6])
        mv = b.sb("mv", [128, 4])
        xT_sb = [b.sb("xT_sb%d" % i, [128, KC, 128], BF16) for i in range(2)]
        if glu:
            wglub = b.sb("wglub", [128, 8, 1024], BF16)
            bg = b.sb("bg", [128, 8])
            sig = b.sb("sig", [128, 8, 128])
            yd = b.sb("yd", [128, 8, 128], BF16)
        po = [b.ps("po%d" % i) for i in range(4)]
        pt = [b.ps("pt%d" % i) for i in range(4)]
        PK = lambda i: "pj%d" % i

        b.dma("sp", cst[:], consts, [], ["cst"], "cst")
        b.dma("sp", g_sb[:], lng, [], ["g_sb"], "g_sb")
        b.dma("sp", b_sb[:], lnb, [], ["b_sb"], "b_sb")
        wv = wout.rearrange("(kc p) n -> p kc n", p=128)
        for kc0 in range(0, KC, 2):
            b.dma("sp", wst[:], wv[:, kc0:kc0 + 2, :], [], ["wst"], "wst")
            b.cp("dve", woutb[:, kc0, :], wst[:, 0, :], ["wst"], ["woutb"])
            b.cp("pool", woutb[:, kc0 + 1, :], wst[:, 1, :], ["wst"], ["woutb"])
        if glu:
            gv = wglu.rearrange("(kc p) n -> p kc n", p=128)
            for kc0 in range(0, 8, 4):
                b.dma("sp", wst[:].rearrange("p a (c n) -> p (a c) n", n=1024), gv[:, kc0:kc0 + 4, :], [], ["wst"], "wst")
                b.cp("pool", wglub[:, kc0:kc0 + 4, :], wst[:].rearrange("p a (c n) -> p (a c) n", n=1024), ["wst"], ["wglub"])
            b.dma("sp", bg[:], bglu, [], ["bg"], "bg")

        def load(ti):
            s = ti % 2
            b.dma("sp", ysb[s][:], yin[ti], [], ["ysb%d" % s], "ysb%d" % s)
            b.dma("sp", xr[s][:], xres[ti * 128:(ti + 1) * 128, :], [], ["xr%d" % s], "xr%d" % s)

        def g_mm(ti):
            s = ti % 2
            yk = "ysb%d" % s
            if glu:
                for f in range(8):
                    bank = pt[f // 4]
                    for kc in range(8):
                        b.mm(bank[:, (f % 4) * 128:(f % 4 + 1) * 128], wglub[:, kc, f * 128:(f + 1) * 128],
                             ysb[s][:, 8 + kc, :], ["wglub", yk], [PK(4 + f // 4)], start=(kc == 0), stop=(kc == 7))
                    yield
                for f in range(8):
                    b.act(sig[:, f, :], pt[f // 4][:, (f % 4) * 128:(f % 4 + 1) * 128], AF.Sigmoid,
                          [PK(4 + f // 4), "bg"], ["sig"], bias=bg[:, f:f + 1])
                    yield
                b.tt("dve", sig[:], sig[:], ysb[s][:, 8:16, :], ALU.mult, ["sig", yk], ["sig"])
                yield
                b.tt("dve", yd[:], sig[:], ysb[s][:, 16:24, :], ALU.mult, ["sig", yk], ["yd"])
                yield
            halves = ((0, 8), (8, KC)) if glu else ((0, KC),)
            for (k0, k1) in halves:
                for n in range(4):
                    for kc in range(k0, k1):
                        if glu and kc >= 8:
                            lhsT, lk = yd[:, kc - 8, :], "yd"
                        else:
                            lhsT, lk = ysb[s][:, kc, :], yk
                        b.mm(po[n][:, :], lhsT, woutb[:, kc, n * 512:(n + 1) * 512], [lk, "woutb"], [PK(n)],
                             start=(kc == 0), stop=(kc == KC - 1))
                    yield

        def evac(ti):
            s = ti % 2
            z = Z2[s]
            for n in range(4):
                b.stt(z[:, n * 512:(n + 1) * 512], xr[s][:, n * 512:(n + 1) * 512], ALPHA, po[n][:, :],
                      ALU.mult, ALU.add, ["xr%d" % s, PK(n)], ["z%d" % s])

        def g_ln(ti):
            s = ti % 2
            z, zk = Z2[s], "z%d" % s
            for n in range(4):
                b.P.op("dve", lambda e, o=stats[:, n, :], i=z[:, n * 512:(n + 1) * 512]: e.bn_stats(out=o, in_=i),
                       reads=[zk], writes=["stats"])
                yield
            b.P.op("dve", lambda e, o=mv[:, 0:2], i=stats[:].rearrange("p a c -> p (a c)"): e.bn_aggr(out=o, in_=i),
                   reads=["stats"], writes=["mv"])
            yield
            b.act(mv[:, 2:3], mv[:, 1:2], AF.Ln, ["mv"], ["mv"], bias=1e-5)
            yield
            b.act(mv[:, 2:3], mv[:, 2:3], AF.Exp, ["mv"], ["mv"], scale=-0.5)
            yield
            b.ts("dve", z[:], z[:], mv[:, 0:1], ALU.subtract, [zk, "mv"], [zk], s2=mv[:, 2:3], op1=ALU.mult)
            yield
            b.tt("pool", z[:], z[:], g_sb[:], ALU.mult, [zk, "g_sb"], [zk])
            yield
            b.tt("dve", z[:], z[:], b_sb[:], ALU.add, [zk, "b_sb"], [zk])
            yield
            b.dma("sp", xo[ti * 128:(ti + 1) * 128, :], z[:], [zk], ["xo"], "xo%d" % s)
            yield
            if emit_T:
                for q in range(4):
                    for kc in range(q * 4, q * 4 + 4):
                        b.tr(pt[kc // 4][:, (kc % 4) * 128:(kc % 4 + 1) * 128], z[:, kc * 128:(kc + 1) * 128], ident,
                             [zk, "cst"], [PK(4 + kc // 4)])
                    b.cp("act" if q % 2 else "dve", xT_sb[s][:, q * 4:(q + 1) * 4, :].rearrange("p a t -> p (a t)"),
                         pt[q][:, :], [PK(4 + q)], ["xT_sb%d" % s])
                    yield
                b.dma("sp", xoT[ti], xT_sb[s][:], ["xT_sb%d" % s], ["xoT"], "xoT%d" % s)
                yield

        load(0)
        if NTT > 1:
            load(1)
        for _ in g_mm(0):
            pass
        for ti in range(NTT):
            evac(ti)
            if ti + 2 < NTT:
                load(ti + 2)
            gens = [g_ln(ti)]
            if ti + 1 < NTT:
                gens.append(g_mm(ti + 1))
            _interleave(*gens)
        P.finalize()
        print("out-phase instrs", P.ninstr, {e: P.cnt[e] for e in ENGS})
    return nc


def _run(nc, in_maps):
    return run_bass_kernel_spmd(nc, in_maps, core_ids=list(range(NCORES))).results


def _tok_major(inp_p, inp_s):
    return np.concatenate([inp_p.reshape(-1, inp_p.shape[-1]), inp_s.reshape(-1, inp_s.shape[-1])], 0)


def _p1_inputs(inp, xT, c):
    H = 128
    W = inp['ev_w_in'][0]
    offs = np.cumsum([0, 3072, 1024, 8, 8, 1024, 1024, 1024, 1024, 1024, 8, 8])

    def colblk(i, sub=0):
        return W[:, offs[i] + sub * 1024 + c * H: offs[i] + sub * 1024 + (c + 1) * H]
    w1 = np.concatenate([colblk(0, 0), colblk(0, 1), colblk(0, 2), colblk(1),
                         colblk(4), colblk(5), colblk(6), colblk(7), colblk(8)], 1)
    wg = np.stack([W[:, offs[2] + c], W[:, offs[3] + c], W[:, offs[9] + c], W[:, offs[10] + c]], 1)
    wg = wg.reshape(KC, 128, 4).transpose(1, 0, 2).reshape(128, KC * 4)
    cw = inp['ev_conv_w'][0]
    convw = np.stack([cw[:, f * 1024 + c * H: f * 1024 + (c + 1) * H].T for f in range(3)], 1).reshape(H, 12)
    scal = np.zeros((128, 8), np.float32)
    scal[:, 0] = inp['ev_a_log'][0, c]
    scal[:, 1] = inp['ev_dt_bias'][0, c]
    scal[:, 2] = inp['ev_ig_bias'][0, c]
    scal[:, 3] = inp['ev_fg_bias'][0, c]
    scal[:, 4] = inp['ev_norm_a'][0]
    scal[:, 5] = inp['ev_norm_b'][0, c * H:(c + 1) * H]
    cs = inp['state_delta_conv'][0]
    conv0 = np.stack([cs[:, :, f * 1024 + c * H: f * 1024 + (c + 1) * H] for f in range(3)], 0)
    conv0 = np.ascontiguousarray(conv0.transpose(3, 1, 0, 2))
    A = np.ascontiguousarray
    return dict(xT=xT, w1=A(w1), wg=A(wg), convw=A(convw), scal=scal, consts=_consts_np(),
                S0=A(inp['state_delta_S'][0, :, c]), conv0=conv0,
                C0=A(inp['state_mlstm_C'][0, :, c]), n0=A(inp['state_mlstm_n'][0, :, c].T),
                m0=A(np.broadcast_to(inp['state_mlstm_m'][0, :, c][None, :], (128, DB))))


def _tile_tokens(Y, c):
    nk = Y.shape[0]
    out = np.zeros((nk, 128, NTT * 128), Y.dtype)
    out[:, :, :TOKC] = Y[:, :, c * TOKC:(c + 1) * TOKC]
    return np.ascontiguousarray(out.reshape(nk, 128, NTT, 128).transpose(2, 1, 0, 3))


def _pad_rows(x, c):
    out = np.zeros((NTT * 128, x.shape[1]), x.dtype)
    out[:TOKC] = x[c * TOKC:(c + 1) * TOKC]
    return out


def _p2_inputs(inp, xtok, Y, c):
    rep = lambda v: np.ascontiguousarray(np.broadcast_to(v[None, :], (128, v.shape[0])))
    return dict(yin=_tile_tokens(Y, c), wout=np.ascontiguousarray(inp['ev_w_out'][0]), xres=_pad_rows(xtok, c),
                lng=rep(inp['ev_ln_g'][0]), lnb=rep(inp['ev_ln_b'][0]), consts=_consts_np())


TWO_PI = 2.0 * math.pi
CW1 = 6.28125
CW2 = TWO_PI - CW1


def _masks3_np():
    m = np.zeros((128, 6, 512), np.float32)
    s = np.arange(128)[:, None]
    t = np.arange(512)[None, :]
    for r in range(4):
        m[:, r, :] = (t > 128 * r + s)
    m[:, 4, :] = ((t % 32) > s)
    m[:, 5, :] = t + 1.0
    return m


def build_phase3(ntiles_prompt=32, do_sample=True, debug=0):
    nc = bass.Bass("TRN2", target_bir_lowering=False)
    with ExitStack() as st:
        b = B(nc, st)
        P = b.P
        x1T = b.din("x1T", [D, NTOK], BF16)
        w3 = b.din("w3", [D, 768])
        consts = b.din("consts", [128, 8, 128])
        masks = b.din("masks", [128, 5, 512], BF16)
        tau = b.din("tau", [128, 512])
        kcT = b.din("kcT", [32, 128, DB, 128])
        vcc = b.din("vcc", [32, 128, DB, 128])
        s5p = b.din("s5p", [128, 4, 4])
        Bbd = b.din("Bbd", [128, 2, 4, 128])
        Cbd = b.din("Cbd", [128, 2, 4, 128])
        h0 = b.din("h0", [128, 2, 4, DB])
        y3T = b.dout("y3T", [384, NTOK], BF16)
        kout = b.dout("kout", [NTOK, 128])
        vout = b.dout("vout", [NTOK, 128])
        hout = b.dout("hout", [128, 2, 4, NB + DB])

        cst = b.sb("cst", [128, 8, 128])
        ident = cst[:, 0, :]
        msk = b.sb("msk", [128, 5, 512], BF16)
        trib = b.sb("trib", [128, 2, 128], BF16)
        w3b = b.sb("w3b", [128, KC, 768], BF16)
        wst = b.sb("wst", [128, 1, 768])
        xb = [b.sb("xb%d" % i, [128, KC, 512], BF16) for i in range(2)]
        KT = b.sb("KT", [128, SEQ], BF16)
        Vt = b.sb("Vt", [128, 64, 128], BF16)
        qb = b.sb("qb", [128, 512], BF16)
        kT32 = b.sb("kT32", [128, 512])
        vT32 = b.sb("vT32", [128, 512])
        zc = b.sb("zc", [128, 512])
        u32 = b.sb("u32", [128, 512])
        ub = b.sb("ub", [128, 512], BF16)
        szd = b.sb("szd", [128, 512], BF16)
        ktok = b.sb("ktok", [128, 4, 128])
        vtok = b.sb("vtok", [128, 4, 128])
        E = [b.sb("E%d" % i, [128, 512]) for i in range(3)]
        SP = [b.sb("SP%d" % i, [128, 512], BF16) for i in range(3)]
        Wt = b.sb("Wt", [128, 512])
        ATT = [b.sb("ATT%d" % i, [128, 512], BF16) for i in range(2)]
        ycb = b.sb("ycb", [128, 512], BF16)
        ysb = b.sb("ysb", [128, 512], BF16)
        sp_ = b.sb("sp_", [128, 4, 4])
        cr = b.sb("cr", [128, 4, 512])
        sr = b.sb("sr", [128, 4, 512])
        ki = b.sb("ki", [128, 512], mybir.dt.int32)
        s5c = b.sb("s5c", [128, 4, 16])
        Bb = b.sb("Bb", [128, 2, 4, 128], BF16)
        Cst = b.sb("Cst", [128, 2, 4, 128])
        Cb = b.sb("Cb", [128, 2, 4, 128], BF16)
        hst = b.sb("hst", [128, 2, 4, DB])
        h0s = b.sb("h0s", [128, 2, 4, DB])
        hfin = b.sb("hfin", [128, 2, 4, NB + DB])
        t1 = b.sb("t1", [128, 512])
        t2 = b.sb("t2", [128, 512])
        bre = b.sb("bre", [128, 512])
        bim = b.sb("bim", [128, 512])
        gre = b.sb("gre", [128, 512])
        gim = b.sb("gim", [128, 512])
        ang, kk = gre, gim
        hreb = b.sb("hreb", [128, 512], BF16)
        himb = b.sb("himb", [128, 512], BF16)
        ys32 = b.sb("ys32", [128, 512])
        kcs = [b.sb("kcs%d" % i, [128, DB, 128]) for i in range(2)]
        vcs = [b.sb("vcs%d" % i, [128, DB, 128]) for i in range(2)]
        kcb = [b.sb("kcb%d" % i, [128, DB, 128], BF16) for i in range(3)]
        vcb = [b.sb("vcb%d" % i, [128, DB, 128], BF16) for i in range(3)]
        vnew = b.sb("vnew", [128, DB, 128], BF16)

        pj = [b.ps("pj%d" % i) for i in range(2)]
        pZ = [b.ps("pZ%d" % i) for i in range(2)]
        pCm = b.ps("pCm")
        pO = b.ps("pO")
        pT = b.ps("pT")
        pY = b.ps("pY")
        ZK = ["pj2", "pj3"]
        CK, OK, TK, YK = "pj4", "pj5", "pj6", "pj7"

        b.dma("sp", cst[:], consts, [], ["cst"], "cst")
        b.dma("sp", msk[:], masks, [], ["msk"], "msk")
        b.dma("sp", t2[:], tau, [], ["t2"], "tau")
        b.dma("sp", sp_[:], s5p, [], ["sp_"], "sp_")
        b.dma("sp", h0s[:], h0, [], ["h0s"], "h0s")
        b.cp("dve", trib[:, 0, :], cst[:, 7, :], ["cst"], ["trib"])
        b.ts("dve", trib[:, 1, :], cst[:, 7, :], -1.0, ALU.mult, ["cst"], ["trib"], s2=1.0, op1=ALU.add)
        w3v = w3.rearrange("(kc p) n -> p kc n", p=128)
        for kc0 in range(KC):
            b.dma("sp", wst[:, 0, :], w3v[:, kc0, :], [], ["wst"], "wst")
            b.cp("dve" if kc0 % 2 else "pool", w3b[:, kc0, :], wst[:, 0, :], ["wst"], ["w3b"])
        for a in range(2):
            b.dma("sp", wst[:, 0, 0:512].rearrange("p (c n) -> p c n", n=128), Bbd[:, a], [], ["wst"], "wst")
            b.cp("dve", Bb[:, a], wst[:, 0, 0:512].rearrange("p (c n) -> p c n", n=128), ["wst"], ["Bb"])
        b.dma("sp", Cst[:], Cbd, [], ["Cst"], "Cst")
        c_ = lambda i: s5c[:, :, i]
        lre, lim = sp_[:, :, 0], sp_[:, :, 1]
        b.act(c_(13), sp_[:, :, 2], AF.Exp, ["sp_"], ["s5c"])
        dtc = c_(13)
        b.tt("dve", c_(8), lre, dtc, ALU.mult, ["sp_", "s5c"], ["s5c"])
        b.act(c_(0), c_(8), AF.Exp, ["s5c"], ["s5c"])
        b.tt("dve", c_(1), lim, dtc, ALU.mult, ["sp_", "s5c"], ["s5c"])
        b.cp("dve", bim[:], t2[:], ["t2"], ["bim"])
        for m in range(4):
            th = s5c[:, m, 1:2]
            b.ts("dve", ang[:], bim[:], th, ALU.mult, ["bim", "s5c"], ["gre"])
            b.ts("dve", kk[:], ang[:], 1.0 / TWO_PI, ALU.mult, ["gre"], ["gim"])
            b.cp("dve", ki[:], kk[:], ["gim"], ["ki"])
            b.cp("dve", kk[:], ki[:], ["ki"], ["gim"])
            b.stt(ang[:], kk[:], -CW1, ang[:], ALU.mult, ALU.add, ["gim", "gre"], ["gre"])
            b.stt(ang[:], kk[:], -CW2, ang[:], ALU.mult, ALU.add, ["gim", "gre"], ["gre"])
            for (dst, shift) in ((sr, 0.0), (cr, 0.5 * math.pi)):
                src = ang
                if shift != 0.0:
                    b.ts("dve", t1[:], ang[:], shift, ALU.add, ["gre"], ["t1"])
                    src = t1
                    sk = "t1"
                else:
                    b.cp("dve", t1[:], ang[:], ["gre"], ["t1"])
                    src, sk = t1, "t1"
                for _ in range(2):
                    b.ts("dve", t2[:], t1[:], math.pi, ALU.is_gt, ["t1"], ["t2"])
                    b.stt(t1[:], t2[:], -TWO_PI, t1[:], ALU.mult, ALU.add, ["t2", "t1"], ["t1"])
                    b.ts("dve", t2[:], t1[:], -math.pi, ALU.is_lt, ["t1"], ["t2"])
                    b.stt(t1[:], t2[:], TWO_PI, t1[:], ALU.mult, ALU.add, ["t2", "t1"], ["t1"])
                b.act(dst[:, m, :], t1[:], AF.Sin, ["t1"], ["cr" if dst is cr else "sr"])
        b.tt("dve", c_(2), c_(0), cr[:, :, 0], ALU.mult, ["s5c", "cr"], ["s5c"])
        b.tt("dve", c_(3), c_(0), sr[:, :, 0], ALU.mult, ["s5c", "sr"], ["s5c"])
        b.ts("dve", c_(2), c_(2), -1.0, ALU.add, ["s5c"], ["s5c"])
        b.tt("dve", c_(4), lre, lre, ALU.mult, ["sp_"], ["s5c"])
        b.tt("dve", c_(8), lim, lim, ALU.mult, ["sp_"], ["s5c"])
        b.tt("dve", c_(4), c_(4), c_(8), ALU.add, ["s5c"], ["s5c"])
        b.P.op("dve", lambda e, o=c_(12), i=c_(4): e.reciprocal(out=o, in_=i), reads=["s5c"], writes=["s5c"])
        b.tt("dve", c_(8), c_(2), lre, ALU.mult, ["s5c", "sp_"], ["s5c"])
        b.tt("dve", c_(9), c_(3), lim, ALU.mult, ["s5c", "sp_"], ["s5c"])
        b.tt("dve", c_(5), c_(8), c_(9), ALU.add, ["s5c"], ["s5c"])
        b.tt("dve", c_(5), c_(5), c_(12), ALU.mult, ["s5c"], ["s5c"])
        b.tt("dve", c_(8), c_(3), lre, ALU.mult, ["s5c", "sp_"], ["s5c"])
        b.tt("dve", c_(9), c_(2), lim, ALU.mult, ["s5c", "sp_"], ["s5c"])
        b.tt("dve", c_(6), c_(8), c_(9), ALU.subtract, ["s5c"], ["s5c"])
        b.tt("dve", c_(6), c_(6), c_(12), ALU.mult, ["s5c"], ["s5c"])
        b.ts("dve", c_(7), c_(6), -1.0, ALU.mult, ["s5c"], ["s5c"])
        b.tt("dve", c_(8), c_(5), c_(5), ALU.mult, ["s5c"], ["s5c"])
        b.tt("dve", c_(9), c_(6), c_(6), ALU.mult, ["s5c"], ["s5c"])
        b.tt("dve", c_(8), c_(8), c_(9), ALU.add, ["s5c"], ["s5c"])
        b.P.op("dve", lambda e, o=c_(9), i=c_(8): e.reciprocal(out=o, in_=i), reads=["s5c"], writes=["s5c"])
        b.tt("dve", c_(10), c_(5), c_(9), ALU.mult, ["s5c"], ["s5c"])
        b.tt("dve", c_(11), c_(7), c_(9), ALU.mult, ["s5c"], ["s5c"])
        for m in range(4):
            fre, fim, nfim = s5c[:, m, 5:6], s5c[:, m, 6:7], s5c[:, m, 7:8]
            b.ts("dve", t1[:, 0:128], Cst[:, 1, m, :], nfim, ALU.mult, ["Cst", "s5c"], ["t1"])
            b.stt(Cb[:, 0, m, :], Cst[:, 0, m, :], fre, t1[:, 0:128], ALU.mult, ALU.add, ["Cst", "s5c", "t1"], ["Cb"])
            b.ts("dve", t1[:, 0:128], Cst[:, 1, m, :], fre, ALU.mult, ["Cst", "s5c"], ["t1"])
            b.stt(t1[:, 0:128], Cst[:, 0, m, :], fim, t1[:, 0:128], ALU.mult, ALU.add, ["Cst", "s5c", "t1"], ["t1"])
            b.ts("dve", Cb[:, 1, m, :], t1[:, 0:128], -1.0, ALU.mult, ["t1"], ["Cb"])
            fr, fi = s5c[:, m, 10:11], s5c[:, m, 11:12]
            b.ts("dve", t2[:, 0:DB], h0s[:, 1, m, :], fi, ALU.mult, ["h0s", "s5c"], ["t2"])
            b.stt(t2[:, 16:16 + DB], h0s[:, 0, m, :], fr, t2[:, 0:DB], ALU.mult, ALU.subtract, ["h0s", "s5c", "t2"], ["t2"])
            b.ts("dve", t2[:, 0:DB], h0s[:, 1, m, :], fr, ALU.mult, ["h0s", "s5c"], ["t2"])
            b.stt(t2[:, 32:32 + DB], h0s[:, 0, m, :], fi, t2[:, 0:DB], ALU.mult, ALU.add, ["h0s", "s5c", "t2"], ["t2"])
            b.cp("dve", h0s[:, 0, m, :], t2[:, 16:16 + DB], ["t2"], ["h0s"])
            b.cp("dve", h0s[:, 1, m, :], t2[:, 32:32 + DB], ["t2"], ["h0s"])

        x1v = x1T.rearrange("(kc p) n -> p kc n", p=128)
        tiles = []
        for bi in range(NB):
            for t in range(ntiles_prompt // NB):
                tiles.append(dict(kind="p", seq=bi, ti=t, tok0=bi * SEQ + t * 512, N=512,
                                  first=(t == 0), last=(t == ntiles_prompt // NB - 1)))
        if do_sample:
            tiles.append(dict(kind="s", seq=None, ti=0, tok0=NTP, N=256, first=True, last=True))

        def load_x(i):
            T = tiles[i]
            s = i % 2
            for h in range(2):
                b.dma("sp", xb[s][:, h * 8:(h + 1) * 8, 0:T["N"]], x1v[:, h * 8:(h + 1) * 8, T["tok0"]:T["tok0"] + T["N"]],
                      [], ["xb%d" % s], "xb%d_%d" % (s, h))

        def load_cache(j):
            s = j % 3
            s2 = j % 2
            b.dma("sp", kcs[s2][:], kcT[j], [], ["kcs%d" % s2], "kcs%d" % s2)
            b.dma("sp", vcs[s2][:], vcc[j], [], ["vcs%d" % s2], "vcs%d" % s2)
            b.cp("pool", kcb[s][:], kcs[s2][:], ["kcs%d" % s2], ["kcb%d" % s])
            b.cp("pool", vcb[s][:], vcs[s2][:], ["vcs%d" % s2], ["vcb%d" % s])

        load_x(0)
        try:
            for i, T in enumerate(tiles):
                s = i % 2
                N = T["N"]
                samp = T["kind"] == "s"
                xk = "xb%d" % s
                if i + 1 < len(tiles):
                    load_x(i + 1)
                for f in range(6):
                    bank, bk = pj[f % 2], "pj%d" % (f % 2)
                    for kc in range(KC):
                        b.mm(bank[:, 0:N], w3b[:, kc, f * 128:(f + 1) * 128], xb[s][:, kc, 0:N], ["w3b", xk], [bk],
                             start=(kc == 0), stop=(kc == KC - 1))
                    if f == 0:
                        b.act(qb[:, 0:N], bank[:, 0:N], AF.Copy, [bk], ["qb"], scale=128.0 ** -0.5)
                    elif f == 1:
                        b.cp("dve", kT32[:, 0:N], bank[:, 0:N], [bk], ["kT32"])
                        if not samp:
                            b.cp("act", KT[:, T["ti"] * 512:T["ti"] * 512 + N], bank[:, 0:N], [bk], ["KT"])
                        else:
                            b.cp("act", KT[:, 0:N], bank[:, 0:N], [bk], ["KT"])
                    elif f == 2:
                        b.cp("dve", vT32[:, 0:N], bank[:, 0:N], [bk], ["vT32"])
                    elif f == 3:
                        b.act(zc[:, 0:N], bank[:, 0:N], AF.Silu, [bk], ["zc"])
                    elif f == 4:
                        b.cp("dve", u32[:, 0:N], bank[:, 0:N], [bk], ["u32"])
                        b.cp("act", ub[:, 0:N], bank[:, 0:N], [bk], ["ub"])
                    else:
                        b.act(szd[:, 0:N], bank[:, 0:N], AF.Silu, [bk], ["szd"])
                        b.dma("sp", y3T[256:384, T["tok0"]:T["tok0"] + N], szd[:, 0:N], ["szd"], ["y3T"], "szd_o")
                if debug == 1:
                    raise _Stop
                nblk = N // 128
                for (src, sk, dst, dk, outap) in ((kT32, "kT32", ktok, "ktok", kout), (vT32, "vT32", vtok, "vtok", vout)):
                    for jb in range(nblk):
                        b.tr(pT[:, jb * 128:(jb + 1) * 128], src[:, jb * 128:(jb + 1) * 128], ident, [sk, "cst"], [TK])
                    b.cp("dve", dst[:, 0:nblk, :].rearrange("p a d -> p (a d)"), pT[:, 0:nblk * 128], [TK], [dk])
                    b.dma("sp", outap[T["tok0"]:T["tok0"] + N, :].rearrange("(a p) d -> p a d", p=128), dst[:, 0:nblk, :],
                          [dk], ["o_" + dk], "o_" + dk)
                if not samp:
                    b.cp("pool", Vt[:, T["ti"] * 4:T["ti"] * 4 + 4, :], vtok[:], ["vtok"], ["Vt"])
                else:
                    for h in range(2):
                        for q4 in range(4):
                            bi = h * 4 + q4
                            b.tr(pT[0:32, q4 * 128:(q4 + 1) * 128], vT32[:, bi * 32:(bi + 1) * 32], ident,
                                 ["vT32", "cst"], [TK])
                        b.cp("dve", vnew[0:32, h * 4:(h + 1) * 4, :].rearrange("p a d -> p (a d)"), pT[0:32, 0:512],
                             [TK], ["vnew"])
                if debug == 2:
                    raise _Stop
                def g_att():
                    if not samp:
                        jmax = T['ti'] * 4 + 3
                        steps = list(range(jmax, -1, -1))
                    else:
                        steps = list(range(32, -1, -1))
                        load_cache(31)
                        yield

                    def stageA(si):
                        j = steps[si]
                        z = si % 2
                        e3 = si % 3
                        Zb, zk = (pZ[z], ZK[z])
                        if not samp:
                            Kp = 128
                            b.mm(Zb[:, 0:N], KT[:, j * 128:(j + 1) * 128], qb[:, 0:N], ['KT', 'qb'], [zk])
                        elif j == 32:
                            Kp = 32
                            for bi in range(DB):
                                b.mm(Zb[0:32, bi * 32:(bi + 1) * 32], KT[:, bi * 32:(bi + 1) * 32], qb[:, bi * 32:(bi + 1) * 32], ['KT', 'qb'], [zk])
                        else:
                            Kp = 128
                            if j - 1 >= 0:
                                load_cache(j - 1)
                            for bi in range(DB):
                                b.mm(Zb[:, bi * 32:(bi + 1) * 32], kcb[j % 3][:, bi, :], qb[:, bi * 32:(bi + 1) * 32], ['kcb%d' % (j % 3), 'qb'], [zk])
                        b.act(E[e3][0:Kp, 0:N], Zb[0:Kp, 0:N], AF.Exp, [zk], ['E%d' % e3])
                        b.act(SP[e3][0:Kp, 0:N], E[e3][0:Kp, 0:N], AF.Ln, ['E%d' % e3], ['SP%d' % e3], bias=1.0)
                        mk = None
                        if samp and j == 32:
                            mk = msk[0:32, 4, 0:N]
                        elif not samp and j >= T['ti'] * 4:
                            mk = msk[:, j - T['ti'] * 4, 0:N]
                        if mk is not None:
                            b.tt('pool', E[e3][0:Kp, 0:N], E[e3][0:Kp, 0:N], mk, ALU.mult, ['E%d' % e3, 'msk'], ['E%d' % e3])
                            b.tt('pool', SP[e3][0:Kp, 0:N], SP[e3][0:Kp, 0:N], mk, ALU.mult, ['SP%d' % e3, 'msk'], ['SP%d' % e3])
                        return Kp

                    def emitAV(si, Kp):
                        j = steps[si]
                        z = si % 2
                        if not samp:
                            b.mm(pO[:, 0:N], Vt[:, j, :], ATT[z][:, 0:N], ['Vt', 'ATT%d' % z], [OK], start=False, stop=True)
                        elif j == 32:
                            for bi in range(DB):
                                b.mm(pO[:, bi * 32:(bi + 1) * 32], vnew[0:32, bi, :], ATT[z][0:32, bi * 32:(bi + 1) * 32], ['vnew', 'ATT%d' % z], [OK], start=False, stop=True)
                        else:
                            for bi in range(DB):
                                b.mm(pO[:, bi * 32:(bi + 1) * 32], vcb[j % 3][:, bi, :], ATT[z][:, bi * 32:(bi + 1) * 32], ['vcb%d' % (j % 3), 'ATT%d' % z], [OK], start=False, stop=True)
                    b.memset('dve', pCm[:, 0:N], 0.0, [CK])
                    yield
                    b.memset('dve', pO[:, 0:N], 0.0, [OK])
                    yield
                    LA = 1 if samp else 2
                    nst = len(steps)
                    KpA = {}
                    for si0 in range(min(LA, nst)):
                        KpA[si0] = stageA(si0)
                        yield
                    for si, j in enumerate(steps):
                        z = si % 2
                        e3 = si % 3
                        Kp = KpA[si]
                        b.mm(pCm[:, 0:N], trib[0:Kp, 0, :], SP[e3][0:Kp, 0:N], ['trib', 'SP%d' % e3], [CK], start=False, stop=True)
                        yield
                        b.act(Wt[0:Kp, 0:N], pCm[0:Kp, 0:N], AF.Exp, [CK], ['Wt'], scale=-1.0)
                        yield
                        b.tt('dve', ATT[z][0:Kp, 0:N], E[e3][0:Kp, 0:N], Wt[0:Kp, 0:N], ALU.mult, ['E%d' % e3, 'Wt'], ['ATT%d' % z])
                        yield
                        if si >= 1:
                            emitAV(si - 1, KpA[si - 1])
                            yield
                        if si + LA < nst:
                            KpA[si + LA] = stageA(si + LA)
                            yield
                        b.mm(pCm[:, 0:N], trib[0:Kp, 1, :], SP[e3][0:Kp, 0:N], ['trib', 'SP%d' % e3], [CK], start=False, stop=True)
                        yield
                    emitAV(nst - 1, KpA[nst - 1])
                    yield
                    b.tt('dve', ycb[:, 0:N], pO[:, 0:N], zc[:, 0:N], ALU.mult, [OK, 'zc'], ['ycb'])
                    yield
                    b.dma('sp', y3T[0:128, T['tok0']:T['tok0'] + N], ycb[:, 0:N], ['ycb'], ['y3T'], 'yc_o')
                    yield
                    yield
                def g_s5():
                    nseq = DB if samp else 1
                    Ls = N // nseq
                    v3 = lambda ap: ap.rearrange('p (s l) -> p s l', l=Ls)

                    def tab(tb, m):
                        if samp:
                            return tb[:, m, 0:Ls].unsqueeze(1).broadcast_to([128, nseq, Ls])
                        return v3(tb[:, m, 0:N])
                    for m in range(4):
                        b.mm(pj[0][:, 0:N], Bb[:, 0, m, :], ub[:, 0:N], ['Bb', 'ub'], ['pj0'])
                        yield
                        b.mm(pj[1][:, 0:N], Bb[:, 1, m, :], ub[:, 0:N], ['Bb', 'ub'], ['pj1'])
                        yield
                        Bre, Bim = (v3(pj[0][:, 0:N]), v3(pj[1][:, 0:N]))
                        b.tt('dve', v3(t1[:, 0:N]), Bre, tab(cr, m), ALU.mult, ['pj0', 'cr'], ['t1'])
                        yield
                        b.tt('dve', v3(t2[:, 0:N]), Bim, tab(sr, m), ALU.mult, ['pj1', 'sr'], ['t2'])
                        yield
                        b.tt('pool', bre[:, 0:N], t1[:, 0:N], t2[:, 0:N], ALU.add, ['t1', 't2'], ['bre'])
                        yield
                        b.tt('dve', v3(t1[:, 0:N]), Bim, tab(cr, m), ALU.mult, ['pj1', 'cr'], ['t1'])
                        yield
                        b.tt('dve', v3(t2[:, 0:N]), Bre, tab(sr, m), ALU.mult, ['pj0', 'sr'], ['t2'])
                        yield
                        b.tt('pool', bim[:, 0:N], t1[:, 0:N], t2[:, 0:N], ALU.subtract, ['t1', 't2'], ['bim'])
                        yield
                        rbc = s5c[:, m, 0:1].to_broadcast([128, Ls])
                        for sq in range(nseq):
                            col = sq if samp else 0
                            for src, sk, dst, dk, a in ((bre, 'bre', gre, 'gre', 0), (bim, 'bim', gim, 'gim', 1)):
                                if samp:
                                    init = h0s[:, a, m, sq:sq + 1]
                                    ik = 'h0s'
                                elif T['first']:
                                    init, ik = (0.0, None)
                                else:
                                    init, ik = (hst[:, a, m, 0:1], 'hst')
                                b.P.op('dve', lambda e, o=dst[:, sq * Ls:(sq + 1) * Ls], d0=rbc, d1=src[:, sq * Ls:(sq + 1) * Ls], ini=init: e.tensor_tensor_scan(out=o, data0=d0, data1=d1, initial=ini, op0=ALU.mult, op1=ALU.add), reads=[sk, 's5c'] + ([ik] if ik else []), writes=[dk])
                                yield
                        b.tt('dve', v3(t1[:, 0:N]), v3(gre[:, 0:N]), tab(cr, m), ALU.mult, ['gre', 'cr'], ['t1'])
                        yield
                        b.tt('dve', v3(t2[:, 0:N]), v3(gim[:, 0:N]), tab(sr, m), ALU.mult, ['gim', 'sr'], ['t2'])
                        yield
                        b.tt('pool', hreb[:, 0:N], t1[:, 0:N], t2[:, 0:N], ALU.subtract, ['t1', 't2'], ['hreb'])
                        yield
                        lastv = lambda ap: v3(ap[:, 0:N])[:, :, Ls - 1]
                        if samp:
                            cols = slice(NB, NB + DB)
                        else:
                            cols = slice(T['seq'], T['seq'] + 1)
                        b.tt('dve', hst[:, 0, m, 0:nseq], lastv(t1), lastv(t2), ALU.subtract, ['t1', 't2'], ['hst'])
                        yield
                        b.tt('dve', v3(t1[:, 0:N]), v3(gre[:, 0:N]), tab(sr, m), ALU.mult, ['gre', 'sr'], ['t1'])
                        yield
                        b.tt('dve', v3(t2[:, 0:N]), v3(gim[:, 0:N]), tab(cr, m), ALU.mult, ['gim', 'cr'], ['t2'])
                        yield
                        b.tt('pool', himb[:, 0:N], t1[:, 0:N], t2[:, 0:N], ALU.add, ['t1', 't2'], ['himb'])
                        yield
                        b.tt('dve', hst[:, 1, m, 0:nseq], lastv(t1), lastv(t2), ALU.add, ['t1', 't2'], ['hst'])
                        yield
                        if T['last']:
                            fre, fim, nfim = (s5c[:, m, 5:6], s5c[:, m, 6:7], s5c[:, m, 7:8])
                            b.ts('dve', t1[:, 0:nseq], hst[:, 1, m, 0:nseq], nfim, ALU.mult, ['hst', 's5c'], ['t1'])
                            yield
                            b.stt(hfin[:, 0, m, cols], hst[:, 0, m, 0:nseq], fre, t1[:, 0:nseq], ALU.mult, ALU.add, ['hst', 's5c', 't1'], ['hfin'])
                            yield
                            b.ts('dve', t1[:, 0:nseq], hst[:, 1, m, 0:nseq], fre, ALU.mult, ['hst', 's5c'], ['t1'])
                            yield
                            b.stt(hfin[:, 1, m, cols], hst[:, 0, m, 0:nseq], fim, t1[:, 0:nseq], ALU.mult, ALU.add, ['hst', 's5c', 't1'], ['hfin'])
                            yield
                        b.mm(pY[:, 0:N], Cb[:, 0, m, :], hreb[:, 0:N], ['Cb', 'hreb'], [YK], start=m == 0, stop=False)
                        yield
                        b.mm(pY[:, 0:N], Cb[:, 1, m, :], himb[:, 0:N], ['Cb', 'himb'], [YK], start=False, stop=m == 3)
                        yield
                    b.stt(ys32[:, 0:N], u32[:, 0:N], sp_[:, 0, 3:4], pY[:, 0:N], ALU.mult, ALU.add, ['u32', 'sp_', YK], ['ys32'])
                    yield
                    b.tt('pool', t1[:, 0:N], ys32[:, 0:N], ys32[:, 0:N], ALU.mult, ['ys32'], ['t1'])
                    yield
                    b.ts('dve', t1[:, 0:N], t1[:, 0:N], 0.044715, ALU.mult, ['t1'], ['t1'], s2=1.0, op1=ALU.add)
                    yield
                    b.tt('pool', t1[:, 0:N], t1[:, 0:N], ys32[:, 0:N], ALU.mult, ['t1', 'ys32'], ['t1'])
                    yield
                    b.act(t1[:, 0:N], t1[:, 0:N], AF.Sigmoid, ['t1'], ['t1'], scale=2.0 * math.sqrt(2.0 / math.pi))
                    yield
                    b.tt('dve', ysb[:, 0:N], t1[:, 0:N], ys32[:, 0:N], ALU.mult, ['t1', 'ys32'], ['ysb'])
                    yield
                    b.dma('sp', y3T[128:256, T['tok0']:T['tok0'] + N], ysb[:, 0:N], ['ysb'], ['y3T'], 'ys_o')
                    yield
                    yield
                _interleave(g_att(), g_s5())
        except _Stop:
            pass
        b.dma("sp", hout, hfin[:], ["hfin"], ["hout"], "hout")
        P.finalize()
        print("phase3 instrs", P.ninstr, {e: P.cnt[e] for e in ENGS})
    return nc


def _p3_inputs(inp, x1T, c):
    A = np.ascontiguousarray
    W = inp['od_w_in'][0]
    blk = lambda i: W[:, i * 1024 + c * 128: i * 1024 + (c + 1) * 128]
    w3 = np.concatenate([blk(0), blk(1), blk(2), blk(3), blk(4), blk(5)], 1)
    kc = inp['cache_sb_k'][0, :, :, c, :]
    vc = inp['cache_sb_v'][0, :, :, c, :]
    kcT = A(kc.reshape(DB, 32, 128, 128).transpose(1, 3, 0, 2))
    vcc = A(vc.reshape(DB, 32, 128, 128).transpose(1, 2, 0, 3))
    G = slice(8 * c, 8 * c + 8)
    lre = inp['od_lam_re'][0, G]
    lim = inp['od_lam_im'][0, G]
    dt = np.broadcast_to(inp['od_log_dt'][0, G][:, None], (8, 64))
    s5p = np.zeros((128, 4, 4), np.float32)
    qm = lambda a: a.reshape(4, 128).T
    s5p[:, :, 0] = qm(lre)
    s5p[:, :, 1] = qm(lim)
    s5p[:, :, 2] = qm(dt)
    s5p[:, 0, 3] = inp['od_d'][0, c * 128:(c + 1) * 128]
    Bbd = np.zeros((128, 2, 4, 128), np.float32)
    Cbd = np.zeros((128, 2, 4, 128), np.float32)
    for a, (bk, ck) in enumerate((('od_b_re', 'od_c_re'), ('od_b_im', 'od_c_im'))):
        bb = inp[bk][0, G]
        cc = inp[ck][0, G]
        for g in range(8):
            m, g2 = g // 2, g % 2
            Bbd[g * 16:(g + 1) * 16, a, m, g2 * 64:(g2 + 1) * 64] = bb[g].T
            Cbd[g2 * 64:(g2 + 1) * 64, a, m, g * 16:(g + 1) * 16] = cc[g].T
    h0 = np.zeros((128, 2, 4, DB), np.float32)
    for a, k in enumerate(('state_s5_re', 'state_s5_im')):
        hs = inp[k][0, :, G]
        h0[:, a] = hs.reshape(DB, 4, 128).transpose(2, 1, 0)
    return dict(x1T=x1T, w3=A(w3), consts=_consts_np(), masks=np.ascontiguousarray(_masks3_np()[:, 0:5]).astype(ml_dtypes.bfloat16), tau=np.ascontiguousarray(_masks3_np()[:, 5]), kcT=kcT, vcc=vcc, s5p=s5p,
                Bbd=Bbd, Cbd=Cbd, h0=h0)


def _p4_inputs(inp, x1tok, Y3, c):
    rep = lambda v: np.ascontiguousarray(np.broadcast_to(v[None, :], (128, v.shape[0])))
    return dict(yin=_tile_tokens(Y3, c), wout=np.ascontiguousarray(inp['od_w_out'][0]), xres=_pad_rows(x1tok, c),
                lng=rep(inp['od_ln_g'][0]), lnb=rep(inp['od_ln_b'][0]), consts=_consts_np(),
                wglu=np.ascontiguousarray(inp['od_w_glu'][0]),
                bglu=np.ascontiguousarray(inp['od_b_glu'][0].reshape(8, 128).T))


_NC_CACHE = {}


def _get_nc(name, fn):
    if name not in _NC_CACHE:
        _NC_CACHE[name] = fn()
    return _NC_CACHE[name]


def kernel(**inp):
    inp = {k: np.asarray(v) for k, v in inp.items()}
    R = range(NCORES)
    xtok = _tok_major(inp['x_prompt'], inp['x_sample'])
    xT = np.ascontiguousarray(xtok.T)
    r1 = _run(_get_nc("p1", build_phase1), [_p1_inputs(inp, xT, c) for c in R])
    del xT
    Y = np.stack([r1[k]['yT'][0:128] for k in R] + [r1[k]['yT'][128:256] for k in R])
    r2 = _run(_get_nc("p2", lambda: build_out_phase(False, True)), [_p2_inputs(inp, xtok, Y, c) for c in R])
    x1tok = np.concatenate([r2[c]['xo'][:TOKC] for c in R], 0)
    x1T = np.ascontiguousarray(
        np.concatenate([r2[c]['xoT'].transpose(2, 1, 0, 3).reshape(D, -1)[:, :TOKC] for c in R], 1))
    r3 = _run(_get_nc("p3", build_phase3), [_p3_inputs(inp, x1T, c) for c in R])
    Y3 = np.stack([r3[k]['y3T'][0:128] for k in R] + [r3[k]['y3T'][128:256] for k in R]
                  + [r3[k]['y3T'][256:384] for k in R])
    r4 = _run(_get_nc("p4", lambda: build_out_phase(True, False)), [_p4_inputs(inp, x1tok, Y3, c) for c in R])
    ytok = np.concatenate([r4[c]['xo'][:TOKC] for c in R], 0)
    f32 = np.float32
    y_prompt = ytok[:NTP].reshape(NB, SEQ, D).astype(f32)
    y_sample = ytok[NTP:].reshape(DB, DS, D).astype(f32)

    def heads(key, sl):
        return np.stack([r1[c][key][sl] for c in R], 1)[None].astype(f32)
    pS, sS = slice(0, NB), slice(NB, NB + DB)
    dS_p, dS_s = heads('oS', pS), heads('oS', sS)
    mC_p, mC_s = heads('oC', pS), heads('oC', sS)

    def convout(sl):
        n = sl.stop - sl.start
        o = np.zeros((1, n, 3, 3072), f32)
        for c in R:
            oc = r1[c]['oconv'][:, sl]
            for f in range(3):
                o[0, :, :, f * 1024 + c * 128: f * 1024 + (c + 1) * 128] = oc[:, :, f, :].transpose(1, 2, 0)
        return o
    dc_p, dc_s = convout(pS), convout(sS)
    mn = lambda sl: np.stack([r1[c]['on'][:, sl].T for c in R], 1)[None].astype(f32)
    mm = lambda sl: np.stack([r1[c]['om'][0, sl] for c in R], 1)[None].astype(f32)

    def kv(key, lo, hi, shape):
        return np.stack([r3[c][key][lo:hi] for c in R], 1).reshape(shape).astype(f32)
    sk_p = kv('kout', 0, NTP, (NB, SEQ, 8, 128))[None]
    sk_s = kv('kout', NTP, NTOK, (DB, DS, 8, 128))[None]
    sv_p = kv('vout', 0, NTP, (NB, SEQ, 8, 128))[None]
    sv_s = kv('vout', NTP, NTOK, (DB, DS, 8, 128))[None]

    def s5state(a, sl):
        n = sl.stop - sl.start
        o = np.zeros((1, n, 64, 64), f32)
        for c in R:
            ho = r3[c]['hout'][:, a, :, sl]
            o[0, :, 8 * c:8 * c + 8, :] = ho.transpose(2, 1, 0).reshape(n, 8, 64)
        return o
    return (y_prompt, y_sample, dS_p, dS_s, dc_p, dc_s, mC_p, mC_s, mn(pS), mn(sS), mm(pS), mm(sS),
            sk_p, sk_s, sv_p, sv_s, s5state(0, pS), s5state(0, sS), s5state(1, pS), s5state(1, sS))
```

```python
import math
from contextlib import ExitStack

import numpy as np
import ml_dtypes
import concourse.bass as bass
import concourse.mybir as mybir
from concourse.bass_utils import run_bass_kernel_spmd

F32 = mybir.dt.float32
BF16 = mybir.dt.bfloat16
AF = mybir.ActivationFunctionType
ALU = mybir.AluOpType
AX = mybir.AxisListType
ENGS = ("pe", "act", "dve", "pool", "sp")
NCORES = 8

D = 2048
KC = D // 128
SEQ = 8192
NB = 2
DB = 8
DS = 32
PAST = 4096
NTP = NB * SEQ
NTS = DB * DS
NTOK = NTP + NTS
TOKC = NTOK // NCORES
ALPHA = (2 * 2) ** 0.25
NEG = -30000.0
SAME_ENG_GAP = 10 ** 9
F32R = mybir.dt.float32r


class Prog:
    def __init__(self, nc, stack):
        self.nc = nc
        self.stack = stack
        self.q = {e: [] for e in ENGS}
        self.cnt = {e: 0 for e in ENGS}
        self.sem = {e: stack.enter_context(nc.semaphore("sem_" + e)) for e in ENGS}
        self.waited = {e: {} for e in ENGS}
        self.lastw = {}
        self.readers = {}
        self.dsem = {}
        self.dcnt = {}
        self.ninstr = 0

    def _deps(self, reads, writes):
        deps = []
        for k in reads:
            t = self.lastw.get(k)
            if t is not None:
                deps.append(t)
        for k in writes:
            t = self.lastw.get(k)
            if t is not None:
                deps.append(t)
            deps.extend(self.readers.get(k, ()))
        return deps

    def _emit_waits(self, eng, deps, skip_self=False):
        need = {}
        for (sid, sem, val) in deps:
            if skip_self and sid == eng:
                continue
            if sid == eng and self.cnt[eng] - val >= SAME_ENG_GAP:
                continue
            if self.waited[eng].get(sid, 0) >= val:
                continue
            if need.get(sid, (None, 0))[1] < val:
                need[sid] = (sem, val)
        for sid, (sem, val) in need.items():
            self.waited[eng][sid] = val
            self.q[eng].append(lambda e, sem=sem, val=val: e.wait_ge(sem, val))
            self.ninstr += 1

    def _commit(self, tok, reads, writes):
        for k in writes:
            self.lastw[k] = tok
            self.readers[k] = []
        for k in reads:
            self.readers.setdefault(k, []).append(tok)

    @staticmethod
    def _bank(k):
        if isinstance(k, str) and len(k) >= 2 and k[0] == "p" and (k[1].isupper() or k[:2] in ("pj", "pg")):
            return k[:3] if k[:2] == "pj" else k[:2]
        return None

    def _norm(self, reads, writes):
        r2, w2 = [], []
        for k in reads:
            bk = self._bank(k)
            if bk is None:
                r2.append(k)
            elif bk not in w2:
                w2.append(bk)
        for k in writes:
            bk = self._bank(k)
            k2 = k if bk is None else bk
            if k2 not in w2:
                w2.append(k2)
        return r2, w2

    def op(self, eng, fn, reads=(), writes=(), skip_self=False):
        reads, writes = self._norm(reads, writes)
        deps = self._deps(reads, writes)
        self._emit_waits(eng, deps, skip_self=skip_self)
        self.cnt[eng] += 1
        sem = self.sem[eng]
        tok = (eng, sem, self.cnt[eng])
        self.q[eng].append(lambda e, fn=fn, sem=sem: fn(e).then_inc(sem, 1))
        self.ninstr += 1
        self._commit(tok, reads, writes)
        return tok

    def dma(self, eng, out, in_, reads=(), writes=(), slot=None, **kw):
        assert slot is not None
        if slot not in self.dsem:
            self.dsem[slot] = self.stack.enter_context(self.nc.semaphore("dsem_%d" % len(self.dsem)))
            self.dcnt[slot] = 0
        deps = self._deps(reads, writes)
        self._emit_waits(eng, deps)
        self.dcnt[slot] += 16
        sem = self.dsem[slot]
        tok = (("d", slot), sem, self.dcnt[slot])
        self.q[eng].append(
            lambda e, out=out, in_=in_, sem=sem, kw=kw: e.dma_start(out=out, in_=in_, **kw).then_inc(sem, 16))
        self.ninstr += 1
        self._commit(tok, reads, writes)
        return tok

    def finalize(self):
        nc = self.nc
        alltok = list(self.lastw.values())
        for ts in self.readers.values():
            alltok.extend(ts)
        self._emit_waits("sp", alltok)
        engmap = {"pe": "tensor", "act": "scalar", "dve": "vector", "pool": "gpsimd", "sp": "sync"}
        with nc.Block() as block:
            for e in ENGS:
                lst = self.q[e]
                if not lst:
                    continue

                def body(engine, lst=lst):
                    for f in lst:
                        f(engine)
                getattr(block, engmap[e])(body)


class B:
    def __init__(self, nc, st):
        self.nc = nc
        self.st = st
        self.P = Prog(nc, st)
        self.rr = 0

    def sb(self, name, shape, dt=F32):
        return self.st.enter_context(self.nc.sbuf_tensor(name, list(shape), dt))

    def ps(self, name, shape=(128, 512), dt=F32):
        return self.st.enter_context(self.nc.psum_tensor(name, list(shape), dt))

    def din(self, name, shape, dt=F32):
        return self.nc.dram_tensor(name, list(shape), dt, kind="ExternalInput").ap()

    def dout(self, name, shape, dt=F32):
        return self.nc.dram_tensor(name, list(shape), dt, kind="ExternalOutput").ap()

    def mm(self, out, lhsT, rhs, r, w, start=True, stop=True):
        self.P.op("pe", lambda e: e.matmul(out, lhsT=lhsT, rhs=rhs, start=start, stop=stop, skip_group_check=True),
                  reads=r, writes=w, skip_self=True)

    def mmr(self, out, lhsT, rhs, r, w, start=True, stop=True):
        self.mm(out, lhsT.bitcast(F32R), rhs.bitcast(F32R), r, w, start=start, stop=stop)

    def tr(self, out, in_, ident, r, w):
        self.P.op("pe", lambda e: e.transpose(out=out, in_=in_, identity=ident), reads=r, writes=w, skip_self=True)

    def act(self, out, in_, func, r, w, bias=None, scale=None, accum_out=None):
        kw = {}
        if bias is not None:
            kw["bias"] = bias
        if scale is not None:
            kw["scale"] = scale
        if accum_out is not None:
            kw["accum_out"] = accum_out
        self.P.op("act", lambda e: e.activation(out=out, in_=in_, func=func, **kw), reads=r, writes=w)

    def ts(self, eng, out, in0, s1, op0, r, w, s2=None, op1=None, accum_out=None):
        kw = {}
        if op1 is not None:
            kw["op1"] = op1
        if accum_out is not None:
            kw["accum_out"] = accum_out
        self.P.op(eng, lambda e: e.tensor_scalar(out=out, in0=in0, scalar1=s1, scalar2=s2, op0=op0, **kw),
                  reads=r, writes=w)

    def tt(self, eng, out, in0, in1, op, r, w):
        self.P.op(eng, lambda e: e.tensor_tensor(out=out, in0=in0, in1=in1, op=op), reads=r, writes=w)

    def stt(self, out, in0, scalar, in1, op0, op1, r, w, accum_out=None):
        kw = {}
        if accum_out is not None:
            kw["accum_out"] = accum_out
        self.P.op("dve", lambda e: e.scalar_tensor_tensor(out=out, in0=in0, scalar=scalar, in1=in1, op0=op0, op1=op1, **kw),
                  reads=r, writes=w)

    def cp(self, eng, out, in_, r, w):
        if eng == "act":
            self.P.op("act", lambda e: e.copy(out=out, in_=in_), reads=r, writes=w)
        else:
            self.P.op(eng, lambda e: e.tensor_copy(out=out, in_=in_), reads=r, writes=w)

    def memset(self, eng, ap, val, w):
        self.P.op(eng, lambda e: e.memset(ap, val), writes=w)

    def dma(self, eng, out, in_, r, w, slot, **kw):
        self.P.dma(eng, out, in_, reads=r, writes=w, slot=slot, **kw)


def _consts_np():
    c = np.zeros((128, 8, 128), np.float32)
    i = np.arange(128)
    c[:, 0, :] = np.eye(128)
    c[:, 1, :] = (i[:, None] <= i[None, :])
    c[:, 2, :] = np.where(i[None, :] < i[:, None], NEG, 0.0)
    c[:, 3, :] = np.where(i[None, :] > i[:, None], NEG, 0.0)
    c[:, 4, :] = 1.0
    c[:, 5, :] = (i[:, None] == 127)
    c[:, 6, :] = (i[:, None] == 31)
    c[:, 7, :] = (i[:, None] >= i[None, :])
    return c


class _Stop(Exception):
    pass


def _interleave(*gens):
    gens = list(gens)
    while gens:
        for g in list(gens):
            try:
                next(g)
            except StopIteration:
                gens.remove(g)


def build_phase1(ntiles_prompt=32, do_sample=True, debug=0):
    nc = bass.Bass("TRN2", target_bir_lowering=False)
    with ExitStack() as st:
        b = B(nc, st)
        P = b.P
        xT = b.din("xT", [D, NTOK])
        w1 = b.din("w1", [D, 1152])
        wg = b.din("wg", [128, KC * 4])
        convw = b.din("convw", [128, 12])
        scal = b.din("scal", [128, 8])
        consts = b.din("consts", [128, 8, 128])
        S0 = b.din("S0", [DB, 128, 128])
        conv0 = b.din("conv0", [128, DB, 3, 3])
        C0 = b.din("C0", [DB, 128, 128])
        n0 = b.din("n0", [128, DB])
        m0 = b.din("m0", [128, DB])
        yT = b.dout("yT", [256, NTOK], BF16)
        oS = b.dout("oS", [NB + DB, 128, 128])
        oconv = b.dout("oconv", [128, NB + DB, 3, 3])
        oC = b.dout("oC", [NB + DB, 128, 128])
        on = b.dout("on", [128, NB + DB])
        om = b.dout("om", [1, NB + DB])

        cst = b.sb("cst", [128, 8, 128])
        ident, triU, NEGU, NEGL, ones, SEL127, SEL31 = (cst[:, i, :] for i in range(7))
        w1b = b.sb("w1b", [128, KC, 1152], BF16)
        wgb = b.sb("wgb", [128, KC, 4], BF16)
        wst = b.sb("wst", [128, 2, 1152])
        cw = b.sb("cw", [128, 12])
        sc = b.sb("sc", [128, 8])
        dsc = b.sb("dsc", [128, 8])
        xst = [b.sb("xst%d" % i, [128, 4, 512]) for i in range(2)]
        xb = [b.sb("xb%d" % i, [128, KC, 512], BF16) for i in range(2)]
        cin = [b.sb("cin%d" % f, [128, 520]) for f in range(3)]
        PR = [[(b.sb("pr%d_%d" % (f, q), [128, 512]) if f >= 3 else None) for f in range(9)] for q in range(2)]
        CV = [[b.sb("cv%d_%d" % (f, q), [128, 512]) for f in range(3)] for q in range(2)]
        tmpA = b.sb("tmpA", [128, 512])
        tmpB = b.sb("tmpB", [128, 512])
        QN = [b.sb("qn_%d" % q, [128, 512]) for q in range(2)]
        KN = [b.sb("kn_%d" % q, [128, 512]) for q in range(2)]
        tmpC = b.sb("tmpC", [128, 512])
        tmpD = b.sb("tmpD", [128, 512])
        oT = b.sb("oT", [128, 512])
        hT = b.sb("hT", [128, 512])
        yb = b.sb("yb", [128, 2, 512], BF16)
        GAT = [b.sb("gat_%d" % q, [128, 8, 12]) for q in range(2)]
        GB = []
        for k in range(2):
            GB.append({n: b.sb("%s_%d" % (n, k), [128, 128]) for n in
                       ("Gbc", "grow", "DEC", "DECs", "X", "XT", "IXT", "Pm", "QKd", "kg", "kdec", "vtok", "nW", "qdec")})
            GB[k]["colg"] = b.sb("colg_%d" % k, [128, 4])
        ee = b.sb("ee", [128, 128])
        Sst = b.sb("Sst", [128, 128])
        col = b.sb("col", [128, 16])
        mprev = b.sb("mprev", [128, 1])
        bc2 = b.sb("bc2", [128, 2])
        c2 = b.sb("c2", [128, 2])
        Abc = b.sb("Abc", [128, 128])
        pmat = b.sb("pmat", [128, 128])
        pqk = b.sb("pqk", [128, 128])
        pqkT = b.sb("pqkT", [128, 128])
        v1 = b.sb("v1", [128, 129])
        kp = b.sb("kp", [128, 128])
        CN = b.sb("CN", [128, 129])
        n2s = b.sb("n2s", [128, 129])
        num = b.sb("num", [128, 128])
        htok = b.sb("htok", [128, 128])
        m0s = b.sb("m0s", [128, DB])
        n0s = b.sb("n0s", [128, DB])
        c0s = b.sb("c0s", [128, DB * 9])

        pj = [b.ps("pj%d" % i) for i in range(2)]
        pg = b.ps("pg")
        pA = b.ps("pA")
        pB = b.ps("pB")
        pC = b.ps("pC")
        pD = b.ps("pD")
        pE = b.ps("pE")

        b.dma("sp", cst[:], consts, [], ["cst"], "cst")
        b.dma("sp", cw[:], convw, [], ["cw"], "cw")
        b.dma("sp", sc[:], scal, [], ["sc"], "sc")
        b.dma("sp", m0s[:], m0, [], ["m0s"], "m0s")
        b.dma("sp", n0s[:], n0, [], ["n0s"], "n0s")
        b.dma("sp", c0s[:], conv0.rearrange("p b f r -> p (b f r)"), [], ["c0s"], "c0s")
        w1v = w1.rearrange("(kc p) n -> p kc n", p=128)
        for kc0 in range(0, KC, 2):
            b.dma("sp", wst[:, :, :], w1v[:, kc0:kc0 + 2, :], [], ["wst"], "wst")
            b.cp("dve", w1b[:, kc0, :], wst[:, 0, :], ["wst"], ["w1b"])
            b.cp("pool", w1b[:, kc0 + 1, :], wst[:, 1, :], ["wst"], ["w1b"])
        b.dma("sp", wst[:, 0, 0:KC * 4], wg, [], ["wst"], "wst")
        b.cp("dve", wgb[:].rearrange("p kc n -> p (kc n)"), wst[:, 0, 0:KC * 4], ["wst"], ["wgb"])
        b.act(dsc[:, 0:1], sc[:, 0:1], AF.Exp, ["sc"], ["dsc"])
        b.ts("dve", dsc[:, 0:1], dsc[:, 0:1], -1.0, ALU.mult, ["dsc"], ["dsc"])
        b.ts("dve", dsc[:, 1:2], sc[:, 3:4], -1.0, ALU.mult, ["sc"], ["dsc"])
        b.memset("pool", v1[:, 128:129], 1.0, ["v1"])

        xTv = xT.rearrange("(kc p) n -> p kc n", p=128)

        tiles = []
        for bi in range(NB):
            for t in range(ntiles_prompt // NB):
                tiles.append(dict(kind="p", seq=bi, tok0=bi * SEQ + t * 512, N=512, L=128, nch=4,
                                  first=(t == 0), last=(t == ntiles_prompt // NB - 1)))
        if do_sample:
            tiles.append(dict(kind="s", seq=None, tok0=NTP, N=256, L=32, nch=8, first=True, last=True))

        def load_x(ti, slot):
            T = tiles[ti]
            N = T["N"]
            for q4 in range(4):
                b.dma("sp", xst[q4 % 2][:, :, 0:N], xTv[:, q4 * 4:(q4 + 1) * 4, T["tok0"]:T["tok0"] + N],
                      [], ["xst%d" % (q4 % 2)], "xst%d" % (q4 % 2))
                b.cp("pool", xb[slot][:, q4 * 4:(q4 + 1) * 4, 0:N], xst[q4 % 2][:, :, 0:N],
                     ["xst%d" % (q4 % 2)], ["xb%d" % slot])

        load_x(0, 0)
        if len(tiles) > 1:
            load_x(1, 1)
        def g_P(ti):
            T = tiles[ti]
            slot = ti % 2
            pset = ti % 2
            N, L, nch = (T['N'], T['L'], T['nch'])
            xbk = 'xb%d' % slot
            KY = lambda n: '%s_%d' % (n, pset)
            pr, cv, qn, kn, gat = (PR[pset], CV[pset], QN[pset], KN[pset], GAT[pset])
            nseq = 1 if T['kind'] == 'p' else DB
            Ls = N // nseq
            W = Ls + 3

            def cin3(f, a, bnd):
                return cin[f][:, 0:nseq * W].rearrange('p (s l) -> p s l', l=W)[:, :, a:bnd]
            if T['first']:
                if T['kind'] == 'p':
                    for f in range(3):
                        b.memset('pool', cin[f][:, 0:3], 0.0, ['cin%d' % f])
                        yield
                else:
                    for f in range(3):
                        b.cp('pool', cin3(f, 0, 3), c0s[:].rearrange('p (b f r) -> p b f r', f=3, r=3)[:, :, f, :], ['c0s', 'cin%d' % f], ['cin%d' % f])
                        yield
            for f in range(9):
                bank = pj[f % 2]
                bk = 'pj%d' % (f % 2)
                for kc in range(KC):
                    b.mm(bank[:, 0:N], w1b[:, kc, f * 128:(f + 1) * 128], xb[slot][:, kc, 0:N], ['w1b', xbk], [bk], start=kc == 0, stop=kc == KC - 1)
                yield
                if f < 3:
                    b.cp('act', cin3(f, 3, W), bank[:, 0:N].rearrange('p (s l) -> p s l', l=Ls), [bk], ['cin%d' % f])
                    yield
                elif f == 5:
                    b.act(pr[f][:, 0:N], bank[:, 0:N], AF.Copy, [bk], [KY('pr%d' % f)], scale=128.0 ** (-0.5))
                    yield
                else:
                    b.cp('dve' if f % 2 else 'act', pr[f][:, 0:N], bank[:, 0:N], [bk], [KY('pr%d' % f)])
                    yield
            for j in range(nch):
                for kc in range(KC):
                    b.mm(pg[0:L, j * 4:(j + 1) * 4], xb[slot][:, kc, j * L:(j + 1) * L], wgb[:, kc, :], [xbk, 'wgb'], ['pg_g'], start=kc == 0, stop=kc == KC - 1)
                yield
            G = pg[0:L, 0:nch * 4].rearrange('p (j g) -> p j g', g=4)
            gt = lambda i: gat[0:L, 0:nch, i]
            b.act(gt(0), G[:, :, 0], AF.Sigmoid, ['pg_g'], [KY('gat')])
            yield
            b.ts('dve', gt(1), gt(0), -1.0, ALU.mult, [KY('gat')], [KY('gat')])
            yield
            b.act(gt(9), G[:, :, 1], AF.Exp, ['pg_g', 'sc'], [KY('gat')], bias=sc[0:L, 1:2])
            yield
            b.act(gt(9), gt(9), AF.Ln, [KY('gat')], [KY('gat')], bias=1.0)
            yield
            b.ts('dve', gt(2), gt(9), dsc[0:L, 0:1], ALU.mult, [KY('gat'), 'dsc'], [KY('gat')])
            yield
            b.ts('dve', gt(3), G[:, :, 2], sc[0:L, 2:3], ALU.add, ['pg_g', 'sc'], [KY('gat')])
            yield
            b.act(gt(10), G[:, :, 3], AF.Exp, ['pg_g', 'dsc'], [KY('gat')], bias=dsc[0:L, 1:2], scale=-1.0)
            yield
            b.act(gt(10), gt(10), AF.Ln, [KY('gat')], [KY('gat')], bias=1.0)
            yield
            b.ts('dve', gt(4), gt(10), -1.0, ALU.mult, [KY('gat')], [KY('gat')])
            yield
            gtmp = tmpA[0:L, 0:2 * nch]
            b.cp('dve', gtmp[:, 0:nch], gt(2), [KY('gat')], ['tmpA'])
            yield
            b.cp('dve', gtmp[:, nch:2 * nch], gt(4), [KY('gat')], ['tmpA'])
            yield
            b.mm(pg[0:L, 32:32 + 2 * nch], triU[0:L, 0:L], gtmp, ['cst', 'tmpA'], ['pg_c'])
            yield
            b.cp('dve', gt(5), pg[0:L, 32:32 + nch], ['pg_c'], [KY('gat')])
            yield
            b.cp('dve', gt(7), pg[0:L, 32 + nch:32 + 2 * nch], ['pg_c'], [KY('gat')])
            yield
            b.act(gt(6), gt(5), AF.Exp, [KY('gat')], [KY('gat')])
            yield
            b.ts('dve', gt(11), gt(5), -1.0, ALU.mult, [KY('gat')], [KY('gat')])
            yield
            b.tt('dve', gt(8), gt(3), gt(7), ALU.subtract, [KY('gat')], [KY('gat')])
            yield
            for f in range(3):
                ck = 'cin%d' % f
                o3 = tmpA[:, 0:N].rearrange('p (s l) -> p s l', l=Ls)
                b.ts('dve', o3, cin3(f, 0, Ls), cw[:, f * 4:f * 4 + 1], ALU.mult, [ck, 'cw'], ['tmpA'])
                yield
                for i in range(1, 4):
                    b.stt(o3, cin3(f, i, i + Ls), cw[:, f * 4 + i:f * 4 + i + 1], o3, ALU.mult, ALU.add, [ck, 'cw', 'tmpA'], ['tmpA'])
                    yield
                b.act(cv[f][:, 0:N], tmpA[:, 0:N], AF.Silu, ['tmpA'], [KY('cv%d' % f)])
                yield
                if T['last']:
                    if T['kind'] == 'p':
                        b.dma('sp', oconv[:, T['seq'], f, :], cin[f][:, Ls:Ls + 3], [ck], ['oconv'], 'oconv%d' % f)
                        yield
                    else:
                        b.dma('sp', oconv[:, NB:NB + DB, f, :], cin3(f, Ls, Ls + 3), [ck], ['oconv'], 'oconv%d' % f)
                        yield
                else:
                    b.cp('pool', cin[f][:, 0:3], cin[f][:, Ls:Ls + 3], [ck], [ck])
                    yield
            for f, dst, dk, lnb in ((0, qn, KY('qn'), -0.5 * math.log(128.0)), (1, kn, KY('kn'), 0.0)):
                b.tt('pool', tmpB[:, 0:N], cv[f][:, 0:N], cv[f][:, 0:N], ALU.mult, [KY('cv%d' % f)], ['tmpB'])
                yield
                b.mm(pj[f][:, 0:N], ones, tmpB[:, 0:N], ['cst', 'tmpB'], ['pj%d' % f])
                yield
                b.act(tmpB[:, 0:N], pj[f][:, 0:N], AF.Ln, ['pj%d' % f], ['tmpB'], bias=1e-06)
                yield
                b.act(tmpB[:, 0:N], tmpB[:, 0:N], AF.Exp, ['tmpB'], ['tmpB'], scale=-0.5, bias=lnb)
                yield
                b.tt('dve', dst[:, 0:N], cv[f][:, 0:N], tmpB[:, 0:N], ALU.mult, [KY('cv%d' % f), 'tmpB'], [dk])
                yield
            yield

        def chunk_gens(ti):
            T = tiles[ti]
            slot = ti % 2
            pset = ti % 2
            N, L, nch = (T['N'], T['L'], T['nch'])
            xbk = 'xb%d' % slot
            KY = lambda n: '%s_%d' % (n, pset)
            pr, cv, qn, kn, gat = (PR[pset], CV[pset], QN[pset], KN[pset], GAT[pset])
            I_L = ident[0:L, 0:L]
            NK = 2
            pre_done = [False] * nch
            seq_done = [False] * nch
            PBK = [(pA, 'pA0'), (pB, 'pB0')]

            def pre_chain(k):
                G_ = GB[k]
                bk, bkey = PBK[k]
                kx = lambda n: '%s_%d' % (n, k)
                Gbc, grow, DEC, DECs, X, XT, IXT, Pm = (G_[n] for n in ('Gbc', 'grow', 'DEC', 'DECs', 'X', 'XT', 'IXT', 'Pm'))
                QKd, kg, kdec, vtok, nW, qdec, colg = (G_[n] for n in ('QKd', 'kg', 'kdec', 'vtok', 'nW', 'qdec', 'colg'))
                R0, R1, R2, R3 = (bk[:, i * 128:(i + 1) * 128] for i in range(4))
                for j in range(k, nch, NK):
                    if j >= NK:
                        while not seq_done[j - NK]:
                            yield
                    c0, c1 = (j * L, (j + 1) * L)
                    gcol = gat[0:L, j, 2:3]
                    b.ts('dve', Gbc[0:L, :], ones[0:L, :], gcol, ALU.mult, ['cst', KY('gat')], [kx('Gbc')])
                    yield
                    b.mm(R0[:, 0:L], Gbc[0:L, :], triU[0:L, 0:L], [kx('Gbc'), 'cst'], [bkey])
                    yield
                    b.mm(R1[0:L, 0:L], Gbc[0:L, 0:L], triU[0:L, 0:L], [kx('Gbc'), 'cst'], [bkey], start=True, stop=False)
                    yield
                    b.mm(R1[0:L, 0:L], I_L, NEGU[0:L, 0:L], ['cst'], [bkey], start=False, stop=True)
                    yield
                    b.mm(R2[0:L, 0:L], kn[:, c0:c1], kn[:, c0:c1], [KY('kn')], [bkey])
                    yield
                    b.mm(R3[0:L, 0:L], kn[:, c0:c1], qn[:, c0:c1], [KY('kn'), KY('qn')], [bkey])
                    yield
                    b.act(grow[:, 0:L], R0[:, 0:L], AF.Exp, [bkey], [kx('grow')])
                    yield
                    b.act(DEC[0:L, 0:L], R1[0:L, 0:L], AF.Exp, [bkey, KY('gat')], [kx('DEC')], bias=gat[0:L, j, 11:12])
                    yield
                    b.cp('dve', colg[:, 0:1], R0[:, L - 1:L], [bkey], [kx('col0')])
                    yield
                    b.act(colg[:, 1:2], colg[:, 0:1], AF.Exp, [kx('col0')], [kx('col1')])
                    yield
                    b.act(colg[0:L, 2:3], gat[0:L, j, 5:6], AF.Exp, [KY('gat'), kx('col0')], [kx('col2')], scale=-1.0, bias=colg[0:L, 0:1])
                    yield
                    b.tt('dve', DECs[0:L, 0:L], DEC[0:L, 0:L], I_L, ALU.subtract, [kx('DEC'), 'cst'], [kx('DECs')])
                    yield
                    b.stt(X[0:L, 0:L], R2[0:L, 0:L], gat[0:L, j, 1:2], DECs[0:L, 0:L], ALU.mult, ALU.mult, [bkey, KY('gat'), kx('DECs')], [kx('X')])
                    yield
                    b.tt('dve', QKd[0:L, 0:L], R3[0:L, 0:L], DEC[0:L, 0:L], ALU.mult, [bkey, kx('DEC')], [kx('QKd')])
                    yield
                    b.tt('dve', qdec[:, 0:L], qn[:, c0:c1], grow[:, 0:L], ALU.mult, [KY('qn'), kx('grow')], [kx('qdec')])
                    yield
                    b.tr(R0[0:L, 0:128], kn[:, c0:c1], ident, [KY('kn'), 'cst'], [bkey])
                    yield
                    b.tr(R1[0:L, 0:128], cv[2][:, c0:c1], ident, [KY('cv2'), 'cst'], [bkey])
                    yield
                    b.tr(R2[0:L, 0:L], X[0:L, 0:L], I_L, [kx('X'), 'cst'], [bkey])
                    yield
                    b.ts('dve', kg[0:L, :], R0[0:L, 0:128], gat[0:L, j, 6:7], ALU.mult, [bkey, KY('gat')], [kx('kg')])
                    yield
                    b.ts('dve', kdec[0:L, :], R0[0:L, 0:128], colg[0:L, 2:3], ALU.mult, [bkey, kx('col2')], [kx('kdec')])
                    yield
                    b.cp('act', vtok[0:L, :], R1[0:L, 0:128], [bkey], [kx('vtok')])
                    yield
                    b.cp('act', XT[0:L, 0:L], R2[0:L, 0:L], [bkey], [kx('XT')])
                    yield
                    b.tt('dve', Pm[0:L, 0:L], X[0:L, 0:L], I_L, ALU.add, [kx('X'), 'cst'], [kx('Pm')])
                    yield
                    nsq = int(math.log2(L)) - 1
                    for it in range(nsq):
                        lastit = it == nsq - 1
                        if not lastit:
                            b.mm(R0[0:L, 0:L], XT[0:L, 0:L], X[0:L, 0:L], [kx('XT'), kx('X')], [bkey])
                            yield
                        b.mm(R1[0:L, 0:L], X[0:L, 0:L], XT[0:L, 0:L], [kx('XT'), kx('X')], [bkey])
                        yield
                        if not lastit:
                            b.cp('act', X[0:L, 0:L], R0[0:L, 0:L], [bkey], [kx('X')])
                            yield
                            b.cp('dve', XT[0:L, 0:L], R1[0:L, 0:L], [bkey], [kx('XT')])
                            yield
                        b.tt('dve', IXT[0:L, 0:L], R1[0:L, 0:L], I_L, ALU.add, [bkey, 'cst'], [kx('IXT')])
                        yield
                        b.mm(R3[0:L, 0:L], IXT[0:L, 0:L], Pm[0:L, 0:L], [kx('IXT'), kx('Pm')], [bkey])
                        yield
                        b.cp('act', Pm[0:L, 0:L], R3[0:L, 0:L], [bkey], [kx('Pm')])
                        yield
                    b.mm(R2[:, 0:L], kg[0:L, :], Pm[0:L, 0:L], [kx('kg'), kx('Pm')], [bkey])
                    yield
                    b.ts('dve', nW[:, 0:L], R2[:, 0:L], -1.0, ALU.mult, [bkey], [kx('nW')])
                    yield
                    pre_done[j] = True

            def seq_chain():
                for j in range(nch):
                    while not pre_done[j]:
                        yield
                    k = j % NK
                    G_ = GB[k]
                    kx = lambda n: '%s_%d' % (n, k)
                    Pm, QKd, kdec, vtok, nW, qdec, colg = (G_[n] for n in ('Pm', 'QKd', 'kdec', 'vtok', 'nW', 'qdec', 'colg'))
                    c0, c1 = (j * L, (j + 1) * L)
                    if T['kind'] == 's':
                        seq, seq_last = (NB + j, True)
                        b.dma('sp', Sst[:], S0[j], [], ['S'], 'S_in')
                    else:
                        seq = T['seq']
                        seq_last = T['last'] and j == nch - 1
                        if T['first'] and j == 0:
                            b.memset('pool', Sst[:], 0.0, ['S'])
                    yield
                    b.mm(pC[0:L, 0:128], Pm[0:L, 0:L], vtok[0:L, :], [kx('Pm'), kx('vtok')], ['pC0'], start=True, stop=False)
                    yield
                    b.mm(pC[0:L, 0:128], nW[:, 0:L], Sst[:], [kx('nW'), 'S'], ['pC0'], start=False, stop=True)
                    yield
                    b.ts('dve', ee[0:L, :], pC[0:L, 0:128], gat[0:L, j, 0:1], ALU.mult, ['pC0', KY('gat')], ['ee'])
                    yield
                    b.mm(pC[:, 128:128 + L], Sst[:], qdec[:, 0:L], ['S', kx('qdec')], ['pC1'], start=True, stop=False)
                    yield
                    b.mm(pC[:, 128:128 + L], ee[0:L, :], QKd[0:L, 0:L], ['ee', kx('QKd')], ['pC1'], start=False, stop=True)
                    yield
                    b.mm(pC[:, 256:384], kdec[0:L, :], ee[0:L, :], [kx('kdec'), 'ee'], ['pC2'])
                    yield
                    b.cp('act', oT[:, c0:c1], pC[:, 128:128 + L], ['pC1'], ['oT'])
                    yield
                    b.stt(Sst[:], Sst[:], colg[:, 1:2], pC[:, 256:384], ALU.mult, ALU.add, ['S', kx('col1'), 'pC2'], ['S'])
                    yield
                    if seq_last:
                        b.dma('sp', oS[seq], Sst[:], ['S'], ['oS'], 'oS')
                        yield
                    seq_done[j] = True

            def ml_chain():
                for j in range(nch):
                    c0, c1 = (j * L, (j + 1) * L)
                    if T['kind'] == 's':
                        seq, seq_last = (NB + j, True)
                        b.dma('sp', CN[:, 0:128], C0[j], [], ['CN'], 'CN_in')
                        b.cp('pool', CN[:, 128:129], n0s[:, j:j + 1], ['n0s'], ['CN'])
                        b.cp('pool', mprev[:], m0s[:, j:j + 1], ['m0s'], ['mprev'])
                    else:
                        seq = T['seq']
                        seq_last = T['last'] and j == nch - 1
                        if T['first'] and j == 0:
                            b.memset('pool', CN[:], 0.0, ['CN'])
                            b.memset('pool', mprev[:], 0.0, ['mprev'])
                    yield
                    qb, kb, vb = (pr[4], pr[5], pr[6])
                    acol = gat[0:L, j, 8:9]
                    b.ts('dve', Abc[0:L, 0:L], ones[0:L, 0:L], acol, ALU.mult, ['cst', KY('gat')], ['Abc'])
                    yield
                    b.mm(pD[0:L, 256:256 + L], Abc[0:L, 0:L], I_L, ['Abc', 'cst'], ['pD2'], start=True, stop=False)
                    yield
                    b.mm(pD[0:L, 256:256 + L], I_L, NEGL[0:L, 0:L], ['cst'], ['pD2'], start=False, stop=True)
                    yield
                    b.P.op('dve', lambda e, o=col[0:L, 3:4], i=pD[0:L, 256:256 + L]: e.tensor_reduce(out=o, in_=i, axis=AX.X, op=ALU.max), reads=['pD2'], writes=['col3'])
                    yield
                    b.tt('dve', col[0:L, 4:5], col[0:L, 3:4], mprev[0:L, :], ALU.max, ['col3', 'mprev'], ['col4'])
                    yield
                    b.ts('dve', col[0:L, 5:6], col[0:L, 4:5], -1.0, ALU.mult, ['col4'], ['col5'])
                    yield
                    b.act(pmat[0:L, 0:L], pD[0:L, 256:256 + L], AF.Exp, ['pD2', 'col5'], ['pmat'], bias=col[0:L, 5:6])
                    yield
                    b.act(col[0:L, 6:7], mprev[0:L, :], AF.Exp, ['mprev', 'col5'], ['col6'], bias=col[0:L, 5:6])
                    yield
                    b.tt('dve', col[0:L, 7:8], gat[0:L, j, 7:8], col[0:L, 4:5], ALU.add, [KY('gat'), 'col4'], ['col7'])
                    yield
                    b.act(col[0:L, 8:9], col[0:L, 7:8], AF.Exp, ['col7'], ['col8'], scale=-1.0)
                    yield
                    b.mm(pD[0:L, 384:384 + L], qb[:, c0:c1], kb[:, c0:c1], [KY('pr4'), KY('pr5')], ['pD3'])
                    yield
                    b.stt(pqk[0:L, 0:L], pD[0:L, 384:384 + L], 1.0, pmat[0:L, 0:L], ALU.mult, ALU.mult, ['pD3', 'pmat'], ['pqk', 'col9'], accum_out=col[0:L, 9:10])
                    yield
                    b.tr(pE[0:L, 0:L], pqk[0:L, 0:L], I_L, ['pqk', 'cst'], ['pE0'])
                    yield
                    b.cp('act', pqkT[0:L, 0:L], pE[0:L, 0:L], ['pE0'], ['pqkT'])
                    yield
                    b.cp('dve', c2[0:L, 0:1], col[0:L, 4:5], ['col4'], ['c2'])
                    yield
                    b.cp('dve', c2[0:L, 1:2], gat[0:L, j, 7:8], [KY('gat')], ['c2'])
                    yield
                    SEL = SEL127 if L == 128 else SEL31
                    b.mm(pg[:, 48:50], SEL[0:L, :], c2[0:L, :], ['cst', 'c2'], ['pg_b'])
                    yield
                    b.cp('dve', bc2[:], pg[:, 48:50], ['pg_b'], ['bc2'])
                    yield
                    b.ts('dve', col[:, 10:11], bc2[:, 0:1], -1.0, ALU.mult, ['bc2'], ['col10'])
                    yield
                    b.act(col[0:L, 11:12], acol, AF.Exp, [KY('gat'), 'col10'], ['col11'], bias=col[0:L, 10:11])
                    yield
                    b.act(col[:, 12:13], mprev[:], AF.Exp, ['mprev', 'col10'], ['col12'], bias=col[:, 10:11])
                    yield
                    b.tr(pE[0:L, 128:256], kb[:, c0:c1], ident, [KY('pr5'), 'cst'], ['pE1'])
                    yield
                    b.tr(pE[0:L, 256:384], vb[:, c0:c1], ident, [KY('pr6'), 'cst'], ['pE2'])
                    yield
                    b.ts('dve', kp[0:L, :], pE[0:L, 128:256], col[0:L, 11:12], ALU.mult, ['pE1', 'col11'], ['kp'])
                    yield
                    b.cp('act', v1[0:L, 0:128], pE[0:L, 256:384], ['pE2'], ['v1'])
                    yield
                    b.mm(pg[0:L, 64:192], pqkT[0:L, 0:L], v1[0:L, 0:128], ['pqkT', 'v1'], ['pg_n1'])
                    yield
                    b.mm(pg[0:L, 192:321], qb[:, c0:c1], CN[:, 0:129], [KY('pr4'), 'CN'], ['pg_n2'])
                    yield
                    b.act(n2s[0:L, :], pg[0:L, 192:321], AF.Copy, ['pg_n2', 'col6'], ['n2s'], scale=col[0:L, 6:7])
                    yield
                    b.tt('dve', num[0:L, :], n2s[0:L, 0:128], pg[0:L, 64:192], ALU.add, ['n2s', 'pg_n1'], ['num'])
                    yield
                    b.tt('dve', col[0:L, 13:14], n2s[0:L, 128:129], col[0:L, 9:10], ALU.add, ['n2s', 'col9'], ['col13'])
                    yield
                    b.act(col[0:L, 13:14], col[0:L, 13:14], AF.Abs, ['col13'], ['col13'])
                    yield
                    b.tt('dve', col[0:L, 13:14], col[0:L, 13:14], col[0:L, 8:9], ALU.max, ['col13', 'col8'], ['col13'])
                    yield
                    b.P.op('dve', lambda e, o=col[0:L, 14:15], i=col[0:L, 13:14]: e.reciprocal(out=o, in_=i), reads=['col13'], writes=['col14'])
                    yield
                    b.ts('dve', htok[0:L, :], num[0:L, :], col[0:L, 14:15], ALU.mult, ['num', 'col14'], ['htok'])
                    yield
                    b.tr(pE[:, 384:384 + L], htok[0:L, :], I_L, ['htok', 'cst'], ['pE3'])
                    yield
                    b.cp('act', hT[:, c0:c1], pE[:, 384:384 + L], ['pE3'], ['hT'])
                    yield
                    b.mm(pg[:, 321:450], kp[0:L, :], v1[0:L, :], ['kp', 'v1'], ['pg_c2'])
                    yield
                    b.stt(CN[:], CN[:], col[:, 12:13], pg[:, 321:450], ALU.mult, ALU.add, ['CN', 'col12', 'pg_c2'], ['CN'])
                    yield
                    b.tt('dve', mprev[:], bc2[:, 0:1], bc2[:, 1:2], ALU.add, ['bc2'], ['mprev'])
                    yield
                    if seq_last:
                        b.dma('sp', oC[seq], CN[:, 0:128], ['CN'], ['oC'], 'oC')
                        yield
                        b.dma('sp', on[:, seq:seq + 1], CN[:, 128:129], ['CN'], ['on'], 'on', allow_slow_non_contiguous=True)
                        yield
                        b.dma('sp', om[0:1, seq:seq + 1], mprev[0:1, 0:1], ['mprev'], ['om'], 'om')
                        yield
                    yield
            return [pre_chain(k) for k in range(NK)] + [seq_chain(), ml_chain()]

        def post(ti):
            T = tiles[ti]
            slot = ti % 2
            pset = ti % 2
            N, L, nch = (T['N'], T['L'], T['nch'])
            xbk = 'xb%d' % slot
            KY = lambda n: '%s_%d' % (n, pset)
            pr, cv, qn, kn, gat = (PR[pset], CV[pset], QN[pset], KN[pset], GAT[pset])
            nseq = 1 if T['kind'] == 'p' else DB
            for src, srck, zf, of, ncol, half in ((oT, 'oT', 3, None, 4, 0), (hT, 'hT', 8, 7, 5, 1)):
                if of is not None:
                    b.act(tmpC[:, 0:N], pr[of][:, 0:N], AF.Sigmoid, [KY('pr%d' % of)], ['tmpC'])
                    b.tt('dve', src[:, 0:N], src[:, 0:N], tmpC[:, 0:N], ALU.mult, [srck, 'tmpC'], [srck])
                b.tt('pool', tmpD[:, 0:N], src[:, 0:N], src[:, 0:N], ALU.mult, [srck], ['tmpD'])
                b.mm(pj[half][:, 0:N], ones, tmpD[:, 0:N], ['cst', 'tmpD'], ['pj%d' % half])
                b.act(tmpD[:, 0:N], pj[half][:, 0:N], AF.Ln, ['pj%d' % half], ['tmpD'], scale=1.0 / 128.0, bias=1e-06)
                b.act(tmpD[:, 0:N], tmpD[:, 0:N], AF.Exp, ['tmpD'], ['tmpD'], scale=-0.5)
                b.act(tmpC[:, 0:N], pr[zf][:, 0:N], AF.Silu, [KY('pr%d' % zf)], ['tmpC'])
                b.stt(tmpD[:, 0:N], tmpD[:, 0:N], sc[:, ncol:ncol + 1], tmpC[:, 0:N], ALU.mult, ALU.mult, ['tmpD', 'sc', 'tmpC'], ['tmpD'])
                b.tt('dve', yb[:, half, 0:N], src[:, 0:N], tmpD[:, 0:N], ALU.mult, [srck, 'tmpD'], ['yb%d' % half])
                b.dma('sp', yT[half * 128:(half + 1) * 128, T['tok0']:T['tok0'] + N], yb[:, half, 0:N], ['yb%d' % half], ['yT'], 'yT%d' % half)

        for _ in g_P(0):
            pass
        for ti in range(len(tiles)):
            if ti + 2 < len(tiles):
                load_x(ti + 2, ti % 2)
            gens = chunk_gens(ti)
            if ti + 1 < len(tiles):
                gens.append(g_P(ti + 1))
            _interleave(*gens)
            post(ti)
        P.finalize()
        print("phase1 instrs", P.ninstr, {e: P.cnt[e] for e in ENGS})
    return nc


NTT = 17


def build_out_phase(glu, emit_T):
    nc = bass.Bass("TRN2", target_bir_lowering=False)
    KCI = 24 if glu else 16
    with ExitStack() as st:
        b = B(nc, st)
        P = b.P
        yin = b.din("yin", [NTT, 128, KCI, 128], BF16)
        wout = b.din("wout", [D, D])
        xres = b.din("xres", [NTT * 128, D])
        lng = b.din("lng", [128, D])
        lnb = b.din("lnb", [128, D])
        consts = b.din("consts", [128, 8, 128])
        if glu:
            wglu = b.din("wglu", [1024, 1024])
            bglu = b.din("bglu", [128, 8])
        xo = b.dout("xo", [NTT * 128, D])
        if emit_T:
            xoT = b.dout("xoT", [NTT, 128, KC, 128], BF16)

        cst = b.sb("cst", [128, 8, 128])
        ident = cst[:, 0, :]
        woutb = b.sb("woutb", [128, KC, D], BF16)
        wst = b.sb("wst", [128, 2, D])
        g_sb = b.sb("g_sb", [128, D])
        b_sb = b.sb("b_sb", [128, D])
        ysb = [b.sb("ysb%d" % i, [128, KCI, 128], BF16) for i in range(2)]
        xr = [b.sb("xr%d" % i, [128, D]) for i in range(2)]
        Z2 = [b.sb("z%d" % i, [128, D]) for i in range(2)]
        stats = b.sb("stats", [128, 4, 6])
        mv = b.sb("mv", [128, 4])
        xT_sb = [b.sb("xT_sb%d" % i, [128, KC, 128], BF16) for i in range(2)]
        if glu:
            wglub = b.sb("wglub", [128, 8, 1024], BF16)
            bg = b.sb("bg", [128, 8])
            sig = b.sb("sig", [128, 8, 128])
            yd = b.sb("yd", [128, 8, 128], BF16)
        po = [b.ps("po%d" % i) for i in range(4)]
        pt = [b.ps("pt%d" % i) for i in range(4)]
        PK = lambda i: "pj%d" % i

        b.dma("sp", cst[:], consts, [], ["cst"], "cst")
        b.dma("sp", g_sb[:], lng, [], ["g_sb"], "g_sb")
        b.dma("sp", b_sb[:], lnb, [], ["b_sb"], "b_sb")
        wv = wout.rearrange("(kc p) n -> p kc n", p=128)
        for kc0 in range(0, KC, 2):
            b.dma("sp", wst[:], wv[:, kc0:kc0 + 2, :], [], ["wst"], "wst")
            b.cp("dve", woutb[:, kc0, :], wst[:, 0, :], ["wst"], ["woutb"])
            b.cp("pool", woutb[:, kc0 + 1, :], wst[:, 1, :], ["wst"], ["woutb"])
        if glu:
            gv = wglu.rearrange("(kc p) n -> p kc n", p=128)
            for kc0 in range(0, 8, 4):
                b.dma("sp", wst[:].rearrange("p a (c n) -> p (a c) n", n=1024), gv[:, kc0:kc0 + 4, :], [], ["wst"], "wst")
                b.cp("pool", wglub[:, kc0:kc0 + 4, :], wst[:].rearrange("p a (c n) -> p (a c) n", n=1024), ["wst"], ["wglub"])
            b.dma("sp", bg[:], bglu, [], ["bg"], "bg")

        def load(ti):
            s = ti % 2
            b.dma("sp", ysb[s][:], yin[ti], [], ["ysb%d" % s], "ysb%d" % s)
            b.dma("sp", xr[s][:], xres[ti * 128:(ti + 1) * 128, :], [], ["xr%d" % s], "xr%d" % s)

        def g_mm(ti):
            s = ti % 2
            yk = "ysb%d" % s
            if glu:
                for f in range(8):
                    bank = pt[f // 4]
                    for kc in range(8):
                        b.mm(bank[:, (f % 4) * 128:(f % 4 + 1) * 128], wglub[:, kc, f * 128:(f + 1) * 128],
                             ysb[s][:, 8 + kc, :], ["wglub", yk], [PK(4 + f // 4)], start=(kc == 0), stop=(kc == 7))
                    yield
                for f in range(8):
                    b.act(sig[:, f, :], pt[f // 4][:, (f % 4) * 128:(f % 4 + 1) * 128], AF.Sigmoid,
                          [PK(4 + f // 4), "bg"], ["sig"], bias=bg[:, f:f + 1])
                    yield
                b.tt("dve", sig[:], sig[:], ysb[s][:, 8:16, :], ALU.mult, ["sig", yk], ["sig"])
                yield
                b.tt("dve", yd[:], sig[:], ysb[s][:, 16:24, :], ALU.mult, ["sig", yk], ["yd"])
                yield
            halves = ((0, 8), (8, KC)) if glu else ((0, KC),)
            for (k0, k1) in halves:
                for n in range(4):
                    for kc in range(k0, k1):
                        if glu and kc >= 8:
                            lhsT, lk = yd[:, kc - 8, :], "yd"
                        else:
                            lhsT, lk = ysb[s][:, kc, :], yk
                        b.mm(po[n][:, :], lhsT, woutb[:, kc, n * 512:(n + 1) * 512], [lk, "woutb"], [PK(n)],
                             start=(kc == 0), stop=(kc == KC - 1))
                    yield

        def evac(ti):
            s = ti % 2
            z = Z2[s]
            for n in range(4):
                b.stt(z[:, n * 512:(n + 1) * 512], xr[s][:, n * 512:(n + 1) * 512], ALPHA, po[n][:, :],
                      ALU.mult, ALU.add, ["xr%d" % s, PK(n)], ["z%d" % s])

        def g_ln(ti):
            s = ti % 2
            z, zk = Z2[s], "z%d" % s
            for n in range(4):
                b.P.op("dve", lambda e, o=stats[:, n, :], i=z[:, n * 512:(n + 1) * 512]: e.bn_stats(out=o, in_=i),
                       reads=[zk], writes=["stats"])
                yield
            b.P.op("dve", lambda e, o=mv[:, 0:2], i=stats[:].rearrange("p a c -> p (a c)"): e.bn_aggr(out=o, in_=i),
                   reads=["stats"], writes=["mv"])
            yield
            b.act(mv[:, 2:3], mv[:, 1:2], AF.Ln, ["mv"], ["mv"], bias=1e-5)
            yield
            b.act(mv[:, 2:3], mv[:, 2:3], AF.Exp, ["mv"], ["mv"], scale=-0.5)
            yield
            b.ts("dve", z[:], z[:], mv[:, 0:1], ALU.subtract, [zk, "mv"], [zk], s2=mv[:, 2:3], op1=ALU.mult)
            yield
            b.tt("pool", z[:], z[:], g_sb[:], ALU.mult, [zk, "g_sb"], [zk])
            yield
            b.tt("dve", z[:], z[:], b_sb[:], ALU.add, [zk, "b_sb"], [zk])
            yield
            b.dma("sp", xo[ti * 128:(ti + 1) * 128, :], z[:], [zk], ["xo"], "xo%d" % s)
            yield
            if emit_T:
                for q in range(4):
                    for kc in range(q * 4, q * 4 + 4):
                        b.tr(pt[kc // 4][:, (kc % 4) * 128:(kc % 4 + 1) * 128], z[:, kc * 128:(kc + 1) * 128], ident,
                             [zk, "cst"], [PK(4 + kc // 4)])
                    b.cp("act" if q % 2 else "dve", xT_sb[s][:, q * 4:(q + 1) * 4, :].rearrange("p a t -> p (a t)"),
                         pt[q][:, :], [PK(4 + q)], ["xT_sb%d" % s])
                    yield
                b.dma("sp", xoT[ti], xT_sb[s][:], ["xT_sb%d" % s], ["xoT"], "xoT%d" % s)
                yield

        load(0)
        if NTT > 1:
            load(1)
        for _ in g_mm(0):
            pass
        for ti in range(NTT):
            evac(ti)
            if ti + 2 < NTT:
                load(ti + 2)
            gens = [g_ln(ti)]
            if ti + 1 < NTT:
                gens.append(g_mm(ti + 1))
            _interleave(*gens)
        P.finalize()
        print("out-phase instrs", P.ninstr, {e: P.cnt[e] for e in ENGS})
    return nc


def _run(nc, in_maps):
    return run_bass_kernel_spmd(nc, in_maps, core_ids=list(range(NCORES))).results


def _tok_major(inp_p, inp_s):
    return np.concatenate([inp_p.reshape(-1, inp_p.shape[-1]), inp_s.reshape(-1, inp_s.shape[-1])], 0)


def _p1_inputs(inp, xT, c):
    H = 128
    W = inp['ev_w_in'][0]
    offs = np.cumsum([0, 3072, 1024, 8, 8, 1024, 1024, 1024, 1024, 1024, 8, 8])

    def colblk(i, sub=0):
        return W[:, offs[i] + sub * 1024 + c * H: offs[i] + sub * 1024 + (c + 1) * H]
    w1 = np.concatenate([colblk(0, 0), colblk(0, 1), colblk(0, 2), colblk(1),
                         colblk(4), colblk(5), colblk(6), colblk(7), colblk(8)], 1)
    wg = np.stack([W[:, offs[2] + c], W[:, offs[3] + c], W[:, offs[9] + c], W[:, offs[10] + c]], 1)
    wg = wg.reshape(KC, 128, 4).transpose(1, 0, 2).reshape(128, KC * 4)
    cw = inp['ev_conv_w'][0]
    convw = np.stack([cw[:, f * 1024 + c * H: f * 1024 + (c + 1) * H].T for f in range(3)], 1).reshape(H, 12)
    scal = np.zeros((128, 8), np.float32)
    scal[:, 0] = inp['ev_a_log'][0, c]
    scal[:, 1] = inp['ev_dt_bias'][0, c]
    scal[:, 2] = inp['ev_ig_bias'][0, c]
    scal[:, 3] = inp['ev_fg_bias'][0, c]
    scal[:, 4] = inp['ev_norm_a'][0]
    scal[:, 5] = inp['ev_norm_b'][0, c * H:(c + 1) * H]
    cs = inp['state_delta_conv'][0]
    conv0 = np.stack([cs[:, :, f * 1024 + c * H: f * 1024 + (c + 1) * H] for f in range(3)], 0)
    conv0 = np.ascontiguousarray(conv0.transpose(3, 1, 0, 2))
    A = np.ascontiguousarray
    return dict(xT=xT, w1=A(w1), wg=A(wg), convw=A(convw), scal=scal, consts=_consts_np(),
                S0=A(inp['state_delta_S'][0, :, c]), conv0=conv0,
                C0=A(inp['state_mlstm_C'][0, :, c]), n0=A(inp['state_mlstm_n'][0, :, c].T),
                m0=A(np.broadcast_to(inp['state_mlstm_m'][0, :, c][None, :], (128, DB))))


def _tile_tokens(Y, c):
    nk = Y.shape[0]
    out = np.zeros((nk, 128, NTT * 128), Y.dtype)
    out[:, :, :TOKC] = Y[:, :, c * TOKC:(c + 1) * TOKC]
    return np.ascontiguousarray(out.reshape(nk, 128, NTT, 128).transpose(2, 1, 0, 3))


def _pad_rows(x, c):
    out = np.zeros((NTT * 128, x.shape[1]), x.dtype)
    out[:TOKC] = x[c * TOKC:(c + 1) * TOKC]
    return out


def _p2_inputs(inp, xtok, Y, c):
    rep = lambda v: np.ascontiguousarray(np.broadcast_to(v[None, :], (128, v.shape[0])))
    return dict(yin=_tile_tokens(Y, c), wout=np.ascontiguousarray(inp['ev_w_out'][0]), xres=_pad_rows(xtok, c),
                lng=rep(inp['ev_ln_g'][0]), lnb=rep(inp['ev_ln_b'][0]), consts=_consts_np())


TWO_PI = 2.0 * math.pi
CW1 = 6.28125
CW2 = TWO_PI - CW1


def _masks3_np():
    m = np.zeros((128, 6, 512), np.float32)
    s = np.arange(128)[:, None]
    t = np.arange(512)[None, :]
    for r in range(4):
        m[:, r, :] = (t > 128 * r + s)
    m[:, 4, :] = ((t % 32) > s)
    m[:, 5, :] = t + 1.0
    return m


def build_phase3(ntiles_prompt=32, do_sample=True, debug=0):
    nc = bass.Bass("TRN2", target_bir_lowering=False)
    with ExitStack() as st:
        b = B(nc, st)
        P = b.P
        x1T = b.din("x1T", [D, NTOK], BF16)
        w3 = b.din("w3", [D, 768])
        consts = b.din("consts", [128, 8, 128])
        masks = b.din("masks", [128, 5, 512], BF16)
        tau = b.din("tau", [128, 512])
        kcT = b.din("kcT", [32, 128, DB, 128])
        vcc = b.din("vcc", [32, 128, DB, 128])
        s5p = b.din("s5p", [128, 4, 4])
        Bbd = b.din("Bbd", [128, 2, 4, 128])
        Cbd = b.din("Cbd", [128, 2, 4, 128])
        h0 = b.din("h0", [128, 2, 4, DB])
        y3T = b.dout("y3T", [384, NTOK], BF16)
        kout = b.dout("kout", [NTOK, 128])
        vout = b.dout("vout", [NTOK, 128])
        hout = b.dout("hout", [128, 2, 4, NB + DB])

        cst = b.sb("cst", [128, 8, 128])
        ident = cst[:, 0, :]
        msk = b.sb("msk", [128, 5, 512], BF16)
        trib = b.sb("trib", [128, 2, 128], BF16)
        w3b = b.sb("w3b", [128, KC, 768], BF16)
        wst = b.sb("wst", [128, 1, 768])
        xb = [b.sb("xb%d" % i, [128, KC, 512], BF16) for i in range(2)]
        KT = b.sb("KT", [128, SEQ], BF16)
        Vt = b.sb("Vt", [128, 64, 128], BF16)
        qb = b.sb("qb", [128, 512], BF16)
        kT32 = b.sb("kT32", [128, 512])
        vT32 = b.sb("vT32", [128, 512])
        zc = b.sb("zc", [128, 512])
        u32 = b.sb("u32", [128, 512])
        ub = b.sb("ub", [128, 512], BF16)
        szd = b.sb("szd", [128, 512], BF16)
        ktok = b.sb("ktok", [128, 4, 128])
        vtok = b.sb("vtok", [128, 4, 128])
        E = [b.sb("E%d" % i, [128, 512]) for i in range(3)]
        SP = [b.sb("SP%d" % i, [128, 512], BF16) for i in range(3)]
        Wt = b.sb("Wt", [128, 512])
        ATT = [b.sb("ATT%d" % i, [128, 512], BF16) for i in range(2)]
        ycb = b.sb("ycb", [128, 512], BF16)
        ysb = b.sb("ysb", [128, 512], BF16)
        sp_ = b.sb("sp_", [128, 4, 4])
        cr = b.sb("cr", [128, 4, 512])
        sr = b.sb("sr", [128, 4, 512])
        ki = b.sb("ki", [128, 512], mybir.dt.int32)
        s5c = b.sb("s5c", [128, 4, 16])
        Bb = b.sb("Bb", [128, 2, 4, 128], BF16)
        Cst = b.sb("Cst", [128, 2, 4, 128])
        Cb = b.sb("Cb", [128, 2, 4, 128], BF16)
        hst = b.sb("hst", [128, 2, 4, DB])
        h0s = b.sb("h0s", [128, 2, 4, DB])
        hfin = b.sb("hfin", [128, 2, 4, NB + DB])
        t1 = b.sb("t1", [128, 512])
        t2 = b.sb("t2", [128, 512])
        bre = b.sb("bre", [128, 512])
        bim = b.sb("bim", [128, 512])
        gre = b.sb("gre", [128, 512])
        gim = b.sb("gim", [128, 512])
        ang, kk = gre, gim
        hreb = b.sb("hreb", [128, 512], BF16)
        himb = b.sb("himb", [128, 512], BF16)
        ys32 = b.sb("ys32", [128, 512])
        kcs = [b.sb("kcs%d" % i, [128, DB, 128]) for i in range(2)]
        vcs = [b.sb("vcs%d" % i, [128, DB, 128]) for i in range(2)]
        kcb = [b.sb("kcb%d" % i, [128, DB, 128], BF16) for i in range(3)]
        vcb = [b.sb("vcb%d" % i, [128, DB, 128], BF16) for i in range(3)]
        vnew = b.sb("vnew", [128, DB, 128], BF16)

        pj = [b.ps("pj%d" % i) for i in range(2)]
        pZ = [b.ps("pZ%d" % i) for i in range(2)]
        pCm = b.ps("pCm")
        pO = b.ps("pO")
        pT = b.ps("pT")
        pY = b.ps("pY")
        ZK = ["pj2", "pj3"]
        CK, OK, TK, YK = "pj4", "pj5", "pj6", "pj7"

        b.dma("sp", cst[:], consts, [], ["cst"], "cst")
        b.dma("sp", msk[:], masks, [], ["msk"], "msk")
        b.dma("sp", t2[:], tau, [], ["t2"], "tau")
        b.dma("sp", sp_[:], s5p, [], ["sp_"], "sp_")
        b.dma("sp", h0s[:], h0, [], ["h0s"], "h0s")
        b.cp("dve", trib[:, 0, :], cst[:, 7, :], ["cst"], ["trib"])
        b.ts("dve", trib[:, 1, :], cst[:, 7, :], -1.0, ALU.mult, ["cst"], ["trib"], s2=1.0, op1=ALU.add)
        w3v = w3.rearrange("(kc p) n -> p kc n", p=128)
        for kc0 in range(KC):
            b.dma("sp", wst[:, 0, :], w3v[:, kc0, :], [], ["wst"], "wst")
            b.cp("dve" if kc0 % 2 else "pool", w3b[:, kc0, :], wst[:, 0, :], ["wst"], ["w3b"])
        for a in range(2):
            b.dma("sp", wst[:, 0, 0:512].rearrange("p (c n) -> p c n", n=128), Bbd[:, a], [], ["wst"], "wst")
            b.cp("dve", Bb[:, a], wst[:, 0, 0:512].rearrange("p (c n) -> p c n", n=128), ["wst"], ["Bb"])
        b.dma("sp", Cst[:], Cbd, [], ["Cst"], "Cst")
        c_ = lambda i: s5c[:, :, i]
        lre, lim = sp_[:, :, 0], sp_[:, :, 1]
        b.act(c_(13), sp_[:, :, 2], AF.Exp, ["sp_"], ["s5c"])
        dtc = c_(13)
        b.tt("dve", c_(8), lre, dtc, ALU.mult, ["sp_", "s5c"], ["s5c"])
        b.act(c_(0), c_(8), AF.Exp, ["s5c"], ["s5c"])
        b.tt("dve", c_(1), lim, dtc, ALU.mult, ["sp_", "s5c"], ["s5c"])
        b.cp("dve", bim[:], t2[:], ["t2"], ["bim"])
        for m in range(4):
            th = s5c[:, m, 1:2]
            b.ts("dve", ang[:], bim[:], th, ALU.mult, ["bim", "s5c"], ["gre"])
            b.ts("dve", kk[:], ang[:], 1.0 / TWO_PI, ALU.mult, ["gre"], ["gim"])
            b.cp("dve", ki[:], kk[:], ["gim"], ["ki"])
            b.cp("dve", kk[:], ki[:], ["ki"], ["gim"])
            b.stt(ang[:], kk[:], -CW1, ang[:], ALU.mult, ALU.add, ["gim", "gre"], ["gre"])
            b.stt(ang[:], kk[:], -CW2, ang[:], ALU.mult, ALU.add, ["gim", "gre"], ["gre"])
            for (dst, shift) in ((sr, 0.0), (cr, 0.5 * math.pi)):
                src = ang
                if shift != 0.0:
                    b.ts("dve", t1[:], ang[:], shift, ALU.add, ["gre"], ["t1"])
                    src = t1
                    sk = "t1"
                else:
                    b.cp("dve", t1[:], ang[:], ["gre"], ["t1"])
                    src, sk = t1, "t1"
                for _ in range(2):
                    b.ts("dve", t2[:], t1[:], math.pi, ALU.is_gt, ["t1"], ["t2"])
                    b.stt(t1[:], t2[:], -TWO_PI, t1[:], ALU.mult, ALU.add, ["t2", "t1"], ["t1"])
                    b.ts("dve", t2[:], t1[:], -math.pi, ALU.is_lt, ["t1"], ["t2"])
                    b.stt(t1[:], t2[:], TWO_PI, t1[:], ALU.mult, ALU.add, ["t2", "t1"], ["t1"])
                b.act(dst[:, m, :], t1[:], AF.Sin, ["t1"], ["cr" if dst is cr else "sr"])
        b.tt("dve", c_(2), c_(0), cr[:, :, 0], ALU.mult, ["s5c", "cr"], ["s5c"])
        b.tt("dve", c_(3), c_(0), sr[:, :, 0], ALU.mult, ["s5c", "sr"], ["s5c"])
        b.ts("dve", c_(2), c_(2), -1.0, ALU.add, ["s5c"], ["s5c"])
        b.tt("dve", c_(4), lre, lre, ALU.mult, ["sp_"], ["s5c"])
        b.tt("dve", c_(8), lim, lim, ALU.mult, ["sp_"], ["s5c"])
        b.tt("dve", c_(4), c_(4), c_(8), ALU.add, ["s5c"], ["s5c"])
        b.P.op("dve", lambda e, o=c_(12), i=c_(4): e.reciprocal(out=o, in_=i), reads=["s5c"], writes=["s5c"])
        b.tt("dve", c_(8), c_(2), lre, ALU.mult, ["s5c", "sp_"], ["s5c"])
        b.tt("dve", c_(9), c_(3), lim, ALU.mult, ["s5c", "sp_"], ["s5c"])
        b.tt("dve", c_(5), c_(8), c_(9), ALU.add, ["s5c"], ["s5c"])
        b.tt("dve", c_(5), c_(5), c_(12), ALU.mult, ["s5c"], ["s5c"])
        b.tt("dve", c_(8), c_(3), lre, ALU.mult, ["s5c", "sp_"], ["s5c"])
        b.tt("dve", c_(9), c_(2), lim, ALU.mult, ["s5c", "sp_"], ["s5c"])
        b.tt("dve", c_(6), c_(8), c_(9), ALU.subtract, ["s5c"], ["s5c"])
        b.tt("dve", c_(6), c_(6), c_(12), ALU.mult, ["s5c"], ["s5c"])
        b.ts("dve", c_(7), c_(6), -1.0, ALU.mult, ["s5c"], ["s5c"])
        b.tt("dve", c_(8), c_(5), c_(5), ALU.mult, ["s5c"], ["s5c"])
        b.tt("dve", c_(9), c_(6), c_(6), ALU.mult, ["s5c"], ["s5c"])
        b.tt("dve", c_(8), c_(8), c_(9), ALU.add, ["s5c"], ["s5c"])
        b.P.op("dve", lambda e, o=c_(9), i=c_(8): e.reciprocal(out=o, in_=i), reads=["s5c"], writes=["s5c"])
        b.tt("dve", c_(10), c_(5), c_(9), ALU.mult, ["s5c"], ["s5c"])
        b.tt("dve", c_(11), c_(7), c_(9), ALU.mult, ["s5c"], ["s5c"])
        for m in range(4):
            fre, fim, nfim = s5c[:, m, 5:6], s5c[:, m, 6:7], s5c[:, m, 7:8]
            b.ts("dve", t1[:, 0:128], Cst[:, 1, m, :], nfim, ALU.mult, ["Cst", "s5c"], ["t1"])
            b.stt(Cb[:, 0, m, :], Cst[:, 0, m, :], fre, t1[:, 0:128], ALU.mult, ALU.add, ["Cst", "s5c", "t1"], ["Cb"])
            b.ts("dve", t1[:, 0:128], Cst[:, 1, m, :], fre, ALU.mult, ["Cst", "s5c"], ["t1"])
            b.stt(t1[:, 0:128], Cst[:, 0, m, :], fim, t1[:, 0:128], ALU.mult, ALU.add, ["Cst", "s5c", "t1"], ["t1"])
            b.ts("dve", Cb[:, 1, m, :], t1[:, 0:128], -1.0, ALU.mult, ["t1"], ["Cb"])
            fr, fi = s5c[:, m, 10:11], s5c[:, m, 11:12]
            b.ts("dve", t2[:, 0:DB], h0s[:, 1, m, :], fi, ALU.mult, ["h0s", "s5c"], ["t2"])
            b.stt(t2[:, 16:16 + DB], h0s[:, 0, m, :], fr, t2[:, 0:DB], ALU.mult, ALU.subtract, ["h0s", "s5c", "t2"], ["t2"])
            b.ts("dve", t2[:, 0:DB], h0s[:, 1, m, :], fr, ALU.mult, ["h0s", "s5c"], ["t2"])
            b.stt(t2[:, 32:32 + DB], h0s[:, 0, m, :], fi, t2[:, 0:DB], ALU.mult, ALU.add, ["h0s", "s5c", "t2"], ["t2"])
            b.cp("dve", h0s[:, 0, m, :], t2[:, 16:16 + DB], ["t2"], ["h0s"])
            b.cp("dve", h0s[:, 1, m, :], t2[:, 32:32 + DB], ["t2"], ["h0s"])

        x1v = x1T.rearrange("(kc p) n -> p kc n", p=128)
        tiles = []
        for bi in range(NB):
            for t in range(ntiles_prompt // NB):
                tiles.append(dict(kind="p", seq=bi, ti=t, tok0=bi * SEQ + t * 512, N=512,
                                  first=(t == 0), last=(t == ntiles_prompt // NB - 1)))
        if do_sample:
            tiles.append(dict(kind="s", seq=None, ti=0, tok0=NTP, N=256, first=True, last=True))

        def load_x(i):
            T = tiles[i]
            s = i % 2
            for h in range(2):
                b.dma("sp", xb[s][:, h * 8:(h + 1) * 8, 0:T["N"]], x1v[:, h * 8:(h + 1) * 8, T["tok0"]:T["tok0"] + T["N"]],
                      [], ["xb%d" % s], "xb%d_%d" % (s, h))

        def load_cache(j):
            s = j % 3
            s2 = j % 2
            b.dma("sp", kcs[s2][:], kcT[j], [], ["kcs%d" % s2], "kcs%d" % s2)
            b.dma("sp", vcs[s2][:], vcc[j], [], ["vcs%d" % s2], "vcs%d" % s2)
            b.cp("pool", kcb[s][:], kcs[s2][:], ["kcs%d" % s2], ["kcb%d" % s])
            b.cp("pool", vcb[s][:], vcs[s2][:], ["vcs%d" % s2], ["vcb%d" % s])

        load_x(0)
        try:
            for i, T in enumerate(tiles):
                s = i % 2
                N = T["N"]
                samp = T["kind"] == "s"
                xk = "xb%d" % s
                if i + 1 < len(tiles):
                    load_x(i + 1)
                for f in range(6):
                    bank, bk = pj[f % 2], "pj%d" % (f % 2)
                    for kc in range(KC):
                        b.mm(bank[:, 0:N], w3b[:, kc, f * 128:(f + 1) * 128], xb[s][:, kc, 0:N], ["w3b", xk], [bk],
                             start=(kc == 0), stop=(kc == KC - 1))
                    if f == 0:
                        b.act(qb[:, 0:N], bank[:, 0:N], AF.Copy, [bk], ["qb"], scale=128.0 ** -0.5)
                    elif f == 1:
                        b.cp("dve", kT32[:, 0:N], bank[:, 0:N], [bk], ["kT32"])
                        if not samp:
                            b.cp("act", KT[:, T["ti"] * 512:T["ti"] * 512 + N], bank[:, 0:N], [bk], ["KT"])
                        else:
                            b.cp("act", KT[:, 0:N], bank[:, 0:N], [bk], ["KT"])
                    elif f == 2:
                        b.cp("dve", vT32[:, 0:N], bank[:, 0:N], [bk], ["vT32"])
                    elif f == 3:
                        b.act(zc[:, 0:N], bank[:, 0:N], AF.Silu, [bk], ["zc"])
                    elif f == 4:
                        b.cp("dve", u32[:, 0:N], bank[:, 0:N], [bk], ["u32"])
                        b.cp("act", ub[:, 0:N], bank[:, 0:N], [bk], ["ub"])
                    else:
                        b.act(szd[:, 0:N], bank[:, 0:N], AF.Silu, [bk], ["szd"])
                        b.dma("sp", y3T[256:384, T["tok0"]:T["tok0"] + N], szd[:, 0:N], ["szd"], ["y3T"], "szd_o")
                if debug == 1:
                    raise _Stop
                nblk = N // 128
                for (src, sk, dst, dk, outap) in ((kT32, "kT32", ktok, "ktok", kout), (vT32, "vT32", vtok, "vtok", vout)):
                    for jb in range(nblk):
                        b.tr(pT[:, jb * 128:(jb + 1) * 128], src[:, jb * 128:(jb + 1) * 128], ident, [sk, "cst"], [TK])
                    b.cp("dve", dst[:, 0:nblk, :].rearrange("p a d -> p (a d)"), pT[:, 0:nblk * 128], [TK], [dk])
                    b.dma("sp", outap[T["tok0"]:T["tok0"] + N, :].rearrange("(a p) d -> p a d", p=128), dst[:, 0:nblk, :],
                          [dk], ["o_" + dk], "o_" + dk)
                if not samp:
                    b.cp("pool", Vt[:, T["ti"] * 4:T["ti"] * 4 + 4, :], vtok[:], ["vtok"], ["Vt"])
                else:
                    for h in range(2):
                        for q4 in range(4):
                            bi = h * 4 + q4
                            b.tr(pT[0:32, q4 * 128:(q4 + 1) * 128], vT32[:, bi * 32:(bi + 1) * 32], ident,
                                 ["vT32", "cst"], [TK])
                        b.cp("dve", vnew[0:32, h * 4:(h + 1) * 4, :].rearrange("p a d -> p (a d)"), pT[0:32, 0:512],
                             [TK], ["vnew"])
                if debug == 2:
                    raise _Stop
                def g_att():
                    if not samp:
                        jmax = T['ti'] * 4 + 3
                        steps = list(range(jmax, -1, -1))
                    else:
                        steps = list(range(32, -1, -1))
                        load_cache(31)
                        yield

                    def stageA(si):
                        j = steps[si]
                        z = si % 2
                        e3 = si % 3
                        Zb, zk = (pZ[z], ZK[z])
                        if not samp:
                            Kp = 128
                            b.mm(Zb[:, 0:N], KT[:, j * 128:(j + 1) * 128], qb[:, 0:N], ['KT', 'qb'], [zk])
                        elif j == 32:
                            Kp = 32
                            for bi in range(DB):
                                b.mm(Zb[0:32, bi * 32:(bi + 1) * 32], KT[:, bi * 32:(bi + 1) * 32], qb[:, bi * 32:(bi + 1) * 32], ['KT', 'qb'], [zk])
                        else:
                            Kp = 128
                            if j - 1 >= 0:
                                load_cache(j - 1)
                            for bi in range(DB):
                                b.mm(Zb[:, bi * 32:(bi + 1) * 32], kcb[j % 3][:, bi, :], qb[:, bi * 32:(bi + 1) * 32], ['kcb%d' % (j % 3), 'qb'], [zk])
                        b.act(E[e3][0:Kp, 0:N], Zb[0:Kp, 0:N], AF.Exp, [zk], ['E%d' % e3])
                        b.act(SP[e3][0:Kp, 0:N], E[e3][0:Kp, 0:N], AF.Ln, ['E%d' % e3], ['SP%d' % e3], bias=1.0)
                        mk = None
                        if samp and j == 32:
                            mk = msk[0:32, 4, 0:N]
                        elif not samp and j >= T['ti'] * 4:
                            mk = msk[:, j - T['ti'] * 4, 0:N]
                        if mk is not None:
                            b.tt('pool', E[e3][0:Kp, 0:N], E[e3][0:Kp, 0:N], mk, ALU.mult, ['E%d' % e3, 'msk'], ['E%d' % e3])
                            b.tt('pool', SP[e3][0:Kp, 0:N], SP[e3][0:Kp, 0:N], mk, ALU.mult, ['SP%d' % e3, 'msk'], ['SP%d' % e3])
                        return Kp

                    def emitAV(si, Kp):
                        j = steps[si]
                        z = si % 2
                        if not samp:
                            b.mm(pO[:, 0:N], Vt[:, j, :], ATT[z][:, 0:N], ['Vt', 'ATT%d' % z], [OK], start=False, stop=True)
                        elif j == 32:
                            for bi in range(DB):
                                b.mm(pO[:, bi * 32:(bi + 1) * 32], vnew[0:32, bi, :], ATT[z][0:32, bi * 32:(bi + 1) * 32], ['vnew', 'ATT%d' % z], [OK], start=False, stop=True)
                        else:
                            for bi in range(DB):
                                b.mm(pO[:, bi * 32:(bi + 1) * 32], vcb[j % 3][:, bi, :], ATT[z][:, bi * 32:(bi + 1) * 32], ['vcb%d' % (j % 3), 'ATT%d' % z], [OK], start=False, stop=True)
                    b.memset('dve', pCm[:, 0:N], 0.0, [CK])
                    yield
                    b.memset('dve', pO[:, 0:N], 0.0, [OK])
                    yield
                    LA = 1 if samp else 2
                    nst = len(steps)
                    KpA = {}
                    for si0 in range(min(LA, nst)):
                        KpA[si0] = stageA(si0)
                        yield
                    for si, j in enumerate(steps):
                        z = si % 2
                        e3 = si % 3
                        Kp = KpA[si]
                        b.mm(pCm[:, 0:N], trib[0:Kp, 0, :], SP[e3][0:Kp, 0:N], ['trib', 'SP%d' % e3], [CK], start=False, stop=True)
                        yield
                        b.act(Wt[0:Kp, 0:N], pCm[0:Kp, 0:N], AF.Exp, [CK], ['Wt'], scale=-1.0)
                        yield
                        b.tt('dve', ATT[z][0:Kp, 0:N], E[e3][0:Kp, 0:N], Wt[0:Kp, 0:N], ALU.mult, ['E%d' % e3, 'Wt'], ['ATT%d' % z])
                        yield
                        if si >= 1:
                            emitAV(si - 1, KpA[si - 1])
                            yield
                        if si + LA < nst:
                            KpA[si + LA] = stageA(si + LA)
                            yield
                        b.mm(pCm[:, 0:N], trib[0:Kp, 1, :], SP[e3][0:Kp, 0:N], ['trib', 'SP%d' % e3], [CK], start=False, stop=True)
                        yield
                    emitAV(nst - 1, KpA[nst - 1])
                    yield
                    b.tt('dve', ycb[:, 0:N], pO[:, 0:N], zc[:, 0:N], ALU.mult, [OK, 'zc'], ['ycb'])
                    yield
                    b.dma('sp', y3T[0:128, T['tok0']:T['tok0'] + N], ycb[:, 0:N], ['ycb'], ['y3T'], 'yc_o')
                    yield
                    yield
                def g_s5():
                    nseq = DB if samp else 1
                    Ls = N // nseq
                    v3 = lambda ap: ap.rearrange('p (s l) -> p s l', l=Ls)

                    def tab(tb, m):
                        if samp:
                            return tb[:, m, 0:Ls].unsqueeze(1).broadcast_to([128, nseq, Ls])
                        return v3(tb[:, m, 0:N])
                    for m in range(4):
                        b.mm(pj[0][:, 0:N], Bb[:, 0, m, :], ub[:, 0:N], ['Bb', 'ub'], ['pj0'])
                        yield
                        b.mm(pj[1][:, 0:N], Bb[:, 1, m, :], ub[:, 0:N], ['Bb', 'ub'], ['pj1'])
                        yield
                        Bre, Bim = (v3(pj[0][:, 0:N]), v3(pj[1][:, 0:N]))
                        b.tt('dve', v3(t1[:, 0:N]), Bre, tab(cr, m), ALU.mult, ['pj0', 'cr'], ['t1'])
                        yield
                        b.tt('dve', v3(t2[:, 0:N]), Bim, tab(sr, m), ALU.mult, ['pj1', 'sr'], ['t2'])
                        yield
                        b.tt('pool', bre[:, 0:N], t1[:, 0:N], t2[:, 0:N], ALU.add, ['t1', 't2'], ['bre'])
                        yield
                        b.tt('dve', v3(t1[:, 0:N]), Bim, tab(cr, m), ALU.mult, ['pj1', 'cr'], ['t1'])
                        yield
                        b.tt('dve', v3(t2[:, 0:N]), Bre, tab(sr, m), ALU.mult, ['pj0', 'sr'], ['t2'])
                        yield
                        b.tt('pool', bim[:, 0:N], t1[:, 0:N], t2[:, 0:N], ALU.subtract, ['t1', 't2'], ['bim'])
                        yield
                        rbc = s5c[:, m, 0:1].to_broadcast([128, Ls])
                        for sq in range(nseq):
                            col = sq if samp else 0
                            for src, sk, dst, dk, a in ((bre, 'bre', gre, 'gre', 0), (bim, 'bim', gim, 'gim', 1)):
                                if samp:
                                    init = h0s[:, a, m, sq:sq + 1]
                                    ik = 'h0s'
                                elif T['first']:
                                    init, ik = (0.0, None)
                                else:
                                    init, ik = (hst[:, a, m, 0:1], 'hst')
                                b.P.op('dve', lambda e, o=dst[:, sq * Ls:(sq + 1) * Ls], d0=rbc, d1=src[:, sq * Ls:(sq + 1) * Ls], ini=init: e.tensor_tensor_scan(out=o, data0=d0, data1=d1, initial=ini, op0=ALU.mult, op1=ALU.add), reads=[sk, 's5c'] + ([ik] if ik else []), writes=[dk])
                                yield
                        b.tt('dve', v3(t1[:, 0:N]), v3(gre[:, 0:N]), tab(cr, m), ALU.mult, ['gre', 'cr'], ['t1'])
                        yield
                        b.tt('dve', v3(t2[:, 0:N]), v3(gim[:, 0:N]), tab(sr, m), ALU.mult, ['gim', 'sr'], ['t2'])
                        yield
                        b.tt('pool', hreb[:, 0:N], t1[:, 0:N], t2[:, 0:N], ALU.subtract, ['t1', 't2'], ['hreb'])
                        yield
                        lastv = lambda ap: v3(ap[:, 0:N])[:, :, Ls - 1]
                        if samp:
                            cols = slice(NB, NB + DB)
                        else:
                            cols = slice(T['seq'], T['seq'] + 1)
                        b.tt('dve', hst[:, 0, m, 0:nseq], lastv(t1), lastv(t2), ALU.subtract, ['t1', 't2'], ['hst'])
                        yield
                        b.tt('dve', v3(t1[:, 0:N]), v3(gre[:, 0:N]), tab(sr, m), ALU.mult, ['gre', 'sr'], ['t1'])
                        yield
                        b.tt('dve', v3(t2[:, 0:N]), v3(gim[:, 0:N]), tab(cr, m), ALU.mult, ['gim', 'cr'], ['t2'])
                        yield
                        b.tt('pool', himb[:, 0:N], t1[:, 0:N], t2[:, 0:N], ALU.add, ['t1', 't2'], ['himb'])
                        yield
                        b.tt('dve', hst[:, 1, m, 0:nseq], lastv(t1), lastv(t2), ALU.add, ['t1', 't2'], ['hst'])
                        yield
                        if T['last']:
                            fre, fim, nfim = (s5c[:, m, 5:6], s5c[:, m, 6:7], s5c[:, m, 7:8])
                            b.ts('dve', t1[:, 0:nseq], hst[:, 1, m, 0:nseq], nfim, ALU.mult, ['hst', 's5c'], ['t1'])
                            yield
                            b.stt(hfin[:, 0, m, cols], hst[:, 0, m, 0:nseq], fre, t1[:, 0:nseq], ALU.mult, ALU.add, ['hst', 's5c', 't1'], ['hfin'])
                            yield
                            b.ts('dve', t1[:, 0:nseq], hst[:, 1, m, 0:nseq], fre, ALU.mult, ['hst', 's5c'], ['t1'])
                            yield
                            b.stt(hfin[:, 1, m, cols], hst[:, 0, m, 0:nseq], fim, t1[:, 0:nseq], ALU.mult, ALU.add, ['hst', 's5c', 't1'], ['hfin'])
                            yield
                        b.mm(pY[:, 0:N], Cb[:, 0, m, :], hreb[:, 0:N], ['Cb', 'hreb'], [YK], start=m == 0, stop=False)
                        yield
                        b.mm(pY[:, 0:N], Cb[:, 1, m, :], himb[:, 0:N], ['Cb', 'himb'], [YK], start=False, stop=m == 3)
                        yield
                    b.stt(ys32[:, 0:N], u32[:, 0:N], sp_[:, 0, 3:4], pY[:, 0:N], ALU.mult, ALU.add, ['u32', 'sp_', YK], ['ys32'])
                    yield
                    b.tt('pool', t1[:, 0:N], ys32[:, 0:N], ys32[:, 0:N], ALU.mult, ['ys32'], ['t1'])
                    yield
                    b.ts('dve', t1[:, 0:N], t1[:, 0:N], 0.044715, ALU.mult, ['t1'], ['t1'], s2=1.0, op1=ALU.add)
                    yield
                    b.tt('pool', t1[:, 0:N], t1[:, 0:N], ys32[:, 0:N], ALU.mult, ['t1', 'ys32'], ['t1'])
                    yield
                    b.act(t1[:, 0:N], t1[:, 0:N], AF.Sigmoid, ['t1'], ['t1'], scale=2.0 * math.sqrt(2.0 / math.pi))
                    yield
                    b.tt('dve', ysb[:, 0:N], t1[:, 0:N], ys32[:, 0:N], ALU.mult, ['t1', 'ys32'], ['ysb'])
                    yield
                    b.dma('sp', y3T[128:256, T['tok0']:T['tok0'] + N], ysb[:, 0:N], ['ysb'], ['y3T'], 'ys_o')
                    yield
                    yield
                _interleave(g_att(), g_s5())
        except _Stop:
            pass
        b.dma("sp", hout, hfin[:], ["hfin"], ["hout"], "hout")
        P.finalize()
        print("phase3 instrs", P.ninstr, {e: P.cnt[e] for e in ENGS})
    return nc


def _p3_inputs(inp, x1T, c):
    A = np.ascontiguousarray
    W = inp['od_w_in'][0]
    blk = lambda i: W[:, i * 1024 + c * 128: i * 1024 + (c + 1) * 128]
    w3 = np.concatenate([blk(0), blk(1), blk(2), blk(3), blk(4), blk(5)], 1)
    kc = inp['cache_sb_k'][0, :, :, c, :]
    vc = inp['cache_sb_v'][0, :, :, c, :]
    kcT = A(kc.reshape(DB, 32, 128, 128).transpose(1, 3, 0, 2))
    vcc = A(vc.reshape(DB, 32, 128, 128).transpose(1, 2, 0, 3))
    G = slice(8 * c, 8 * c + 8)
    lre = inp['od_lam_re'][0, G]
    lim = inp['od_lam_im'][0, G]
    dt = np.broadcast_to(inp['od_log_dt'][0, G][:, None], (8, 64))
    s5p = np.zeros((128, 4, 4), np.float32)
    qm = lambda a: a.reshape(4, 128).T
    s5p[:, :, 0] = qm(lre)
    s5p[:, :, 1] = qm(lim)
    s5p[:, :, 2] = qm(dt)
    s5p[:, 0, 3] = inp['od_d'][0, c * 128:(c + 1) * 128]
    Bbd = np.zeros((128, 2, 4, 128), np.float32)
    Cbd = np.zeros((128, 2, 4, 128), np.float32)
    for a, (bk, ck) in enumerate((('od_b_re', 'od_c_re'), ('od_b_im', 'od_c_im'))):
        bb = inp[bk][0, G]
        cc = inp[ck][0, G]
        for g in range(8):
            m, g2 = g // 2, g % 2
            Bbd[g * 16:(g + 1) * 16, a, m, g2 * 64:(g2 + 1) * 64] = bb[g].T
            Cbd[g2 * 64:(g2 + 1) * 64, a, m, g * 16:(g + 1) * 16] = cc[g].T
    h0 = np.zeros((128, 2, 4, DB), np.float32)
    for a, k in enumerate(('state_s5_re', 'state_s5_im')):
        hs = inp[k][0, :, G]
        h0[:, a] = hs.reshape(DB, 4, 128).transpose(2, 1, 0)
    return dict(x1T=x1T, w3=A(w3), consts=_consts_np(), masks=np.ascontiguousarray(_masks3_np()[:, 0:5]).astype(ml_dtypes.bfloat16), tau=np.ascontiguousarray(_masks3_np()[:, 5]), kcT=kcT, vcc=vcc, s5p=s5p,
                Bbd=Bbd, Cbd=Cbd, h0=h0)


def _p4_inputs(inp, x1tok, Y3, c):
    rep = lambda v: np.ascontiguousarray(np.broadcast_to(v[None, :], (128, v.shape[0])))
    return dict(yin=_tile_tokens(Y3, c), wout=np.ascontiguousarray(inp['od_w_out'][0]), xres=_pad_rows(x1tok, c),
                lng=rep(inp['od_ln_g'][0]), lnb=rep(inp['od_ln_b'][0]), consts=_consts_np(),
                wglu=np.ascontiguousarray(inp['od_w_glu'][0]),
                bglu=np.ascontiguousarray(inp['od_b_glu'][0].reshape(8, 128).T))


_NC_CACHE = {}


def _get_nc(name, fn):
    if name not in _NC_CACHE:
        _NC_CACHE[name] = fn()
    return _NC_CACHE[name]


def kernel(**inp):
    inp = {k: np.asarray(v) for k, v in inp.items()}
    R = range(NCORES)
    xtok = _tok_major(inp['x_prompt'], inp['x_sample'])
    xT = np.ascontiguousarray(xtok.T)
    r1 = _run(_get_nc("p1", build_phase1), [_p1_inputs(inp, xT, c) for c in R])
    del xT
    Y = np.stack([r1[k]['yT'][0:128] for k in R] + [r1[k]['yT'][128:256] for k in R])
    r2 = _run(_get_nc("p2", lambda: build_out_phase(False, True)), [_p2_inputs(inp, xtok, Y, c) for c in R])
    x1tok = np.concatenate([r2[c]['xo'][:TOKC] for c in R], 0)
    x1T = np.ascontiguousarray(
        np.concatenate([r2[c]['xoT'].transpose(2, 1, 0, 3).reshape(D, -1)[:, :TOKC] for c in R], 1))
    r3 = _run(_get_nc("p3", build_phase3), [_p3_inputs(inp, x1T, c) for c in R])
    Y3 = np.stack([r3[k]['y3T'][0:128] for k in R] + [r3[k]['y3T'][128:256] for k in R]
                  + [r3[k]['y3T'][256:384] for k in R])
    r4 = _run(_get_nc("p4", lambda: build_out_phase(True, False)), [_p4_inputs(inp, x1tok, Y3, c) for c in R])
    ytok = np.concatenate([r4[c]['xo'][:TOKC] for c in R], 0)
    f32 = np.float32
    y_prompt = ytok[:NTP].reshape(NB, SEQ, D).astype(f32)
    y_sample = ytok[NTP:].reshape(DB, DS, D).astype(f32)

    def heads(key, sl):
        return np.stack([r1[c][key][sl] for c in R], 1)[None].astype(f32)
    pS, sS = slice(0, NB), slice(NB, NB + DB)
    dS_p, dS_s = heads('oS', pS), heads('oS', sS)
    mC_p, mC_s = heads('oC', pS), heads('oC', sS)

    def convout(sl):
        n = sl.stop - sl.start
        o = np.zeros((1, n, 3, 3072), f32)
        for c in R:
            oc = r1[c]['oconv'][:, sl]
            for f in range(3):
                o[0, :, :, f * 1024 + c * 128: f * 1024 + (c + 1) * 128] = oc[:, :, f, :].transpose(1, 2, 0)
        return o
    dc_p, dc_s = convout(pS), convout(sS)
    mn = lambda sl: np.stack([r1[c]['on'][:, sl].T for c in R], 1)[None].astype(f32)
    mm = lambda sl: np.stack([r1[c]['om'][0, sl] for c in R], 1)[None].astype(f32)

    def kv(key, lo, hi, shape):
        return np.stack([r3[c][key][lo:hi] for c in R], 1).reshape(shape).astype(f32)
    sk_p = kv('kout', 0, NTP, (NB, SEQ, 8, 128))[None]
    sk_s = kv('kout', NTP, NTOK, (DB, DS, 8, 128))[None]
    sv_p = kv('vout', 0, NTP, (NB, SEQ, 8, 128))[None]
    sv_s = kv('vout', NTP, NTOK, (DB, DS, 8, 128))[None]

    def s5state(a, sl):
        n = sl.stop - sl.start
        o = np.zeros((1, n, 64, 64), f32)
        for c in R:
            ho = r3[c]['hout'][:, a, :, sl]
            o[0, :, 8 * c:8 * c + 8, :] = ho.transpose(2, 1, 0).reshape(n, 8, 64)
        return o
    return (y_prompt, y_sample, dS_p, dS_s, dc_p, dc_s, mC_p, mC_s, mn(pS), mn(sS), mm(pS), mm(sS),
            sk_p, sk_s, sv_p, sv_s, s5state(0, pS), s5state(0, sS), s5state(1, pS), s5state(1, sS))
```
